# Optimizing a Trainium2 kernel written in Bass

```python
import math
import jax, jax.numpy as jnp
from jax import lax
import numpy as np

D_MODEL = 1024
BATCH = 16
SEQ = 2048
DEPTH = 4

GRID_W = 64
CTX_LEN = 256
HEAD_DIM = 64
GROUP_WIDTH = D_MODEL // 4
A_HEADS = GROUP_WIDTH // HEAD_DIM
A_QK_DIM = HEAD_DIM // 2
A_V_DIM = HEAD_DIM
B_HEADS = GROUP_WIDTH // HEAD_DIM
B_KEY_DIM = HEAD_DIM
B_VAL_DIM = HEAD_DIM
CHUNK = 64
C_Q_HEADS = GROUP_WIDTH // HEAD_DIM
C_KV_HEADS = C_Q_HEADS // 2
D_GROUPS = GROUP_WIDTH // HEAD_DIM
D_GROUP_DIM = HEAD_DIM
MIX_WIDTH = A_HEADS * A_V_DIM + B_HEADS * B_VAL_DIM + C_Q_HEADS * HEAD_DIM + D_GROUPS * D_GROUP_DIM
IN_SPLITS = (
    A_HEADS * 2 * A_QK_DIM, A_HEADS * 2 * A_QK_DIM, A_HEADS * A_V_DIM,
    B_HEADS * B_KEY_DIM, B_HEADS * B_KEY_DIM, B_HEADS * B_KEY_DIM,
    B_HEADS * B_VAL_DIM, B_HEADS * B_VAL_DIM,
    C_Q_HEADS * HEAD_DIM, C_KV_HEADS * HEAD_DIM, C_KV_HEADS * HEAD_DIM,
    D_GROUPS * D_GROUP_DIM,
)
IN_WIDTH = sum(IN_SPLITS)
FFN_HIDDEN = -(-8 * D_MODEL // (3 * 256)) * 256
Q_BLOCK = 128
ROPE_THETA = 10000.0
RMS_EPS = 1e-6

kernel_name = 'hybrid_parallel_heads_flow_block'

F32 = jnp.float32


def rmsnorm(x, gain):
    xf = x.astype(F32)
    y = xf * lax.rsqrt(jnp.mean(xf * xf, axis=-1, keepdims=True) + RMS_EPS)
    return (y * gain.astype(F32)).astype(x.dtype)


def modulate(h, shift, scale):
    return h * (1 + scale) + shift


def split_proj(p):
    offsets = np.cumsum(IN_SPLITS)[:-1].tolist()
    return jnp.split(p, offsets, axis=-1)


def axial_rope_tables(rows, dim):
    row = jnp.repeat(jnp.arange(rows), GRID_W).astype(F32)
    col = jnp.tile(jnp.arange(GRID_W), rows).astype(F32)
    n_freq = dim // 4
    inv_freq = ROPE_THETA ** (-jnp.arange(n_freq, dtype=F32) / n_freq)
    ang_r = row[:, None] * inv_freq[None, :]
    ang_c = col[:, None] * inv_freq[None, :]
    return (jnp.cos(ang_r), jnp.sin(ang_r), jnp.cos(ang_c), jnp.sin(ang_c))


def _rotate(x, cos, sin):
    n = x.shape[-1] // 2
    x1, x2 = x[..., :n], x[..., n:]
    cs, sn = cos[None, :, None, :], sin[None, :, None, :]
    return jnp.concatenate([x1 * cs - x2 * sn, x2 * cs + x1 * sn], axis=-1)


def apply_axial_rope(x, tabs):
    cos_r, sin_r, cos_c, sin_c = tabs
    xf = x.astype(F32)
    half = x.shape[-1] // 2
    out = jnp.concatenate([_rotate(xf[..., :half], cos_r, sin_r),
                           _rotate(xf[..., half:], cos_c, sin_c)], axis=-1)
    return out.astype(x.dtype)


def sweep_query_blocks(fn, qs):
    B, S = qs[0].shape[:2]
    nb = S // Q_BLOCK
    blocks = tuple(t.reshape(B, nb, Q_BLOCK, *t.shape[2:]).swapaxes(0, 1) for t in qs)
    out = lax.map(lambda bl: fn(*bl), blocks)
    return out.swapaxes(0, 1).reshape(B, S, *out.shape[3:])


def diff_attn_core(q1, q2, k1, k2, v, lam):
    scale = A_QK_DIM ** -0.5
    s1 = jnp.einsum('bqhd,bkhd->bhqk', q1.astype(F32), k1.astype(F32)) * scale
    s2 = jnp.einsum('bqhd,bkhd->bhqk', q2.astype(F32), k2.astype(F32)) * scale
    w = jax.nn.softmax(s1, axis=-1) - lam * jax.nn.softmax(s2, axis=-1)
    return jnp.einsum('bhqk,bkhd->bqhd', w, v.astype(F32)).astype(v.dtype)


def diff_attention_mixer(q, k, v, cq, ck, cv, lam_params, sub_gain, layer_idx, rope, need_ctx):
    B, S = q.shape[:2]
    Tc = cq.shape[1]

    def split_qk(t, pos):
        T = t.shape[1]
        t = t.reshape(B, T, A_HEADS * 2, A_QK_DIM)
        if pos is not None:
            t = apply_axial_rope(t, pos)
        t = t.reshape(B, T, A_HEADS, 2, A_QK_DIM)
        return t[..., 0, :], t[..., 1, :]

    q1, q2 = split_qk(q, rope)
    k1, k2 = split_qk(k, rope)
    cq1, cq2 = split_qk(cq, None)
    ck1, ck2 = split_qk(ck, None)
    vx = v.reshape(B, S, A_HEADS, A_V_DIM)
    vc = cv.reshape(B, Tc, A_HEADS, A_V_DIM)

    lam_init = 0.8 - 0.6 * math.exp(-0.3 * layer_idx)
    lp = lam_params.astype(F32)
    lam = jnp.exp(jnp.sum(lp[0] * lp[1])) - jnp.exp(jnp.sum(lp[2] * lp[3])) + lam_init

    k1_all = jnp.concatenate([ck1, k1], axis=1)
    k2_all = jnp.concatenate([ck2, k2], axis=1)
    v_all = jnp.concatenate([vc, vx], axis=1)
    o_x = sweep_query_blocks(lambda a, b: diff_attn_core(a, b, k1_all, k2_all, v_all, lam), (q1, q2))

    def finish(o):
        return (rmsnorm(o, sub_gain) * (1 - lam_init)).reshape(o.shape[0], o.shape[1], -1)

    out_c = finish(diff_attn_core(cq1, cq2, ck1, ck2, vc, lam)) if need_ctx else None
    return finish(o_x), out_c


def forget_gate(z, lb):
    sig = jax.nn.sigmoid(z)
    log_f = jnp.log(lb + (1 - lb) * sig)
    key = (1 - lb) * jax.nn.sigmoid(-z)
    return log_f, key


def gla_chunked(q, k, v, log_f, s0):
    B, T, H, _ = q.shape
    dv = v.shape[-1]
    n = T // CHUNK

    def to_chunks(t):
        return t.reshape(B, n, CHUNK, H, t.shape[-1]).transpose(1, 0, 3, 2, 4)

    incl = jnp.tril(jnp.ones((CHUNK, CHUNK), dtype=bool))[:, :, None]

    def step(state, inp):
        qc, kc, vc, gc = inp
        b = jnp.cumsum(gc, axis=2)
        o_inter = jnp.einsum('bhtd,bhde->bhte', qc * jnp.exp(b), state)
        rel = jnp.where(incl, b[:, :, :, None, :] - b[:, :, None, :, :], 0.0)
        decay = jnp.where(incl, jnp.exp(rel), 0.0)
        scores = jnp.einsum('bhtd,bhsd,bhtsd->bhts', qc, kc, decay)
        o_intra = jnp.einsum('bhts,bhse->bhte', scores, vc)
        b_end = b[:, :, -1:, :]
        new_state = (jnp.exp(b_end[:, :, 0, :])[..., None] * state
                     + jnp.einsum('bhsd,bhse->bhde', kc * jnp.exp(b_end - b), vc))
        return new_state, o_inter + o_intra

    s_end, o = lax.scan(step, s0, (to_chunks(q), to_chunks(k), to_chunks(v), to_chunks(log_f)))
    return o.transpose(1, 0, 3, 2, 4).reshape(B, T, H, dv), s_end


def gla_final_state(k, v, log_f):
    b = jnp.cumsum(log_f, axis=1)
    w = k * jnp.exp(b[:, -1:] - b)
    return jnp.einsum('bthd,bthe->bhde', w, v)


def hgrn2_mixer(q, f_fwd, f_bwd, i, g, cq, cf_fwd, cf_bwd, ci, cg, lb_fwd, lb_bwd, norm_gain, need_ctx):
    B, S = q.shape[:2]
    H, dk, dv = B_HEADS, B_KEY_DIM, B_VAL_DIM

    def heads(t, d):
        return t.reshape(t.shape[0], t.shape[1], H, d).astype(F32)

    def flip(t):
        return jnp.flip(t, axis=1)

    def ident(t):
        return t

    qx, ix, qc, ic = heads(q, dk), heads(i, dv), heads(cq, dk), heads(ci, dv)
    s0 = jnp.zeros((B, H, dk, dv), F32)
    o_x = jnp.zeros((B, S, H, dv), F32)
    o_c = jnp.zeros((B, cq.shape[1], H, dv), F32) if need_ctx else None
    for zx, zc, lb, order in ((f_fwd, cf_fwd, lb_fwd, ident), (f_bwd, cf_bwd, lb_bwd, flip)):
        lb = lb.reshape(H, dk)
        lfx, kx = forget_gate(heads(zx, dk), lb)
        lfc, kc = forget_gate(heads(zc, dk), lb)
        if need_ctx:
            oc, state = gla_chunked(order(qc), order(kc), order(ic), order(lfc), s0)
            o_c = o_c + order(oc)
        else:
            state = gla_final_state(order(kc), order(ic), order(lfc))
        ox, _ = gla_chunked(order(qx), order(kx), order(ix), order(lfx), state)
        o_x = o_x + order(ox)

    def finish(o, gate):
        T = o.shape[1]
        y = rmsnorm(o, norm_gain).reshape(B, T, H * dv) * jax.nn.silu(gate.astype(F32))
        return y.astype(gate.dtype)

    return finish(o_x, g), (finish(o_c, cg) if need_ctx else None)


def gqa_core(q, k, v):
    s = jnp.einsum('bqhgd,bkhd->bhgqk', q.astype(F32), k.astype(F32)) * HEAD_DIM ** -0.5
    p = jax.nn.softmax(s, axis=-1)
    return jnp.einsum('bhgqk,bkhd->bqhgd', p, v.astype(F32)).astype(v.dtype)


def gqa_mixer(q, k, v, cq, ck, cv, q_gain, k_gain, rope, need_ctx):
    B, S = q.shape[:2]
    G = C_Q_HEADS // C_KV_HEADS

    def prep(qt, kt, vt, pos):
        T = qt.shape[1]
        qh = rmsnorm(qt.reshape(B, T, C_Q_HEADS, HEAD_DIM), q_gain)
        kh = rmsnorm(kt.reshape(B, T, C_KV_HEADS, HEAD_DIM), k_gain)
        if pos is not None:
            qh = apply_axial_rope(qh, pos)
            kh = apply_axial_rope(kh, pos)
        return qh.reshape(B, T, C_KV_HEADS, G, HEAD_DIM), kh, vt.reshape(B, T, C_KV_HEADS, HEAD_DIM)

    qx, kx, vx = prep(q, k, v, rope)
    qc, kc, vc = prep(cq, ck, cv, None)
    k_all = jnp.concatenate([kc, kx], axis=1)
    v_all = jnp.concatenate([vc, vx], axis=1)
    out_x = sweep_query_blocks(lambda qb: gqa_core(qb, k_all, v_all), (qx,)).reshape(B, S, -1)
    out_c = gqa_core(qc, kc, vc).reshape(B, cq.shape[1], -1) if need_ctx else None
    return out_x, out_c


def fourier_mixer(u):
    B, T = u.shape[:2]
    grp = u.reshape(B, T, D_GROUPS, D_GROUP_DIM).astype(F32)
    y = jnp.fft.fftn(grp, axes=(1, 3), norm='ortho').real
    return y.reshape(B, T, D_GROUPS * D_GROUP_DIM).astype(u.dtype)


def swiglu(h, w_ffn_in, w_ffn_out):
    gate, up = jnp.split(h @ w_ffn_in, 2, axis=-1)
    return (jax.nn.silu(gate) * up) @ w_ffn_out


def hybrid_layer(x, ctx, c, c_ctx, w_mod, b_mod, norm1, w_in, diff_lambda, diff_norm,
                 lb_fwd, lb_bwd, hgrn_norm, q_norm, k_norm, w_out, norm2, w_ffn_in, w_ffn_out,
                 layer_idx, rope_a, rope_c, need_ctx):
    mod = jax.nn.silu(c) @ w_mod + b_mod
    mod_c = jax.nn.silu(c_ctx) @ w_mod + b_mod
    sh1, sc1, g1, sh2, sc2, g2 = jnp.split(mod[:, None, :], 6, axis=-1)
    csh1, csc1, cg1, csh2, csc2, cg2 = jnp.split(mod_c, 6, axis=-1)

    hx = modulate(rmsnorm(x, norm1), sh1, sc1)
    hc = modulate(rmsnorm(ctx, norm1), csh1, csc1)
    (xa_q, xa_k, xa_v, xb_q, xb_ff, xb_fb, xb_i, xb_g, xc_q, xc_k, xc_v, xd_u) = split_proj(hx @ w_in)
    (ca_q, ca_k, ca_v, cb_q, cb_ff, cb_fb, cb_i, cb_g, cc_q, cc_k, cc_v, cd_u) = split_proj(hc @ w_in)

    mA_x, mA_c = diff_attention_mixer(xa_q, xa_k, xa_v, ca_q, ca_k, ca_v, diff_lambda, diff_norm,
                                      layer_idx, rope_a, need_ctx)
    mB_x, mB_c = hgrn2_mixer(xb_q, xb_ff, xb_fb, xb_i, xb_g, cb_q, cb_ff, cb_fb, cb_i, cb_g,
                             lb_fwd, lb_bwd, hgrn_norm, need_ctx)
    mC_x, mC_c = gqa_mixer(xc_q, xc_k, xc_v, cc_q, cc_k, cc_v, q_norm, k_norm, rope_c, need_ctx)
    mD_x = fourier_mixer(xd_u)

    mix_x = jnp.concatenate([mA_x, mB_x, mC_x, mD_x], axis=-1)
    x = x + g1 * (mix_x @ w_out)
    x = x + g2 * swiglu(modulate(rmsnorm(x, norm2), sh2, sc2), w_ffn_in, w_ffn_out)

    if need_ctx:
        mix_c = jnp.concatenate([mA_c, mB_c, mC_c, fourier_mixer(cd_u)], axis=-1)
        ctx = ctx + cg1 * (mix_c @ w_out)
        ctx = ctx + cg2 * swiglu(modulate(rmsnorm(ctx, norm2), csh2, csc2), w_ffn_in, w_ffn_out)
    return x, ctx


def setup_inputs(seed: int = 0) -> dict:
    key = jax.random.key(seed)
    ks = jax.random.split(key, 19)
    D = D_MODEL
    nrm = jax.random.normal
    return {
        'x': nrm(ks[0], (BATCH, SEQ, D), F32),
        'c': nrm(ks[1], (BATCH, D), F32),
        'ctx': nrm(ks[2], (BATCH, CTX_LEN, D), F32),
        'c_ctx': nrm(ks[3], (D,), F32),
        'w_mod': nrm(ks[4], (DEPTH, D, 6 * D), F32) * (0.5 * D ** -0.5),
        'b_mod': nrm(ks[5], (DEPTH, 6 * D), F32) * 0.02,
        'norm1': 1.0 + 0.05 * nrm(ks[6], (DEPTH, D), F32),
        'w_in': nrm(ks[7], (DEPTH, D, IN_WIDTH), F32) * D ** -0.5,
        'diff_lambda': nrm(ks[8], (DEPTH, 4, A_QK_DIM), F32) * 0.1,
        'diff_norm': 1.0 + 0.05 * nrm(ks[9], (DEPTH, A_V_DIM), F32),
        'hgrn_lb_logits': nrm(ks[10], (2, DEPTH, B_HEADS * B_KEY_DIM), F32) * 0.5,
        'hgrn_norm': 1.0 + 0.05 * nrm(ks[11], (DEPTH, B_VAL_DIM), F32),
        'q_norm': 1.0 + 0.05 * nrm(ks[12], (DEPTH, HEAD_DIM), F32),
        'k_norm': 1.0 + 0.05 * nrm(ks[13], (DEPTH, HEAD_DIM), F32),
        'w_out': nrm(ks[14], (DEPTH, MIX_WIDTH, D), F32) * MIX_WIDTH ** -0.5,
        'norm2': 1.0 + 0.05 * nrm(ks[15], (DEPTH, D), F32),
        'w_ffn_in': nrm(ks[16], (DEPTH, D, 2 * FFN_HIDDEN), F32) * D ** -0.5,
        'w_ffn_out': nrm(ks[17], (DEPTH, FFN_HIDDEN, D), F32) * FFN_HIDDEN ** -0.5,
        'final_norm': 1.0 + 0.05 * nrm(ks[18], (D,), F32),
    }


def reference(x, c, ctx, c_ctx, w_mod, b_mod, norm1, w_in, diff_lambda, diff_norm,
              hgrn_lb_logits, hgrn_norm, q_norm, k_norm, w_out, norm2, w_ffn_in, w_ffn_out,
              final_norm):
    rows = x.shape[1] // GRID_W
    rope_a = axial_rope_tables(rows, A_QK_DIM)
    rope_c = axial_rope_tables(rows, HEAD_DIM)
    p = jax.nn.softmax(hgrn_lb_logits.astype(F32), axis=1)
    lower_bounds = jnp.cumsum(p, axis=1) - p[:, :1]
    for l in range(DEPTH):
        x, ctx = hybrid_layer(x, ctx, c, c_ctx, w_mod[l], b_mod[l], norm1[l], w_in[l],
                              diff_lambda[l], diff_norm[l], lower_bounds[0, l], lower_bounds[1, l],
                              hgrn_norm[l], q_norm[l], k_norm[l], w_out[l], norm2[l],
                              w_ffn_in[l], w_ffn_out[l], l, rope_a, rope_c, l < DEPTH - 1)
    return rmsnorm(x, final_norm)
```

```python
import numpy as np
import ml_dtypes
from contextlib import ExitStack
import concourse.bass as bass
import concourse.mybir as mybir
from concourse.bass_utils import run_bass_kernel_spmd

F32 = mybir.dt.float32
BF16 = mybir.dt.bfloat16
AF = mybir.ActivationFunctionType
ALU = mybir.AluOpType

D = 1024
KC = 8
TCX = 256
SEQ = 2048
T = TCX + SEQ
DEPTH = 4
FH = 2816
NHB = 6
TB = [(0, 256)] + [(256 + 512 * i, 512) for i in range(4)]
NTB = len(TB)
EPS = 1e-6
NSLOT = 3
AW = 15360
COL_CQK = 2048
COL_CV = 2560
COL_D = 2688
WIN_COLS = 2944
ENGS = ("tensor", "vector", "scalar", "gpsimd", "sync")
_BIDX = np.concatenate([np.arange(768 + 256 * j + 128 * hc, 768 + 256 * j + 128 * hc + 128)
                        for hc in range(2) for j in range(4)])
WIN_IDX = np.concatenate([np.arange(0, 768), _BIDX, np.arange(1792, 2304), np.arange(2304, 2368), np.arange(2304, 2368),
                          np.arange(2368, 2432), np.arange(2368, 2432), np.arange(2432, 2816)])


class Prog:
    def __init__(self, dry=False, wlist=None):
        self.dry = dry
        self.streams = {e: [] for e in ENGS}
        self.cnt = {}
        self.waited = {e: {} for e in ENGS}
        self.res = {}
        self.barrier = {e: [] for e in ENGS}
        self.semnames = set(ENGS)
        self.final_waits = {}
        self.wlist = wlist if wlist is not None else []
        self.wrec = []
        self.wpos = 0
        self.wissued = 0
        self.wreleased = 0
        self.wloader = None

    def _need(self, eng, sem, val, weng, waits):
        if weng == eng and eng == "tensor":
            return
        if self.waited[eng].get(sem, 0) >= val:
            return
        if waits.get(sem, 0) < val:
            waits[sem] = val

    def op(self, eng, fn, reads=(), writes=(), dma_sem=None):
        if self.dry:
            return
        waits = {}
        for (sem, val, weng) in self.barrier[eng]:
            self._need(eng, sem, val, weng, waits)
        self.barrier[eng] = []
        for r in reads:
            st = self.res.get(r)
            if st is not None and st["w"] is not None:
                self._need(eng, *st["w"], waits)
            if st is not None and r.startswith("PS"):
                for sem, (val, reng) in st["r"].items():
                    if reng != eng:
                        self._need(eng, sem, val, reng, waits)
        for w in writes:
            st = self.res.get(w)
            if st is not None:
                if st["w"] is not None:
                    self._need(eng, *st["w"], waits)
                for sem, (val, reng) in st["r"].items():
                    self._need(eng, sem, val, reng, waits)
        for sem, val in waits.items():
            self.waited[eng][sem] = val
        if dma_sem is None:
            sem, inc, weng = eng, 1, eng
        else:
            sem, inc, weng = dma_sem, 16, "dma"
            self.semnames.add(sem)
        val = self.cnt.get(sem, 0) + inc
        self.cnt[sem] = val
        for r in reads:
            st = self.res.setdefault(r, {"w": None, "r": {}})
            st["r"][sem] = (val, weng)
        for w in writes:
            self.res[w] = {"w": (sem, val, weng), "r": {}}
        self.streams[eng].append((list(waits.items()), fn, sem, inc))

    def seal(self, resources, sem):
        if self.dry:
            return
        ev = (sem, self.cnt[sem], "dma")
        for r in resources:
            self.res[r] = {"w": ev, "r": {}}

    def sync_all(self, engines=("tensor", "vector", "scalar")):
        if self.dry:
            return
        evs = [(e, self.cnt.get(e, 0), e) for e in engines if self.cnt.get(e, 0) > 0]
        for e in engines:
            self.barrier[e] = list(evs)

    def _wpump(self):
        while self.wissued < len(self.wlist) and self.wissued < self.wreleased + NSLOT:
            j = self.wissued
            self.wissued += 1
            self.wloader(self, self.wlist[j], j % NSLOT)

    def wnext(self, desc):
        if self.dry:
            self.wrec.append(desc)
            return 0
        i = self.wpos
        assert self.wlist[i] == desc, (i, self.wlist[i], desc)
        self._wpump()
        assert i < self.wissued, "weight ring too small for this access pattern"
        self.wpos += 1
        return i % NSLOT

    def wrelease(self, n=1):
        if self.dry:
            return
        self.wreleased += n
        self._wpump()


def build_program(NS, NL, mixers=""):
    nc = bass.Bass("TRN2", target_bir_lowering=False)

    def dt(name, shape, dtype=F32, kind="ExternalInput"):
        return nc.dram_tensor(name, shape, dtype, kind=kind).ap()

    xT = dt("xT", [NS, D, T])
    cvec = dt("cvec", [128, KC * 3])
    w_mod = dt("w_mod", [DEPTH, D, 6 * D])
    b_modT = dt("b_modT", [128, DEPTH * 48])
    n1T = dt("n1T", [128, DEPTH * KC])
    n2T = dt("n2T", [128, DEPTH * KC])
    fnT = dt("fnT", [128, KC])
    w_in = dt("w_in", [DEPTH, D, WIN_COLS])
    w_out = dt("w_out", [DEPTH, D, D])
    w_fi = dt("w_fi", [DEPTH, D, 2 * FH])
    w_fo = dt("w_fo", [DEPTH, FH, D])
    outT = dt("outT", [NS, D, SEQ], kind="ExternalOutput")
    bd_d = dt("bd", [128, 256], BF16)
    perm_d = dt("perm", [128, 256], BF16)
    dftl_d = dt("dftl", [2, SEQ, SEQ], BF16)
    dftc_d = dt("dftc", [128, 1024], BF16)
    rope_d = dt("rope", [4, 128, SEQ], BF16)
    qkg_d = dt("qkg", [128, DEPTH * 2])
    dng_d = dt("dng", [128, DEPTH])
    dlam_d = dt("dlam", [128, DEPTH * 128])
    ident_d = dt("ident", [128, 128], BF16)
    bmask_d = dt("bmask", [128, 256], BF16)
    lbl_d = dt("lbl", [128, 4 * DEPTH])
    hng_d = dt("hng", [128, DEPTH])

    es = ExitStack()

    def sb(name, shape, dtype):
        return es.enter_context(nc.sbuf_tensor(name, shape, dtype))

    XT = sb("XT", [128, KC, T], F32)
    HT = sb("HT", [128, KC, T], BF16)
    RING = sb("RING", [128, NSLOT, 4096], BF16)
    ARENA = sb("ARENA", [128, AW], F32)
    MOD = sb("MOD", [128, DEPTH, 48, 3], F32)
    A1 = sb("A1", [128, DEPTH, KC, 3], F32)
    A2 = sb("A2", [128, DEPTH, KC, 3], F32)
    BMOD = sb("BMOD", [128, DEPTH, 48], F32)
    N1 = sb("N1", [128, DEPTH, KC], F32)
    N2 = sb("N2", [128, DEPTH, KC], F32)
    FN = sb("FN", [128, KC], F32)
    CV = sb("CV", [128, KC * 3], F32)
    SCT = sb("SCT", [128, KC, 3], BF16)
    ONES = sb("ONES", [128, 128], BF16)
    BONES = sb("BONES", [128, 128], BF16)
    BD = sb("BD", [128, 2, 128], BF16)
    PERM = sb("PERM", [128, 2, 128], BF16)
    QKG = sb("QKG", [128, DEPTH, 2], F32)
    DNG = sb("DNG", [128, DEPTH], F32)
    NLAM = sb("NLAM", [128, DEPTH], F32)
    IDENT = sb("IDENT", [128, 128], BF16)
    BMASK = sb("BMASK", [128, 2, 128], BF16)
    LB = sb("LB", [128, 4, DEPTH], F32)
    OML = sb("OML", [128, 4, DEPTH], F32)
    HNG = sb("HNG", [128, DEPTH], F32)
    PS = [es.enter_context(nc.psum_tensor("PS%d" % i, [128, 512], F32)) for i in range(8)]

    class Carve:
        def __init__(self):
            self.off = 0

        def f32(self, n):
            v = ARENA[:, self.off:self.off + n]
            self.off += n
            assert self.off <= AW, self.off
            return v

        def bf16(self, n):
            w = (n + 1) // 2
            v = ARENA[:, self.off:self.off + w].bitcast(BF16)
            self.off += w
            assert self.off <= AW, self.off
            return v

    def xtn(ks, tbi):
        return ["XT%d_%d" % (k, tbi) for k in ks]

    def htn(ks, tbi):
        return ["HT%d_%d" % (k, tbi) for k in ks]

    ALLK = list(range(8))

    def gen(p):
        psi = [0]
        bset = [list(range(8))]

        def bank():
            b = bset[0][psi[0] % len(bset[0])]
            psi[0] += 1
            return b

        def wloader(p, desc, slot):
            kind = desc[0]
            dst = RING[:, slot, :]
            if kind in ("mod", "fi", "wi"):
                _, l, c0, n = desc
                wsrc = {"mod": w_mod, "fi": w_fi, "wi": w_in}[kind]
                src = wsrc[l].rearrange("(k p) f -> p k f", p=128)[:, :, c0:c0 + n]
                d3 = dst[:, 0:8 * n].rearrange("p (k f) -> p k f", k=8)
            elif kind in ("fo", "wo"):
                _, l, r0, nch = desc
                wsrc = {"fo": w_fo, "wo": w_out}[kind]
                src = wsrc[l][r0:r0 + nch * 128, :].rearrange("(c p) f -> p c f", p=128)
                d3 = dst[:, 0:nch * 1024].rearrange("p (c f) -> p c f", c=nch)
            elif kind == "dft":
                _, part, half, tb = desc
                src = dftl_d[part][half * 1024:(half + 1) * 1024, tb * 512:(tb + 1) * 512].rearrange(
                    "(st p) t -> p st t", p=128)
                d3 = dst.rearrange("p (k f) -> p k f", k=8)
            elif kind == "dftc":
                src = dftc_d[:, :]
                d3 = dst[:, 0:1024]
            else:
                raise ValueError(kind)
            p.op("gpsimd", lambda e, d3=d3, src=src: e.dma_start(out=d3, in_=src),
                 writes=["WS%d" % slot], dma_sem="ws%d" % slot)

        p.wloader = wloader

        consts = []

        def cload(dst, src, name):
            p.op("sync", lambda e: e.dma_start(out=dst, in_=src), writes=[name], dma_sem="cst")
            consts.append(name)

        cload(CV[:], cvec[:, :], "CV")
        cload(BMOD[:].rearrange("p l j -> p (l j)"), b_modT[:, :], "BMOD")
        cload(N1[:].rearrange("p l k -> p (l k)"), n1T[:, :], "N1")
        cload(N2[:].rearrange("p l k -> p (l k)"), n2T[:, :], "N2")
        cload(FN[:], fnT[:, :], "FN")
        cload(BD[:].rearrange("p a b -> p (a b)"), bd_d[:, :], "BD")
        cload(PERM[:].rearrange("p a b -> p (a b)"), perm_d[:, :], "PERM")
        cload(QKG[:].rearrange("p a b -> p (a b)"), qkg_d[:, :], "QKG")
        cload(DNG[:], dng_d[:, :], "DNG")
        cload(IDENT[:], ident_d[:, :], "IDENT")
        cload(BMASK[:].rearrange("p a b -> p (a b)"), bmask_d[:, :], "BMASK")
        cload(HNG[:], hng_d[:, :], "HNG")
        cload(OML[:].rearrange("p a b -> p (a b)"), lbl_d[:, :], "OML")
        cv0 = Carve()
        DLAM = cv0.f32(DEPTH * 128)
        PRD = cv0.f32(DEPTH * 64)
        SMX = cv0.f32(DEPTH * 2)
        cload(DLAM, dlam_d[:, :], "DLAM")
        p.seal(consts, "cst")
        p.op("vector", lambda e: e.memset(ONES[:], 1.0), writes=["ONES"])
        p.op("vector", lambda e: e.memset(BONES[:], 0.0), writes=["BONES"])
        p.op("vector", lambda e: e.memset(BONES[0:64, 0:64], 1.0), reads=["BONES"], writes=["BONES"])
        p.op("vector", lambda e: e.memset(BONES[64:128, 64:128], 1.0), reads=["BONES"], writes=["BONES"])
        dl4 = DLAM.rearrange("p (l a d) -> p l a d", l=DEPTH, a=4)
        pr4 = PRD.rearrange("p (l a d) -> p l a d", l=DEPTH, a=2)
        for l in range(DEPTH):
            p.op("vector", lambda e, l=l: e.tensor_tensor(out=pr4[:, l], in0=dl4[:, l, 0:4:2, :], in1=dl4[:, l, 1:4:2, :],
                                                          op=ALU.mult), reads=["DLAM"], writes=["PRD"])
        p.op("vector", lambda e: e.tensor_reduce(out=SMX, in_=PRD.rearrange("p (a d) -> p a d", d=32),
                                                 axis=mybir.AxisListType.X, op=ALU.add), reads=["PRD"], writes=["SMX"])
        p.op("scalar", lambda e: e.activation(out=SMX, in_=SMX, func=AF.Exp), reads=["SMX"], writes=["SMX"])
        for l in range(DEPTH):
            li = 0.8 - 0.6 * float(np.exp(-0.3 * l))
            p.op("vector", lambda e, l=l, li=li: e.scalar_tensor_tensor(
                out=NLAM[:, l:l + 1], in0=SMX[:, 2 * l + 1:2 * l + 2], scalar=-li, in1=SMX[:, 2 * l:2 * l + 1],
                op0=ALU.add, op1=ALU.subtract), reads=["SMX"], writes=["NLAM"])

        LSM = cv0.f32(4)
        p.op("scalar", lambda e: e.activation(out=OML[:], in_=OML[:], func=AF.Exp), reads=["OML"], writes=["OML"])
        p.op("vector", lambda e: e.tensor_reduce(out=LSM, in_=OML[:], axis=mybir.AxisListType.X, op=ALU.add),
             reads=["OML"], writes=["LSM"])
        p.op("vector", lambda e: e.reciprocal(out=LSM, in_=LSM), reads=["LSM"], writes=["LSM"])
        p.op("vector", lambda e: e.tensor_tensor(out=OML[:], in0=OML[:], in1=LSM.unsqueeze(2).broadcast_to([128, 4, DEPTH]),
                                                 op=ALU.mult), reads=["OML", "LSM"], writes=["OML"])
        p.op("vector", lambda e: e.memset(LB[:, :, 0:1], 0.0), writes=["LB"])
        for l in range(1, DEPTH):
            p.op("vector", lambda e, l=l: e.tensor_tensor(out=LB[:, :, l:l + 1], in0=LB[:, :, l - 1:l], in1=OML[:, :, l:l + 1],
                                                          op=ALU.add), reads=["LB", "OML"], writes=["LB"])
        p.op("vector", lambda e: e.tensor_scalar(out=OML[:], in0=LB[:], scalar1=-1.0, scalar2=1.0, op0=ALU.mult, op1=ALU.add),
             reads=["LB", "OML"], writes=["OML"])

        p.op("scalar", lambda e: e.activation(out=SCT[:].rearrange("p k j -> p (k j)"), in_=CV[:], func=AF.Silu),
             reads=["CV"], writes=["SCT"])
        for l in range(NL):
            b = bank()
            for jb in range(12):
                slot = p.wnext(("mod", l, jb * 512, 512))
                W = RING[:, slot, :].rearrange("p (k f) -> p k f", k=8)

                def f(e, W=W, b=b, jb=jb):
                    ins = None
                    for j in range(4):
                        jj = jb * 4 + j
                        for k in range(8):
                            ins = e.matmul(PS[b][:, jj * 3:jj * 3 + 3], lhsT=W[:, k, j * 128:(j + 1) * 128],
                                           rhs=SCT[:, k, :], start=(k == 0), stop=(k == 7))
                    return ins
                p.op("tensor", f, reads=["WS%d" % slot, "SCT"], writes=["PS%d" % b])
                p.wrelease()
            psv = PS[b][:, 0:144].rearrange("p (j c) -> p j c", c=3)
            for c in range(3):
                p.op("vector", lambda e, c=c, l=l, psv=psv: e.tensor_tensor(
                    out=MOD[:, l, :, c], in0=psv[:, :, c], in1=BMOD[:, l, :], op=ALU.add),
                    reads=["PS%d" % b, "BMOD"], writes=["MOD%d_%d" % (l, c)])
            for c in range(3):
                p.op("vector", lambda e, c=c, l=l: e.scalar_tensor_tensor(
                    out=A1[:, l, :, c], in0=MOD[:, l, 8:16, c], scalar=1.0, in1=N1[:, l, :],
                    op0=ALU.add, op1=ALU.mult), reads=["MOD%d_%d" % (l, c), "N1"], writes=["A1_%d_%d" % (l, c)])
                p.op("vector", lambda e, c=c, l=l: e.scalar_tensor_tensor(
                    out=A2[:, l, :, c], in0=MOD[:, l, 32:40, c], scalar=1.0, in1=N2[:, l, :],
                    op0=ALU.add, op1=ALU.mult), reads=["MOD%d_%d" % (l, c), "N2"], writes=["A2_%d_%d" % (l, c)])

        def rstd_block(cv_bufs, tbi):
            SQB, LNV, RSTD = cv_bufs
            t0, n = TB[tbi]
            sq3 = SQB[:, 0:8 * n].rearrange("p (k t) -> p k t", k=8)
            p.op("scalar", lambda e: e.activation(out=sq3, in_=XT[:, :, t0:t0 + n], func=AF.Square),
                 reads=xtn(ALLK, tbi), writes=["SQ"])
            b = bank()

            def f(e):
                ins = None
                for k in range(8):
                    ins = e.matmul(PS[b][:, 0:n], lhsT=ONES[:], rhs=sq3[:, k, :], start=(k == 0), stop=(k == 7))
                return ins
            p.op("tensor", f, reads=["SQ", "ONES"], writes=["PS%d" % b])
            p.op("scalar", lambda e: e.activation(out=LNV[:, 0:n], in_=PS[b][:, 0:n], func=AF.Ln,
                                                  scale=1.0 / D, bias=EPS),
                 reads=["PS%d" % b], writes=["LNV"])
            p.op("scalar", lambda e: e.activation(out=RSTD[:, 0:n], in_=LNV[:, 0:n], func=AF.Exp, scale=-0.5),
                 reads=["LNV"], writes=["RSTD"])

        def norm_phase(l, s, which):
            p.sync_all()
            cv = Carve()
            bufs = (cv.bf16(4096), cv.f32(512), cv.f32(512))
            RSTD = bufs[2]
            TMP = [cv.f32(512) for _ in range(2)]
            Aw = A1 if which == 1 else A2
            awn = "A1_%d_" % l if which == 1 else "A2_%d_" % l
            sh0 = 0 if which == 1 else 24
            ti = 0
            for tbi, (t0, n) in enumerate(TB):
                col = 0 if tbi == 0 else 1 + s
                rstd_block(bufs, tbi)
                for k in range(8):
                    tm = TMP[ti % 2]
                    tmn = "TMP%d" % (ti % 2)
                    ti += 1
                    p.op("vector", lambda e, k=k, tm=tm, t0=t0, n=n, col=col: e.scalar_tensor_tensor(
                        out=tm[:, 0:n], in0=XT[:, k, t0:t0 + n], scalar=Aw[:, l, k, col:col + 1], in1=RSTD[:, 0:n],
                        op0=ALU.mult, op1=ALU.mult),
                        reads=xtn([k], tbi) + ["RSTD", awn + str(col)], writes=[tmn])
                    p.op("scalar", lambda e, k=k, tm=tm, t0=t0, n=n, col=col: e.activation(
                        out=HT[:, k, t0:t0 + n], in_=tm[:, 0:n], func=AF.Identity,
                        bias=MOD[:, l, sh0 + k, col:col + 1], scale=1.0),
                        reads=[tmn, "MOD%d_%d" % (l, col)], writes=htn([k], tbi))

        def ffn_phase(l, s):
            p.sync_all()
            cv = Carve()
            ACT3 = cv.bf16(4 * T).rearrange("p (c t) -> p c t", c=4)
            SG = [cv.f32(512) for _ in range(2)]
            sgi = 0
            for hb in range(NHB):
                nch = 4 if hb < 5 else 2
                sg_ = p.wnext(("fi", l, hb * 512, nch * 128))
                su_ = p.wnext(("fi", l, FH + hb * 512, nch * 128))
                Wg = RING[:, sg_, 0:8 * nch * 128].rearrange("p (k f) -> p k f", k=8)
                Wu = RING[:, su_, 0:8 * nch * 128].rearrange("p (k f) -> p k f", k=8)
                for tbi, (t0, n) in enumerate(TB):
                    for c in range(nch):
                        bg = bank()
                        bu = bank()

                        def fg(e, b=bg, W=Wg, c=c, t0=t0, n=n):
                            ins = None
                            for k in range(8):
                                ins = e.matmul(PS[b][:, 0:n], lhsT=W[:, k, c * 128:(c + 1) * 128],
                                               rhs=HT[:, k, t0:t0 + n], start=(k == 0), stop=(k == 7))
                            return ins
                        p.op("tensor", fg, reads=["WS%d" % sg_] + htn(ALLK, tbi), writes=["PS%d" % bg])
                        p.op("tensor", lambda e, b=bu, W=Wu, c=c, t0=t0, n=n, fg=fg: fg(e, b, W, c, t0, n),
                             reads=["WS%d" % su_] + htn(ALLK, tbi), writes=["PS%d" % bu])
                        sg = SG[sgi % 2]
                        sgn = "SG%d" % (sgi % 2)
                        sgi += 1
                        p.op("scalar", lambda e, b=bg, sg=sg, n=n: e.activation(
                            out=sg[:, 0:n], in_=PS[b][:, 0:n], func=AF.Silu),
                            reads=["PS%d" % bg], writes=[sgn])
                        p.op("vector", lambda e, b=bu, sg=sg, c=c, t0=t0, n=n: e.tensor_tensor(
                            out=ACT3[:, c, t0:t0 + n], in0=PS[b][:, 0:n], in1=sg[:, 0:n], op=ALU.mult),
                            reads=["PS%d" % bu, sgn], writes=["ACT%d_%d" % (c, tbi)])
                p.wrelease(2)
                so_ = p.wnext(("fo", l, hb * 512, nch))
                Wo = RING[:, so_, 0:nch * 1024].rearrange("p (c f) -> p c f", c=nch)
                for tbi, (t0, n) in enumerate(TB):
                    col = 0 if tbi == 0 else 1 + s
                    for m in range(8):
                        b = bank()

                        def fo(e, b=b, Wo=Wo, m=m, t0=t0, n=n, nch=nch):
                            ins = None
                            for c in range(nch):
                                ins = e.matmul(PS[b][:, 0:n], lhsT=Wo[:, c, m * 128:(m + 1) * 128],
                                               rhs=ACT3[:, c, t0:t0 + n], start=(c == 0), stop=(c == nch - 1))
                            return ins
                        p.op("tensor", fo, reads=["WS%d" % so_] + ["ACT%d_%d" % (c, tbi) for c in range(nch)],
                             writes=["PS%d" % b])
                        p.op("vector", lambda e, b=b, m=m, t0=t0, n=n, col=col: e.scalar_tensor_tensor(
                            out=XT[:, m, t0:t0 + n], in0=PS[b][:, 0:n], scalar=MOD[:, l, 40 + m, col:col + 1],
                            in1=XT[:, m, t0:t0 + n], op0=ALU.mult, op1=ALU.add),
                            reads=["PS%d" % b, "MOD%d_%d" % (l, col)] + xtn([m], tbi), writes=xtn([m], tbi))
                p.wrelease()

        def final_phase(s):
            p.sync_all()
            cv = Carve()
            bufs = (cv.bf16(4096), cv.f32(512), cv.f32(512))
            RSTD = bufs[2]
            OB = [cv.f32(512) for _ in range(2)]
            oi = 0
            for tbi in range(1, NTB):
                t0, n = TB[tbi]
                rstd_block(bufs, tbi)
                for k in range(8):
                    ob = OB[oi % 2]
                    obn = "OB%d" % (oi % 2)
                    oi += 1
                    p.op("vector", lambda e, k=k, ob=ob, t0=t0, n=n: e.scalar_tensor_tensor(
                        out=ob[:, 0:n], in0=XT[:, k, t0:t0 + n], scalar=FN[:, k:k + 1], in1=RSTD[:, 0:n],
                        op0=ALU.mult, op1=ALU.mult), reads=xtn([k], tbi) + ["RSTD", "FN"], writes=[obn])
                    p.op("sync", lambda e, k=k, ob=ob, t0=t0, n=n: e.dma_start(
                        out=outT[s, k * 128:(k + 1) * 128, t0 - TCX:t0 - TCX + n], in_=ob[:, 0:n]),
                        reads=[obn], dma_sem="o" + obn)


        def tbi_of_tile(tt):
            return 0 if tt < 2 else 1 + (tt - 2) // 4

        def proj_fm(slot, W, c0, tbi, b):
            t0, n = TB[tbi]

            def f(e):
                ins = None
                for k in range(8):
                    ins = e.matmul(PS[b][:, 0:n], lhsT=W[:, k, c0:c0 + 128], rhs=HT[:, k, t0:t0 + n],
                                   start=(k == 0), stop=(k == 7))
                return ins
            p.op("tensor", f, reads=["WS%d" % slot] + htn(ALLK, tbi), writes=["PS%d" % b])

        def proj_tm(slot, W, c0, ncols, tt, b):
            def f(e):
                ins = None
                for k in range(8):
                    ins = e.matmul(PS[b][:, 0:ncols], lhsT=HT[:, k, tt * 128:(tt + 1) * 128], rhs=W[:, k, c0:c0 + ncols],
                                   start=(k == 0), stop=(k == 7))
                return ins
            p.op("tensor", f, reads=["WS%d" % slot] + htn(ALLK, tbi_of_tile(tt)), writes=["PS%d" % b])

        evi = [0]

        def evac(out, in_, reads, writes, eng=None):
            if eng is None:
                eng = ("scalar", "vector")[evi[0] % 2]
                evi[0] += 1
            if eng == "scalar":
                p.op("scalar", lambda e: e.activation(out=out, in_=in_, func=AF.Copy), reads=reads, writes=writes)
            else:
                p.op("vector", lambda e: e.tensor_copy(out=out, in_=in_), reads=reads, writes=writes)

        def outproj_block(l, s, so, mx_fn, mxnames, tbi):
            Wo = RING[:, so, 0:2048].rearrange("p (c f) -> p c f", c=2)
            t0, n = TB[tbi]
            col = 0 if tbi == 0 else 1 + s
            for m in range(8):
                b = bank()

                def fo(e, b=b, m=m):
                    ins = None
                    for c in range(2):
                        ins = e.matmul(PS[b][:, 0:n], lhsT=Wo[:, c, m * 128:(m + 1) * 128], rhs=mx_fn(c),
                                       start=(c == 0), stop=(c == 1))
                    return ins
                p.op("tensor", fo, reads=["WS%d" % so] + mxnames, writes=["PS%d" % b])
                p.op("vector", lambda e, b=b, m=m: e.scalar_tensor_tensor(
                    out=XT[:, m, t0:t0 + n], in0=PS[b][:, 0:n], scalar=MOD[:, l, 16 + m, col:col + 1],
                    in1=XT[:, m, t0:t0 + n], op0=ALU.mult, op1=ALU.add),
                    reads=["PS%d" % b, "MOD%d_%d" % (l, col)] + xtn([m], tbi), writes=xtn([m], tbi))

        def mixer_D(l, s):
            p.sync_all()
            cv = Carve()
            UT = cv.bf16(2 * T).rearrange("p (c t) -> p c t", c=2)
            UCS = cv.bf16(18 * 512).rearrange("p (a f) -> p a f", f=512)
            MX = cv.bf16(2 * T).rearrange("p (c t) -> p c t", c=2)
            sw = p.wnext(("wi", l, COL_D, 256))
            W = RING[:, sw, 0:2048].rearrange("p (k f) -> p k f", k=8)
            for fc in range(2):
                for tbi, (t0, n) in enumerate(TB):
                    b = bank()
                    proj_fm(sw, W, fc * 128, tbi, b)
                    evac(UT[:, fc, t0:t0 + n], PS[b][:, 0:n], ["PS%d" % b], ["UT%d_%d" % (fc, tbi)])
            p.wrelease()
            for tt in range(18):
                b = bank()

                def f(e, tt=tt, b=b):
                    ins = None
                    for part in range(2):
                        for fc in range(2):
                            i = part * 2 + fc
                            ins = e.matmul(PS[b][:, i * 128:(i + 1) * 128], lhsT=UT[:, fc, tt * 128:(tt + 1) * 128],
                                           rhs=BD[:, part, :], start=True, stop=True)
                    return ins
                p.op("tensor", f, reads=["BD"] + ["UT%d_%d" % (fc, tbi_of_tile(tt)) for fc in range(2)],
                     writes=["PS%d" % b])
                evac(UCS[:, tt, :], PS[b][:, :], ["PS%d" % b], ["UCS%d" % tt])
            sc = p.wnext(("dftc",))
            DC = RING[:, sc, 0:1024].rearrange("p (st pt t) -> p st pt t", st=2, pt=2)
            for fc in range(2):
                b = bank()

                def f(e, fc=fc, b=b):
                    ins = None
                    i = 0
                    for st in range(2):
                        for part in range(2):
                            ins = e.matmul(PS[b][:, 0:256], lhsT=UCS[:, st, (part * 2 + fc) * 128:(part * 2 + fc + 1) * 128],
                                           rhs=DC[:, st, part, :], start=(i == 0), stop=(i == 3))
                            i += 1
                    return ins
                p.op("tensor", f, reads=["WS%d" % sc, "UCS0", "UCS1"], writes=["PS%d" % b])
                evac(MX[:, fc, 0:256], PS[b][:, 0:256], ["PS%d" % b], ["MX%d_0" % fc])
            p.wrelease()
            for tb in range(4):
                t0, n = TB[1 + tb]
                bb = [bank(), bank()]
                for part in range(2):
                    for half in range(2):
                        sd = p.wnext(("dft", part, half, tb))
                        DT = RING[:, sd, :].rearrange("p (st t) -> p st t", st=8)
                        for fc in range(2):
                            def f(e, fc=fc, part=part, half=half, DT=DT, b=bb[fc]):
                                ins = None
                                for st in range(8):
                                    ins = e.matmul(PS[b][:, :], lhsT=UCS[:, 2 + half * 8 + st, (part * 2 + fc) * 128:(part * 2 + fc + 1) * 128],
                                                   rhs=DT[:, st, :], start=(part == 0 and half == 0 and st == 0),
                                                   stop=(part == 1 and half == 1 and st == 7))
                                return ins
                            p.op("tensor", f, reads=["WS%d" % sd] + ["UCS%d" % (2 + half * 8 + st) for st in range(8)],
                                 writes=["PS%d" % bb[fc]])
                        p.wrelease()
                for fc in range(2):
                    evac(MX[:, fc, t0:t0 + n], PS[bb[fc]][:, :], ["PS%d" % bb[fc]], ["MX%d_%d" % (fc, 1 + tb)])
            so = p.wnext(("wo", l, 768, 2))
            for tbi, (t0, n) in enumerate(TB):
                outproj_block(l, s, so, lambda c, t0=t0, n=n: MX[:, c, t0:t0 + n], ["MX0_%d" % tbi, "MX1_%d" % tbi], tbi)
            p.wrelease()

        def load_rope(cv, which):
            TABc = cv.bf16(SEQ)
            TABs = cv.bf16(SEQ)
            if not p.dry:
                evs = [(e_, p.cnt.get(e_, 0), e_) for e_ in ("tensor", "vector", "scalar") if p.cnt.get(e_, 0) > 0]
                p.barrier["sync"] = list(evs)
            p.op("sync", lambda e: e.dma_start(out=TABc, in_=rope_d[2 * which]), writes=["TABC"], dma_sem="tab")
            p.op("sync", lambda e: e.dma_start(out=TABs, in_=rope_d[2 * which + 1]), writes=["TABS"], dma_sem="tab")
            p.seal(["TABC", "TABS"], "tab")
            return TABc, TABs

        def rope_apply(pidx, QN, TABc, TABs, S1, S2, dest, destnames, tbi):
            t0, n = TB[tbi]
            lt0 = t0 - TCX
            b3 = bank()
            p.op("tensor", lambda e: e.matmul(PS[b3][:, 0:n], lhsT=PERM[:, pidx, :], rhs=QN[:, 0:n], start=True, stop=True),
                 reads=["QN", "PERM"], writes=["PS%d" % b3])
            p.op("vector", lambda e: e.tensor_tensor(out=S1[:, 0:n], in0=QN[:, 0:n], in1=TABc[:, lt0:lt0 + n], op=ALU.mult),
                 reads=["QN", "TABC"], writes=["S1"])
            p.op("vector", lambda e: e.tensor_tensor(out=S2[:, 0:n], in0=PS[b3][:, 0:n], in1=TABs[:, lt0:lt0 + n], op=ALU.mult),
                 reads=["PS%d" % b3, "TABS"], writes=["S2"])
            p.op("vector", lambda e: e.tensor_tensor(out=dest, in0=S1[:, 0:n], in1=S2[:, 0:n], op=ALU.add),
                 reads=["S1", "S2"], writes=destnames)

        NWARM = int(os.environ.get("NWARM", "0"))
        NBURST = int(os.environ.get("NBURST", "0"))
        NGB = int(os.environ.get("NGB", "0"))

        def pe_warm(n):
            if n <= 0:
                return

            def f(e):
                ins = None
                for _ in range(n):
                    ins = e.matmul(PS[7][:, :], lhsT=ONES[:], rhs=HT[:, 0, 0:512], start=True, stop=True)
                return ins
            p.op("tensor", f, reads=["ONES"], writes=["PS7"])

        def attention_core(groups, PT, sbanks, G=2):
            batches = []
            sid = 0
            for g in groups:
                nk = len(g["kts"])
                st = []
                for ki, kt in enumerate(g["kts"]):
                    for mp in g["maps"]:
                        st.append((kt, mp, ki == 0, ki == nk - 1, sid))
                        sid += 1
                for i in range(0, len(st), G):
                    batches.append((g, st[i:i + G], i + G >= len(st)))
            assert len(sbanks) >= 2 * G and len(PT) >= 2 * G
            nb = len(batches)
            pe_warm(NBURST)
            for j in range(nb + 1):
                if j < nb:
                    g, st, _ = batches[j]
                    nq = g["nq"]

                    def fs(e, st=st, nq=nq):
                        ins = None
                        for (kt, mp, first, last, sid_) in st:
                            ins = mp[0](e, PS[sbanks[sid_ % len(sbanks)]], kt)
                        return ins
                    p.op("tensor", fs, reads=g["reads"], writes=["PS%d" % sbanks[x[4] % len(sbanks)] for x in st])
                    for (kt, mp, first, last, sid_) in st:
                        sbk = sbanks[sid_ % len(sbanks)]
                        pt = PT[sid_ % len(PT)]
                        p.op("scalar", lambda e, pt=pt, sbk=sbk, nq=nq, scale=mp[1]: e.activation(
                            out=pt[:, 0:nq], in_=PS[sbk][:, 0:nq], func=AF.Exp, scale=scale),
                            reads=["PS%d" % sbk], writes=["PT%d" % (sid_ % len(PT))])
                if j >= 1:
                    g, st, glast = batches[j - 1]
                    nq = g["nq"]

                    def fpv(e, st=st, nq=nq):
                        ins = None
                        for (kt, mp, first, last, sid_) in st:
                            ins = mp[2](e, PS[mp[3]], kt, PT[sid_ % len(PT)][:, 0:nq], first, last)
                        return ins
                    p.op("tensor", fpv, reads=["PT%d" % (x[4] % len(PT)) for x in st] + g["vreads"],
                         writes=sorted(set("PS%d" % x[1][3] for x in st)))
                    for _ in range(NWARM):
                        p.op("tensor", lambda e: e.matmul(PS[7][:, :], lhsT=ONES[:], rhs=HT[:, 0, 0:512], start=True, stop=True),
                             reads=["ONES"], writes=["PS7"])
                    if glast:
                        g["fin"]()
                        pe_warm(NGB)


        def mixer_A(l, s):
            p.sync_all()
            cv = Carve()
            TABc, TABs = load_rope(cv, 0)
            QT = cv.bf16(2 * T).rearrange("p (c t) -> p c t", c=2)
            KT = cv.bf16(2 * T).rearrange("p (c t) -> p c t", c=2)
            VA = cv.bf16(18 * 4 * 128).rearrange("p (a c h f) -> p a c h f", c=2, h=2, f=128)
            MXB = [cv.bf16(1024).rearrange("p (c t) -> p c t", c=2) for _ in range(2)]
            PT = [cv.bf16(512) for _ in range(4)]
            S0 = cv.f32(512)
            S1 = cv.f32(512)
            S2 = cv.f32(512)
            SQ = cv.bf16(512)
            QN = cv.bf16(512)
            li = 0.8 - 0.6 * float(np.exp(-0.3 * l))
            lnf = float(np.log(1.0 - li))
            VA2 = VA.rearrange("p a c h f -> p (a c) h f")
            p.op("vector", lambda e: e.memset(VA2[:, :, 0, 64:128], 1.0), writes=["VA1a"])
            p.op("vector", lambda e: e.memset(VA2[:, :, 1, 0:64], 1.0), writes=["VA1b"])
            sw = p.wnext(("wi", l, 0, 512))
            W = RING[:, sw, :].rearrange("p (k f) -> p k f", k=8)
            for X in range(4):
                isq = X < 2
                c = X % 2
                dst = QT if isq else KT
                dn = "QT" if isq else "KT"
                for tbi, (t0, n) in enumerate(TB):
                    b = bank()
                    proj_fm(sw, W, X * 128, tbi, b)
                    if tbi == 0:
                        evac(dst[:, c, t0:t0 + n], PS[b][:, 0:n], ["PS%d" % b], ["%s%d_%d" % (dn, c, tbi)])
                    else:
                        evac(QN[:, 0:n], PS[b][:, 0:n], ["PS%d" % b], ["QN"], eng="scalar")
                        rope_apply(0, QN, TABc, TABs, S1, S2, dst[:, c, t0:t0 + n], ["%s%d_%d" % (dn, c, tbi)], tbi)
            p.wrelease()
            sv = p.wnext(("wi", l, 512, 256))
            Wv = RING[:, sv, 0:2048].rearrange("p (k f) -> p k f", k=8)
            for tt in range(18):
                b = bank()
                proj_tm(sv, Wv, 0, 256, tt, b)
                psv = PS[b][:, 0:256].rearrange("p (c h f) -> p c h f", c=2, h=2)
                evac(VA[:, tt, :, 0, 0:64], psv[:, :, 0, :], ["PS%d" % b], ["VAa%d" % tt])
                evac(VA[:, tt, :, 1, 64:128], psv[:, :, 1, :], ["PS%d" % b], ["VAb%d" % tt])
            p.wrelease()
            so = p.wnext(("wo", l, 0, 2))
            pend = []

            def flush():
                while pend:
                    pend.pop(0)()
            groups = []
            gi = 0
            for qb, (q0, nq) in enumerate(TB):
                kts = [0, 1] if qb == 0 else list(range(18))
                mxb = MXB[qb % 2]
                mxn = "MXB%d" % (qb % 2)
                for c in range(2):
                    for hh in range(2):
                        obs = [4, 5]
                        gi += 1
                        lo = 64 * hh
                        maps = []
                        for comp in range(2):
                            u = 2 * hh + comp

                            def s_fn(e, ps, kt, c=c, u=u, q0=q0, nq=nq):
                                kw = dict(tile_position=(96, 0)) if u == 3 else {}
                                return e.matmul(ps[:, 0:nq], lhsT=KT[32 * u:32 * u + 32, c, kt * 128:(kt + 1) * 128],
                                                rhs=QT[32 * u:32 * u + 32, c, q0:q0 + nq], start=True, stop=True, **kw)

                            def pv_fn(e, ps, kt, pt, first, last, c=c, hh=hh, nq=nq):
                                return e.matmul(ps[:, 0:nq], lhsT=VA[:, kt, c, hh, :], rhs=pt, start=first, stop=last)
                            maps.append((s_fn, 32.0 ** -0.5, pv_fn, obs[comp]))

                        def fin(obs=obs, lo=lo, c=c, nq=nq, mxb=mxb, mxn=mxn, hh=hh, qb=qb):
                            flush()
                            dp = 64 - lo
                            R = slice(lo, lo + 64)
                            DP = slice(dp, dp + 64)
                            o1, o2 = obs
                            p.op("vector", lambda e: e.tensor_copy(out=S0[:, 0:nq], in_=PS[o1][:, 0:nq]),
                                 reads=["PS%d" % o1], writes=["S0"])
                            p.op("vector", lambda e: e.tensor_copy(out=S2[:, 0:nq], in_=PS[o2][:, 0:nq]),
                                 reads=["PS%d" % o2], writes=["S2"])
                            p.op("vector", lambda e: e.tensor_copy(out=S1[R, 0:nq], in_=S0[DP, 0:nq]), reads=["S0"], writes=["S1"])
                            p.op("vector", lambda e: e.reciprocal(out=S1[R, 0:nq], in_=S1[R, 0:nq]), reads=["S1"], writes=["S1"])
                            p.op("vector", lambda e: e.tensor_tensor(out=S0[R, 0:nq], in0=S0[R, 0:nq], in1=S1[R, 0:nq], op=ALU.mult),
                                 reads=["S0", "S1"], writes=["S0"])
                            p.op("vector", lambda e: e.tensor_copy(out=S1[R, 0:nq], in_=S2[DP, 0:nq]), reads=["S2", "S0"], writes=["S1"])
                            p.op("vector", lambda e: e.reciprocal(out=S1[R, 0:nq], in_=S1[R, 0:nq]), reads=["S1"], writes=["S1"])
                            p.op("vector", lambda e: e.tensor_tensor(out=S2[R, 0:nq], in0=S2[R, 0:nq], in1=S1[R, 0:nq], op=ALU.mult),
                                 reads=["S2", "S1"], writes=["S2"])
                            p.op("vector", lambda e: e.scalar_tensor_tensor(out=S0[R, 0:nq], in0=S2[R, 0:nq], scalar=NLAM[R, l:l + 1],
                                                                            in1=S0[R, 0:nq], op0=ALU.mult, op1=ALU.add),
                                 reads=["S2", "S0", "NLAM"], writes=["S0"])
                            p.op("scalar", lambda e: e.activation(out=SQ[R, 0:nq], in_=S0[R, 0:nq], func=AF.Square),
                                 reads=["S0"], writes=["SQ"])
                            mb = bank()
                            p.op("tensor", lambda e: e.matmul(PS[mb][:, 0:nq], lhsT=ONES[R, :], rhs=SQ[R, 0:nq], start=True, stop=True),
                                 reads=["SQ", "ONES"], writes=["PS%d" % mb])
                            p.op("scalar", lambda e: e.activation(out=S1[R, 0:nq], in_=PS[mb][R, 0:nq], func=AF.Ln,
                                                                  scale=1.0 / 64, bias=EPS), reads=["PS%d" % mb], writes=["S1"])
                            p.op("scalar", lambda e: e.activation(out=S1[R, 0:nq], in_=S1[R, 0:nq], func=AF.Exp, scale=-0.5, bias=lnf),
                                 reads=["S1"], writes=["S1"])
                            p.op("vector", lambda e: e.scalar_tensor_tensor(out=mxb[R, c, 0:nq], in0=S0[R, 0:nq], scalar=DNG[R, l:l + 1],
                                                                            in1=S1[R, 0:nq], op0=ALU.mult, op1=ALU.mult),
                                 reads=["S0", "S1", "DNG"], writes=[mxn + "_%d%d" % (c, hh)])
                            if c == 1 and hh == 1:
                                pend.append(lambda: outproj_block(l, s, so, lambda cc: mxb[:, cc, 0:nq],
                                            [mxn + "_%d%d" % (a_, b_) for a_ in range(2) for b_ in range(2)], qb))
                        kn = ["KT%d_%d" % (c, t_) for t_ in range(NTB)]
                        groups.append(dict(maps=maps, nq=nq, kts=kts, reads=kn + ["QT%d_%d" % (c, qb)],
                                           vreads=["VA1a", "VA1b"] + ["VAa%d" % t_ for t_ in kts] + ["VAb%d" % t_ for t_ in kts],
                                           fin=fin))
            bset[0] = [6]
            attention_core(groups, PT, sbanks=[0, 1, 2, 3], G=2)
            flush()
            bset[0] = list(range(8))
            p.wrelease()


        def mixer_B(l, s):
            for hc in range(2):
                mixer_B_hc(l, s, hc)

        def mixer_B_hc(l, s, hc):
            p.sync_all()
            cv = Carve()
            VZ = cv.bf16(18 * 2 * 128).rearrange("p (a h f) -> p a h f", h=2, f=128)
            OF = cv.f32(T)
            ST = cv.f32(128)
            SZ = [cv.bf16(128) for _ in range(2)]
            F_ = cv.f32(512)
            L_ = cv.f32(512)
            K_ = cv.f32(512)
            P_ = cv.f32(512)
            EE = cv.f32(512)
            BBb = cv.f32(512)
            T1 = cv.f32(512)
            T2 = cv.f32(512)
            T3 = cv.f32(512)
            T4 = cv.f32(512)
            EK = cv.f32(512)
            RST = cv.bf16(512)
            QTL = cv.bf16(512)
            KTL = cv.bf16(512)
            QB = cv.bf16(512)
            KET = cv.bf16(512)
            EB = cv.f32(16)
            AT = [cv.bf16(256).rearrange("p (h t) -> p h t", h=2) for _ in range(2)]
            KEZ = [cv.bf16(384) for _ in range(2)]
            MXB = [cv.bf16(512) for _ in range(2)]
            SQ = cv.bf16(512)
            p.op("vector", lambda e: e.memset(VZ.rearrange("p a h f -> p (a h f)"), 0.0), writes=["VZ0"])
            for i in range(2):
                p.op("vector", lambda e, i=i: e.memset(KEZ[i], 0.0), writes=["KEZ%d" % i])
            p.op("vector", lambda e: e.memset(RST, 1.0), writes=["RST"])
            p.op("vector", lambda e: e.memset(RST.rearrange("p (c t) -> p c t", t=32)[:, :, 0:1], 0.0),
                 reads=["RST"], writes=["RST"])
            sW = p.wnext(("wi", l, 768 + 512 * hc, 512))
            sG = p.wnext(("wi", l, 1792 + 128 * hc, 128))
            so = p.wnext(("wo", l, 256 + 128 * hc, 1))
            W = RING[:, sW, :].rearrange("p (k f) -> p k f", k=8)
            WG = RING[:, sG, 0:1024].rearrange("p (k f) -> p k f", k=8)
            Wo = RING[:, so, 0:1024]
            bset[0] = [0, 1]
            for tt in range(18):
                b = bank()
                proj_tm(sW, W, 384, 128, tt, b)
                evac(VZ[:, tt].rearrange("p h f -> p (h f)")[:, 0:256].rearrange("p (h x) -> p h x", h=2)[:, :, 0:64]
                     if False else VZ[:, tt, 0, 0:64], PS[b][:, 0:64], ["PS%d" % b, "VZ0"], ["VZa%d" % tt])
                evac(VZ[:, tt, 1, 64:128], PS[b][:, 64:128], ["PS%d" % b, "VZ0"], ["VZb%d" % tt])
            vzn = lambda tt: ["VZa%d" % tt, "VZb%d" % tt]

            def ew(dr, blk):
                t0, n = TB[blk]
                ntl = n // 128
                order = list(range(ntl)) if dr == 0 else list(range(ntl - 1, -1, -1))
                zc = 128 if dr == 0 else 256
                lbp = LB[:, dr * 2 + hc, l:l + 1]
                omp = OML[:, dr * 2 + hc, l:l + 1]
                zb, qb_ = 2, 3
                proj_fm(sW, W, zc, blk, zb)
                proj_fm(sW, W, 0, blk, qb_)

                def v3(buf, S):
                    return buf[:, S].rearrange("p (c t) -> p c t", t=32)

                def st_sig(tl, S, x):
                    p.op("scalar", lambda e: e.activation(out=F_[:, S], in_=PS[zb][:, S], func=AF.Sigmoid),
                         reads=["PS%d" % zb], writes=["FF" + x])

                def st_f(tl, S, x):
                    p.op("vector", lambda e: e.tensor_scalar(out=F_[:, S], in0=F_[:, S], scalar1=omp, scalar2=lbp,
                                                             op0=ALU.mult, op1=ALU.add), reads=["FF" + x, "LB", "OML"], writes=["FF" + x])

                def st_ln(tl, S, x):
                    p.op("scalar", lambda e: e.activation(out=L_[:, S], in_=F_[:, S], func=AF.Ln), reads=["FF" + x], writes=["LF" + x])

                def st_k(tl, S, x):
                    p.op("vector", lambda e: e.tensor_scalar(out=K_[:, S], in0=F_[:, S], scalar1=-1.0, scalar2=1.0,
                                                             op0=ALU.mult, op1=ALU.add), reads=["FF" + x], writes=["KK" + x])

                def st_scan(tl, S, x):
                    p.op("vector", lambda e: e.tensor_tensor_scan(out=P_[:, S], data0=RST[:, S], data1=L_[:, S], initial=0.0,
                                                                  op0=ALU.mult, op1=ALU.add), reads=["RST", "LF" + x], writes=["PP" + x])

                def st_b(tl, S, x):
                    P3 = v3(P_, S)
                    TOTb = P3[:, :, 31:32].broadcast_to([128, 4, 32])
                    if dr == 1:
                        p.op("vector", lambda e: e.tensor_tensor(out=v3(EE, S), in0=TOTb, in1=P3, op=ALU.subtract),
                             reads=["PP" + x], writes=["EE" + x])
                        p.op("vector", lambda e: e.tensor_tensor(out=BBb[:, S], in0=EE[:, S], in1=L_[:, S], op=ALU.add),
                             reads=["EE" + x, "LF" + x], writes=["BB" + x])

                def st_e(tl, S, x):
                    P3 = v3(P_, S)
                    TOTb = P3[:, :, 31:32].broadcast_to([128, 4, 32])
                    B3 = P3 if dr == 0 else v3(BBb, S)
                    bn = ("PP" if dr == 0 else "BB") + x
                    p.op("vector", lambda e: e.tensor_tensor(out=v3(EE, S), in0=B3, in1=B3[:, :, 16:17].broadcast_to([128, 4, 32]),
                                                             op=ALU.subtract), reads=[bn], writes=["EE" + x])
                    if dr == 0:
                        p.op("vector", lambda e: e.tensor_tensor(out=v3(EK, S), in0=TOTb, in1=P3, op=ALU.subtract),
                             reads=["PP" + x], writes=["EK" + x])
                    else:
                        p.op("vector", lambda e: e.tensor_tensor(out=EK[:, S], in0=P_[:, S], in1=L_[:, S], op=ALU.subtract),
                             reads=["PP" + x, "LF" + x], writes=["EK" + x])

                def st_exp(tl, S, x):
                    Bv = P_ if dr == 0 else BBb
                    bn = ("PP" if dr == 0 else "BB") + x
                    p.op("scalar", lambda e: e.activation(out=T1[:, S], in_=EE[:, S], func=AF.Exp), reads=["EE" + x], writes=["T1" + x])
                    p.op("scalar", lambda e: e.activation(out=T2[:, S], in_=EE[:, S], func=AF.Exp, scale=-1.0), reads=["EE" + x], writes=["T2" + x])
                    p.op("scalar", lambda e: e.activation(out=T3[:, S], in_=Bv[:, S], func=AF.Exp), reads=[bn], writes=["T3" + x])
                    p.op("scalar", lambda e: e.activation(out=T4[:, S], in_=EK[:, S], func=AF.Exp), reads=["EK" + x], writes=["T4" + x])
                    p.op("scalar", lambda e: e.activation(out=EB[:, tl * 4:tl * 4 + 4], in_=v3(P_, S)[:, :, 31], func=AF.Exp),
                         reads=["PP" + x], writes=["EB" + x])

                def st_mul(tl, S, x):
                    p.op("vector", lambda e: e.tensor_tensor(out=QTL[:, S], in0=PS[qb_][:, S], in1=T1[:, S], op=ALU.mult),
                         reads=["PS%d" % qb_, "T1" + x], writes=["QTL" + x])
                    p.op("vector", lambda e: e.tensor_tensor(out=KTL[:, S], in0=K_[:, S], in1=T2[:, S], op=ALU.mult),
                         reads=["KK" + x, "T2" + x], writes=["KTL" + x])
                    p.op("vector", lambda e: e.tensor_tensor(out=QB[:, S], in0=PS[qb_][:, S], in1=T3[:, S], op=ALU.mult),
                         reads=["PS%d" % qb_, "T3" + x], writes=["QB" + x])
                    p.op("vector", lambda e: e.tensor_tensor(out=KET[:, S], in0=K_[:, S], in1=T4[:, S], op=ALU.mult),
                         reads=["KK" + x, "T4" + x], writes=["KET" + x])

                for stg in (st_sig, st_f, st_ln, st_k, st_scan, st_b, st_e, st_exp, st_mul):
                    for tl in order:
                        stg(tl, slice(tl * 128, tl * 128 + 128), str(tl))

            cnt = {"sc": 0, "tr": 0, "o": 0, "st": 0, "sz": 0}

            def tile_ops(dr, blk, tl):
                t0, n = TB[blk]
                lo = tl * 128
                tt = (t0 + lo) // 128
                sbs = [0, 1]
                ai = cnt["sc"] % 2
                cnt["sc"] += 1
                at = AT[ai]
                for h in range(2):
                    R = slice(64 * h, 64 * h + 64)
                    p.op("tensor", lambda e, h=h, R=R: e.matmul(PS[sbs[h]][:, 0:128], lhsT=KTL[R, lo:lo + 128],
                                                                rhs=QTL[R, lo:lo + 128], start=True, stop=True),
                         reads=["KTL%d" % tl, "QTL%d" % tl], writes=["PS%d" % sbs[h]])
                for h in range(2):
                    p.op("vector", lambda e, h=h: e.tensor_tensor(out=at[:, h, :], in0=PS[sbs[h]][:, 0:128],
                                                                  in1=BMASK[:, dr, :], op=ALU.mult),
                         reads=["PS%d" % sbs[h], "BMASK"], writes=["AT%d_%d" % (ai, h)])
                tb_ = 4
                ki = cnt["tr"] % 2
                cnt["tr"] += 1
                trv = PS[tb_][:, 0:64].bitcast(BF16)
                kez = KEZ[ki]
                if "b_notr" not in KD:
                    p.op("tensor", lambda e: e.transpose(out=trv, in_=KET[:, lo:lo + 128], identity=IDENT[:]),
                         reads=["KET%d" % tl, "IDENT"], writes=["PS%d" % tb_])
                    evac(kez.rearrange("p (h x) -> p h x", x=192)[:, :, 0:64], trv.rearrange("p (h x) -> p h x", x=64),
                         ["PS%d" % tb_], ["KEZ%d" % ki])
                ob = 5
                cnt["o"] += 1

                def fpv(e):
                    ins = None
                    for h in range(2):
                        ins = e.matmul(PS[ob][:, 0:128], lhsT=VZ[:, tt, h, :], rhs=at[:, h, :], start=(h == 0), stop=False)
                    return ins
                p.op("tensor", fpv, reads=["AT%d_0" % ai, "AT%d_1" % ai] + vzn(tt), writes=["PS%d" % ob])
                chs = range(4) if dr == 0 else range(3, -1, -1)
                if "b_nostate" in KD:
                    chs = []
                for ci, c in enumerate(chs):
                    zi = cnt["sz"] % 2
                    szr = SZ[zi]
                    co = lo + c * 32

                    def fin_(e, c=c, ci=ci, szr=szr, co=co):
                        return e.matmul(PS[ob][:, c * 32:(c + 1) * 32], lhsT=szr[:, :], rhs=QB[:, co:co + 32],
                                        start=False, stop=(ci == 3))
                    p.op("tensor", fin_, reads=["SZ%d" % zi, "QB%d" % tl], writes=["PS%d" % ob])
                    stb = 6 + cnt["st"] % 2
                    cnt["st"] += 1

                    def fst(e, c=c, stb=stb):
                        ins = None
                        for h in range(2):
                            kw = dict(tile_position=(96, 0)) if c == 3 else {}
                            ins = e.matmul(PS[stb][:, 0:128], lhsT=kez[32 * c:32 * c + 32, 128 * h:128 * h + 128],
                                           rhs=VZ[32 * c:32 * c + 32, tt, h, :], start=(h == 0), stop=(h == 1), **kw)
                        return ins
                    p.op("tensor", fst, reads=["KEZ%d" % ki] + vzn(tt), writes=["PS%d" % stb])
                    cg = (lo // 32) + c
                    p.op("vector", lambda e, stb=stb, cg=cg: e.scalar_tensor_tensor(
                        out=ST, in0=ST, scalar=EB[:, cg:cg + 1], in1=PS[stb][:, 0:128],
                        op0=ALU.mult, op1=ALU.add), reads=["ST", "EB%d" % tl, "PS%d" % stb], writes=["ST"])
                    cnt["sz"] += 1
                    zn = cnt["sz"] % 2
                    p.op("vector", lambda e, zn=zn: e.tensor_copy(out=SZ[zn], in_=ST), reads=["ST"], writes=["SZ%d" % zn])
                tok = slice(t0 + lo, t0 + lo + 128)
                if dr == 0:
                    p.op("scalar", lambda e: e.activation(out=OF[:, tok], in_=PS[ob][:, 0:128], func=AF.Copy),
                         reads=["PS%d" % ob], writes=["OF%d" % tt])
                else:
                    p.op("vector", lambda e: e.tensor_tensor(out=OF[:, tok], in0=PS[ob][:, 0:128], in1=OF[:, tok], op=ALU.add),
                         reads=["PS%d" % ob, "OF%d" % tt], writes=["OF%d" % tt])

            for dr in range(2):
                p.op("vector", lambda e: e.memset(ST, 0.0), reads=["ST"], writes=["ST"])
                zi = cnt["sz"] % 2
                p.op("vector", lambda e, zi=zi: e.memset(SZ[zi], 0.0), writes=["SZ%d" % zi])
                blocks = [0, 1, 2, 3, 4] if dr == 0 else [0, 4, 3, 2, 1]
                for blk in blocks:
                    ntl = TB[blk][1] // 128
                    if "b_noew" in KD:
                        continue
                    ew(dr, blk)
                    tls = range(ntl) if dr == 0 else range(ntl - 1, -1, -1)
                    for tl in tls:
                        if "b_notile" not in KD:
                            tile_ops(dr, blk, tl)
            p.wrelease()
            bset[0] = [0, 1, 2, 3]
            S0, S1 = F_, L_
            for blk, (t0, n) in enumerate(TB):
                N = slice(0, n)
                col = 0 if blk == 0 else 1 + s
                mxb = MXB[blk % 2]
                mxn = "MXB%d" % (blk % 2)
                tok = slice(t0, t0 + n)
                ofn = ["OF%d" % t_ for t_ in range(t0 // 128, (t0 + n) // 128)]
                p.op("scalar", lambda e, tok=tok, N=N: e.activation(out=SQ[:, N], in_=OF[:, tok], func=AF.Square),
                     reads=ofn, writes=["SQ"])
                b2 = bank()
                p.op("tensor", lambda e, b2=b2, N=N: e.matmul(PS[b2][:, N], lhsT=BONES[:], rhs=SQ[:, N], start=True, stop=True),
                     reads=["SQ", "BONES"], writes=["PS%d" % b2])
                p.op("scalar", lambda e, b2=b2, N=N: e.activation(out=S0[:, N], in_=PS[b2][:, N], func=AF.Ln, scale=1.0 / 64, bias=EPS),
                     reads=["PS%d" % b2], writes=["FF"])
                p.op("scalar", lambda e, N=N: e.activation(out=S0[:, N], in_=S0[:, N], func=AF.Exp, scale=-0.5), reads=["FF"], writes=["FF"])
                p.op("vector", lambda e, tok=tok, N=N: e.scalar_tensor_tensor(
                    out=S0[:, N], in0=OF[:, tok], scalar=HNG[:, l:l + 1], in1=S0[:, N], op0=ALU.mult, op1=ALU.mult),
                    reads=ofn + ["FF", "HNG"], writes=["FF"])
                bg = bank()
                proj_fm(sG, WG, 0, blk, bg)
                p.op("scalar", lambda e, bg=bg, N=N: e.activation(out=S1[:, N], in_=PS[bg][:, N], func=AF.Silu),
                     reads=["PS%d" % bg], writes=["LF"])
                p.op("vector", lambda e, mxb=mxb, N=N: e.tensor_tensor(out=mxb[:, N], in0=S0[:, N], in1=S1[:, N], op=ALU.mult),
                     reads=["FF", "LF"], writes=[mxn])
                bset[0] = [4, 5, 6, 7]
                for m in range(8):
                    b = bank()
                    p.op("tensor", lambda e, b=b, m=m, mxb=mxb, N=N: e.matmul(PS[b][:, N], lhsT=Wo[:, m * 128:(m + 1) * 128],
                                                                              rhs=mxb[:, N], start=True, stop=True),
                         reads=["WS%d" % so, mxn], writes=["PS%d" % b])
                    p.op("vector", lambda e, b=b, m=m, N=N, tok=tok, col=col: e.scalar_tensor_tensor(
                        out=XT[:, m, tok], in0=PS[b][:, N], scalar=MOD[:, l, 16 + m, col:col + 1],
                        in1=XT[:, m, tok], op0=ALU.mult, op1=ALU.add),
                        reads=["PS%d" % b, "MOD%d_%d" % (l, col)] + xtn([m], blk), writes=xtn([m], blk))
                bset[0] = [0, 1, 2, 3]
            bset[0] = list(range(8))
            p.wrelease(2)

        def mixer_C(l, s):
            p.sync_all()
            cv = Carve()
            TABc, TABs = load_rope(cv, 1)
            QT = cv.bf16(2 * T).rearrange("p (c t) -> p c t", c=2)
            KT = cv.bf16(2 * T).rearrange("p (c t) -> p c t", c=2)
            VA = cv.bf16(18 * 2 * 192).rearrange("p (a h f) -> p a h f", h=2, f=192)
            MXB = [cv.bf16(1024).rearrange("p (c t) -> p c t", c=2) for _ in range(2)]
            PT = [cv.bf16(512) for _ in range(4)]
            S0 = cv.f32(512)
            S1 = cv.f32(512)
            S2 = cv.f32(512)
            SQ = cv.bf16(512)
            QN = cv.bf16(512)
            if "nomemset" not in KD:
                p.op("vector", lambda e: e.memset(VA.rearrange("p a h f -> p (a h) f")[:, :, 64:128], 1.0), writes=["VA1"])
            sw = p.wnext(("wi", l, COL_CQK, 512))
            W = RING[:, sw, :].rearrange("p (k f) -> p k f", k=8)
            for X in range(4):
                isq = X < 2
                c = X % 2
                dst = QT if isq else KT
                dn = "QT" if isq else "KT"
                for tbi, (t0, n) in enumerate(TB):
                    b = bank()
                    proj_fm(sw, W, X * 128, tbi, b)
                    p.op("scalar", lambda e, b=b, n=n: e.activation(out=SQ[:, 0:n], in_=PS[b][:, 0:n], func=AF.Square),
                         reads=["PS%d" % b], writes=["SQ"])
                    b2 = bank()
                    p.op("tensor", lambda e, b2=b2, n=n: e.matmul(PS[b2][:, 0:n], lhsT=BONES[:], rhs=SQ[:, 0:n], start=True, stop=True),
                         reads=["SQ", "BONES"], writes=["PS%d" % b2])
                    p.op("scalar", lambda e, b2=b2, n=n: e.activation(out=S0[:, 0:n], in_=PS[b2][:, 0:n], func=AF.Ln,
                                                                     scale=1.0 / 64, bias=EPS), reads=["PS%d" % b2], writes=["S0"])
                    p.op("scalar", lambda e, n=n: e.activation(out=S0[:, 0:n], in_=S0[:, 0:n], func=AF.Exp, scale=-0.5),
                         reads=["S0"], writes=["S0"])
                    gap = QKG[:, l, (0 if isq else 1):(1 if isq else 2)]
                    if tbi == 0:
                        p.op("vector", lambda e, b=b, n=n, t0=t0, dst=dst, c=c, gap=gap: e.scalar_tensor_tensor(
                            out=dst[:, c, t0:t0 + n], in0=PS[b][:, 0:n], scalar=gap, in1=S0[:, 0:n],
                            op0=ALU.mult, op1=ALU.mult), reads=["PS%d" % b, "S0", "QKG"], writes=["%s%d_%d" % (dn, c, tbi)])
                    else:
                        p.op("vector", lambda e, b=b, n=n, gap=gap: e.scalar_tensor_tensor(
                            out=QN[:, 0:n], in0=PS[b][:, 0:n], scalar=gap, in1=S0[:, 0:n],
                            op0=ALU.mult, op1=ALU.mult), reads=["PS%d" % b, "S0", "QKG"], writes=["QN"])
                        if "norope" in KD:
                            evac(dst[:, c, t0:t0 + n], QN[:, 0:n], ["QN"], ["%s%d_%d" % (dn, c, tbi)], eng="vector")
                        else:
                            rope_apply(1, QN, TABc, TABs, S1, S2, dst[:, c, t0:t0 + n], ["%s%d_%d" % (dn, c, tbi)], tbi)
            p.wrelease()
            vwn = 256 if "vw256" in KD else 128
            sv = p.wnext(("wi", l, COL_CV, vwn))
            Wv = RING[:, sv, 0:8 * vwn].rearrange("p (k f) -> p k f", k=8)
            for tt in range(0 if "nov" in KD else 18):
                b = bank()
                proj_tm(sv, Wv, 0, 128, tt, b)
                psv = PS[b][:, 0:128].rearrange("p (h f) -> p h f", h=2)
                if "vnoevac" in KD:
                    continue
                if "v2d" in KD:
                    ve = "scalar" if "vact" in KD else ("vector" if "vdve" in KD else None)
                    for h in range(2):
                        evac(VA[:, tt, h, 0:64], psv[:, h, :], ["PS%d" % b], ["VAa%d_%d" % (tt, h)], eng=ve)
                        if "vone" not in KD:
                            evac(VA[:, tt, h, 128:192], psv[:, h, :], ["PS%d" % b], ["VAb%d_%d" % (tt, h)], eng=ve)
                    continue
                evac(VA[:, tt, :, 0:64], psv, ["PS%d" % b], ["VAa%d" % tt])
                evac(VA[:, tt, :, 128:192], psv, ["PS%d" % b], ["VAb%d" % tt])
            p.wrelease()
            so = p.wnext(("wo", l, 512, 2))
            pend = []

            def flush():
                while pend:
                    pend.pop(0)()
            groups = []
            for qb, (q0, nq) in enumerate(TB):
                kts = [0, 1] if qb == 0 else list(range(18))
                mxb = MXB[qb % 2]
                mxn = "MXB%d" % (qb % 2)
                for c in range(2):
                    for hh in range(2):
                        ob = 4 + ((qb * 4 + c * 2 + hh) % 2)
                        lo = 64 * hh

                        def s_fn(e, ps, kt, c=c, lo=lo, q0=q0, nq=nq):
                            return e.matmul(ps[:, 0:nq], lhsT=KT[lo:lo + 64, c, kt * 128:(kt + 1) * 128],
                                            rhs=QT[lo:lo + 64, c, q0:q0 + nq], start=True, stop=True)

                        def pv_fn(e, ps, kt, pt, first, last, c=c, lo=lo, nq=nq):
                            return e.matmul(ps[:, 0:nq], lhsT=VA[:, kt, c, lo:lo + 128], rhs=pt, start=first, stop=last)

                        def fin(ob=ob, lo=lo, c=c, nq=nq, mxb=mxb, mxn=mxn, hh=hh, qb=qb):
                            flush()
                            dp = 64 - lo
                            p.op("vector", lambda e: e.tensor_copy(out=S0[:, 0:nq], in_=PS[ob][:, 0:nq]),
                                 reads=["PS%d" % ob], writes=["S0"])
                            p.op("vector", lambda e: e.tensor_copy(out=S1[lo:lo + 64, 0:nq], in_=S0[dp:dp + 64, 0:nq]),
                                 reads=["S0"], writes=["S1"])
                            p.op("vector", lambda e: e.reciprocal(out=S2[lo:lo + 64, 0:nq], in_=S1[lo:lo + 64, 0:nq]),
                                 reads=["S1"], writes=["S2"])
                            p.op("vector", lambda e: e.tensor_tensor(out=mxb[lo:lo + 64, c, 0:nq], in0=S0[lo:lo + 64, 0:nq],
                                                                     in1=S2[lo:lo + 64, 0:nq], op=ALU.mult),
                                 reads=["S0", "S2"], writes=[mxn + "_%d%d" % (c, hh)])
                            if c == 1 and hh == 1:
                                pend.append(lambda: outproj_block(l, s, so, lambda cc: mxb[:, cc, 0:nq],
                                            [mxn + "_%d%d" % (a_, b_) for a_ in range(2) for b_ in range(2)], qb))
                        kn = ["KT%d_%d" % (c, t_) for t_ in range(NTB)]
                        groups.append(dict(maps=[(s_fn, 0.125, pv_fn, ob)], nq=nq, kts=kts,
                                           reads=kn + ["QT%d_%d" % (c, qb)],
                                           vreads=["VA1"] + ["VAa%d" % t_ for t_ in kts] + ["VAb%d" % t_ for t_ in kts], fin=fin))
            bset[0] = [6]
            attention_core(groups, PT, sbanks=[0, 1, 2, 3], G=2)
            flush()
            bset[0] = list(range(8))
            p.wrelease()

        for s in range(NS):
            for k in range(8):
                p.op("sync", lambda e, k=k, s=s: e.dma_start(out=XT[:, k, :], in_=xT[s, k * 128:(k + 1) * 128, :]),
                     writes=xtn([k], 0) + xtn([k], 1) + xtn([k], 2) + xtn([k], 3) + xtn([k], 4), dma_sem="xt")
            p.seal([n_ for tbi in range(NTB) for n_ in xtn(ALLK, tbi)], "xt")
            for l in range(NL):
                norm_phase(l, s, 1)
                if "A" in mixers:
                    mixer_A(l, s)
                if "B" in mixers:
                    mixer_B(l, s)
                if "C" in mixers:
                    mixer_C(l, s)
                if "D" in mixers:
                    mixer_D(l, s)
                norm_phase(l, s, 2)
                ffn_phase(l, s)
            final_phase(s)
        if not p.dry:
            for sem in ("oOB0", "oOB1"):
                p.final_waits[sem] = p.cnt[sem]

    p0 = Prog(dry=True)
    gen(p0)
    p = Prog(dry=False, wlist=p0.wrec)
    gen(p)
    assert p.wpos == len(p.wlist)

    sems = {name: es.enter_context(nc.semaphore(name)) for name in sorted(p.semnames)}
    with nc.Block() as block:
        for eng in ENGS:
            def section(e, eng=eng):
                for waits, fn, sem, inc in p.streams[eng]:
                    for wsem, wval in waits:
                        e.wait_ge(sems[wsem], wval)
                    ins = fn(e)
                    ins.then_inc(sems[sem], inc)
                if eng == "sync":
                    for wsem, wval in p.final_waits.items():
                        e.wait_ge(sems[wsem], wval)
            getattr(block, eng)(section)
    es.close()
    ninstr = {e: len(p.streams[e]) for e in ENGS}
    return nc, ninstr


_CONST = {}
import os
KD = set(os.environ.get('KDBG', '').split(','))


def _consts():
    if _CONST:
        return _CONST
    bf = ml_dtypes.bfloat16
    i = np.arange(64)
    ang = 2 * np.pi * np.outer(i, i) / 64
    C64, S64 = np.cos(ang), np.sin(ang)
    bd = np.zeros((128, 256))
    for h in range(2):
        bd[64 * h:64 * h + 64, 64 * h:64 * h + 64] = C64
        bd[64 * h:64 * h + 64, 128 + 64 * h:128 + 64 * h + 64] = -S64
    n = np.arange(SEQ)
    ang = 2 * np.pi * ((np.outer(n, n) % SEQ).astype(np.float64)) / SEQ
    sc = 1.0 / np.sqrt(64.0 * SEQ)
    dftl = np.stack([np.cos(ang) * sc, np.sin(ang) * sc]).astype(bf)
    m = np.arange(TCX)
    angc = 2 * np.pi * ((np.outer(m, m) % TCX).astype(np.float64)) / TCX
    scc = 1.0 / np.sqrt(64.0 * TCX)
    dftc = np.stack([np.cos(angc) * scc, np.sin(angc) * scc], axis=1)
    dftc = dftc.reshape(2, 128, 2, TCX).transpose(1, 0, 2, 3).reshape(128, 1024)
    row = np.repeat(np.arange(SEQ // 64), 64).astype(np.float64)
    col = np.tile(np.arange(64), SEQ // 64).astype(np.float64)
    rope = np.zeros((4, 128, SEQ))
    perm = np.zeros((128, 256))
    for w, unit in enumerate((32, 64)):
        half, nf = unit // 2, unit // 4
        inv = (np.float32(10000.0) ** (-np.arange(nf, dtype=np.float32) / np.float32(nf))).astype(np.float64)
        for pp in range(128):
            d = pp % unit
            dh = d % half
            pos = row if d < half else col
            a = pos * inv[dh % nf]
            rope[2 * w, pp] = np.cos(a)
            rope[2 * w + 1, pp] = np.sin(a)
            if dh < nf:
                perm[pp + nf, 128 * w + pp] = -1.0
            else:
                perm[pp - nf, 128 * w + pp] = 1.0
    ss_, tt_ = np.meshgrid(np.arange(128), np.arange(128), indexing="ij")
    same = (ss_ // 32) == (tt_ // 32)
    bmask = np.concatenate([(same & (ss_ <= tt_)), (same & (ss_ >= tt_))], axis=1).astype(np.float64)
    _CONST.update(ident=np.eye(128).astype(bf), bmask=bmask.astype(bf))
    _CONST.update(bd=bd.astype(bf), dftl=dftl, dftc=dftc.astype(bf), rope=rope.astype(bf), perm=perm.astype(bf))
    return _CONST


def host_prepare(inputs, ncores, NS):
    f = lambda a: np.ascontiguousarray(np.asarray(a, dtype=np.float32))
    x, c, ctx, c_ctx = (np.asarray(inputs[k]) for k in ("x", "c", "ctx", "c_ctx"))

    def fm(v):
        v = np.asarray(v, dtype=np.float32)
        lead = v.shape[:-1]
        return np.moveaxis(v.reshape(*lead, 8, 128), -1, 0)

    shared = {
        "w_mod": f(inputs["w_mod"]),
        "b_modT": f(np.moveaxis(np.asarray(inputs["b_mod"]).reshape(DEPTH, 48, 128), -1, 0).reshape(128, DEPTH * 48)),
        "n1T": f(fm(inputs["norm1"]).reshape(128, DEPTH * KC)),
        "n2T": f(fm(inputs["norm2"]).reshape(128, DEPTH * KC)),
        "fnT": f(fm(inputs["final_norm"]).reshape(128, KC)),
        "w_in": f(np.asarray(inputs["w_in"])[:, :, WIN_IDX]),
        "qkg": f(np.stack([np.tile(np.asarray(inputs["q_norm"]), (1, 2)), np.tile(np.asarray(inputs["k_norm"]), (1, 2))],
                          axis=-1).transpose(1, 0, 2).reshape(128, DEPTH * 2)),
        "dng": f(np.tile(np.asarray(inputs["diff_norm"]), (1, 2)).T),
        "hng": f(np.tile(np.asarray(inputs["hgrn_norm"]), (1, 2)).T),
        "lbl": f(np.asarray(inputs["hgrn_lb_logits"]).reshape(2, DEPTH, 2, 128).transpose(3, 0, 2, 1).reshape(128, 4 * DEPTH)),
        "dlam": f(np.broadcast_to(np.asarray(inputs["diff_lambda"]).reshape(1, DEPTH * 128), (128, DEPTH * 128))),
        "w_out": f(inputs["w_out"]),
        "w_fi": f(inputs["w_ffn_in"]),
        "w_fo": f(inputs["w_ffn_out"]),
    }
    shared.update(_consts())
    in_maps = []
    for ci in range(ncores):
        bs = list(range(ci * NS, (ci + 1) * NS))
        xt = np.empty((NS, D, T), np.float32)
        for j, b in enumerate(bs):
            xt[j, :, :TCX] = ctx[b].T
            xt[j, :, TCX:] = x[b].T
        cols = [c_ctx] + [c[b] for b in bs] + [c_ctx] * (2 - NS)
        cv = np.stack([np.asarray(v, np.float32) for v in cols], axis=-1)
        cv = cv.reshape(8, 128, 3).transpose(1, 0, 2).reshape(128, 24)
        m = dict(shared)
        m["xT"] = xt
        m["cvec"] = f(cv)
        in_maps.append(m)
    return in_maps


_CACHE = {}


def run(inputs, ncores=8, NS=2, NL=DEPTH, mixers="ABCD"):
    key = (NS, NL, mixers)
    if key not in _CACHE:
        _CACHE[key] = build_program(NS, NL, mixers)
    nc, ninstr = _CACHE[key]
    in_maps = host_prepare(inputs, ncores, NS)
    res = run_bass_kernel_spmd(nc, in_maps, core_ids=list(range(ncores)))
    out = np.empty((ncores * NS, SEQ, D), np.float32)
    for ci in range(ncores):
        o = res.results[ci]["outT"]
        for j in range(NS):
            out[ci * NS + j] = o[j].T
    return out


def kernel(**inputs):
    return run(inputs, ncores=8, NS=2, NL=DEPTH, mixers="ABCD")
```

```python
import numpy as np
import ml_dtypes
from contextlib import ExitStack
import concourse.bass as bass
import concourse.mybir as mybir
from concourse.bass_utils import run_bass_kernel_spmd

F32 = mybir.dt.float32
BF16 = mybir.dt.bfloat16
AF = mybir.ActivationFunctionType
ALU = mybir.AluOpType

D = 1024
KC = 8
TCX = 256
SEQ = 2048
T = TCX + SEQ
DEPTH = 4
FH = 2816
NHB = 6
TB = [(0, 256)] + [(256 + 512 * i, 512) for i in range(4)]
NTB = len(TB)
EPS = 1e-6
NSLOT = 3
AW = 15360
COL_CQK = 2048
COL_CV = 2560
COL_D = 2688
WIN_COLS = 2944
ENGS = ("tensor", "vector", "scalar", "gpsimd", "sync")
_BIDX = np.concatenate([np.arange(768 + 256 * j + 128 * hc, 768 + 256 * j + 128 * hc + 128)
                        for hc in range(2) for j in range(4)])
WIN_IDX = np.concatenate([np.arange(0, 768), _BIDX, np.arange(1792, 2304), np.arange(2304, 2368), np.arange(2304, 2368),
                          np.arange(2368, 2432), np.arange(2368, 2432), np.arange(2432, 2816)])


class Prog:
    def __init__(self, dry=False, wlist=None):
        self.dry = dry
        self.streams = {e: [] for e in ENGS}
        self.cnt = {}
        self.waited = {e: {} for e in ENGS}
        self.res = {}
        self.barrier = {e: [] for e in ENGS}
        self.semnames = set(ENGS)
        self.final_waits = {}
        self.wlist = wlist if wlist is not None else []
        self.wrec = []
        self.wpos = 0
        self.wissued = 0
        self.wreleased = 0
        self.wloader = None

    def _need(self, eng, sem, val, weng, waits):
        if weng == eng and eng == "tensor":
            return
        if self.waited[eng].get(sem, 0) >= val:
            return
        if waits.get(sem, 0) < val:
            waits[sem] = val

    def op(self, eng, fn, reads=(), writes=(), dma_sem=None):
        if self.dry:
            return
        waits = {}
        for (sem, val, weng) in self.barrier[eng]:
            self._need(eng, sem, val, weng, waits)
        self.barrier[eng] = []
        for r in reads:
            st = self.res.get(r)
            if st is not None and st["w"] is not None:
                self._need(eng, *st["w"], waits)
            if st is not None and r.startswith("PS"):
                for sem, (val, reng) in st["r"].items():
                    if reng != eng:
                        self._need(eng, sem, val, reng, waits)
        for w in writes:
            st = self.res.get(w)
            if st is not None:
                if st["w"] is not None:
                    self._need(eng, *st["w"], waits)
                for sem, (val, reng) in st["r"].items():
                    self._need(eng, sem, val, reng, waits)
        for sem, val in waits.items():
            self.waited[eng][sem] = val
        if dma_sem is None:
            sem, inc, weng = eng, 1, eng
        else:
            sem, inc, weng = dma_sem, 16, "dma"
            self.semnames.add(sem)
        val = self.cnt.get(sem, 0) + inc
        self.cnt[sem] = val
        for r in reads:
            st = self.res.setdefault(r, {"w": None, "r": {}})
            st["r"][sem] = (val, weng)
        for w in writes:
            self.res[w] = {"w": (sem, val, weng), "r": {}}
        self.streams[eng].append((list(waits.items()), fn, sem, inc))

    def seal(self, resources, sem):
        if self.dry:
            return
        ev = (sem, self.cnt[sem], "dma")
        for r in resources:
            self.res[r] = {"w": ev, "r": {}}

    def sync_all(self, engines=("tensor", "vector", "scalar")):
        if self.dry:
            return
        evs = [(e, self.cnt.get(e, 0), e) for e in engines if self.cnt.get(e, 0) > 0]
        for e in engines:
            self.barrier[e] = list(evs)

    def _wpump(self):
        while self.wissued < len(self.wlist) and self.wissued < self.wreleased + NSLOT:
            j = self.wissued
            self.wissued += 1
            self.wloader(self, self.wlist[j], j % NSLOT)

    def wnext(self, desc):
        if self.dry:
            self.wrec.append(desc)
            return 0
        i = self.wpos
        assert self.wlist[i] == desc, (i, self.wlist[i], desc)
        self._wpump()
        assert i < self.wissued, "weight ring too small for this access pattern"
        self.wpos += 1
        return i % NSLOT

    def wrelease(self, n=1):
        if self.dry:
            return
        self.wreleased += n
        self._wpump()


def build_program(NS, NL, mixers=""):
    nc = bass.Bass("TRN2", target_bir_lowering=False)

    def dt(name, shape, dtype=F32, kind="ExternalInput"):
        return nc.dram_tensor(name, shape, dtype, kind=kind).ap()

    xT = dt("xT", [NS, D, T])
    cvec = dt("cvec", [128, KC * 3])
    w_mod = dt("w_mod", [DEPTH, D, 6 * D])
    b_modT = dt("b_modT", [128, DEPTH * 48])
    n1T = dt("n1T", [128, DEPTH * KC])
    n2T = dt("n2T", [128, DEPTH * KC])
    fnT = dt("fnT", [128, KC])
    w_in = dt("w_in", [DEPTH, D, WIN_COLS])
    w_out = dt("w_out", [DEPTH, D, D])
    w_fi = dt("w_fi", [DEPTH, D, 2 * FH])
    w_fo = dt("w_fo", [DEPTH, FH, D])
    outT = dt("outT", [NS, D, SEQ], kind="ExternalOutput")
    bd_d = dt("bd", [128, 256], BF16)
    perm_d = dt("perm", [128, 256], BF16)
    dftl_d = dt("dftl", [2, SEQ, SEQ], BF16)
    dftc_d = dt("dftc", [128, 1024], BF16)
    rope_d = dt("rope", [4, 128, SEQ], BF16)
    qkg_d = dt("qkg", [128, DEPTH * 2])
    dng_d = dt("dng", [128, DEPTH])
    dlam_d = dt("dlam", [128, DEPTH * 128])
    ident_d = dt("ident", [128, 128], BF16)
    bmask_d = dt("bmask", [128, 256], BF16)
    lbl_d = dt("lbl", [128, 4 * DEPTH])
    hng_d = dt("hng", [128, DEPTH])

    es = ExitStack()

    def sb(name, shape, dtype):
        return es.enter_context(nc.sbuf_tensor(name, shape, dtype))

    XT = sb("XT", [128, KC, T], F32)
    HT = sb("HT", [128, KC, T], BF16)
    RING = sb("RING", [128, NSLOT, 4096], BF16)
    ARENA = sb("ARENA", [128, AW], F32)
    MOD = sb("MOD", [128, DEPTH, 48, 3], F32)
    A1 = sb("A1", [128, DEPTH, KC, 3], F32)
    A2 = sb("A2", [128, DEPTH, KC, 3], F32)
    BMOD = sb("BMOD", [128, DEPTH, 48], F32)
    N1 = sb("N1", [128, DEPTH, KC], F32)
    N2 = sb("N2", [128, DEPTH, KC], F32)
    FN = sb("FN", [128, KC], F32)
    CV = sb("CV", [128, KC * 3], F32)
    SCT = sb("SCT", [128, KC, 3], BF16)
    ONES = sb("ONES", [128, 128], BF16)
    BONES = sb("BONES", [128, 128], BF16)
    BD = sb("BD", [128, 2, 128], BF16)
    PERM = sb("PERM", [128, 2, 128], BF16)
    QKG = sb("QKG", [128, DEPTH, 2], F32)
    DNG = sb("DNG", [128, DEPTH], F32)
    NLAM = sb("NLAM", [128, DEPTH], F32)
    IDENT = sb("IDENT", [128, 128], BF16)
    BMASK = sb("BMASK", [128, 2, 128], BF16)
    LB = sb("LB", [128, 4, DEPTH], F32)
    OML = sb("OML", [128, 4, DEPTH], F32)
    HNG = sb("HNG", [128, DEPTH], F32)
    PS = [es.enter_context(nc.psum_tensor("PS%d" % i, [128, 512], F32)) for i in range(8)]

    class Carve:
        def __init__(self):
            self.off = 0

        def f32(self, n):
            v = ARENA[:, self.off:self.off + n]
            self.off += n
            assert self.off <= AW, self.off
            return v

        def bf16(self, n):
            w = (n + 1) // 2
            v = ARENA[:, self.off:self.off + w].bitcast(BF16)
            self.off += w
            assert self.off <= AW, self.off
            return v

    def xtn(ks, tbi):
        return ["XT%d_%d" % (k, tbi) for k in ks]

    def htn(ks, tbi):
        return ["HT%d_%d" % (k, tbi) for k in ks]

    ALLK = list(range(8))

    def gen(p):
        psi = [0]
        bset = [list(range(8))]

        def bank():
            b = bset[0][psi[0] % len(bset[0])]
            psi[0] += 1
            return b

        def wloader(p, desc, slot):
            kind = desc[0]
            dst = RING[:, slot, :]
            if kind in ("mod", "fi", "wi"):
                _, l, c0, n = desc
                wsrc = {"mod": w_mod, "fi": w_fi, "wi": w_in}[kind]
                src = wsrc[l].rearrange("(k p) f -> p k f", p=128)[:, :, c0:c0 + n]
                d3 = dst[:, 0:8 * n].rearrange("p (k f) -> p k f", k=8)
            elif kind in ("fo", "wo"):
                _, l, r0, nch = desc
                wsrc = {"fo": w_fo, "wo": w_out}[kind]
                src = wsrc[l][r0:r0 + nch * 128, :].rearrange("(c p) f -> p c f", p=128)
                d3 = dst[:, 0:nch * 1024].rearrange("p (c f) -> p c f", c=nch)
            elif kind == "dft":
                _, part, half, tb = desc
                src = dftl_d[part][half * 1024:(half + 1) * 1024, tb * 512:(tb + 1) * 512].rearrange(
                    "(st p) t -> p st t", p=128)
                d3 = dst.rearrange("p (k f) -> p k f", k=8)
            elif kind == "dftc":
                src = dftc_d[:, :]
                d3 = dst[:, 0:1024]
            else:
                raise ValueError(kind)
            p.op("gpsimd", lambda e, d3=d3, src=src: e.dma_start(out=d3, in_=src),
                 writes=["WS%d" % slot], dma_sem="ws%d" % slot)

        p.wloader = wloader

        consts = []

        def cload(dst, src, name):
            p.op("sync", lambda e: e.dma_start(out=dst, in_=src), writes=[name], dma_sem="cst")
            consts.append(name)

        cload(CV[:], cvec[:, :], "CV")
        cload(BMOD[:].rearrange("p l j -> p (l j)"), b_modT[:, :], "BMOD")
        cload(N1[:].rearrange("p l k -> p (l k)"), n1T[:, :], "N1")
        cload(N2[:].rearrange("p l k -> p (l k)"), n2T[:, :], "N2")
        cload(FN[:], fnT[:, :], "FN")
        cload(BD[:].rearrange("p a b -> p (a b)"), bd_d[:, :], "BD")
        cload(PERM[:].rearrange("p a b -> p (a b)"), perm_d[:, :], "PERM")
        cload(QKG[:].rearrange("p a b -> p (a b)"), qkg_d[:, :], "QKG")
        cload(DNG[:], dng_d[:, :], "DNG")
        cload(IDENT[:], ident_d[:, :], "IDENT")
        cload(BMASK[:].rearrange("p a b -> p (a b)"), bmask_d[:, :], "BMASK")
        cload(HNG[:], hng_d[:, :], "HNG")
        cload(OML[:].rearrange("p a b -> p (a b)"), lbl_d[:, :], "OML")
        cv0 = Carve()
        DLAM = cv0.f32(DEPTH * 128)
        PRD = cv0.f32(DEPTH * 64)
        SMX = cv0.f32(DEPTH * 2)
        cload(DLAM, dlam_d[:, :], "DLAM")
        p.seal(consts, "cst")
        p.op("vector", lambda e: e.memset(ONES[:], 1.0), writes=["ONES"])
        p.op("vector", lambda e: e.memset(BONES[:], 0.0), writes=["BONES"])
        p.op("vector", lambda e: e.memset(BONES[0:64, 0:64], 1.0), reads=["BONES"], writes=["BONES"])
        p.op("vector", lambda e: e.memset(BONES[64:128, 64:128], 1.0), reads=["BONES"], writes=["BONES"])
        dl4 = DLAM.rearrange("p (l a d) -> p l a d", l=DEPTH, a=4)
        pr4 = PRD.rearrange("p (l a d) -> p l a d", l=DEPTH, a=2)
        for l in range(DEPTH):
            p.op("vector", lambda e, l=l: e.tensor_tensor(out=pr4[:, l], in0=dl4[:, l, 0:4:2, :], in1=dl4[:, l, 1:4:2, :],
                                                          op=ALU.mult), reads=["DLAM"], writes=["PRD"])
        p.op("vector", lambda e: e.tensor_reduce(out=SMX, in_=PRD.rearrange("p (a d) -> p a d", d=32),
                                                 axis=mybir.AxisListType.X, op=ALU.add), reads=["PRD"], writes=["SMX"])
        p.op("scalar", lambda e: e.activation(out=SMX, in_=SMX, func=AF.Exp), reads=["SMX"], writes=["SMX"])
        for l in range(DEPTH):
            li = 0.8 - 0.6 * float(np.exp(-0.3 * l))
            p.op("vector", lambda e, l=l, li=li: e.scalar_tensor_tensor(
                out=NLAM[:, l:l + 1], in0=SMX[:, 2 * l + 1:2 * l + 2], scalar=-li, in1=SMX[:, 2 * l:2 * l + 1],
                op0=ALU.add, op1=ALU.subtract), reads=["SMX"], writes=["NLAM"])

        LSM = cv0.f32(4)
        p.op("scalar", lambda e: e.activation(out=OML[:], in_=OML[:], func=AF.Exp), reads=["OML"], writes=["OML"])
        p.op("vector", lambda e: e.tensor_reduce(out=LSM, in_=OML[:], axis=mybir.AxisListType.X, op=ALU.add),
             reads=["OML"], writes=["LSM"])
        p.op("vector", lambda e: e.reciprocal(out=LSM, in_=LSM), reads=["LSM"], writes=["LSM"])
        p.op("vector", lambda e: e.tensor_tensor(out=OML[:], in0=OML[:], in1=LSM.unsqueeze(2).broadcast_to([128, 4, DEPTH]),
                                                 op=ALU.mult), reads=["OML", "LSM"], writes=["OML"])
        p.op("vector", lambda e: e.memset(LB[:, :, 0:1], 0.0), writes=["LB"])
        for l in range(1, DEPTH):
            p.op("vector", lambda e, l=l: e.tensor_tensor(out=LB[:, :, l:l + 1], in0=LB[:, :, l - 1:l], in1=OML[:, :, l:l + 1],
                                                          op=ALU.add), reads=["LB", "OML"], writes=["LB"])
        p.op("vector", lambda e: e.tensor_scalar(out=OML[:], in0=LB[:], scalar1=-1.0, scalar2=1.0, op0=ALU.mult, op1=ALU.add),
             reads=["LB", "OML"], writes=["OML"])

        p.op("scalar", lambda e: e.activation(out=SCT[:].rearrange("p k j -> p (k j)"), in_=CV[:], func=AF.Silu),
             reads=["CV"], writes=["SCT"])
        for l in range(NL):
            b = bank()
            for jb in range(12):
                slot = p.wnext(("mod", l, jb * 512, 512))
                W = RING[:, slot, :].rearrange("p (k f) -> p k f", k=8)

                def f(e, W=W, b=b, jb=jb):
                    ins = None
                    for j in range(4):
                        jj = jb * 4 + j
                        for k in range(8):
                            ins = e.matmul(PS[b][:, jj * 3:jj * 3 + 3], lhsT=W[:, k, j * 128:(j + 1) * 128],
                                           rhs=SCT[:, k, :], start=(k == 0), stop=(k == 7))
                    return ins
                p.op("tensor", f, reads=["WS%d" % slot, "SCT"], writes=["PS%d" % b])
                p.wrelease()
            psv = PS[b][:, 0:144].rearrange("p (j c) -> p j c", c=3)
            for c in range(3):
                p.op("vector", lambda e, c=c, l=l, psv=psv: e.tensor_tensor(
                    out=MOD[:, l, :, c], in0=psv[:, :, c], in1=BMOD[:, l, :], op=ALU.add),
                    reads=["PS%d" % b, "BMOD"], writes=["MOD%d_%d" % (l, c)])
            for c in range(3):
                p.op("vector", lambda e, c=c, l=l: e.scalar_tensor_tensor(
                    out=A1[:, l, :, c], in0=MOD[:, l, 8:16, c], scalar=1.0, in1=N1[:, l, :],
                    op0=ALU.add, op1=ALU.mult), reads=["MOD%d_%d" % (l, c), "N1"], writes=["A1_%d_%d" % (l, c)])
                p.op("vector", lambda e, c=c, l=l: e.scalar_tensor_tensor(
                    out=A2[:, l, :, c], in0=MOD[:, l, 32:40, c], scalar=1.0, in1=N2[:, l, :],
                    op0=ALU.add, op1=ALU.mult), reads=["MOD%d_%d" % (l, c), "N2"], writes=["A2_%d_%d" % (l, c)])

        def rstd_block(cv_bufs, tbi):
            SQB, LNV, RSTD = cv_bufs
            t0, n = TB[tbi]
            sq3 = SQB[:, 0:8 * n].rearrange("p (k t) -> p k t", k=8)
            p.op("scalar", lambda e: e.activation(out=sq3, in_=XT[:, :, t0:t0 + n], func=AF.Square),
                 reads=xtn(ALLK, tbi), writes=["SQ"])
            b = bank()

            def f(e):
                ins = None
                for k in range(8):
                    ins = e.matmul(PS[b][:, 0:n], lhsT=ONES[:], rhs=sq3[:, k, :], start=(k == 0), stop=(k == 7))
                return ins
            p.op("tensor", f, reads=["SQ", "ONES"], writes=["PS%d" % b])
            p.op("scalar", lambda e: e.activation(out=LNV[:, 0:n], in_=PS[b][:, 0:n], func=AF.Ln,
                                                  scale=1.0 / D, bias=EPS),
                 reads=["PS%d" % b], writes=["LNV"])
            p.op("scalar", lambda e: e.activation(out=RSTD[:, 0:n], in_=LNV[:, 0:n], func=AF.Exp, scale=-0.5),
                 reads=["LNV"], writes=["RSTD"])

        def norm_phase(l, s, which):
            p.sync_all()
            cv = Carve()
            bufs = (cv.bf16(4096), cv.f32(512), cv.f32(512))
            RSTD = bufs[2]
            TMP = [cv.f32(512) for _ in range(2)]
            Aw = A1 if which == 1 else A2
            awn = "A1_%d_" % l if which == 1 else "A2_%d_" % l
            sh0 = 0 if which == 1 else 24
            ti = 0
            for tbi, (t0, n) in enumerate(TB):
                col = 0 if tbi == 0 else 1 + s
                rstd_block(bufs, tbi)
                for k in range(8):
                    tm = TMP[ti % 2]
                    tmn = "TMP%d" % (ti % 2)
                    ti += 1
                    p.op("vector", lambda e, k=k, tm=tm, t0=t0, n=n, col=col: e.scalar_tensor_tensor(
                        out=tm[:, 0:n], in0=XT[:, k, t0:t0 + n], scalar=Aw[:, l, k, col:col + 1], in1=RSTD[:, 0:n],
                        op0=ALU.mult, op1=ALU.mult),
                        reads=xtn([k], tbi) + ["RSTD", awn + str(col)], writes=[tmn])
                    p.op("scalar", lambda e, k=k, tm=tm, t0=t0, n=n, col=col: e.activation(
                        out=HT[:, k, t0:t0 + n], in_=tm[:, 0:n], func=AF.Identity,
                        bias=MOD[:, l, sh0 + k, col:col + 1], scale=1.0),
                        reads=[tmn, "MOD%d_%d" % (l, col)], writes=htn([k], tbi))

        def ffn_phase(l, s):
            p.sync_all()
            cv = Carve()
            ACT3 = cv.bf16(4 * T).rearrange("p (c t) -> p c t", c=4)
            SG = [cv.f32(512) for _ in range(2)]
            sgi = 0
            for hb in range(NHB):
                nch = 4 if hb < 5 else 2
                sg_ = p.wnext(("fi", l, hb * 512, nch * 128))
                su_ = p.wnext(("fi", l, FH + hb * 512, nch * 128))
                Wg = RING[:, sg_, 0:8 * nch * 128].rearrange("p (k f) -> p k f", k=8)
                Wu = RING[:, su_, 0:8 * nch * 128].rearrange("p (k f) -> p k f", k=8)
                for tbi, (t0, n) in enumerate(TB):
                    for c in range(nch):
                        bg = bank()
                        bu = bank()

                        def fg(e, b=bg, W=Wg, c=c, t0=t0, n=n):
                            ins = None
                            for k in range(8):
                                ins = e.matmul(PS[b][:, 0:n], lhsT=W[:, k, c * 128:(c + 1) * 128],
                                               rhs=HT[:, k, t0:t0 + n], start=(k == 0), stop=(k == 7))
                            return ins
                        p.op("tensor", fg, reads=["WS%d" % sg_] + htn(ALLK, tbi), writes=["PS%d" % bg])
                        p.op("tensor", lambda e, b=bu, W=Wu, c=c, t0=t0, n=n, fg=fg: fg(e, b, W, c, t0, n),
                             reads=["WS%d" % su_] + htn(ALLK, tbi), writes=["PS%d" % bu])
                        sg = SG[sgi % 2]
                        sgn = "SG%d" % (sgi % 2)
                        sgi += 1
                        p.op("scalar", lambda e, b=bg, sg=sg, n=n: e.activation(
                            out=sg[:, 0:n], in_=PS[b][:, 0:n], func=AF.Silu),
                            reads=["PS%d" % bg], writes=[sgn])
                        p.op("vector", lambda e, b=bu, sg=sg, c=c, t0=t0, n=n: e.tensor_tensor(
                            out=ACT3[:, c, t0:t0 + n], in0=PS[b][:, 0:n], in1=sg[:, 0:n], op=ALU.mult),
                            reads=["PS%d" % bu, sgn], writes=["ACT%d_%d" % (c, tbi)])
                p.wrelease(2)
                so_ = p.wnext(("fo", l, hb * 512, nch))
                Wo = RING[:, so_, 0:nch * 1024].rearrange("p (c f) -> p c f", c=nch)
                for tbi, (t0, n) in enumerate(TB):
                    col = 0 if tbi == 0 else 1 + s
                    for m in range(8):
                        b = bank()

                        def fo(e, b=b, Wo=Wo, m=m, t0=t0, n=n, nch=nch):
                            ins = None
                            for c in range(nch):
                                ins = e.matmul(PS[b][:, 0:n], lhsT=Wo[:, c, m * 128:(m + 1) * 128],
                                               rhs=ACT3[:, c, t0:t0 + n], start=(c == 0), stop=(c == nch - 1))
                            return ins
                        p.op("tensor", fo, reads=["WS%d" % so_] + ["ACT%d_%d" % (c, tbi) for c in range(nch)],
                             writes=["PS%d" % b])
                        p.op("vector", lambda e, b=b, m=m, t0=t0, n=n, col=col: e.scalar_tensor_tensor(
                            out=XT[:, m, t0:t0 + n], in0=PS[b][:, 0:n], scalar=MOD[:, l, 40 + m, col:col + 1],
                            in1=XT[:, m, t0:t0 + n], op0=ALU.mult, op1=ALU.add),
                            reads=["PS%d" % b, "MOD%d_%d" % (l, col)] + xtn([m], tbi), writes=xtn([m], tbi))
                p.wrelease()

        def final_phase(s):
            p.sync_all()
            cv = Carve()
            bufs = (cv.bf16(4096), cv.f32(512), cv.f32(512))
            RSTD = bufs[2]
            OB = [cv.f32(512) for _ in range(2)]
            oi = 0
            for tbi in range(1, NTB):
                t0, n = TB[tbi]
                rstd_block(bufs, tbi)
                for k in range(8):
                    ob = OB[oi % 2]
                    obn = "OB%d" % (oi % 2)
                    oi += 1
                    p.op("vector", lambda e, k=k, ob=ob, t0=t0, n=n: e.scalar_tensor_tensor(
                        out=ob[:, 0:n], in0=XT[:, k, t0:t0 + n], scalar=FN[:, k:k + 1], in1=RSTD[:, 0:n],
                        op0=ALU.mult, op1=ALU.mult), reads=xtn([k], tbi) + ["RSTD", "FN"], writes=[obn])
                    p.op("sync", lambda e, k=k, ob=ob, t0=t0, n=n: e.dma_start(
                        out=outT[s, k * 128:(k + 1) * 128, t0 - TCX:t0 - TCX + n], in_=ob[:, 0:n]),
                        reads=[obn], dma_sem="o" + obn)


        def tbi_of_tile(tt):
            return 0 if tt < 2 else 1 + (tt - 2) // 4

        def proj_fm(slot, W, c0, tbi, b):
            t0, n = TB[tbi]

            def f(e):
                ins = None
                for k in range(8):
                    ins = e.matmul(PS[b][:, 0:n], lhsT=W[:, k, c0:c0 + 128], rhs=HT[:, k, t0:t0 + n],
                                   start=(k == 0), stop=(k == 7))
                return ins
            p.op("tensor", f, reads=["WS%d" % slot] + htn(ALLK, tbi), writes=["PS%d" % b])

        def proj_tm(slot, W, c0, ncols, tt, b):
            def f(e):
                ins = None
                for k in range(8):
                    ins = e.matmul(PS[b][:, 0:ncols], lhsT=HT[:, k, tt * 128:(tt + 1) * 128], rhs=W[:, k, c0:c0 + ncols],
                                   start=(k == 0), stop=(k == 7))
                return ins
            p.op("tensor", f, reads=["WS%d" % slot] + htn(ALLK, tbi_of_tile(tt)), writes=["PS%d" % b])

        evi = [0]

        def evac(out, in_, reads, writes, eng=None):
            if eng is None:
                eng = ("scalar", "vector")[evi[0] % 2]
                evi[0] += 1
            if eng == "scalar":
                p.op("scalar", lambda e: e.activation(out=out, in_=in_, func=AF.Copy), reads=reads, writes=writes)
            else:
                p.op("vector", lambda e: e.tensor_copy(out=out, in_=in_), reads=reads, writes=writes)

        def outproj_block(l, s, so, mx_fn, mxnames, tbi):
            Wo = RING[:, so, 0:2048].rearrange("p (c f) -> p c f", c=2)
            t0, n = TB[tbi]
            col = 0 if tbi == 0 else 1 + s
            for m in range(8):
                b = bank()

                def fo(e, b=b, m=m):
                    ins = None
                    for c in range(2):
                        ins = e.matmul(PS[b][:, 0:n], lhsT=Wo[:, c, m * 128:(m + 1) * 128], rhs=mx_fn(c),
                                       start=(c == 0), stop=(c == 1))
                    return ins
                p.op("tensor", fo, reads=["WS%d" % so] + mxnames, writes=["PS%d" % b])
                p.op("vector", lambda e, b=b, m=m: e.scalar_tensor_tensor(
                    out=XT[:, m, t0:t0 + n], in0=PS[b][:, 0:n], scalar=MOD[:, l, 16 + m, col:col + 1],
                    in1=XT[:, m, t0:t0 + n], op0=ALU.mult, op1=ALU.add),
                    reads=["PS%d" % b, "MOD%d_%d" % (l, col)] + xtn([m], tbi), writes=xtn([m], tbi))

        def mixer_D(l, s):
            p.sync_all()
            cv = Carve()
            UT = cv.bf16(2 * T).rearrange("p (c t) -> p c t", c=2)
            UCS = cv.bf16(18 * 512).rearrange("p (a f) -> p a f", f=512)
            MX = cv.bf16(2 * T).rearrange("p (c t) -> p c t", c=2)
            sw = p.wnext(("wi", l, COL_D, 256))
            W = RING[:, sw, 0:2048].rearrange("p (k f) -> p k f", k=8)
            for fc in range(2):
                for tbi, (t0, n) in enumerate(TB):
                    b = bank()
                    proj_fm(sw, W, fc * 128, tbi, b)
                    evac(UT[:, fc, t0:t0 + n], PS[b][:, 0:n], ["PS%d" % b], ["UT%d_%d" % (fc, tbi)])
            p.wrelease()
            for tt in range(18):
                b = bank()

                def f(e, tt=tt, b=b):
                    ins = None
                    for part in range(2):
                        for fc in range(2):
                            i = part * 2 + fc
                            ins = e.matmul(PS[b][:, i * 128:(i + 1) * 128], lhsT=UT[:, fc, tt * 128:(tt + 1) * 128],
                                           rhs=BD[:, part, :], start=True, stop=True)
                    return ins
                p.op("tensor", f, reads=["BD"] + ["UT%d_%d" % (fc, tbi_of_tile(tt)) for fc in range(2)],
                     writes=["PS%d" % b])
                evac(UCS[:, tt, :], PS[b][:, :], ["PS%d" % b], ["UCS%d" % tt])
            sc = p.wnext(("dftc",))
            DC = RING[:, sc, 0:1024].rearrange("p (st pt t) -> p st pt t", st=2, pt=2)
            for fc in range(2):
                b = bank()

                def f(e, fc=fc, b=b):
                    ins = None
                    i = 0
                    for st in range(2):
                        for part in range(2):
                            ins = e.matmul(PS[b][:, 0:256], lhsT=UCS[:, st, (part * 2 + fc) * 128:(part * 2 + fc + 1) * 128],
                                           rhs=DC[:, st, part, :], start=(i == 0), stop=(i == 3))
                            i += 1
                    return ins
                p.op("tensor", f, reads=["WS%d" % sc, "UCS0", "UCS1"], writes=["PS%d" % b])
                evac(MX[:, fc, 0:256], PS[b][:, 0:256], ["PS%d" % b], ["MX%d_0" % fc])
            p.wrelease()
            for tb in range(4):
                t0, n = TB[1 + tb]
                bb = [bank(), bank()]
                for part in range(2):
                    for half in range(2):
                        sd = p.wnext(("dft", part, half, tb))
                        DT = RING[:, sd, :].rearrange("p (st t) -> p st t", st=8)
                        for fc in range(2):
                            def f(e, fc=fc, part=part, half=half, DT=DT, b=bb[fc]):
                                ins = None
                                for st in range(8):
                                    ins = e.matmul(PS[b][:, :], lhsT=UCS[:, 2 + half * 8 + st, (part * 2 + fc) * 128:(part * 2 + fc + 1) * 128],
                                                   rhs=DT[:, st, :], start=(part == 0 and half == 0 and st == 0),
                                                   stop=(part == 1 and half == 1 and st == 7))
                                return ins
                            p.op("tensor", f, reads=["WS%d" % sd] + ["UCS%d" % (2 + half * 8 + st) for st in range(8)],
                                 writes=["PS%d" % bb[fc]])
                        p.wrelease()
                for fc in range(2):
                    evac(MX[:, fc, t0:t0 + n], PS[bb[fc]][:, :], ["PS%d" % bb[fc]], ["MX%d_%d" % (fc, 1 + tb)])
            so = p.wnext(("wo", l, 768, 2))
            for tbi, (t0, n) in enumerate(TB):
                outproj_block(l, s, so, lambda c, t0=t0, n=n: MX[:, c, t0:t0 + n], ["MX0_%d" % tbi, "MX1_%d" % tbi], tbi)
            p.wrelease()

        def load_rope(cv, which):
            TABc = cv.bf16(SEQ)
            TABs = cv.bf16(SEQ)
            if not p.dry:
                evs = [(e_, p.cnt.get(e_, 0), e_) for e_ in ("tensor", "vector", "scalar") if p.cnt.get(e_, 0) > 0]
                p.barrier["sync"] = list(evs)
            p.op("sync", lambda e: e.dma_start(out=TABc, in_=rope_d[2 * which]), writes=["TABC"], dma_sem="tab")
            p.op("sync", lambda e: e.dma_start(out=TABs, in_=rope_d[2 * which + 1]), writes=["TABS"], dma_sem="tab")
            p.seal(["TABC", "TABS"], "tab")
            return TABc, TABs

        def rope_apply(pidx, QN, TABc, TABs, S1, S2, dest, destnames, tbi):
            t0, n = TB[tbi]
            lt0 = t0 - TCX
            b3 = bank()
            p.op("tensor", lambda e: e.matmul(PS[b3][:, 0:n], lhsT=PERM[:, pidx, :], rhs=QN[:, 0:n], start=True, stop=True),
                 reads=["QN", "PERM"], writes=["PS%d" % b3])
            p.op("vector", lambda e: e.tensor_tensor(out=S1[:, 0:n], in0=QN[:, 0:n], in1=TABc[:, lt0:lt0 + n], op=ALU.mult),
                 reads=["QN", "TABC"], writes=["S1"])
            p.op("vector", lambda e: e.tensor_tensor(out=S2[:, 0:n], in0=PS[b3][:, 0:n], in1=TABs[:, lt0:lt0 + n], op=ALU.mult),
                 reads=["PS%d" % b3, "TABS"], writes=["S2"])
            p.op("vector", lambda e: e.tensor_tensor(out=dest, in0=S1[:, 0:n], in1=S2[:, 0:n], op=ALU.add),
                 reads=["S1", "S2"], writes=destnames)

        NWARM = int(os.environ.get("NWARM", "0"))
        NBURST = int(os.environ.get("NBURST", "0"))
        NGB = int(os.environ.get("NGB", "0"))

        def pe_warm(n):
            if n <= 0:
                return

            def f(e):
                ins = None
                for _ in range(n):
                    ins = e.matmul(PS[7][:, :], lhsT=ONES[:], rhs=HT[:, 0, 0:512], start=True, stop=True)
                return ins
            p.op("tensor", f, reads=["ONES"], writes=["PS7"])

        def attention_core(groups, PT, sbanks, G=2):
            batches = []
            sid = 0
            for g in groups:
                nk = len(g["kts"])
                st = []
                for ki, kt in enumerate(g["kts"]):
                    for mp in g["maps"]:
                        st.append((kt, mp, ki == 0, ki == nk - 1, sid))
                        sid += 1
                for i in range(0, len(st), G):
                    batches.append((g, st[i:i + G], i + G >= len(st)))
            assert len(sbanks) >= 2 * G and len(PT) >= 2 * G
            nb = len(batches)
            pe_warm(NBURST)
            for j in range(nb + 1):
                if j < nb:
                    g, st, _ = batches[j]
                    nq = g["nq"]

                    def fs(e, st=st, nq=nq):
                        ins = None
                        for (kt, mp, first, last, sid_) in st:
                            ins = mp[0](e, PS[sbanks[sid_ % len(sbanks)]], kt)
                        return ins
                    p.op("tensor", fs, reads=g["reads"], writes=["PS%d" % sbanks[x[4] % len(sbanks)] for x in st])
                    for (kt, mp, first, last, sid_) in st:
                        sbk = sbanks[sid_ % len(sbanks)]
                        pt = PT[sid_ % len(PT)]
                        p.op("scalar", lambda e, pt=pt, sbk=sbk, nq=nq, scale=mp[1]: e.activation(
                            out=pt[:, 0:nq], in_=PS[sbk][:, 0:nq], func=AF.Exp, scale=scale),
                            reads=["PS%d" % sbk], writes=["PT%d" % (sid_ % len(PT))])
                if j >= 1:
                    g, st, glast = batches[j - 1]
                    nq = g["nq"]

                    def fpv(e, st=st, nq=nq):
                        ins = None
                        for (kt, mp, first, last, sid_) in st:
                            ins = mp[2](e, PS[mp[3]], kt, PT[sid_ % len(PT)][:, 0:nq], first, last)
                        return ins
                    p.op("tensor", fpv, reads=["PT%d" % (x[4] % len(PT)) for x in st] + g["vreads"],
                         writes=sorted(set("PS%d" % x[1][3] for x in st)))
                    for _ in range(NWARM):
                        p.op("tensor", lambda e: e.matmul(PS[7][:, :], lhsT=ONES[:], rhs=HT[:, 0, 0:512], start=True, stop=True),
                             reads=["ONES"], writes=["PS7"])
                    if glast:
                        g["fin"]()
                        pe_warm(NGB)


        def mixer_A(l, s):
            p.sync_all()
            cv = Carve()
            TABc, TABs = load_rope(cv, 0)
            QT = cv.bf16(2 * T).rearrange("p (c t) -> p c t", c=2)
            KT = cv.bf16(2 * T).rearrange("p (c t) -> p c t", c=2)
            VA = cv.bf16(18 * 4 * 128).rearrange("p (a c h f) -> p a c h f", c=2, h=2, f=128)
            MXB = [cv.bf16(1024).rearrange("p (c t) -> p c t", c=2) for _ in range(2)]
            PT = [cv.bf16(512) for _ in range(4)]
            S0 = cv.f32(512)
            S1 = cv.f32(512)
            S2 = cv.f32(512)
            SQ = cv.bf16(512)
            QN = cv.bf16(512)
            li = 0.8 - 0.6 * float(np.exp(-0.3 * l))
            lnf = float(np.log(1.0 - li))
            VA2 = VA.rearrange("p a c h f -> p (a c) h f")
            p.op("vector", lambda e: e.memset(VA2[:, :, 0, 64:128], 1.0), writes=["VA1a"])
            p.op("vector", lambda e: e.memset(VA2[:, :, 1, 0:64], 1.0), writes=["VA1b"])
            sw = p.wnext(("wi", l, 0, 512))
            W = RING[:, sw, :].rearrange("p (k f) -> p k f", k=8)
            for X in range(4):
                isq = X < 2
                c = X % 2
                dst = QT if isq else KT
                dn = "QT" if isq else "KT"
                for tbi, (t0, n) in enumerate(TB):
                    b = bank()
                    proj_fm(sw, W, X * 128, tbi, b)
                    if tbi == 0:
                        evac(dst[:, c, t0:t0 + n], PS[b][:, 0:n], ["PS%d" % b], ["%s%d_%d" % (dn, c, tbi)])
                    else:
                        evac(QN[:, 0:n], PS[b][:, 0:n], ["PS%d" % b], ["QN"], eng="scalar")
                        rope_apply(0, QN, TABc, TABs, S1, S2, dst[:, c, t0:t0 + n], ["%s%d_%d" % (dn, c, tbi)], tbi)
            p.wrelease()
            sv = p.wnext(("wi", l, 512, 256))
            Wv = RING[:, sv, 0:2048].rearrange("p (k f) -> p k f", k=8)
            for tt in range(18):
                b = bank()
                proj_tm(sv, Wv, 0, 256, tt, b)
                psv = PS[b][:, 0:256].rearrange("p (c h f) -> p c h f", c=2, h=2)
                evac(VA[:, tt, :, 0, 0:64], psv[:, :, 0, :], ["PS%d" % b], ["VAa%d" % tt])
                evac(VA[:, tt, :, 1, 64:128], psv[:, :, 1, :], ["PS%d" % b], ["VAb%d" % tt])
            p.wrelease()
            so = p.wnext(("wo", l, 0, 2))
            pend = []

            def flush():
                while pend:
                    pend.pop(0)()
            groups = []
            gi = 0
            for qb, (q0, nq) in enumerate(TB):
                kts = [0, 1] if qb == 0 else list(range(18))
                mxb = MXB[qb % 2]
                mxn = "MXB%d" % (qb % 2)
                for c in range(2):
                    for hh in range(2):
                        obs = [4, 5]
                        gi += 1
                        lo = 64 * hh
                        maps = []
                        for comp in range(2):
                            u = 2 * hh + comp

                            def s_fn(e, ps, kt, c=c, u=u, q0=q0, nq=nq):
                                kw = dict(tile_position=(96, 0)) if u == 3 else {}
                                return e.matmul(ps[:, 0:nq], lhsT=KT[32 * u:32 * u + 32, c, kt * 128:(kt + 1) * 128],
                                                rhs=QT[32 * u:32 * u + 32, c, q0:q0 + nq], start=True, stop=True, **kw)

                            def pv_fn(e, ps, kt, pt, first, last, c=c, hh=hh, nq=nq):
                                return e.matmul(ps[:, 0:nq], lhsT=VA[:, kt, c, hh, :], rhs=pt, start=first, stop=last)
                            maps.append((s_fn, 32.0 ** -0.5, pv_fn, obs[comp]))

                        def fin(obs=obs, lo=lo, c=c, nq=nq, mxb=mxb, mxn=mxn, hh=hh, qb=qb):
                            flush()
                            dp = 64 - lo
                            R = slice(lo, lo + 64)
                            DP = slice(dp, dp + 64)
                            o1, o2 = obs
                            p.op("vector", lambda e: e.tensor_copy(out=S0[:, 0:nq], in_=PS[o1][:, 0:nq]),
                                 reads=["PS%d" % o1], writes=["S0"])
                            p.op("vector", lambda e: e.tensor_copy(out=S2[:, 0:nq], in_=PS[o2][:, 0:nq]),
                                 reads=["PS%d" % o2], writes=["S2"])
                            p.op("vector", lambda e: e.tensor_copy(out=S1[R, 0:nq], in_=S0[DP, 0:nq]), reads=["S0"], writes=["S1"])
                            p.op("vector", lambda e: e.reciprocal(out=S1[R, 0:nq], in_=S1[R, 0:nq]), reads=["S1"], writes=["S1"])
                            p.op("vector", lambda e: e.tensor_tensor(out=S0[R, 0:nq], in0=S0[R, 0:nq], in1=S1[R, 0:nq], op=ALU.mult),
                                 reads=["S0", "S1"], writes=["S0"])
                            p.op("vector", lambda e: e.tensor_copy(out=S1[R, 0:nq], in_=S2[DP, 0:nq]), reads=["S2", "S0"], writes=["S1"])
                            p.op("vector", lambda e: e.reciprocal(out=S1[R, 0:nq], in_=S1[R, 0:nq]), reads=["S1"], writes=["S1"])
                            p.op("vector", lambda e: e.tensor_tensor(out=S2[R, 0:nq], in0=S2[R, 0:nq], in1=S1[R, 0:nq], op=ALU.mult),
                                 reads=["S2", "S1"], writes=["S2"])
                            p.op("vector", lambda e: e.scalar_tensor_tensor(out=S0[R, 0:nq], in0=S2[R, 0:nq], scalar=NLAM[R, l:l + 1],
                                                                            in1=S0[R, 0:nq], op0=ALU.mult, op1=ALU.add),
                                 reads=["S2", "S0", "NLAM"], writes=["S0"])
                            p.op("scalar", lambda e: e.activation(out=SQ[R, 0:nq], in_=S0[R, 0:nq], func=AF.Square),
                                 reads=["S0"], writes=["SQ"])
                            mb = bank()
                            p.op("tensor", lambda e: e.matmul(PS[mb][:, 0:nq], lhsT=ONES[R, :], rhs=SQ[R, 0:nq], start=True, stop=True),
                                 reads=["SQ", "ONES"], writes=["PS%d" % mb])
                            p.op("scalar", lambda e: e.activation(out=S1[R, 0:nq], in_=PS[mb][R, 0:nq], func=AF.Ln,
                                                                  scale=1.0 / 64, bias=EPS), reads=["PS%d" % mb], writes=["S1"])
                            p.op("scalar", lambda e: e.activation(out=S1[R, 0:nq], in_=S1[R, 0:nq], func=AF.Exp, scale=-0.5, bias=lnf),
                                 reads=["S1"], writes=["S1"])
                            p.op("vector", lambda e: e.scalar_tensor_tensor(out=mxb[R, c, 0:nq], in0=S0[R, 0:nq], scalar=DNG[R, l:l + 1],
                                                                            in1=S1[R, 0:nq], op0=ALU.mult, op1=ALU.mult),
                                 reads=["S0", "S1", "DNG"], writes=[mxn + "_%d%d" % (c, hh)])
                            if c == 1 and hh == 1:
                                pend.append(lambda: outproj_block(l, s, so, lambda cc: mxb[:, cc, 0:nq],
                                            [mxn + "_%d%d" % (a_, b_) for a_ in range(2) for b_ in range(2)], qb))
                        kn = ["KT%d_%d" % (c, t_) for t_ in range(NTB)]
                        groups.append(dict(maps=maps, nq=nq, kts=kts, reads=kn + ["QT%d_%d" % (c, qb)],
                                           vreads=["VA1a", "VA1b"] + ["VAa%d" % t_ for t_ in kts] + ["VAb%d" % t_ for t_ in kts],
                                           fin=fin))
            bset[0] = [6]
            attention_core(groups, PT, sbanks=[0, 1, 2, 3], G=2)
            flush()
            bset[0] = list(range(8))
            p.wrelease()


        def mixer_B(l, s):
            for hc in range(2):
                mixer_B_hc(l, s, hc)

        def mixer_B_hc(l, s, hc):
            p.sync_all()
            cv = Carve()
            VZ = cv.bf16(18 * 2 * 128).rearrange("p (a h f) -> p a h f", h=2, f=128)
            OF = cv.f32(T)
            SZd = [[cv.f32(128) for _ in range(2)] for _ in range(2)]
            SHd = [[cv.bf16(128) for _ in range(2)] for _ in range(2)]
            F_ = cv.f32(512)
            L_ = cv.f32(512)
            K_ = cv.f32(512)
            P_ = cv.f32(512)
            EE = cv.f32(512)
            BBb = cv.f32(512)
            T1 = cv.f32(512)
            T2 = cv.f32(512)
            T3 = cv.f32(512)
            T4 = cv.f32(512)
            EK = cv.f32(512)
            RST = cv.bf16(512)
            QTLd = [cv.bf16(512) for _ in range(2)]
            KTLd = [cv.bf16(512) for _ in range(2)]
            QBd = [cv.bf16(512) for _ in range(2)]
            KETd = [cv.bf16(512) for _ in range(2)]
            EBd = [cv.f32(16) for _ in range(2)]
            AT = [cv.bf16(256).rearrange("p (h t) -> p h t", h=2) for _ in range(2)]
            KEZ = [cv.bf16(384) for _ in range(2)]
            MXB = [cv.bf16(512) for _ in range(2)]
            SQ = cv.bf16(512)
            p.op("vector", lambda e: e.memset(VZ.rearrange("p a h f -> p (a h f)"), 0.0), writes=["VZ0"])
            for i in range(2):
                p.op("vector", lambda e, i=i: e.memset(KEZ[i], 0.0), writes=["KEZ%d" % i])
            p.op("vector", lambda e: e.memset(RST, 1.0), writes=["RST"])
            p.op("vector", lambda e: e.memset(RST.rearrange("p (c t) -> p c t", t=32)[:, :, 0:1], 0.0),
                 reads=["RST"], writes=["RST"])
            sW = p.wnext(("wi", l, 768 + 512 * hc, 512))
            sG = p.wnext(("wi", l, 1792 + 128 * hc, 128))
            so = p.wnext(("wo", l, 256 + 128 * hc, 1))
            W = RING[:, sW, :].rearrange("p (k f) -> p k f", k=8)
            WG = RING[:, sG, 0:1024].rearrange("p (k f) -> p k f", k=8)
            Wo = RING[:, so, 0:1024]
            bset[0] = [0, 1]
            for tt in range(18):
                b = bank()
                proj_tm(sW, W, 384, 128, tt, b)
                evac(VZ[:, tt].rearrange("p h f -> p (h f)")[:, 0:256].rearrange("p (h x) -> p h x", h=2)[:, :, 0:64]
                     if False else VZ[:, tt, 0, 0:64], PS[b][:, 0:64], ["PS%d" % b, "VZ0"], ["VZa%d" % tt])
                evac(VZ[:, tt, 1, 64:128], PS[b][:, 64:128], ["PS%d" % b, "VZ0"], ["VZb%d" % tt])
            vzn = lambda tt: ["VZa%d" % tt, "VZb%d" % tt]

            def ew(dr, blk):
                t0, n = TB[blk]
                ntl = n // 128
                order = list(range(ntl)) if dr == 0 else list(range(ntl - 1, -1, -1))
                zc = 128 if dr == 0 else 256
                lbp = LB[:, dr * 2 + hc, l:l + 1]
                omp = OML[:, dr * 2 + hc, l:l + 1]
                zb, qb_ = 2, 3
                QTL, KTL, QB, KET, EB = QTLd[dr], KTLd[dr], QBd[dr], KETd[dr], EBd[dr]
                D_ = "d%d_" % dr
                proj_fm(sW, W, zc, blk, zb)
                proj_fm(sW, W, 0, blk, qb_)

                def v3(buf, S):
                    return buf[:, S].rearrange("p (c t) -> p c t", t=32)

                def st_sig(tl, S, x):
                    p.op("scalar", lambda e: e.activation(out=F_[:, S], in_=PS[zb][:, S], func=AF.Sigmoid),
                         reads=["PS%d" % zb], writes=["FF" + x])

                def st_f(tl, S, x):
                    p.op("vector", lambda e: e.tensor_scalar(out=F_[:, S], in0=F_[:, S], scalar1=omp, scalar2=lbp,
                                                             op0=ALU.mult, op1=ALU.add), reads=["FF" + x, "LB", "OML"], writes=["FF" + x])

                def st_ln(tl, S, x):
                    p.op("scalar", lambda e: e.activation(out=L_[:, S], in_=F_[:, S], func=AF.Ln), reads=["FF" + x], writes=["LF" + x])

                def st_k(tl, S, x):
                    p.op("vector", lambda e: e.tensor_scalar(out=K_[:, S], in0=F_[:, S], scalar1=-1.0, scalar2=1.0,
                                                             op0=ALU.mult, op1=ALU.add), reads=["FF" + x], writes=["KK" + x])

                def st_scan(tl, S, x):
                    p.op("vector", lambda e: e.tensor_tensor_scan(out=P_[:, S], data0=RST[:, S], data1=L_[:, S], initial=0.0,
                                                                  op0=ALU.mult, op1=ALU.add), reads=["RST", "LF" + x], writes=["PP" + x])

                def st_b(tl, S, x):
                    P3 = v3(P_, S)
                    TOTb = P3[:, :, 31:32].broadcast_to([128, 4, 32])
                    if dr == 1:
                        p.op("vector", lambda e: e.tensor_tensor(out=v3(EE, S), in0=TOTb, in1=P3, op=ALU.subtract),
                             reads=["PP" + x], writes=["EE" + x])
                        p.op("vector", lambda e: e.tensor_tensor(out=BBb[:, S], in0=EE[:, S], in1=L_[:, S], op=ALU.add),
                             reads=["EE" + x, "LF" + x], writes=["BB" + x])

                def st_e(tl, S, x):
                    P3 = v3(P_, S)
                    TOTb = P3[:, :, 31:32].broadcast_to([128, 4, 32])
                    B3 = P3 if dr == 0 else v3(BBb, S)
                    bn = ("PP" if dr == 0 else "BB") + x
                    p.op("vector", lambda e: e.tensor_tensor(out=v3(EE, S), in0=B3, in1=B3[:, :, 16:17].broadcast_to([128, 4, 32]),
                                                             op=ALU.subtract), reads=[bn], writes=["EE" + x])
                    if dr == 0:
                        p.op("vector", lambda e: e.tensor_tensor(out=v3(EK, S), in0=TOTb, in1=P3, op=ALU.subtract),
                             reads=["PP" + x], writes=["EK" + x])
                    else:
                        p.op("vector", lambda e: e.tensor_tensor(out=EK[:, S], in0=P_[:, S], in1=L_[:, S], op=ALU.subtract),
                             reads=["PP" + x, "LF" + x], writes=["EK" + x])

                def st_exp(tl, S, x):
                    Bv = P_ if dr == 0 else BBb
                    bn = ("PP" if dr == 0 else "BB") + x
                    p.op("scalar", lambda e: e.activation(out=T1[:, S], in_=EE[:, S], func=AF.Exp), reads=["EE" + x], writes=["T1" + x])
                    p.op("scalar", lambda e: e.activation(out=T2[:, S], in_=EE[:, S], func=AF.Exp, scale=-1.0), reads=["EE" + x], writes=["T2" + x])
                    p.op("scalar", lambda e: e.activation(out=T3[:, S], in_=Bv[:, S], func=AF.Exp), reads=[bn], writes=["T3" + x])
                    p.op("scalar", lambda e: e.activation(out=T4[:, S], in_=EK[:, S], func=AF.Exp), reads=["EK" + x], writes=["T4" + x])
                    p.op("scalar", lambda e: e.activation(out=EB[:, tl * 4:tl * 4 + 4], in_=v3(P_, S)[:, :, 31], func=AF.Exp),
                         reads=["PP" + x], writes=[D_ + "EB" + x])

                def st_mul(tl, S, x):
                    p.op("vector", lambda e: e.tensor_tensor(out=QTL[:, S], in0=PS[qb_][:, S], in1=T1[:, S], op=ALU.mult),
                         reads=["PS%d" % qb_, "T1" + x], writes=[D_ + "QTL" + x])
                    p.op("vector", lambda e: e.tensor_tensor(out=KTL[:, S], in0=K_[:, S], in1=T2[:, S], op=ALU.mult),
                         reads=["KK" + x, "T2" + x], writes=[D_ + "KTL" + x])
                    p.op("vector", lambda e: e.tensor_tensor(out=QB[:, S], in0=PS[qb_][:, S], in1=T3[:, S], op=ALU.mult),
                         reads=["PS%d" % qb_, "T3" + x], writes=[D_ + "QB" + x])
                    p.op("vector", lambda e: e.tensor_tensor(out=KET[:, S], in0=K_[:, S], in1=T4[:, S], op=ALU.mult),
                         reads=["KK" + x, "T4" + x], writes=[D_ + "KET" + x])

                for stg in (st_sig, st_f, st_ln, st_k, st_scan, st_b, st_e, st_exp, st_mul):
                    for tl in order:
                        stg(tl, slice(tl * 128, tl * 128 + 128), str(tl))

            cnt = {"sc": 0, "tr": 0, "o": 0, "st": 0, "sz0": 0, "sz1": 0}
            of_done = set()

            def tile_ops(dr, blk, tl):
                t0, n = TB[blk]
                lo = tl * 128
                tt = (t0 + lo) // 128
                QTL, KTL, QB, KET, EB = QTLd[dr], KTLd[dr], QBd[dr], KETd[dr], EBd[dr]
                SZ = SZd[dr]
                D_ = "d%d_" % dr
                sbs = [0, 1]
                ai = cnt["sc"] % 2
                cnt["sc"] += 1
                at = AT[ai]
                for h in range(2):
                    R = slice(64 * h, 64 * h + 64)
                    p.op("tensor", lambda e, h=h, R=R: e.matmul(PS[sbs[h]][:, 0:128], lhsT=KTL[R, lo:lo + 128],
                                                                rhs=QTL[R, lo:lo + 128], start=True, stop=True),
                         reads=[D_ + "KTL%d" % tl, D_ + "QTL%d" % tl], writes=["PS%d" % sbs[h]])
                for h in range(2):
                    p.op("vector", lambda e, h=h: e.tensor_tensor(out=at[:, h, :], in0=PS[sbs[h]][:, 0:128],
                                                                  in1=BMASK[:, dr, :], op=ALU.mult),
                         reads=["PS%d" % sbs[h], "BMASK"], writes=["AT%d_%d" % (ai, h)])
                tb_ = 4
                ki = cnt["tr"] % 2
                cnt["tr"] += 1
                trv = PS[tb_][:, 0:64].bitcast(BF16)
                kez = KEZ[ki]
                if "b_notr" not in KD:
                    p.op("tensor", lambda e: e.transpose(out=trv, in_=KET[:, lo:lo + 128], identity=IDENT[:]),
                         reads=[D_ + "KET%d" % tl, "IDENT"], writes=["PS%d" % tb_])
                    evac(kez.rearrange("p (h x) -> p h x", x=192)[:, :, 0:64], trv.rearrange("p (h x) -> p h x", x=64),
                         ["PS%d" % tb_], ["KEZ%d" % ki])
                ob = 5 + dr
                cnt["o"] += 1

                def fpv(e):
                    ins = None
                    for h in range(2):
                        ins = e.matmul(PS[ob][:, 0:128], lhsT=VZ[:, tt, h, :], rhs=at[:, h, :], start=(h == 0), stop=False)
                    return ins
                p.op("tensor", fpv, reads=["AT%d_0" % ai, "AT%d_1" % ai] + vzn(tt), writes=["PS%d" % ob])
                yield
                chs = range(4) if dr == 0 else range(3, -1, -1)
                if "b_nostate" in KD:
                    chs = []
                for ci, c in enumerate(chs):
                    zi = cnt["sz%d" % dr] % 2
                    szr = SHd[dr][zi]
                    co = lo + c * 32

                    def fin_(e, c=c, ci=ci, szr=szr, co=co):
                        return e.matmul(PS[ob][:, c * 32:(c + 1) * 32], lhsT=szr[:, :], rhs=QB[:, co:co + 32],
                                        start=False, stop=(ci == 3))
                    p.op("tensor", fin_, reads=[D_ + "SH%d" % zi, D_ + "QB%d" % tl], writes=["PS%d" % ob])
                    stb = 7
                    cnt["st"] += 1

                    def fst(e, c=c, stb=stb):
                        ins = None
                        for h in range(2):
                            kw = dict(tile_position=(96, 0)) if c == 3 else {}
                            ins = e.matmul(PS[stb][:, 0:128], lhsT=kez[32 * c:32 * c + 32, 128 * h:128 * h + 128],
                                           rhs=VZ[32 * c:32 * c + 32, tt, h, :], start=(h == 0), stop=(h == 1), **kw)
                        return ins
                    p.op("tensor", fst, reads=["KEZ%d" % ki] + vzn(tt), writes=["PS%d" % stb])
                    cg = (lo // 32) + c
                    cnt["sz%d" % dr] += 1
                    zn = cnt["sz%d" % dr] % 2
                    p.op("vector", lambda e, stb=stb, cg=cg, zi=zi, zn=zn: e.scalar_tensor_tensor(
                        out=SZ[zn], in0=SZ[zi], scalar=EB[:, cg:cg + 1], in1=PS[stb][:, 0:128],
                        op0=ALU.mult, op1=ALU.add), reads=[D_ + "SZ%d" % zi, D_ + "EB%d" % tl, "PS%d" % stb], writes=[D_ + "SZ%d" % zn])
                    p.op("gpsimd", lambda e, zn=zn: e.tensor_copy(out=SHd[dr][zn], in_=SZ[zn]),
                         reads=[D_ + "SZ%d" % zn], writes=[D_ + "SH%d" % zn])
                    yield
                tok = slice(t0 + lo, t0 + lo + 128)
                if tt not in of_done:
                    of_done.add(tt)
                    p.op("scalar", lambda e: e.activation(out=OF[:, tok], in_=PS[ob][:, 0:128], func=AF.Copy),
                         reads=["PS%d" % ob], writes=["OF%d" % tt])
                else:
                    p.op("vector", lambda e: e.tensor_tensor(out=OF[:, tok], in0=PS[ob][:, 0:128], in1=OF[:, tok], op=ALU.add),
                         reads=["PS%d" % ob, "OF%d" % tt], writes=["OF%d" % tt])

            def sweep(dr):
                SZ = SZd[dr]
                D_ = "d%d_" % dr
                p.op("vector", lambda e: e.memset(SZ[cnt["sz%d" % dr] % 2], 0.0), writes=[D_ + "SZ%d" % (cnt["sz%d" % dr] % 2)])
                p.op("vector", lambda e: e.memset(SHd[dr][cnt["sz%d" % dr] % 2], 0.0), writes=[D_ + "SH%d" % (cnt["sz%d" % dr] % 2)])
                blocks = [0, 1, 2, 3, 4] if dr == 0 else [0, 4, 3, 2, 1]
                for blk in blocks:
                    ntl = TB[blk][1] // 128
                    ew(dr, blk)
                    yield
                    tls = range(ntl) if dr == 0 else range(ntl - 1, -1, -1)
                    for tl in tls:
                        yield from tile_ops(dr, blk, tl)
                        yield

            gens = [sweep(0), sweep(1)]
            while gens:
                for g_ in list(gens):
                    try:
                        next(g_)
                    except StopIteration:
                        gens.remove(g_)
            p.wrelease()
            bset[0] = [0, 1, 2, 3]
            S0, S1 = F_, L_
            for blk, (t0, n) in enumerate(TB):
                N = slice(0, n)
                col = 0 if blk == 0 else 1 + s
                mxb = MXB[blk % 2]
                mxn = "MXB%d" % (blk % 2)
                tok = slice(t0, t0 + n)
                ofn = ["OF%d" % t_ for t_ in range(t0 // 128, (t0 + n) // 128)]
                p.op("scalar", lambda e, tok=tok, N=N: e.activation(out=SQ[:, N], in_=OF[:, tok], func=AF.Square),
                     reads=ofn, writes=["SQ"])
                b2 = bank()
                p.op("tensor", lambda e, b2=b2, N=N: e.matmul(PS[b2][:, N], lhsT=BONES[:], rhs=SQ[:, N], start=True, stop=True),
                     reads=["SQ", "BONES"], writes=["PS%d" % b2])
                p.op("scalar", lambda e, b2=b2, N=N: e.activation(out=S0[:, N], in_=PS[b2][:, N], func=AF.Ln, scale=1.0 / 64, bias=EPS),
                     reads=["PS%d" % b2], writes=["FF"])
                p.op("scalar", lambda e, N=N: e.activation(out=S0[:, N], in_=S0[:, N], func=AF.Exp, scale=-0.5), reads=["FF"], writes=["FF"])
                p.op("vector", lambda e, tok=tok, N=N: e.scalar_tensor_tensor(
                    out=S0[:, N], in0=OF[:, tok], scalar=HNG[:, l:l + 1], in1=S0[:, N], op0=ALU.mult, op1=ALU.mult),
                    reads=ofn + ["FF", "HNG"], writes=["FF"])
                bg = bank()
                proj_fm(sG, WG, 0, blk, bg)
                p.op("scalar", lambda e, bg=bg, N=N: e.activation(out=S1[:, N], in_=PS[bg][:, N], func=AF.Silu),
                     reads=["PS%d" % bg], writes=["LF"])
                p.op("vector", lambda e, mxb=mxb, N=N: e.tensor_tensor(out=mxb[:, N], in0=S0[:, N], in1=S1[:, N], op=ALU.mult),
                     reads=["FF", "LF"], writes=[mxn])
                bset[0] = [4, 5, 6, 7]
                for m in range(8):
                    b = bank()
                    p.op("tensor", lambda e, b=b, m=m, mxb=mxb, N=N: e.matmul(PS[b][:, N], lhsT=Wo[:, m * 128:(m + 1) * 128],
                                                                              rhs=mxb[:, N], start=True, stop=True),
                         reads=["WS%d" % so, mxn], writes=["PS%d" % b])
                    p.op("vector", lambda e, b=b, m=m, N=N, tok=tok, col=col: e.scalar_tensor_tensor(
                        out=XT[:, m, tok], in0=PS[b][:, N], scalar=MOD[:, l, 16 + m, col:col + 1],
                        in1=XT[:, m, tok], op0=ALU.mult, op1=ALU.add),
                        reads=["PS%d" % b, "MOD%d_%d" % (l, col)] + xtn([m], blk), writes=xtn([m], blk))
                bset[0] = [0, 1, 2, 3]
            bset[0] = list(range(8))
            p.wrelease(2)

        def mixer_C(l, s):
            p.sync_all()
            cv = Carve()
            TABc, TABs = load_rope(cv, 1)
            QT = cv.bf16(2 * T).rearrange("p (c t) -> p c t", c=2)
            KT = cv.bf16(2 * T).rearrange("p (c t) -> p c t", c=2)
            VA = cv.bf16(18 * 2 * 192).rearrange("p (a h f) -> p a h f", h=2, f=192)
            MXB = [cv.bf16(1024).rearrange("p (c t) -> p c t", c=2) for _ in range(2)]
            PT = [cv.bf16(512) for _ in range(4)]
            S0 = cv.f32(512)
            S1 = cv.f32(512)
            S2 = cv.f32(512)
            SQ = cv.bf16(512)
            QN = cv.bf16(512)
            if "nomemset" not in KD:
                p.op("vector", lambda e: e.memset(VA.rearrange("p a h f -> p (a h) f")[:, :, 64:128], 1.0), writes=["VA1"])
            sw = p.wnext(("wi", l, COL_CQK, 512))
            W = RING[:, sw, :].rearrange("p (k f) -> p k f", k=8)
            for X in range(4):
                isq = X < 2
                c = X % 2
                dst = QT if isq else KT
                dn = "QT" if isq else "KT"
                for tbi, (t0, n) in enumerate(TB):
                    b = bank()
                    proj_fm(sw, W, X * 128, tbi, b)
                    p.op("scalar", lambda e, b=b, n=n: e.activation(out=SQ[:, 0:n], in_=PS[b][:, 0:n], func=AF.Square),
                         reads=["PS%d" % b], writes=["SQ"])
                    b2 = bank()
                    p.op("tensor", lambda e, b2=b2, n=n: e.matmul(PS[b2][:, 0:n], lhsT=BONES[:], rhs=SQ[:, 0:n], start=True, stop=True),
                         reads=["SQ", "BONES"], writes=["PS%d" % b2])
                    p.op("scalar", lambda e, b2=b2, n=n: e.activation(out=S0[:, 0:n], in_=PS[b2][:, 0:n], func=AF.Ln,
                                                                     scale=1.0 / 64, bias=EPS), reads=["PS%d" % b2], writes=["S0"])
                    p.op("scalar", lambda e, n=n: e.activation(out=S0[:, 0:n], in_=S0[:, 0:n], func=AF.Exp, scale=-0.5),
                         reads=["S0"], writes=["S0"])
                    gap = QKG[:, l, (0 if isq else 1):(1 if isq else 2)]
                    if tbi == 0:
                        p.op("vector", lambda e, b=b, n=n, t0=t0, dst=dst, c=c, gap=gap: e.scalar_tensor_tensor(
                            out=dst[:, c, t0:t0 + n], in0=PS[b][:, 0:n], scalar=gap, in1=S0[:, 0:n],
                            op0=ALU.mult, op1=ALU.mult), reads=["PS%d" % b, "S0", "QKG"], writes=["%s%d_%d" % (dn, c, tbi)])
                    else:
                        p.op("vector", lambda e, b=b, n=n, gap=gap: e.scalar_tensor_tensor(
                            out=QN[:, 0:n], in0=PS[b][:, 0:n], scalar=gap, in1=S0[:, 0:n],
                            op0=ALU.mult, op1=ALU.mult), reads=["PS%d" % b, "S0", "QKG"], writes=["QN"])
                        if "norope" in KD:
                            evac(dst[:, c, t0:t0 + n], QN[:, 0:n], ["QN"], ["%s%d_%d" % (dn, c, tbi)], eng="vector")
                        else:
                            rope_apply(1, QN, TABc, TABs, S1, S2, dst[:, c, t0:t0 + n], ["%s%d_%d" % (dn, c, tbi)], tbi)
            p.wrelease()
            vwn = 256 if "vw256" in KD else 128
            sv = p.wnext(("wi", l, COL_CV, vwn))
            Wv = RING[:, sv, 0:8 * vwn].rearrange("p (k f) -> p k f", k=8)
            for tt in range(0 if "nov" in KD else 18):
                b = bank()
                proj_tm(sv, Wv, 0, 128, tt, b)
                psv = PS[b][:, 0:128].rearrange("p (h f) -> p h f", h=2)
                if "vnoevac" in KD:
                    continue
                if "v2d" in KD:
                    ve = "scalar" if "vact" in KD else ("vector" if "vdve" in KD else None)
                    for h in range(2):
                        evac(VA[:, tt, h, 0:64], psv[:, h, :], ["PS%d" % b], ["VAa%d_%d" % (tt, h)], eng=ve)
                        if "vone" not in KD:
                            evac(VA[:, tt, h, 128:192], psv[:, h, :], ["PS%d" % b], ["VAb%d_%d" % (tt, h)], eng=ve)
                    continue
                evac(VA[:, tt, :, 0:64], psv, ["PS%d" % b], ["VAa%d" % tt])
                evac(VA[:, tt, :, 128:192], psv, ["PS%d" % b], ["VAb%d" % tt])
            p.wrelease()
            so = p.wnext(("wo", l, 512, 2))
            pend = []

            def flush():
                while pend:
                    pend.pop(0)()
            groups = []
            for qb, (q0, nq) in enumerate(TB):
                kts = [0, 1] if qb == 0 else list(range(18))
                mxb = MXB[qb % 2]
                mxn = "MXB%d" % (qb % 2)
                for c in range(2):
                    for hh in range(2):
                        ob = 4 + ((qb * 4 + c * 2 + hh) % 2)
                        lo = 64 * hh

                        def s_fn(e, ps, kt, c=c, lo=lo, q0=q0, nq=nq):
                            return e.matmul(ps[:, 0:nq], lhsT=KT[lo:lo + 64, c, kt * 128:(kt + 1) * 128],
                                            rhs=QT[lo:lo + 64, c, q0:q0 + nq], start=True, stop=True)

                        def pv_fn(e, ps, kt, pt, first, last, c=c, lo=lo, nq=nq):
                            return e.matmul(ps[:, 0:nq], lhsT=VA[:, kt, c, lo:lo + 128], rhs=pt, start=first, stop=last)

                        def fin(ob=ob, lo=lo, c=c, nq=nq, mxb=mxb, mxn=mxn, hh=hh, qb=qb):
                            flush()
                            dp = 64 - lo
                            p.op("vector", lambda e: e.tensor_copy(out=S0[:, 0:nq], in_=PS[ob][:, 0:nq]),
                                 reads=["PS%d" % ob], writes=["S0"])
                            p.op("vector", lambda e: e.tensor_copy(out=S1[lo:lo + 64, 0:nq], in_=S0[dp:dp + 64, 0:nq]),
                                 reads=["S0"], writes=["S1"])
                            p.op("vector", lambda e: e.reciprocal(out=S2[lo:lo + 64, 0:nq], in_=S1[lo:lo + 64, 0:nq]),
                                 reads=["S1"], writes=["S2"])
                            p.op("vector", lambda e: e.tensor_tensor(out=mxb[lo:lo + 64, c, 0:nq], in0=S0[lo:lo + 64, 0:nq],
                                                                     in1=S2[lo:lo + 64, 0:nq], op=ALU.mult),
                                 reads=["S0", "S2"], writes=[mxn + "_%d%d" % (c, hh)])
                            if c == 1 and hh == 1:
                                pend.append(lambda: outproj_block(l, s, so, lambda cc: mxb[:, cc, 0:nq],
                                            [mxn + "_%d%d" % (a_, b_) for a_ in range(2) for b_ in range(2)], qb))
                        kn = ["KT%d_%d" % (c, t_) for t_ in range(NTB)]
                        groups.append(dict(maps=[(s_fn, 0.125, pv_fn, ob)], nq=nq, kts=kts,
                                           reads=kn + ["QT%d_%d" % (c, qb)],
                                           vreads=["VA1"] + ["VAa%d" % t_ for t_ in kts] + ["VAb%d" % t_ for t_ in kts], fin=fin))
            bset[0] = [6]
            attention_core(groups, PT, sbanks=[0, 1, 2, 3], G=2)
            flush()
            bset[0] = list(range(8))
            p.wrelease()

        for s in range(NS):
            for k in range(8):
                p.op("sync", lambda e, k=k, s=s: e.dma_start(out=XT[:, k, :], in_=xT[s, k * 128:(k + 1) * 128, :]),
                     writes=xtn([k], 0) + xtn([k], 1) + xtn([k], 2) + xtn([k], 3) + xtn([k], 4), dma_sem="xt")
            p.seal([n_ for tbi in range(NTB) for n_ in xtn(ALLK, tbi)], "xt")
            for l in range(NL):
                norm_phase(l, s, 1)
                if "A" in mixers:
                    mixer_A(l, s)
                if "B" in mixers:
                    mixer_B(l, s)
                if "C" in mixers:
                    mixer_C(l, s)
                if "D" in mixers:
                    mixer_D(l, s)
                norm_phase(l, s, 2)
                ffn_phase(l, s)
            final_phase(s)
        if not p.dry:
            for sem in ("oOB0", "oOB1"):
                p.final_waits[sem] = p.cnt[sem]

    p0 = Prog(dry=True)
    gen(p0)
    p = Prog(dry=False, wlist=p0.wrec)
    gen(p)
    assert p.wpos == len(p.wlist)

    sems = {name: es.enter_context(nc.semaphore(name)) for name in sorted(p.semnames)}
    with nc.Block() as block:
        for eng in ENGS:
            def section(e, eng=eng):
                for waits, fn, sem, inc in p.streams[eng]:
                    for wsem, wval in waits:
                        e.wait_ge(sems[wsem], wval)
                    ins = fn(e)
                    ins.then_inc(sems[sem], inc)
                if eng == "sync":
                    for wsem, wval in p.final_waits.items():
                        e.wait_ge(sems[wsem], wval)
            getattr(block, eng)(section)
    es.close()
    ninstr = {e: len(p.streams[e]) for e in ENGS}
    return nc, ninstr


_CONST = {}
import os
KD = set(os.environ.get('KDBG', '').split(','))


def _consts():
    if _CONST:
        return _CONST
    bf = ml_dtypes.bfloat16
    i = np.arange(64)
    ang = 2 * np.pi * np.outer(i, i) / 64
    C64, S64 = np.cos(ang), np.sin(ang)
    bd = np.zeros((128, 256))
    for h in range(2):
        bd[64 * h:64 * h + 64, 64 * h:64 * h + 64] = C64
        bd[64 * h:64 * h + 64, 128 + 64 * h:128 + 64 * h + 64] = -S64
    n = np.arange(SEQ)
    ang = 2 * np.pi * ((np.outer(n, n) % SEQ).astype(np.float64)) / SEQ
    sc = 1.0 / np.sqrt(64.0 * SEQ)
    dftl = np.stack([np.cos(ang) * sc, np.sin(ang) * sc]).astype(bf)
    m = np.arange(TCX)
    angc = 2 * np.pi * ((np.outer(m, m) % TCX).astype(np.float64)) / TCX
    scc = 1.0 / np.sqrt(64.0 * TCX)
    dftc = np.stack([np.cos(angc) * scc, np.sin(angc) * scc], axis=1)
    dftc = dftc.reshape(2, 128, 2, TCX).transpose(1, 0, 2, 3).reshape(128, 1024)
    row = np.repeat(np.arange(SEQ // 64), 64).astype(np.float64)
    col = np.tile(np.arange(64), SEQ // 64).astype(np.float64)
    rope = np.zeros((4, 128, SEQ))
    perm = np.zeros((128, 256))
    for w, unit in enumerate((32, 64)):
        half, nf = unit // 2, unit // 4
        inv = (np.float32(10000.0) ** (-np.arange(nf, dtype=np.float32) / np.float32(nf))).astype(np.float64)
        for pp in range(128):
            d = pp % unit
            dh = d % half
            pos = row if d < half else col
            a = pos * inv[dh % nf]
            rope[2 * w, pp] = np.cos(a)
            rope[2 * w + 1, pp] = np.sin(a)
            if dh < nf:
                perm[pp + nf, 128 * w + pp] = -1.0
            else:
                perm[pp - nf, 128 * w + pp] = 1.0
    ss_, tt_ = np.meshgrid(np.arange(128), np.arange(128), indexing="ij")
    same = (ss_ // 32) == (tt_ // 32)
    bmask = np.concatenate([(same & (ss_ <= tt_)), (same & (ss_ >= tt_))], axis=1).astype(np.float64)
    _CONST.update(ident=np.eye(128).astype(bf), bmask=bmask.astype(bf))
    _CONST.update(bd=bd.astype(bf), dftl=dftl, dftc=dftc.astype(bf), rope=rope.astype(bf), perm=perm.astype(bf))
    return _CONST


def host_prepare(inputs, ncores, NS):
    f = lambda a: np.ascontiguousarray(np.asarray(a, dtype=np.float32))
    x, c, ctx, c_ctx = (np.asarray(inputs[k]) for k in ("x", "c", "ctx", "c_ctx"))

    def fm(v):
        v = np.asarray(v, dtype=np.float32)
        lead = v.shape[:-1]
        return np.moveaxis(v.reshape(*lead, 8, 128), -1, 0)

    shared = {
        "w_mod": f(inputs["w_mod"]),
        "b_modT": f(np.moveaxis(np.asarray(inputs["b_mod"]).reshape(DEPTH, 48, 128), -1, 0).reshape(128, DEPTH * 48)),
        "n1T": f(fm(inputs["norm1"]).reshape(128, DEPTH * KC)),
        "n2T": f(fm(inputs["norm2"]).reshape(128, DEPTH * KC)),
        "fnT": f(fm(inputs["final_norm"]).reshape(128, KC)),
        "w_in": f(np.asarray(inputs["w_in"])[:, :, WIN_IDX]),
        "qkg": f(np.stack([np.tile(np.asarray(inputs["q_norm"]), (1, 2)), np.tile(np.asarray(inputs["k_norm"]), (1, 2))],
                          axis=-1).transpose(1, 0, 2).reshape(128, DEPTH * 2)),
        "dng": f(np.tile(np.asarray(inputs["diff_norm"]), (1, 2)).T),
        "hng": f(np.tile(np.asarray(inputs["hgrn_norm"]), (1, 2)).T),
        "lbl": f(np.asarray(inputs["hgrn_lb_logits"]).reshape(2, DEPTH, 2, 128).transpose(3, 0, 2, 1).reshape(128, 4 * DEPTH)),
        "dlam": f(np.broadcast_to(np.asarray(inputs["diff_lambda"]).reshape(1, DEPTH * 128), (128, DEPTH * 128))),
        "w_out": f(inputs["w_out"]),
        "w_fi": f(inputs["w_ffn_in"]),
        "w_fo": f(inputs["w_ffn_out"]),
    }
    shared.update(_consts())
    in_maps = []
    for ci in range(ncores):
        bs = list(range(ci * NS, (ci + 1) * NS))
        xt = np.empty((NS, D, T), np.float32)
        for j, b in enumerate(bs):
            xt[j, :, :TCX] = ctx[b].T
            xt[j, :, TCX:] = x[b].T
        cols = [c_ctx] + [c[b] for b in bs] + [c_ctx] * (2 - NS)
        cv = np.stack([np.asarray(v, np.float32) for v in cols], axis=-1)
        cv = cv.reshape(8, 128, 3).transpose(1, 0, 2).reshape(128, 24)
        m = dict(shared)
        m["xT"] = xt
        m["cvec"] = f(cv)
        in_maps.append(m)
    return in_maps


_CACHE = {}


def run(inputs, ncores=8, NS=2, NL=DEPTH, mixers="ABCD"):
    key = (NS, NL, mixers)
    if key not in _CACHE:
        _CACHE[key] = build_program(NS, NL, mixers)
    nc, ninstr = _CACHE[key]
    in_maps = host_prepare(inputs, ncores, NS)
    res = run_bass_kernel_spmd(nc, in_maps, core_ids=list(range(ncores)))
    out = np.empty((ncores * NS, SEQ, D), np.float32)
    for ci in range(ncores):
        o = res.results[ci]["outT"]
        for j in range(NS):
            out[ci * NS + j] = o[j].T
    return out


def kernel(**inputs):
    return run(inputs, ncores=8, NS=2, NL=DEPTH, mixers="ABCD")
```

```python
import numpy as np
import ml_dtypes
from contextlib import ExitStack
import concourse.bass as bass
import concourse.mybir as mybir
from concourse.bass_utils import run_bass_kernel_spmd

F32 = mybir.dt.float32
BF16 = mybir.dt.bfloat16
AF = mybir.ActivationFunctionType
ALU = mybir.AluOpType

D = 1024
KC = 8
TCX = 256
SEQ = 2048
T = TCX + SEQ
DEPTH = 4
FH = 2816
NHB = 6
TB = [(0, 256)] + [(256 + 512 * i, 512) for i in range(4)]
NTB = len(TB)
EPS = 1e-6
NSLOT = 3
AW = 15360
COL_CQK = 2048
COL_CV = 2560
COL_D = 2688
WIN_COLS = 2944
ENGS = ("tensor", "vector", "scalar", "gpsimd", "sync")
_BIDX = np.concatenate([np.arange(768 + 256 * j + 128 * hc, 768 + 256 * j + 128 * hc + 128)
                        for hc in range(2) for j in range(4)])
WIN_IDX = np.concatenate([np.arange(0, 768), _BIDX, np.arange(1792, 2304), np.arange(2304, 2368), np.arange(2304, 2368),
                          np.arange(2368, 2432), np.arange(2368, 2432), np.arange(2432, 2816)])


class Prog:
    def __init__(self, dry=False, wlist=None):
        self.dry = dry
        self.streams = {e: [] for e in ENGS}
        self.cnt = {}
        self.waited = {e: {} for e in ENGS}
        self.res = {}
        self.barrier = {e: [] for e in ENGS}
        self.semnames = set(ENGS)
        self.final_waits = {}
        self.wlist = wlist if wlist is not None else []
        self.wrec = []
        self.wpos = 0
        self.wissued = 0
        self.wreleased = 0
        self.wloader = None

    def _need(self, eng, sem, val, weng, waits):
        if weng == eng and eng == "tensor":
            return
        if self.waited[eng].get(sem, 0) >= val:
            return
        if waits.get(sem, 0) < val:
            waits[sem] = val

    def op(self, eng, fn, reads=(), writes=(), dma_sem=None):
        if self.dry:
            return
        waits = {}
        for (sem, val, weng) in self.barrier[eng]:
            self._need(eng, sem, val, weng, waits)
        self.barrier[eng] = []
        for r in reads:
            st = self.res.get(r)
            if st is not None and st["w"] is not None:
                self._need(eng, *st["w"], waits)
            if st is not None and r.startswith("PS"):
                for sem, (val, reng) in st["r"].items():
                    if reng != eng:
                        self._need(eng, sem, val, reng, waits)
        for w in writes:
            st = self.res.get(w)
            if st is not None:
                if st["w"] is not None:
                    self._need(eng, *st["w"], waits)
                for sem, (val, reng) in st["r"].items():
                    self._need(eng, sem, val, reng, waits)
        for sem, val in waits.items():
            self.waited[eng][sem] = val
        if dma_sem is None:
            sem, inc, weng = eng, 1, eng
        else:
            sem, inc, weng = dma_sem, 16, "dma"
            self.semnames.add(sem)
        val = self.cnt.get(sem, 0) + inc
        self.cnt[sem] = val
        for r in reads:
            st = self.res.setdefault(r, {"w": None, "r": {}})
            st["r"][sem] = (val, weng)
        for w in writes:
            self.res[w] = {"w": (sem, val, weng), "r": {}}
        self.streams[eng].append((list(waits.items()), fn, sem, inc))

    def seal(self, resources, sem):
        if self.dry:
            return
        ev = (sem, self.cnt[sem], "dma")
        for r in resources:
            self.res[r] = {"w": ev, "r": {}}

    def sync_all(self, engines=("tensor", "vector", "scalar")):
        if self.dry:
            return
        evs = [(e, self.cnt.get(e, 0), e) for e in engines if self.cnt.get(e, 0) > 0]
        for e in engines:
            self.barrier[e] = list(evs)

    def _wpump(self):
        while self.wissued < len(self.wlist) and self.wissued < self.wreleased + NSLOT:
            j = self.wissued
            self.wissued += 1
            self.wloader(self, self.wlist[j], j % NSLOT)

    def wnext(self, desc):
        if self.dry:
            self.wrec.append(desc)
            return 0
        i = self.wpos
        assert self.wlist[i] == desc, (i, self.wlist[i], desc)
        self._wpump()
        assert i < self.wissued, "weight ring too small for this access pattern"
        self.wpos += 1
        return i % NSLOT

    def wrelease(self, n=1):
        if self.dry:
            return
        self.wreleased += n
        self._wpump()


def build_program(NS, NL, mixers=""):
    nc = bass.Bass("TRN2", target_bir_lowering=False)

    def dt(name, shape, dtype=F32, kind="ExternalInput"):
        return nc.dram_tensor(name, shape, dtype, kind=kind).ap()

    xT = dt("xT", [NS, D, T])
    cvec = dt("cvec", [128, KC * 3])
    w_mod = dt("w_mod", [DEPTH, D, 6 * D])
    b_modT = dt("b_modT", [128, DEPTH * 48])
    n1T = dt("n1T", [128, DEPTH * KC])
    n2T = dt("n2T", [128, DEPTH * KC])
    fnT = dt("fnT", [128, KC])
    w_in = dt("w_in", [DEPTH, D, WIN_COLS])
    w_out = dt("w_out", [DEPTH, D, D])
    w_fi = dt("w_fi", [DEPTH, D, 2 * FH])
    w_fo = dt("w_fo", [DEPTH, FH, D])
    outT = dt("outT", [NS, D, SEQ], kind="ExternalOutput")
    bd_d = dt("bd", [128, 256], BF16)
    perm_d = dt("perm", [128, 256], BF16)
    dftl_d = dt("dftl", [2, SEQ, SEQ], BF16)
    dftc_d = dt("dftc", [128, 1024], BF16)
    rope_d = dt("rope", [4, 128, SEQ], BF16)
    qkg_d = dt("qkg", [128, DEPTH * 2])
    dng_d = dt("dng", [128, DEPTH])
    dlam_d = dt("dlam", [128, DEPTH * 128])
    ident_d = dt("ident", [128, 128], BF16)
    bmask_d = dt("bmask", [128, 256], BF16)
    lbl_d = dt("lbl", [128, 4 * DEPTH])
    hng_d = dt("hng", [128, DEPTH])

    es = ExitStack()

    def sb(name, shape, dtype):
        return es.enter_context(nc.sbuf_tensor(name, shape, dtype))

    XT = sb("XT", [128, KC, T], F32)
    HT = sb("HT", [128, KC, T], BF16)
    RING = sb("RING", [128, NSLOT, 4096], BF16)
    ARENA = sb("ARENA", [128, AW], F32)
    MOD = sb("MOD", [128, DEPTH, 48, 3], F32)
    A1 = sb("A1", [128, DEPTH, KC, 3], F32)
    A2 = sb("A2", [128, DEPTH, KC, 3], F32)
    BMOD = sb("BMOD", [128, DEPTH, 48], F32)
    N1 = sb("N1", [128, DEPTH, KC], F32)
    N2 = sb("N2", [128, DEPTH, KC], F32)
    FN = sb("FN", [128, KC], F32)
    CV = sb("CV", [128, KC * 3], F32)
    SCT = sb("SCT", [128, KC, 3], BF16)
    ONES = sb("ONES", [128, 128], BF16)
    BONES = sb("BONES", [128, 128], BF16)
    BD = sb("BD", [128, 2, 128], BF16)
    PERM = sb("PERM", [128, 2, 128], BF16)
    QKG = sb("QKG", [128, DEPTH, 2], F32)
    DNG = sb("DNG", [128, DEPTH], F32)
    NLAM = sb("NLAM", [128, DEPTH], F32)
    IDENT = sb("IDENT", [128, 128], BF16)
    BMASK = sb("BMASK", [128, 2, 128], BF16)
    LB = sb("LB", [128, 4, DEPTH], F32)
    OML = sb("OML", [128, 4, DEPTH], F32)
    HNG = sb("HNG", [128, DEPTH], F32)
    NOML = sb("NOML", [128, 4, DEPTH], F32)
    PS = [es.enter_context(nc.psum_tensor("PS%d" % i, [128, 512], F32)) for i in range(8)]

    class Carve:
        def __init__(self):
            self.off = 0

        def f32(self, n):
            v = ARENA[:, self.off:self.off + n]
            self.off += n
            assert self.off <= AW, self.off
            return v

        def bf16(self, n):
            w = (n + 1) // 2
            v = ARENA[:, self.off:self.off + w].bitcast(BF16)
            self.off += w
            assert self.off <= AW, self.off
            return v

    def xtn(ks, tbi):
        return ["XT%d_%d" % (k, tbi) for k in ks]

    def htn(ks, tbi):
        return ["HT%d_%d" % (k, tbi) for k in ks]

    ALLK = list(range(8))

    def gen(p):
        psi = [0]
        bset = [list(range(8))]

        def bank():
            b = bset[0][psi[0] % len(bset[0])]
            psi[0] += 1
            return b

        def wloader(p, desc, slot):
            kind = desc[0]
            dst = RING[:, slot, :]
            if kind in ("mod", "fi", "wi"):
                _, l, c0, n = desc
                wsrc = {"mod": w_mod, "fi": w_fi, "wi": w_in}[kind]
                src = wsrc[l].rearrange("(k p) f -> p k f", p=128)[:, :, c0:c0 + n]
                d3 = dst[:, 0:8 * n].rearrange("p (k f) -> p k f", k=8)
            elif kind in ("fo", "wo"):
                _, l, r0, nch = desc
                wsrc = {"fo": w_fo, "wo": w_out}[kind]
                src = wsrc[l][r0:r0 + nch * 128, :].rearrange("(c p) f -> p c f", p=128)
                d3 = dst[:, 0:nch * 1024].rearrange("p (c f) -> p c f", c=nch)
            elif kind == "dft":
                _, part, half, tb = desc
                src = dftl_d[part][half * 1024:(half + 1) * 1024, tb * 512:(tb + 1) * 512].rearrange(
                    "(st p) t -> p st t", p=128)
                d3 = dst.rearrange("p (k f) -> p k f", k=8)
            elif kind == "dftc":
                src = dftc_d[:, :]
                d3 = dst[:, 0:1024]
            else:
                raise ValueError(kind)
            p.op("gpsimd", lambda e, d3=d3, src=src: e.dma_start(out=d3, in_=src),
                 writes=["WS%d" % slot], dma_sem="ws%d" % slot)

        p.wloader = wloader

        consts = []

        def cload(dst, src, name):
            p.op("sync", lambda e: e.dma_start(out=dst, in_=src), writes=[name], dma_sem="cst")
            consts.append(name)

        cload(CV[:], cvec[:, :], "CV")
        cload(BMOD[:].rearrange("p l j -> p (l j)"), b_modT[:, :], "BMOD")
        cload(N1[:].rearrange("p l k -> p (l k)"), n1T[:, :], "N1")
        cload(N2[:].rearrange("p l k -> p (l k)"), n2T[:, :], "N2")
        cload(FN[:], fnT[:, :], "FN")
        cload(BD[:].rearrange("p a b -> p (a b)"), bd_d[:, :], "BD")
        cload(PERM[:].rearrange("p a b -> p (a b)"), perm_d[:, :], "PERM")
        cload(QKG[:].rearrange("p a b -> p (a b)"), qkg_d[:, :], "QKG")
        cload(DNG[:], dng_d[:, :], "DNG")
        cload(IDENT[:], ident_d[:, :], "IDENT")
        cload(BMASK[:].rearrange("p a b -> p (a b)"), bmask_d[:, :], "BMASK")
        cload(HNG[:], hng_d[:, :], "HNG")
        cload(OML[:].rearrange("p a b -> p (a b)"), lbl_d[:, :], "OML")
        cv0 = Carve()
        DLAM = cv0.f32(DEPTH * 128)
        PRD = cv0.f32(DEPTH * 64)
        SMX = cv0.f32(DEPTH * 2)
        cload(DLAM, dlam_d[:, :], "DLAM")
        p.seal(consts, "cst")
        p.op("vector", lambda e: e.memset(ONES[:], 1.0), writes=["ONES"])
        p.op("vector", lambda e: e.memset(BONES[:], 0.0), writes=["BONES"])
        p.op("vector", lambda e: e.memset(BONES[0:64, 0:64], 1.0), reads=["BONES"], writes=["BONES"])
        p.op("vector", lambda e: e.memset(BONES[64:128, 64:128], 1.0), reads=["BONES"], writes=["BONES"])
        dl4 = DLAM.rearrange("p (l a d) -> p l a d", l=DEPTH, a=4)
        pr4 = PRD.rearrange("p (l a d) -> p l a d", l=DEPTH, a=2)
        for l in range(DEPTH):
            p.op("vector", lambda e, l=l: e.tensor_tensor(out=pr4[:, l], in0=dl4[:, l, 0:4:2, :], in1=dl4[:, l, 1:4:2, :],
                                                          op=ALU.mult), reads=["DLAM"], writes=["PRD"])
        p.op("vector", lambda e: e.tensor_reduce(out=SMX, in_=PRD.rearrange("p (a d) -> p a d", d=32),
                                                 axis=mybir.AxisListType.X, op=ALU.add), reads=["PRD"], writes=["SMX"])
        p.op("scalar", lambda e: e.activation(out=SMX, in_=SMX, func=AF.Exp), reads=["SMX"], writes=["SMX"])
        for l in range(DEPTH):
            li = 0.8 - 0.6 * float(np.exp(-0.3 * l))
            p.op("vector", lambda e, l=l, li=li: e.scalar_tensor_tensor(
                out=NLAM[:, l:l + 1], in0=SMX[:, 2 * l + 1:2 * l + 2], scalar=-li, in1=SMX[:, 2 * l:2 * l + 1],
                op0=ALU.add, op1=ALU.subtract), reads=["SMX"], writes=["NLAM"])

        LSM = cv0.f32(4)
        p.op("scalar", lambda e: e.activation(out=OML[:], in_=OML[:], func=AF.Exp), reads=["OML"], writes=["OML"])
        p.op("vector", lambda e: e.tensor_reduce(out=LSM, in_=OML[:], axis=mybir.AxisListType.X, op=ALU.add),
             reads=["OML"], writes=["LSM"])
        p.op("vector", lambda e: e.reciprocal(out=LSM, in_=LSM), reads=["LSM"], writes=["LSM"])
        p.op("vector", lambda e: e.tensor_tensor(out=OML[:], in0=OML[:], in1=LSM.unsqueeze(2).broadcast_to([128, 4, DEPTH]),
                                                 op=ALU.mult), reads=["OML", "LSM"], writes=["OML"])
        p.op("vector", lambda e: e.memset(LB[:, :, 0:1], 0.0), writes=["LB"])
        for l in range(1, DEPTH):
            p.op("vector", lambda e, l=l: e.tensor_tensor(out=LB[:, :, l:l + 1], in0=LB[:, :, l - 1:l], in1=OML[:, :, l:l + 1],
                                                          op=ALU.add), reads=["LB", "OML"], writes=["LB"])
        p.op("vector", lambda e: e.tensor_scalar(out=OML[:], in0=LB[:], scalar1=-1.0, scalar2=1.0, op0=ALU.mult, op1=ALU.add),
             reads=["LB", "OML"], writes=["OML"])
        p.op("vector", lambda e: e.tensor_scalar(out=NOML[:], in0=LB[:], scalar1=1.0, scalar2=-1.0, op0=ALU.mult, op1=ALU.add),
             reads=["LB"], writes=["OML"])

        p.op("scalar", lambda e: e.activation(out=SCT[:].rearrange("p k j -> p (k j)"), in_=CV[:], func=AF.Silu),
             reads=["CV"], writes=["SCT"])
        for l in range(NL):
            b = bank()
            for jb in range(12):
                slot = p.wnext(("mod", l, jb * 512, 512))
                W = RING[:, slot, :].rearrange("p (k f) -> p k f", k=8)

                def f(e, W=W, b=b, jb=jb):
                    ins = None
                    for j in range(4):
                        jj = jb * 4 + j
                        for k in range(8):
                            ins = e.matmul(PS[b][:, jj * 3:jj * 3 + 3], lhsT=W[:, k, j * 128:(j + 1) * 128],
                                           rhs=SCT[:, k, :], start=(k == 0), stop=(k == 7))
                    return ins
                p.op("tensor", f, reads=["WS%d" % slot, "SCT"], writes=["PS%d" % b])
                p.wrelease()
            psv = PS[b][:, 0:144].rearrange("p (j c) -> p j c", c=3)
            for c in range(3):
                p.op("vector", lambda e, c=c, l=l, psv=psv: e.tensor_tensor(
                    out=MOD[:, l, :, c], in0=psv[:, :, c], in1=BMOD[:, l, :], op=ALU.add),
                    reads=["PS%d" % b, "BMOD"], writes=["MOD%d_%d" % (l, c)])
            for c in range(3):
                p.op("vector", lambda e, c=c, l=l: e.scalar_tensor_tensor(
                    out=A1[:, l, :, c], in0=MOD[:, l, 8:16, c], scalar=1.0, in1=N1[:, l, :],
                    op0=ALU.add, op1=ALU.mult), reads=["MOD%d_%d" % (l, c), "N1"], writes=["A1_%d_%d" % (l, c)])
                p.op("vector", lambda e, c=c, l=l: e.scalar_tensor_tensor(
                    out=A2[:, l, :, c], in0=MOD[:, l, 32:40, c], scalar=1.0, in1=N2[:, l, :],
                    op0=ALU.add, op1=ALU.mult), reads=["MOD%d_%d" % (l, c), "N2"], writes=["A2_%d_%d" % (l, c)])

        def rstd_block(cv_bufs, tbi):
            SQB, LNV, RSTD = cv_bufs
            t0, n = TB[tbi]
            sq3 = SQB[:, 0:8 * n].rearrange("p (k t) -> p k t", k=8)
            p.op("scalar", lambda e: e.activation(out=sq3, in_=XT[:, :, t0:t0 + n], func=AF.Square),
                 reads=xtn(ALLK, tbi), writes=["SQ"])
            b = bank()

            def f(e):
                ins = None
                for k in range(8):
                    ins = e.matmul(PS[b][:, 0:n], lhsT=ONES[:], rhs=sq3[:, k, :], start=(k == 0), stop=(k == 7))
                return ins
            p.op("tensor", f, reads=["SQ", "ONES"], writes=["PS%d" % b])
            p.op("scalar", lambda e: e.activation(out=LNV[:, 0:n], in_=PS[b][:, 0:n], func=AF.Ln,
                                                  scale=1.0 / D, bias=EPS),
                 reads=["PS%d" % b], writes=["LNV"])
            p.op("scalar", lambda e: e.activation(out=RSTD[:, 0:n], in_=LNV[:, 0:n], func=AF.Exp, scale=-0.5),
                 reads=["LNV"], writes=["RSTD"])

        def norm_phase(l, s, which):
            p.sync_all()
            cv = Carve()
            bufs = (cv.bf16(4096), cv.f32(512), cv.f32(512))
            RSTD = bufs[2]
            TMP = [cv.f32(512) for _ in range(4)]
            Aw = A1 if which == 1 else A2
            awn = "A1_%d_" % l if which == 1 else "A2_%d_" % l
            sh0 = 0 if which == 1 else 24
            ti = 0
            for tbi, (t0, n) in enumerate(TB):
                col = 0 if tbi == 0 else 1 + s
                rstd_block(bufs, tbi)
                for k in range(8):
                    tm = TMP[ti % 4]
                    tmn = "TMP%d" % (ti % 4)
                    ti += 1
                    p.op("vector", lambda e, k=k, tm=tm, t0=t0, n=n, col=col: e.scalar_tensor_tensor(
                        out=tm[:, 0:n], in0=XT[:, k, t0:t0 + n], scalar=Aw[:, l, k, col:col + 1], in1=RSTD[:, 0:n],
                        op0=ALU.mult, op1=ALU.mult),
                        reads=xtn([k], tbi) + ["RSTD", awn + str(col)], writes=[tmn])
                    if k % 3 == 0:
                        p.op("vector", lambda e, k=k, tm=tm, t0=t0, n=n, col=col: e.tensor_scalar(
                            out=HT[:, k, t0:t0 + n], in0=tm[:, 0:n], scalar1=MOD[:, l, sh0 + k, col:col + 1], scalar2=None,
                            op0=ALU.add), reads=[tmn, "MOD%d_%d" % (l, col)], writes=htn([k], tbi))
                    else:
                        p.op("scalar", lambda e, k=k, tm=tm, t0=t0, n=n, col=col: e.activation(
                            out=HT[:, k, t0:t0 + n], in_=tm[:, 0:n], func=AF.Identity,
                            bias=MOD[:, l, sh0 + k, col:col + 1], scale=1.0),
                            reads=[tmn, "MOD%d_%d" % (l, col)], writes=htn([k], tbi))

        def ffn_phase(l, s):
            p.sync_all()
            cv = Carve()
            ACT3 = cv.bf16(4 * T).rearrange("p (c t) -> p c t", c=4)
            SG = [cv.f32(512) for _ in range(2)]
            sgi = 0
            for hb in range(NHB):
                nch = 4 if hb < 5 else 2
                sg_ = p.wnext(("fi", l, hb * 512, nch * 128))
                su_ = p.wnext(("fi", l, FH + hb * 512, nch * 128))
                Wg = RING[:, sg_, 0:8 * nch * 128].rearrange("p (k f) -> p k f", k=8)
                Wu = RING[:, su_, 0:8 * nch * 128].rearrange("p (k f) -> p k f", k=8)
                for tbi, (t0, n) in enumerate(TB):
                    for c in range(nch):
                        bg = bank()
                        bu = bank()

                        def fg(e, b=bg, W=Wg, c=c, t0=t0, n=n):
                            ins = None
                            for k in range(8):
                                ins = e.matmul(PS[b][:, 0:n], lhsT=W[:, k, c * 128:(c + 1) * 128],
                                               rhs=HT[:, k, t0:t0 + n], start=(k == 0), stop=(k == 7))
                            return ins
                        p.op("tensor", fg, reads=["WS%d" % sg_] + htn(ALLK, tbi), writes=["PS%d" % bg])
                        p.op("tensor", lambda e, b=bu, W=Wu, c=c, t0=t0, n=n, fg=fg: fg(e, b, W, c, t0, n),
                             reads=["WS%d" % su_] + htn(ALLK, tbi), writes=["PS%d" % bu])
                        sg = SG[sgi % 2]
                        sgn = "SG%d" % (sgi % 2)
                        sgi += 1
                        p.op("scalar", lambda e, b=bg, sg=sg, n=n: e.activation(
                            out=sg[:, 0:n], in_=PS[b][:, 0:n], func=AF.Silu),
                            reads=["PS%d" % bg], writes=[sgn])
                        p.op("vector", lambda e, b=bu, sg=sg, c=c, t0=t0, n=n: e.tensor_tensor(
                            out=ACT3[:, c, t0:t0 + n], in0=PS[b][:, 0:n], in1=sg[:, 0:n], op=ALU.mult),
                            reads=["PS%d" % bu, sgn], writes=["ACT%d_%d" % (c, tbi)])
                p.wrelease(2)
                so_ = p.wnext(("fo", l, hb * 512, nch))
                Wo = RING[:, so_, 0:nch * 1024].rearrange("p (c f) -> p c f", c=nch)
                for tbi, (t0, n) in enumerate(TB):
                    col = 0 if tbi == 0 else 1 + s
                    for m in range(8):
                        b = bank()

                        def fo(e, b=b, Wo=Wo, m=m, t0=t0, n=n, nch=nch):
                            ins = None
                            for c in range(nch):
                                ins = e.matmul(PS[b][:, 0:n], lhsT=Wo[:, c, m * 128:(m + 1) * 128],
                                               rhs=ACT3[:, c, t0:t0 + n], start=(c == 0), stop=(c == nch - 1))
                            return ins
                        p.op("tensor", fo, reads=["WS%d" % so_] + ["ACT%d_%d" % (c, tbi) for c in range(nch)],
                             writes=["PS%d" % b])
                        p.op("vector", lambda e, b=b, m=m, t0=t0, n=n, col=col: e.scalar_tensor_tensor(
                            out=XT[:, m, t0:t0 + n], in0=PS[b][:, 0:n], scalar=MOD[:, l, 40 + m, col:col + 1],
                            in1=XT[:, m, t0:t0 + n], op0=ALU.mult, op1=ALU.add),
                            reads=["PS%d" % b, "MOD%d_%d" % (l, col)] + xtn([m], tbi), writes=xtn([m], tbi))
                p.wrelease()

        def final_phase(s):
            p.sync_all()
            cv = Carve()
            bufs = (cv.bf16(4096), cv.f32(512), cv.f32(512))
            RSTD = bufs[2]
            OB = [cv.f32(512) for _ in range(2)]
            oi = 0
            for tbi in range(1, NTB):
                t0, n = TB[tbi]
                rstd_block(bufs, tbi)
                for k in range(8):
                    ob = OB[oi % 2]
                    obn = "OB%d" % (oi % 2)
                    oi += 1
                    p.op("vector", lambda e, k=k, ob=ob, t0=t0, n=n: e.scalar_tensor_tensor(
                        out=ob[:, 0:n], in0=XT[:, k, t0:t0 + n], scalar=FN[:, k:k + 1], in1=RSTD[:, 0:n],
                        op0=ALU.mult, op1=ALU.mult), reads=xtn([k], tbi) + ["RSTD", "FN"], writes=[obn])
                    p.op("sync", lambda e, k=k, ob=ob, t0=t0, n=n: e.dma_start(
                        out=outT[s, k * 128:(k + 1) * 128, t0 - TCX:t0 - TCX + n], in_=ob[:, 0:n]),
                        reads=[obn], dma_sem="o" + obn)


        def tbi_of_tile(tt):
            return 0 if tt < 2 else 1 + (tt - 2) // 4

        def proj_fm(slot, W, c0, tbi, b):
            t0, n = TB[tbi]

            def f(e):
                ins = None
                for k in range(8):
                    ins = e.matmul(PS[b][:, 0:n], lhsT=W[:, k, c0:c0 + 128], rhs=HT[:, k, t0:t0 + n],
                                   start=(k == 0), stop=(k == 7))
                return ins
            p.op("tensor", f, reads=["WS%d" % slot] + htn(ALLK, tbi), writes=["PS%d" % b])

        def proj_tm(slot, W, c0, ncols, tt, b):
            def f(e):
                ins = None
                for k in range(8):
                    ins = e.matmul(PS[b][:, 0:ncols], lhsT=HT[:, k, tt * 128:(tt + 1) * 128], rhs=W[:, k, c0:c0 + ncols],
                                   start=(k == 0), stop=(k == 7))
                return ins
            p.op("tensor", f, reads=["WS%d" % slot] + htn(ALLK, tbi_of_tile(tt)), writes=["PS%d" % b])

        evi = [0]

        def evac(out, in_, reads, writes, eng=None):
            if eng is None:
                eng = ("scalar", "vector")[evi[0] % 2]
                evi[0] += 1
            if eng == "scalar":
                p.op("scalar", lambda e: e.activation(out=out, in_=in_, func=AF.Copy), reads=reads, writes=writes)
            else:
                p.op("vector", lambda e: e.tensor_copy(out=out, in_=in_), reads=reads, writes=writes)

        def outproj_block(l, s, so, mx_fn, mxnames, tbi):
            Wo = RING[:, so, 0:2048].rearrange("p (c f) -> p c f", c=2)
            t0, n = TB[tbi]
            col = 0 if tbi == 0 else 1 + s
            for m in range(8):
                b = bank()

                def fo(e, b=b, m=m):
                    ins = None
                    for c in range(2):
                        ins = e.matmul(PS[b][:, 0:n], lhsT=Wo[:, c, m * 128:(m + 1) * 128], rhs=mx_fn(c),
                                       start=(c == 0), stop=(c == 1))
                    return ins
                p.op("tensor", fo, reads=["WS%d" % so] + mxnames, writes=["PS%d" % b])
                p.op("vector", lambda e, b=b, m=m: e.scalar_tensor_tensor(
                    out=XT[:, m, t0:t0 + n], in0=PS[b][:, 0:n], scalar=MOD[:, l, 16 + m, col:col + 1],
                    in1=XT[:, m, t0:t0 + n], op0=ALU.mult, op1=ALU.add),
                    reads=["PS%d" % b, "MOD%d_%d" % (l, col)] + xtn([m], tbi), writes=xtn([m], tbi))

        def mixer_D(l, s):
            p.sync_all()
            cv = Carve()
            UT = cv.bf16(2 * T).rearrange("p (c t) -> p c t", c=2)
            UCS = cv.bf16(18 * 512).rearrange("p (a f) -> p a f", f=512)
            MX = cv.bf16(2 * T).rearrange("p (c t) -> p c t", c=2)
            sw = p.wnext(("wi", l, COL_D, 256))
            W = RING[:, sw, 0:2048].rearrange("p (k f) -> p k f", k=8)
            for fc in range(2):
                for tbi, (t0, n) in enumerate(TB):
                    b = bank()
                    proj_fm(sw, W, fc * 128, tbi, b)
                    evac(UT[:, fc, t0:t0 + n], PS[b][:, 0:n], ["PS%d" % b], ["UT%d_%d" % (fc, tbi)])
            p.wrelease()
            for tt in range(18):
                b = bank()

                def f(e, tt=tt, b=b):
                    ins = None
                    for part in range(2):
                        for fc in range(2):
                            i = part * 2 + fc
                            ins = e.matmul(PS[b][:, i * 128:(i + 1) * 128], lhsT=UT[:, fc, tt * 128:(tt + 1) * 128],
                                           rhs=BD[:, part, :], start=True, stop=True)
                    return ins
                p.op("tensor", f, reads=["BD"] + ["UT%d_%d" % (fc, tbi_of_tile(tt)) for fc in range(2)],
                     writes=["PS%d" % b])
                evac(UCS[:, tt, :], PS[b][:, :], ["PS%d" % b], ["UCS%d" % tt])
            sc = p.wnext(("dftc",))
            DC = RING[:, sc, 0:1024].rearrange("p (st pt t) -> p st pt t", st=2, pt=2)
            for fc in range(2):
                b = bank()

                def f(e, fc=fc, b=b):
                    ins = None
                    i = 0
                    for st in range(2):
                        for part in range(2):
                            ins = e.matmul(PS[b][:, 0:256], lhsT=UCS[:, st, (part * 2 + fc) * 128:(part * 2 + fc + 1) * 128],
                                           rhs=DC[:, st, part, :], start=(i == 0), stop=(i == 3))
                            i += 1
                    return ins
                p.op("tensor", f, reads=["WS%d" % sc, "UCS0", "UCS1"], writes=["PS%d" % b])
                evac(MX[:, fc, 0:256], PS[b][:, 0:256], ["PS%d" % b], ["MX%d_0" % fc])
            p.wrelease()
            for tb in range(4):
                t0, n = TB[1 + tb]
                bb = [bank(), bank()]
                for part in range(2):
                    for half in range(2):
                        sd = p.wnext(("dft", part, half, tb))
                        DT = RING[:, sd, :].rearrange("p (st t) -> p st t", st=8)
                        for fc in range(2):
                            def f(e, fc=fc, part=part, half=half, DT=DT, b=bb[fc]):
                                ins = None
                                for st in range(8):
                                    ins = e.matmul(PS[b][:, :], lhsT=UCS[:, 2 + half * 8 + st, (part * 2 + fc) * 128:(part * 2 + fc + 1) * 128],
                                                   rhs=DT[:, st, :], start=(part == 0 and half == 0 and st == 0),
                                                   stop=(part == 1 and half == 1 and st == 7))
                                return ins
                            p.op("tensor", f, reads=["WS%d" % sd] + ["UCS%d" % (2 + half * 8 + st) for st in range(8)],
                                 writes=["PS%d" % bb[fc]])
                        p.wrelease()
                for fc in range(2):
                    evac(MX[:, fc, t0:t0 + n], PS[bb[fc]][:, :], ["PS%d" % bb[fc]], ["MX%d_%d" % (fc, 1 + tb)])
            so = p.wnext(("wo", l, 768, 2))
            for tbi, (t0, n) in enumerate(TB):
                outproj_block(l, s, so, lambda c, t0=t0, n=n: MX[:, c, t0:t0 + n], ["MX0_%d" % tbi, "MX1_%d" % tbi], tbi)
            p.wrelease()

        def load_rope(cv, which):
            TABc = cv.bf16(SEQ)
            TABs = cv.bf16(SEQ)
            if not p.dry:
                evs = [(e_, p.cnt.get(e_, 0), e_) for e_ in ("tensor", "vector", "scalar") if p.cnt.get(e_, 0) > 0]
                p.barrier["sync"] = list(evs)
            p.op("sync", lambda e: e.dma_start(out=TABc, in_=rope_d[2 * which]), writes=["TABC"], dma_sem="tab")
            p.op("sync", lambda e: e.dma_start(out=TABs, in_=rope_d[2 * which + 1]), writes=["TABS"], dma_sem="tab")
            p.seal(["TABC", "TABS"], "tab")
            return TABc, TABs

        def rope_apply(pidx, QN, TABc, TABs, S1, S2, dest, destnames, tbi):
            t0, n = TB[tbi]
            lt0 = t0 - TCX
            b3 = bank()
            p.op("tensor", lambda e: e.matmul(PS[b3][:, 0:n], lhsT=PERM[:, pidx, :], rhs=QN[:, 0:n], start=True, stop=True),
                 reads=["QN", "PERM"], writes=["PS%d" % b3])
            p.op("vector", lambda e: e.tensor_tensor(out=S1[:, 0:n], in0=QN[:, 0:n], in1=TABc[:, lt0:lt0 + n], op=ALU.mult),
                 reads=["QN", "TABC"], writes=["S1"])
            p.op("vector", lambda e: e.tensor_tensor(out=S2[:, 0:n], in0=PS[b3][:, 0:n], in1=TABs[:, lt0:lt0 + n], op=ALU.mult),
                 reads=["PS%d" % b3, "TABS"], writes=["S2"])
            p.op("vector", lambda e: e.tensor_tensor(out=dest, in0=S1[:, 0:n], in1=S2[:, 0:n], op=ALU.add),
                 reads=["S1", "S2"], writes=destnames)

        NWARM = int(os.environ.get("NWARM", "0"))
        NBURST = int(os.environ.get("NBURST", "0"))
        NGB = int(os.environ.get("NGB", "0"))

        def pe_warm(n):
            if n <= 0:
                return

            def f(e):
                ins = None
                for _ in range(n):
                    ins = e.matmul(PS[7][:, :], lhsT=ONES[:], rhs=HT[:, 0, 0:512], start=True, stop=True)
                return ins
            p.op("tensor", f, reads=["ONES"], writes=["PS7"])

        def attention_core(groups, PT, sbanks, G=2):
            batches = []
            sid = 0
            for g in groups:
                nk = len(g["kts"])
                st = []
                for ki, kt in enumerate(g["kts"]):
                    for mp in g["maps"]:
                        st.append((kt, mp, ki == 0, ki == nk - 1, sid))
                        sid += 1
                for i in range(0, len(st), G):
                    batches.append((g, st[i:i + G], i + G >= len(st)))
            assert len(sbanks) >= 2 * G and len(PT) >= 2 * G
            nb = len(batches)
            pe_warm(NBURST)
            for j in range(nb + 1):
                if j < nb:
                    g, st, _ = batches[j]
                    nq = g["nq"]

                    def fs(e, st=st, nq=nq, g=g):
                        ins = None
                        if g.get("filler") is not None:
                            g["filler"](e, PS[7], st[0][0])
                        for (kt, mp, first, last, sid_) in st:
                            ins = mp[0](e, PS[sbanks[sid_ % len(sbanks)]], kt)
                        return ins
                    p.op("tensor", fs, reads=g["reads"], writes=["PS%d" % sbanks[x[4] % len(sbanks)] for x in st] +
                         (["PS7"] if g.get("filler") is not None else []))
                    for (kt, mp, first, last, sid_) in st:
                        sbk = sbanks[sid_ % len(sbanks)]
                        pt = PT[sid_ % len(PT)]
                        p.op("scalar", lambda e, pt=pt, sbk=sbk, nq=nq, scale=mp[1]: e.activation(
                            out=pt[:, 0:nq], in_=PS[sbk][:, 0:nq], func=AF.Exp, scale=scale),
                            reads=["PS%d" % sbk], writes=["PT%d" % (sid_ % len(PT))])
                if j >= 1:
                    g, st, glast = batches[j - 1]
                    nq = g["nq"]

                    def fpv(e, st=st, nq=nq):
                        ins = None
                        for (kt, mp, first, last, sid_) in st:
                            ins = mp[2](e, PS[mp[3]], kt, PT[sid_ % len(PT)][:, 0:nq], first, last)
                        return ins
                    p.op("tensor", fpv, reads=["PT%d" % (x[4] % len(PT)) for x in st] + g["vreads"],
                         writes=sorted(set("PS%d" % x[1][3] for x in st)))
                    for _ in range(NWARM):
                        p.op("tensor", lambda e: e.matmul(PS[7][:, :], lhsT=ONES[:], rhs=HT[:, 0, 0:512], start=True, stop=True),
                             reads=["ONES"], writes=["PS7"])
                    if glast:
                        g["fin"]()
                        pe_warm(NGB)


        def mixer_A(l, s):
            p.sync_all()
            cv = Carve()
            TABc, TABs = load_rope(cv, 0)
            QT = cv.bf16(2 * T).rearrange("p (c t) -> p c t", c=2)
            KT = cv.bf16(2 * T).rearrange("p (c t) -> p c t", c=2)
            VA = cv.bf16(18 * 4 * 128).rearrange("p (a c h f) -> p a c h f", c=2, h=2, f=128)
            MXB = [cv.bf16(1024).rearrange("p (c t) -> p c t", c=2) for _ in range(2)]
            PT = [cv.bf16(512) for _ in range(4)]
            S0 = cv.f32(512)
            S1 = cv.f32(512)
            S2 = cv.f32(512)
            SQ = cv.bf16(512)
            QN = cv.bf16(512)
            li = 0.8 - 0.6 * float(np.exp(-0.3 * l))
            lnf = float(np.log(1.0 - li))
            VA2 = VA.rearrange("p a c h f -> p (a c) h f")
            p.op("vector", lambda e: e.memset(VA2[:, :, 0, 64:128], 1.0), writes=["VA1a"])
            p.op("vector", lambda e: e.memset(VA2[:, :, 1, 0:64], 1.0), writes=["VA1b"])
            sw = p.wnext(("wi", l, 0, 512))
            W = RING[:, sw, :].rearrange("p (k f) -> p k f", k=8)
            for X in range(4):
                isq = X < 2
                c = X % 2
                dst = QT if isq else KT
                dn = "QT" if isq else "KT"
                for tbi, (t0, n) in enumerate(TB):
                    b = bank()
                    proj_fm(sw, W, X * 128, tbi, b)
                    if tbi == 0:
                        evac(dst[:, c, t0:t0 + n], PS[b][:, 0:n], ["PS%d" % b], ["%s%d_%d" % (dn, c, tbi)])
                    else:
                        evac(QN[:, 0:n], PS[b][:, 0:n], ["PS%d" % b], ["QN"], eng="scalar")
                        rope_apply(0, QN, TABc, TABs, S1, S2, dst[:, c, t0:t0 + n], ["%s%d_%d" % (dn, c, tbi)], tbi)
            p.wrelease()
            sv = p.wnext(("wi", l, 512, 256))
            Wv = RING[:, sv, 0:2048].rearrange("p (k f) -> p k f", k=8)
            for tt in range(18):
                b = bank()
                proj_tm(sv, Wv, 0, 256, tt, b)
                psv = PS[b][:, 0:256].rearrange("p (c h f) -> p c h f", c=2, h=2)
                evac(VA[:, tt, :, 0, 0:64], psv[:, :, 0, :], ["PS%d" % b], ["VAa%d" % tt])
                evac(VA[:, tt, :, 1, 64:128], psv[:, :, 1, :], ["PS%d" % b], ["VAb%d" % tt])
            p.wrelease()
            so = p.wnext(("wo", l, 0, 2))
            pend = []

            def flush():
                while pend:
                    pend.pop(0)()
            groups = []
            gi = 0
            for qb, (q0, nq) in enumerate(TB):
                kts = [0, 1] if qb == 0 else list(range(18))
                mxb = MXB[qb % 2]
                mxn = "MXB%d" % (qb % 2)
                for c in range(2):
                    for hh in range(2):
                        obs = [4, 5]
                        gi += 1
                        lo = 64 * hh
                        maps = []
                        for comp in range(2):
                            u = 2 * hh + comp

                            def s_fn(e, ps, kt, c=c, u=u, q0=q0, nq=nq):
                                kw = dict(tile_position=(96, 0)) if u == 3 else {}
                                return e.matmul(ps[:, 0:nq], lhsT=KT[32 * u:32 * u + 32, c, kt * 128:(kt + 1) * 128],
                                                rhs=QT[32 * u:32 * u + 32, c, q0:q0 + nq], start=True, stop=True, **kw)

                            def pv_fn(e, ps, kt, pt, first, last, c=c, hh=hh, nq=nq):
                                return e.matmul(ps[:, 0:nq], lhsT=VA[:, kt, c, hh, :], rhs=pt, start=first, stop=last)
                            maps.append((s_fn, 32.0 ** -0.5, pv_fn, obs[comp]))

                        def fin(obs=obs, lo=lo, c=c, nq=nq, mxb=mxb, mxn=mxn, hh=hh, qb=qb):
                            flush()
                            dp = 64 - lo
                            R = slice(lo, lo + 64)
                            DP = slice(dp, dp + 64)
                            o1, o2 = obs
                            p.op("vector", lambda e: e.tensor_copy(out=S0[:, 0:nq], in_=PS[o1][:, 0:nq]),
                                 reads=["PS%d" % o1], writes=["S0"])
                            p.op("vector", lambda e: e.tensor_copy(out=S2[:, 0:nq], in_=PS[o2][:, 0:nq]),
                                 reads=["PS%d" % o2], writes=["S2"])
                            p.op("vector", lambda e: e.tensor_copy(out=S1[R, 0:nq], in_=S0[DP, 0:nq]), reads=["S0"], writes=["S1"])
                            p.op("vector", lambda e: e.reciprocal(out=S1[R, 0:nq], in_=S1[R, 0:nq]), reads=["S1"], writes=["S1"])
                            p.op("vector", lambda e: e.tensor_tensor(out=S0[R, 0:nq], in0=S0[R, 0:nq], in1=S1[R, 0:nq], op=ALU.mult),
                                 reads=["S0", "S1"], writes=["S0"])
                            p.op("vector", lambda e: e.tensor_copy(out=S1[R, 0:nq], in_=S2[DP, 0:nq]), reads=["S2", "S0"], writes=["S1"])
                            p.op("vector", lambda e: e.reciprocal(out=S1[R, 0:nq], in_=S1[R, 0:nq]), reads=["S1"], writes=["S1"])
                            p.op("vector", lambda e: e.tensor_tensor(out=S2[R, 0:nq], in0=S2[R, 0:nq], in1=S1[R, 0:nq], op=ALU.mult),
                                 reads=["S2", "S1"], writes=["S2"])
                            p.op("vector", lambda e: e.scalar_tensor_tensor(out=S0[R, 0:nq], in0=S2[R, 0:nq], scalar=NLAM[R, l:l + 1],
                                                                            in1=S0[R, 0:nq], op0=ALU.mult, op1=ALU.add),
                                 reads=["S2", "S0", "NLAM"], writes=["S0"])
                            p.op("scalar", lambda e: e.activation(out=SQ[R, 0:nq], in_=S0[R, 0:nq], func=AF.Square),
                                 reads=["S0"], writes=["SQ"])
                            mb = bank()
                            p.op("tensor", lambda e: e.matmul(PS[mb][:, 0:nq], lhsT=ONES[R, :], rhs=SQ[R, 0:nq], start=True, stop=True),
                                 reads=["SQ", "ONES"], writes=["PS%d" % mb])
                            p.op("scalar", lambda e: e.activation(out=S1[R, 0:nq], in_=PS[mb][R, 0:nq], func=AF.Ln,
                                                                  scale=1.0 / 64, bias=EPS), reads=["PS%d" % mb], writes=["S1"])
                            p.op("scalar", lambda e: e.activation(out=S1[R, 0:nq], in_=S1[R, 0:nq], func=AF.Exp, scale=-0.5, bias=lnf),
                                 reads=["S1"], writes=["S1"])
                            p.op("vector", lambda e: e.scalar_tensor_tensor(out=mxb[R, c, 0:nq], in0=S0[R, 0:nq], scalar=DNG[R, l:l + 1],
                                                                            in1=S1[R, 0:nq], op0=ALU.mult, op1=ALU.mult),
                                 reads=["S0", "S1", "DNG"], writes=[mxn + "_%d%d" % (c, hh)])
                            if c == 1 and hh == 1:
                                pend.append(lambda: outproj_block(l, s, so, lambda cc: mxb[:, cc, 0:nq],
                                            [mxn + "_%d%d" % (a_, b_) for a_ in range(2) for b_ in range(2)], qb))
                        olo = 64 - lo

                        def filler(e, ps, kt, c=c, olo=olo, q0=q0, nq=nq):
                            return e.matmul(ps[:, 0:nq], lhsT=KT[olo:olo + 64, c, kt * 128:(kt + 1) * 128],
                                            rhs=QT[olo:olo + 64, c, q0:q0 + nq], start=True, stop=True)
                        kn = ["KT%d_%d" % (c, t_) for t_ in range(NTB)]
                        groups.append(dict(maps=maps, nq=nq, kts=kts, reads=kn + ["QT%d_%d" % (c, qb)],
                                           filler=(filler if "afill" in KD else None),
                                           vreads=["VA1a", "VA1b"] + ["VAa%d" % t_ for t_ in kts] + ["VAb%d" % t_ for t_ in kts],
                                           fin=fin))
            bset[0] = [6]
            attention_core(groups, PT, sbanks=[0, 1, 2, 3], G=2)
            flush()
            bset[0] = list(range(8))
            p.wrelease()


        def mixer_B(l, s):
            for hc in range(2):
                mixer_B_hc(l, s, hc)

        def mixer_B_hc(l, s, hc):
            p.sync_all()
            cv = Carve()
            VZ = cv.bf16(18 * 2 * 128).rearrange("p (a h f) -> p a h f", h=2, f=128)
            OF = cv.f32(T)
            SZd = [[cv.f32(128) for _ in range(2)] for _ in range(2)]
            SHd = [[cv.bf16(128) for _ in range(2)] for _ in range(2)]
            F_ = cv.f32(512)
            L_ = cv.f32(512)
            K_ = cv.f32(512)
            P_ = cv.f32(512)
            EE = cv.f32(512)
            BBb = cv.f32(512)
            T1 = cv.f32(512)
            T2 = cv.f32(512)
            T3 = cv.f32(512)
            T4 = cv.f32(512)
            EK = cv.f32(512)
            RST = cv.bf16(512)
            QTLd = [cv.bf16(512) for _ in range(2)]
            KTLd = [cv.bf16(512) for _ in range(2)]
            QBd = [cv.bf16(512) for _ in range(2)]
            KETd = [cv.bf16(512) for _ in range(2)]
            EBd = [cv.f32(16) for _ in range(2)]
            AT = [cv.bf16(256).rearrange("p (h t) -> p h t", h=2) for _ in range(2)]
            KEZ = [cv.bf16(384) for _ in range(2)]
            MXB = [cv.bf16(512) for _ in range(2)]
            SQ = cv.bf16(512)
            p.op("vector", lambda e: e.memset(VZ.rearrange("p a h f -> p (a h f)"), 0.0), writes=["VZ0"])
            for i in range(2):
                p.op("vector", lambda e, i=i: e.memset(KEZ[i], 0.0), writes=["KEZ%d" % i])
            p.op("vector", lambda e: e.memset(RST, 1.0), writes=["RST"])
            p.op("vector", lambda e: e.memset(RST.rearrange("p (c t) -> p c t", t=32)[:, :, 0:1], 0.0),
                 reads=["RST"], writes=["RST"])
            sW = p.wnext(("wi", l, 768 + 512 * hc, 512))
            sG = p.wnext(("wi", l, 1792 + 128 * hc, 128))
            so = p.wnext(("wo", l, 256 + 128 * hc, 1))
            W = RING[:, sW, :].rearrange("p (k f) -> p k f", k=8)
            WG = RING[:, sG, 0:1024].rearrange("p (k f) -> p k f", k=8)
            Wo = RING[:, so, 0:1024]
            bset[0] = [0, 1]
            for tt in range(18):
                b = bank()
                proj_tm(sW, W, 384, 128, tt, b)
                evac(VZ[:, tt].rearrange("p h f -> p (h f)")[:, 0:256].rearrange("p (h x) -> p h x", h=2)[:, :, 0:64]
                     if False else VZ[:, tt, 0, 0:64], PS[b][:, 0:64], ["PS%d" % b, "VZ0"], ["VZa%d" % tt])
                evac(VZ[:, tt, 1, 64:128], PS[b][:, 64:128], ["PS%d" % b, "VZ0"], ["VZb%d" % tt])
            vzn = lambda tt: ["VZa%d" % tt, "VZb%d" % tt]

            def ew(dr, blk):
                t0, n = TB[blk]
                ntl = n // 128
                order = list(range(ntl)) if dr == 0 else list(range(ntl - 1, -1, -1))
                zc = 128 if dr == 0 else 256
                lbp = LB[:, dr * 2 + hc, l:l + 1]
                omp = OML[:, dr * 2 + hc, l:l + 1]
                nomp = NOML[:, dr * 2 + hc, l:l + 1]
                zb, qb_ = 0, 1
                QTL, KTL, QB, KET, EB = QTLd[dr], KTLd[dr], QBd[dr], KETd[dr], EBd[dr]
                D_ = "d%d_" % dr
                proj_fm(sW, W, zc, blk, zb)
                proj_fm(sW, W, 0, blk, qb_)

                def v3(buf, S):
                    return buf[:, S].rearrange("p (c t) -> p c t", t=32)

                def st_sig(tl, S, x):
                    p.op("scalar", lambda e: e.activation(out=F_[:, S], in_=PS[zb][:, S], func=AF.Sigmoid),
                         reads=["PS%d" % zb], writes=["FF" + x])

                def st_f(tl, S, x):
                    pass

                def st_ln(tl, S, x):
                    p.op("scalar", lambda e: e.activation(out=L_[:, S], in_=F_[:, S], func=AF.Ln, scale=omp, bias=lbp),
                         reads=["FF" + x, "LB", "OML"], writes=["LF" + x])

                def st_k(tl, S, x):
                    p.op("scalar", lambda e: e.activation(out=K_[:, S], in_=F_[:, S], func=AF.Identity, scale=nomp, bias=omp),
                         reads=["FF" + x, "OML"], writes=["KK" + x])

                def st_scan(tl, S, x):
                    p.op("vector", lambda e: e.tensor_tensor_scan(out=P_[:, S], data0=RST[:, S], data1=L_[:, S], initial=0.0,
                                                                  op0=ALU.mult, op1=ALU.add), reads=["RST", "LF" + x], writes=["PP" + x])

                def st_b(tl, S, x):
                    P3 = v3(P_, S)
                    TOTb = P3[:, :, 31:32].broadcast_to([128, 4, 32])
                    if dr == 1:
                        p.op("vector", lambda e: e.tensor_tensor(out=v3(EE, S), in0=TOTb, in1=P3, op=ALU.subtract),
                             reads=["PP" + x], writes=["EE" + x])
                        p.op("vector", lambda e: e.tensor_tensor(out=BBb[:, S], in0=EE[:, S], in1=L_[:, S], op=ALU.add),
                             reads=["EE" + x, "LF" + x], writes=["BB" + x])

                def st_e(tl, S, x):
                    P3 = v3(P_, S)
                    TOTb = P3[:, :, 31:32].broadcast_to([128, 4, 32])
                    B3 = P3 if dr == 0 else v3(BBb, S)
                    bn = ("PP" if dr == 0 else "BB") + x
                    p.op("vector", lambda e: e.tensor_tensor(out=v3(EE, S), in0=B3, in1=B3[:, :, 16:17].broadcast_to([128, 4, 32]),
                                                             op=ALU.subtract), reads=[bn], writes=["EE" + x])
                    if dr == 0:
                        p.op("vector", lambda e: e.tensor_tensor(out=v3(EK, S), in0=TOTb, in1=P3, op=ALU.subtract),
                             reads=["PP" + x], writes=["EK" + x])
                    else:
                        p.op("vector", lambda e: e.tensor_tensor(out=EK[:, S], in0=P_[:, S], in1=L_[:, S], op=ALU.subtract),
                             reads=["PP" + x, "LF" + x], writes=["EK" + x])

                def st_exp(tl, S, x):
                    Bv = P_ if dr == 0 else BBb
                    bn = ("PP" if dr == 0 else "BB") + x
                    p.op("scalar", lambda e: e.activation(out=T1[:, S], in_=EE[:, S], func=AF.Exp), reads=["EE" + x], writes=["T1" + x])
                    p.op("scalar", lambda e: e.activation(out=T2[:, S], in_=EE[:, S], func=AF.Exp, scale=-1.0), reads=["EE" + x], writes=["T2" + x])
                    p.op("scalar", lambda e: e.activation(out=T3[:, S], in_=Bv[:, S], func=AF.Exp), reads=[bn], writes=["T3" + x])
                    p.op("scalar", lambda e: e.activation(out=T4[:, S], in_=EK[:, S], func=AF.Exp), reads=["EK" + x], writes=["T4" + x])
                    p.op("scalar", lambda e: e.activation(out=EB[:, tl * 4:tl * 4 + 4], in_=v3(P_, S)[:, :, 31], func=AF.Exp),
                         reads=["PP" + x], writes=[D_ + "EB" + x])

                def st_mul(tl, S, x):
                    p.op("vector", lambda e: e.tensor_tensor(out=QTL[:, S], in0=PS[qb_][:, S], in1=T1[:, S], op=ALU.mult),
                         reads=["PS%d" % qb_, "T1" + x], writes=[D_ + "QTL" + x])
                    p.op("vector", lambda e: e.tensor_tensor(out=KTL[:, S], in0=K_[:, S], in1=T2[:, S], op=ALU.mult),
                         reads=["KK" + x, "T2" + x], writes=[D_ + "KTL" + x])
                    p.op("vector", lambda e: e.tensor_tensor(out=QB[:, S], in0=PS[qb_][:, S], in1=T3[:, S], op=ALU.mult),
                         reads=["PS%d" % qb_, "T3" + x], writes=[D_ + "QB" + x])
                    p.op("vector", lambda e: e.tensor_tensor(out=KET[:, S], in0=K_[:, S], in1=T4[:, S], op=ALU.mult),
                         reads=["KK" + x, "T4" + x], writes=[D_ + "KET" + x])

                for stg in (st_sig, st_f, st_ln, st_k, st_scan, st_b, st_e, st_exp, st_mul):
                    for tl in order:
                        stg(tl, slice(tl * 128, tl * 128 + 128), str(tl))

            cnt = {"sc": 0, "tr": 0, "o": 0, "st": 0, "sz0": 0, "sz1": 0}
            of_done = set()

            def tile_ops(dr, blk, tl):
                t0, n = TB[blk]
                lo = tl * 128
                tt = (t0 + lo) // 128
                QTL, KTL, QB, KET, EB = QTLd[dr], KTLd[dr], QBd[dr], KETd[dr], EBd[dr]
                SZ = SZd[dr]
                D_ = "d%d_" % dr
                sbs = [0, 1]
                ai = cnt["sc"] % 2
                cnt["sc"] += 1
                at = AT[ai]
                for h in range(2):
                    R = slice(64 * h, 64 * h + 64)
                    p.op("tensor", lambda e, h=h, R=R: e.matmul(PS[sbs[h]][:, 0:128], lhsT=KTL[R, lo:lo + 128],
                                                                rhs=QTL[R, lo:lo + 128], start=True, stop=True),
                         reads=[D_ + "KTL%d" % tl, D_ + "QTL%d" % tl], writes=["PS%d" % sbs[h]])
                for h in range(2):
                    p.op("vector", lambda e, h=h: e.tensor_tensor(out=at[:, h, :], in0=PS[sbs[h]][:, 0:128],
                                                                  in1=BMASK[:, dr, :], op=ALU.mult),
                         reads=["PS%d" % sbs[h], "BMASK"], writes=["AT%d_%d" % (ai, h)])
                tb_ = 4
                ki = cnt["tr"] % 2
                cnt["tr"] += 1
                trv = PS[tb_][:, 0:64].bitcast(BF16)
                kez = KEZ[ki]
                if "b_notr" not in KD:
                    p.op("tensor", lambda e: e.transpose(out=trv, in_=KET[:, lo:lo + 128], identity=IDENT[:]),
                         reads=[D_ + "KET%d" % tl, "IDENT"], writes=["PS%d" % tb_])
                    evac(kez.rearrange("p (h x) -> p h x", x=192)[:, :, 0:64], trv.rearrange("p (h x) -> p h x", x=64),
                         ["PS%d" % tb_], ["KEZ%d" % ki])
                ob = 5 + dr
                cnt["o"] += 1

                def fpv(e):
                    ins = None
                    for h in range(2):
                        ins = e.matmul(PS[ob][:, 0:128], lhsT=VZ[:, tt, h, :], rhs=at[:, h, :], start=(h == 0), stop=False)
                    return ins
                p.op("tensor", fpv, reads=["AT%d_0" % ai, "AT%d_1" % ai] + vzn(tt), writes=["PS%d" % ob])
                yield
                chs = range(4) if dr == 0 else range(3, -1, -1)
                if "b_nostate" in KD:
                    chs = []
                for ci, c in enumerate(chs):
                    zi = cnt["sz%d" % dr] % 2
                    szr = SHd[dr][zi]
                    co = lo + c * 32

                    def fin_(e, c=c, ci=ci, szr=szr, co=co):
                        return e.matmul(PS[ob][:, c * 32:(c + 1) * 32], lhsT=szr[:, :], rhs=QB[:, co:co + 32],
                                        start=False, stop=(ci == 3))
                    p.op("tensor", fin_, reads=[D_ + "SH%d" % zi, D_ + "QB%d" % tl], writes=["PS%d" % ob])
                    stb = (2, 3, 7)[cnt["st"] % 3]
                    cnt["st"] += 1

                    def fst(e, c=c, stb=stb):
                        ins = None
                        for h in range(2):
                            kw = dict(tile_position=(96, 0)) if c == 3 else {}
                            ins = e.matmul(PS[stb][:, 0:128], lhsT=kez[32 * c:32 * c + 32, 128 * h:128 * h + 128],
                                           rhs=VZ[32 * c:32 * c + 32, tt, h, :], start=(h == 0), stop=(h == 1), **kw)
                        return ins
                    p.op("tensor", fst, reads=["KEZ%d" % ki] + vzn(tt), writes=["PS%d" % stb])
                    cg = (lo // 32) + c
                    cnt["sz%d" % dr] += 1
                    zn = cnt["sz%d" % dr] % 2
                    p.op("vector", lambda e, stb=stb, cg=cg, zi=zi, zn=zn: e.scalar_tensor_tensor(
                        out=SZ[zn], in0=SZ[zi], scalar=EB[:, cg:cg + 1], in1=PS[stb][:, 0:128],
                        op0=ALU.mult, op1=ALU.add), reads=[D_ + "SZ%d" % zi, D_ + "EB%d" % tl, "PS%d" % stb], writes=[D_ + "SZ%d" % zn])
                    p.op("vector", lambda e, zn=zn: e.tensor_copy(out=SHd[dr][zn], in_=SZ[zn]),
                         reads=[D_ + "SZ%d" % zn], writes=[D_ + "SH%d" % zn])
                    yield
                tok = slice(t0 + lo, t0 + lo + 128)
                if tt not in of_done:
                    of_done.add(tt)
                    p.op("scalar", lambda e: e.activation(out=OF[:, tok], in_=PS[ob][:, 0:128], func=AF.Copy),
                         reads=["PS%d" % ob], writes=["OF%d" % tt])
                else:
                    p.op("vector", lambda e: e.tensor_tensor(out=OF[:, tok], in0=PS[ob][:, 0:128], in1=OF[:, tok], op=ALU.add),
                         reads=["PS%d" % ob, "OF%d" % tt], writes=["OF%d" % tt])

            def sweep(dr):
                SZ = SZd[dr]
                D_ = "d%d_" % dr
                p.op("vector", lambda e: e.memset(SZ[cnt["sz%d" % dr] % 2], 0.0), writes=[D_ + "SZ%d" % (cnt["sz%d" % dr] % 2)])
                p.op("vector", lambda e: e.memset(SHd[dr][cnt["sz%d" % dr] % 2], 0.0), writes=[D_ + "SH%d" % (cnt["sz%d" % dr] % 2)])
                blocks = [0, 1, 2, 3, 4] if dr == 0 else [0, 4, 3, 2, 1]
                for blk in blocks:
                    ntl = TB[blk][1] // 128
                    ew(dr, blk)
                    yield
                    tls = range(ntl) if dr == 0 else range(ntl - 1, -1, -1)
                    for tl in tls:
                        yield from tile_ops(dr, blk, tl)
                        yield

            gens = [sweep(0), sweep(1)]
            for _ in range(int(os.environ.get("BSTAG", "0"))):
                next(gens[0])
            while gens:
                for g_ in list(gens):
                    try:
                        next(g_)
                    except StopIteration:
                        gens.remove(g_)
            p.wrelease()
            bset[0] = [0, 1, 2, 3]
            S0, S1 = F_, L_
            for blk, (t0, n) in enumerate(TB):
                N = slice(0, n)
                col = 0 if blk == 0 else 1 + s
                mxb = MXB[blk % 2]
                mxn = "MXB%d" % (blk % 2)
                tok = slice(t0, t0 + n)
                ofn = ["OF%d" % t_ for t_ in range(t0 // 128, (t0 + n) // 128)]
                p.op("scalar", lambda e, tok=tok, N=N: e.activation(out=SQ[:, N], in_=OF[:, tok], func=AF.Square),
                     reads=ofn, writes=["SQ"])
                b2 = bank()
                p.op("tensor", lambda e, b2=b2, N=N: e.matmul(PS[b2][:, N], lhsT=BONES[:], rhs=SQ[:, N], start=True, stop=True),
                     reads=["SQ", "BONES"], writes=["PS%d" % b2])
                p.op("scalar", lambda e, b2=b2, N=N: e.activation(out=S0[:, N], in_=PS[b2][:, N], func=AF.Ln, scale=1.0 / 64, bias=EPS),
                     reads=["PS%d" % b2], writes=["FF"])
                p.op("scalar", lambda e, N=N: e.activation(out=S0[:, N], in_=S0[:, N], func=AF.Exp, scale=-0.5), reads=["FF"], writes=["FF"])
                p.op("vector", lambda e, tok=tok, N=N: e.scalar_tensor_tensor(
                    out=S0[:, N], in0=OF[:, tok], scalar=HNG[:, l:l + 1], in1=S0[:, N], op0=ALU.mult, op1=ALU.mult),
                    reads=ofn + ["FF", "HNG"], writes=["FF"])
                bg = bank()
                proj_fm(sG, WG, 0, blk, bg)
                p.op("scalar", lambda e, bg=bg, N=N: e.activation(out=S1[:, N], in_=PS[bg][:, N], func=AF.Silu),
                     reads=["PS%d" % bg], writes=["LF"])
                p.op("vector", lambda e, mxb=mxb, N=N: e.tensor_tensor(out=mxb[:, N], in0=S0[:, N], in1=S1[:, N], op=ALU.mult),
                     reads=["FF", "LF"], writes=[mxn])
                bset[0] = [4, 5, 6, 7]
                for m in range(8):
                    b = bank()
                    p.op("tensor", lambda e, b=b, m=m, mxb=mxb, N=N: e.matmul(PS[b][:, N], lhsT=Wo[:, m * 128:(m + 1) * 128],
                                                                              rhs=mxb[:, N], start=True, stop=True),
                         reads=["WS%d" % so, mxn], writes=["PS%d" % b])
                    p.op("vector", lambda e, b=b, m=m, N=N, tok=tok, col=col: e.scalar_tensor_tensor(
                        out=XT[:, m, tok], in0=PS[b][:, N], scalar=MOD[:, l, 16 + m, col:col + 1],
                        in1=XT[:, m, tok], op0=ALU.mult, op1=ALU.add),
                        reads=["PS%d" % b, "MOD%d_%d" % (l, col)] + xtn([m], blk), writes=xtn([m], blk))
                bset[0] = [0, 1, 2, 3]
            bset[0] = list(range(8))
            p.wrelease(2)

        def mixer_C(l, s):
            p.sync_all()
            cv = Carve()
            TABc, TABs = load_rope(cv, 1)
            QT = cv.bf16(2 * T).rearrange("p (c t) -> p c t", c=2)
            KT = cv.bf16(2 * T).rearrange("p (c t) -> p c t", c=2)
            VA = cv.bf16(18 * 2 * 192).rearrange("p (a h f) -> p a h f", h=2, f=192)
            MXB = [cv.bf16(1024).rearrange("p (c t) -> p c t", c=2) for _ in range(2)]
            PT = [cv.bf16(512) for _ in range(4)]
            S0 = cv.f32(512)
            S1 = cv.f32(512)
            S2 = cv.f32(512)
            SQ = cv.bf16(512)
            QN = cv.bf16(512)
            if "nomemset" not in KD:
                p.op("vector", lambda e: e.memset(VA.rearrange("p a h f -> p (a h) f")[:, :, 64:128], 1.0), writes=["VA1"])
            sw = p.wnext(("wi", l, COL_CQK, 512))
            W = RING[:, sw, :].rearrange("p (k f) -> p k f", k=8)
            for X in range(4):
                isq = X < 2
                c = X % 2
                dst = QT if isq else KT
                dn = "QT" if isq else "KT"
                for tbi, (t0, n) in enumerate(TB):
                    b = bank()
                    proj_fm(sw, W, X * 128, tbi, b)
                    p.op("scalar", lambda e, b=b, n=n: e.activation(out=SQ[:, 0:n], in_=PS[b][:, 0:n], func=AF.Square),
                         reads=["PS%d" % b], writes=["SQ"])
                    b2 = bank()
                    p.op("tensor", lambda e, b2=b2, n=n: e.matmul(PS[b2][:, 0:n], lhsT=BONES[:], rhs=SQ[:, 0:n], start=True, stop=True),
                         reads=["SQ", "BONES"], writes=["PS%d" % b2])
                    p.op("scalar", lambda e, b2=b2, n=n: e.activation(out=S0[:, 0:n], in_=PS[b2][:, 0:n], func=AF.Ln,
                                                                     scale=1.0 / 64, bias=EPS), reads=["PS%d" % b2], writes=["S0"])
                    p.op("scalar", lambda e, n=n: e.activation(out=S0[:, 0:n], in_=S0[:, 0:n], func=AF.Exp, scale=-0.5),
                         reads=["S0"], writes=["S0"])
                    gap = QKG[:, l, (0 if isq else 1):(1 if isq else 2)]
                    if tbi == 0:
                        p.op("vector", lambda e, b=b, n=n, t0=t0, dst=dst, c=c, gap=gap: e.scalar_tensor_tensor(
                            out=dst[:, c, t0:t0 + n], in0=PS[b][:, 0:n], scalar=gap, in1=S0[:, 0:n],
                            op0=ALU.mult, op1=ALU.mult), reads=["PS%d" % b, "S0", "QKG"], writes=["%s%d_%d" % (dn, c, tbi)])
                    else:
                        p.op("vector", lambda e, b=b, n=n, gap=gap: e.scalar_tensor_tensor(
                            out=QN[:, 0:n], in0=PS[b][:, 0:n], scalar=gap, in1=S0[:, 0:n],
                            op0=ALU.mult, op1=ALU.mult), reads=["PS%d" % b, "S0", "QKG"], writes=["QN"])
                        if "norope" in KD:
                            evac(dst[:, c, t0:t0 + n], QN[:, 0:n], ["QN"], ["%s%d_%d" % (dn, c, tbi)], eng="vector")
                        else:
                            rope_apply(1, QN, TABc, TABs, S1, S2, dst[:, c, t0:t0 + n], ["%s%d_%d" % (dn, c, tbi)], tbi)
            p.wrelease()
            vwn = 256 if "vw256" in KD else 128
            sv = p.wnext(("wi", l, COL_CV, vwn))
            Wv = RING[:, sv, 0:8 * vwn].rearrange("p (k f) -> p k f", k=8)
            for tt in range(0 if "nov" in KD else 18):
                b = bank()
                proj_tm(sv, Wv, 0, 128, tt, b)
                psv = PS[b][:, 0:128].rearrange("p (h f) -> p h f", h=2)
                if "vnoevac" in KD:
                    continue
                if "v2d" in KD:
                    ve = "scalar" if "vact" in KD else ("vector" if "vdve" in KD else None)
                    for h in range(2):
                        evac(VA[:, tt, h, 0:64], psv[:, h, :], ["PS%d" % b], ["VAa%d_%d" % (tt, h)], eng=ve)
                        if "vone" not in KD:
                            evac(VA[:, tt, h, 128:192], psv[:, h, :], ["PS%d" % b], ["VAb%d_%d" % (tt, h)], eng=ve)
                    continue
                evac(VA[:, tt, :, 0:64], psv, ["PS%d" % b], ["VAa%d" % tt])
                evac(VA[:, tt, :, 128:192], psv, ["PS%d" % b], ["VAb%d" % tt])
            p.wrelease()
            so = p.wnext(("wo", l, 512, 2))
            pend = []

            def flush():
                while pend:
                    pend.pop(0)()
            groups = []
            for qb, (q0, nq) in enumerate(TB):
                kts = [0, 1] if qb == 0 else list(range(18))
                mxb = MXB[qb % 2]
                mxn = "MXB%d" % (qb % 2)
                for c in range(2):
                    maps = []
                    for hh in range(2):
                        lo = 64 * hh

                        def s_fn(e, ps, kt, c=c, lo=lo, q0=q0, nq=nq):
                            return e.matmul(ps[:, 0:nq], lhsT=KT[lo:lo + 64, c, kt * 128:(kt + 1) * 128],
                                            rhs=QT[lo:lo + 64, c, q0:q0 + nq], start=True, stop=True)

                        def pv_fn(e, ps, kt, pt, first, last, c=c, lo=lo, nq=nq):
                            return e.matmul(ps[:, 0:nq], lhsT=VA[:, kt, c, lo:lo + 128], rhs=pt, start=first, stop=last)
                        maps.append((s_fn, 0.125, pv_fn, 4 + hh))

                    def fin(c=c, nq=nq, mxb=mxb, mxn=mxn, qb=qb):
                        flush()
                        N = slice(0, nq)
                        p.op("vector", lambda e: e.tensor_copy(out=S0[:, N], in_=PS[4][:, N]), reads=["PS4"], writes=["S0"])
                        p.op("vector", lambda e: e.tensor_copy(out=S2[:, N], in_=PS[5][:, N]), reads=["PS5"], writes=["S2"])
                        p.op("vector", lambda e: e.tensor_copy(out=S1[0:64, N], in_=S0[64:128, N]), reads=["S0"], writes=["S1"])
                        p.op("vector", lambda e: e.tensor_copy(out=S1[64:128, N], in_=S2[0:64, N]), reads=["S2", "S1"], writes=["S1"])
                        p.op("vector", lambda e: e.reciprocal(out=S1[:, N], in_=S1[:, N]), reads=["S1"], writes=["S1"])
                        p.op("vector", lambda e: e.tensor_tensor(out=mxb[0:64, c, N], in0=S0[0:64, N], in1=S1[0:64, N], op=ALU.mult),
                             reads=["S0", "S1"], writes=[mxn + "_%d0" % c])
                        p.op("vector", lambda e: e.tensor_tensor(out=mxb[64:128, c, N], in0=S2[64:128, N], in1=S1[64:128, N], op=ALU.mult),
                             reads=["S2", "S1"], writes=[mxn + "_%d1" % c])
                        if c == 1:
                            pend.append(lambda: outproj_block(l, s, so, lambda cc: mxb[:, cc, 0:nq],
                                        [mxn + "_%d%d" % (a_, b_) for a_ in range(2) for b_ in range(2)], qb))
                    kn = ["KT%d_%d" % (c, t_) for t_ in range(NTB)]
                    groups.append(dict(maps=maps, nq=nq, kts=kts, reads=kn + ["QT%d_%d" % (c, qb)],
                                       vreads=["VA1"] + ["VAa%d" % t_ for t_ in kts] + ["VAb%d" % t_ for t_ in kts], fin=fin))
            bset[0] = [6]
            attention_core(groups, PT, sbanks=[0, 1, 2, 3], G=2)
            flush()
            bset[0] = list(range(8))
            p.wrelease()

        for s in range(NS):
            for k in range(8):
                p.op("sync", lambda e, k=k, s=s: e.dma_start(out=XT[:, k, :], in_=xT[s, k * 128:(k + 1) * 128, :]),
                     writes=xtn([k], 0) + xtn([k], 1) + xtn([k], 2) + xtn([k], 3) + xtn([k], 4), dma_sem="xt")
            p.seal([n_ for tbi in range(NTB) for n_ in xtn(ALLK, tbi)], "xt")
            for l in range(NL):
                norm_phase(l, s, 1)
                if "A" in mixers:
                    mixer_A(l, s)
                if "B" in mixers:
                    mixer_B(l, s)
                if "C" in mixers:
                    mixer_C(l, s)
                if "D" in mixers:
                    mixer_D(l, s)
                norm_phase(l, s, 2)
                ffn_phase(l, s)
            final_phase(s)
        if not p.dry:
            for sem in ("oOB0", "oOB1"):
                p.final_waits[sem] = p.cnt[sem]

    p0 = Prog(dry=True)
    gen(p0)
    p = Prog(dry=False, wlist=p0.wrec)
    gen(p)
    assert p.wpos == len(p.wlist)

    sems = {name: es.enter_context(nc.semaphore(name)) for name in sorted(p.semnames)}
    with nc.Block() as block:
        for eng in ENGS:
            def section(e, eng=eng):
                for waits, fn, sem, inc in p.streams[eng]:
                    for wsem, wval in waits:
                        e.wait_ge(sems[wsem], wval)
                    ins = fn(e)
                    ins.then_inc(sems[sem], inc)
                if eng == "sync":
                    for wsem, wval in p.final_waits.items():
                        e.wait_ge(sems[wsem], wval)
            getattr(block, eng)(section)
    es.close()
    ninstr = {e: len(p.streams[e]) for e in ENGS}
    return nc, ninstr


_CONST = {}
import os
KD = set(os.environ.get('KDBG', '').split(','))


def _consts():
    if _CONST:
        return _CONST
    bf = ml_dtypes.bfloat16
    i = np.arange(64)
    ang = 2 * np.pi * np.outer(i, i) / 64
    C64, S64 = np.cos(ang), np.sin(ang)
    bd = np.zeros((128, 256))
    for h in range(2):
        bd[64 * h:64 * h + 64, 64 * h:64 * h + 64] = C64
        bd[64 * h:64 * h + 64, 128 + 64 * h:128 + 64 * h + 64] = -S64
    n = np.arange(SEQ)
    ang = 2 * np.pi * ((np.outer(n, n) % SEQ).astype(np.float64)) / SEQ
    sc = 1.0 / np.sqrt(64.0 * SEQ)
    dftl = np.stack([np.cos(ang) * sc, np.sin(ang) * sc]).astype(bf)
    m = np.arange(TCX)
    angc = 2 * np.pi * ((np.outer(m, m) % TCX).astype(np.float64)) / TCX
    scc = 1.0 / np.sqrt(64.0 * TCX)
    dftc = np.stack([np.cos(angc) * scc, np.sin(angc) * scc], axis=1)
    dftc = dftc.reshape(2, 128, 2, TCX).transpose(1, 0, 2, 3).reshape(128, 1024)
    row = np.repeat(np.arange(SEQ // 64), 64).astype(np.float64)
    col = np.tile(np.arange(64), SEQ // 64).astype(np.float64)
    rope = np.zeros((4, 128, SEQ))
    perm = np.zeros((128, 256))
    for w, unit in enumerate((32, 64)):
        half, nf = unit // 2, unit // 4
        inv = (np.float32(10000.0) ** (-np.arange(nf, dtype=np.float32) / np.float32(nf))).astype(np.float64)
        for pp in range(128):
            d = pp % unit
            dh = d % half
            pos = row if d < half else col
            a = pos * inv[dh % nf]
            rope[2 * w, pp] = np.cos(a)
            rope[2 * w + 1, pp] = np.sin(a)
            if dh < nf:
                perm[pp + nf, 128 * w + pp] = -1.0
            else:
                perm[pp - nf, 128 * w + pp] = 1.0
    ss_, tt_ = np.meshgrid(np.arange(128), np.arange(128), indexing="ij")
    same = (ss_ // 32) == (tt_ // 32)
    bmask = np.concatenate([(same & (ss_ <= tt_)), (same & (ss_ >= tt_))], axis=1).astype(np.float64)
    _CONST.update(ident=np.eye(128).astype(bf), bmask=bmask.astype(bf))
    _CONST.update(bd=bd.astype(bf), dftl=dftl, dftc=dftc.astype(bf), rope=rope.astype(bf), perm=perm.astype(bf))
    return _CONST


def host_prepare(inputs, ncores, NS):
    f = lambda a: np.ascontiguousarray(np.asarray(a, dtype=np.float32))
    x, c, ctx, c_ctx = (np.asarray(inputs[k]) for k in ("x", "c", "ctx", "c_ctx"))

    def fm(v):
        v = np.asarray(v, dtype=np.float32)
        lead = v.shape[:-1]
        return np.moveaxis(v.reshape(*lead, 8, 128), -1, 0)

    shared = {
        "w_mod": f(inputs["w_mod"]),
        "b_modT": f(np.moveaxis(np.asarray(inputs["b_mod"]).reshape(DEPTH, 48, 128), -1, 0).reshape(128, DEPTH * 48)),
        "n1T": f(fm(inputs["norm1"]).reshape(128, DEPTH * KC)),
        "n2T": f(fm(inputs["norm2"]).reshape(128, DEPTH * KC)),
        "fnT": f(fm(inputs["final_norm"]).reshape(128, KC)),
        "w_in": f(np.asarray(inputs["w_in"])[:, :, WIN_IDX]),
        "qkg": f(np.stack([np.tile(np.asarray(inputs["q_norm"]), (1, 2)), np.tile(np.asarray(inputs["k_norm"]), (1, 2))],
                          axis=-1).transpose(1, 0, 2).reshape(128, DEPTH * 2)),
        "dng": f(np.tile(np.asarray(inputs["diff_norm"]), (1, 2)).T),
        "hng": f(np.tile(np.asarray(inputs["hgrn_norm"]), (1, 2)).T),
        "lbl": f(np.asarray(inputs["hgrn_lb_logits"]).reshape(2, DEPTH, 2, 128).transpose(3, 0, 2, 1).reshape(128, 4 * DEPTH)),
        "dlam": f(np.broadcast_to(np.asarray(inputs["diff_lambda"]).reshape(1, DEPTH * 128), (128, DEPTH * 128))),
        "w_out": f(inputs["w_out"]),
        "w_fi": f(inputs["w_ffn_in"]),
        "w_fo": f(inputs["w_ffn_out"]),
    }
    shared.update(_consts())
    in_maps = []
    for ci in range(ncores):
        bs = list(range(ci * NS, (ci + 1) * NS))
        xt = np.empty((NS, D, T), np.float32)
        for j, b in enumerate(bs):
            xt[j, :, :TCX] = ctx[b].T
            xt[j, :, TCX:] = x[b].T
        cols = [c_ctx] + [c[b] for b in bs] + [c_ctx] * (2 - NS)
        cv = np.stack([np.asarray(v, np.float32) for v in cols], axis=-1)
        cv = cv.reshape(8, 128, 3).transpose(1, 0, 2).reshape(128, 24)
        m = dict(shared)
        m["xT"] = xt
        m["cvec"] = f(cv)
        in_maps.append(m)
    return in_maps


_CACHE = {}


def run(inputs, ncores=8, NS=2, NL=DEPTH, mixers="ABCD"):
    key = (NS, NL, mixers)
    if key not in _CACHE:
        _CACHE[key] = build_program(NS, NL, mixers)
    nc, ninstr = _CACHE[key]
    in_maps = host_prepare(inputs, ncores, NS)
    res = run_bass_kernel_spmd(nc, in_maps, core_ids=list(range(ncores)))
    out = np.empty((ncores * NS, SEQ, D), np.float32)
    for ci in range(ncores):
        o = res.results[ci]["outT"]
        for j in range(NS):
            out[ci * NS + j] = o[j].T
    return out


def kernel(**inputs):
    return run(inputs, ncores=8, NS=2, NL=DEPTH, mixers="ABCD")
```

```python
import numpy as np
import ml_dtypes
from contextlib import ExitStack
import concourse.bass as bass
import concourse.mybir as mybir
from concourse.bass_utils import run_bass_kernel_spmd

F32 = mybir.dt.float32
BF16 = mybir.dt.bfloat16
AF = mybir.ActivationFunctionType
ALU = mybir.AluOpType

D = 1024
KC = 8
TCX = 256
SEQ = 2048
T = TCX + SEQ
DEPTH = 4
FH = 2816
NHB = 6
TB = [(0, 256)] + [(256 + 512 * i, 512) for i in range(4)]
NTB = len(TB)
EPS = 1e-6
NSLOT = 3
AW = 15360
COL_CQK = 2048
COL_CV = 2560
COL_D = 2688
WIN_COLS = 2944
ENGS = ("tensor", "vector", "scalar", "gpsimd", "sync")
_BIDX = np.concatenate([np.arange(768 + 256 * j + 128 * hc, 768 + 256 * j + 128 * hc + 128)
                        for hc in range(2) for j in range(4)])
WIN_IDX = np.concatenate([np.arange(0, 768), _BIDX, np.arange(1792, 2304), np.arange(2304, 2368), np.arange(2304, 2368),
                          np.arange(2368, 2432), np.arange(2368, 2432), np.arange(2432, 2816)])


class Prog:
    def __init__(self, dry=False, wlist=None):
        self.dry = dry
        self.streams = {e: [] for e in ENGS}
        self.cnt = {}
        self.waited = {e: {} for e in ENGS}
        self.res = {}
        self.barrier = {e: [] for e in ENGS}
        self.semnames = set(ENGS)
        self.final_waits = {}
        self.wlist = wlist if wlist is not None else []
        self.wrec = []
        self.wpos = 0
        self.wissued = 0
        self.wreleased = 0
        self.wloader = None

    def _need(self, eng, sem, val, weng, waits):
        if weng == eng and eng == "tensor":
            return
        if self.waited[eng].get(sem, 0) >= val:
            return
        if waits.get(sem, 0) < val:
            waits[sem] = val

    def op(self, eng, fn, reads=(), writes=(), dma_sem=None):
        if self.dry:
            return
        waits = {}
        for (sem, val, weng) in self.barrier[eng]:
            self._need(eng, sem, val, weng, waits)
        self.barrier[eng] = []
        for r in reads:
            st = self.res.get(r)
            if st is not None and st["w"] is not None:
                self._need(eng, *st["w"], waits)
            if st is not None and r.startswith("PS"):
                for sem, (val, reng) in st["r"].items():
                    if reng != eng:
                        self._need(eng, sem, val, reng, waits)
        for w in writes:
            st = self.res.get(w)
            if st is not None:
                if st["w"] is not None:
                    self._need(eng, *st["w"], waits)
                for sem, (val, reng) in st["r"].items():
                    self._need(eng, sem, val, reng, waits)
        for sem, val in waits.items():
            self.waited[eng][sem] = val
        if dma_sem is None:
            sem, inc, weng = eng, 1, eng
        else:
            sem, inc, weng = dma_sem, 16, "dma"
            self.semnames.add(sem)
        val = self.cnt.get(sem, 0) + inc
        self.cnt[sem] = val
        for r in reads:
            st = self.res.setdefault(r, {"w": None, "r": {}})
            st["r"][sem] = (val, weng)
        for w in writes:
            self.res[w] = {"w": (sem, val, weng), "r": {}}
        self.streams[eng].append((list(waits.items()), fn, sem, inc))

    def seal(self, resources, sem):
        if self.dry:
            return
        ev = (sem, self.cnt[sem], "dma")
        for r in resources:
            self.res[r] = {"w": ev, "r": {}}

    def sync_all(self, engines=("tensor", "vector", "scalar")):
        if self.dry:
            return
        evs = [(e, self.cnt.get(e, 0), e) for e in engines if self.cnt.get(e, 0) > 0]
        for e in engines:
            self.barrier[e] = list(evs)

    def _wpump(self):
        while self.wissued < len(self.wlist) and self.wissued < self.wreleased + NSLOT:
            j = self.wissued
            self.wissued += 1
            self.wloader(self, self.wlist[j], j % NSLOT)

    def wnext(self, desc):
        if self.dry:
            self.wrec.append(desc)
            return 0
        i = self.wpos
        assert self.wlist[i] == desc, (i, self.wlist[i], desc)
        self._wpump()
        assert i < self.wissued, "weight ring too small for this access pattern"
        self.wpos += 1
        return i % NSLOT

    def wrelease(self, n=1):
        if self.dry:
            return
        self.wreleased += n
        self._wpump()


def build_program(NS, NL, mixers=""):
    nc = bass.Bass("TRN2", target_bir_lowering=False)

    def dt(name, shape, dtype=F32, kind="ExternalInput"):
        return nc.dram_tensor(name, shape, dtype, kind=kind).ap()

    xT = dt("xT", [NS, D, T])
    cvec = dt("cvec", [128, KC * 3])
    w_mod = dt("w_mod", [DEPTH, D, 6 * D])
    b_modT = dt("b_modT", [128, DEPTH * 48])
    n1T = dt("n1T", [128, DEPTH * KC])
    n2T = dt("n2T", [128, DEPTH * KC])
    fnT = dt("fnT", [128, KC])
    w_in = dt("w_in", [DEPTH, D, WIN_COLS])
    w_out = dt("w_out", [DEPTH, D, D])
    w_fi = dt("w_fi", [DEPTH, D, 2 * FH])
    w_fo = dt("w_fo", [DEPTH, FH, D])
    outT = dt("outT", [NS, D, SEQ], kind="ExternalOutput")
    bd_d = dt("bd", [128, 256], BF16)
    perm_d = dt("perm", [128, 256], BF16)
    dftl_d = dt("dftl", [2, SEQ, SEQ], BF16)
    dftc_d = dt("dftc", [128, 1024], BF16)
    rope_d = dt("rope", [4, 128, SEQ], BF16)
    qkg_d = dt("qkg", [128, DEPTH * 2])
    dng_d = dt("dng", [128, DEPTH])
    dlam_d = dt("dlam", [128, DEPTH * 128])
    ident_d = dt("ident", [128, 128], BF16)
    bmask_d = dt("bmask", [128, 256], BF16)
    lbl_d = dt("lbl", [128, 4 * DEPTH])
    hng_d = dt("hng", [128, DEPTH])

    es = ExitStack()

    def sb(name, shape, dtype):
        return es.enter_context(nc.sbuf_tensor(name, shape, dtype))

    XT = sb("XT", [128, KC, T], F32)
    HT = sb("HT", [128, KC, T], BF16)
    RING = sb("RING", [128, NSLOT, 4096], BF16)
    ARENA = sb("ARENA", [128, AW], F32)
    MOD = sb("MOD", [128, DEPTH, 48, 3], F32)
    A1 = sb("A1", [128, DEPTH, KC, 3], F32)
    A2 = sb("A2", [128, DEPTH, KC, 3], F32)
    BMOD = sb("BMOD", [128, DEPTH, 48], F32)
    N1 = sb("N1", [128, DEPTH, KC], F32)
    N2 = sb("N2", [128, DEPTH, KC], F32)
    FN = sb("FN", [128, KC], F32)
    CV = sb("CV", [128, KC * 3], F32)
    SCT = sb("SCT", [128, KC, 3], BF16)
    ONES = sb("ONES", [128, 128], BF16)
    BONES = sb("BONES", [128, 128], BF16)
    BD = sb("BD", [128, 2, 128], BF16)
    PERM = sb("PERM", [128, 2, 128], BF16)
    QKG = sb("QKG", [128, DEPTH, 2], F32)
    DNG = sb("DNG", [128, DEPTH], F32)
    NLAM = sb("NLAM", [128, DEPTH], F32)
    IDENT = sb("IDENT", [128, 128], BF16)
    BMASK = sb("BMASK", [128, 2, 128], BF16)
    LB = sb("LB", [128, 4, DEPTH], F32)
    OML = sb("OML", [128, 4, DEPTH], F32)
    HNG = sb("HNG", [128, DEPTH], F32)
    NOML = sb("NOML", [128, 4, DEPTH], F32)
    PS = [es.enter_context(nc.psum_tensor("PS%d" % i, [128, 512], F32)) for i in range(8)]

    class Carve:
        def __init__(self):
            self.off = 0

        def f32(self, n):
            v = ARENA[:, self.off:self.off + n]
            self.off += n
            assert self.off <= AW, self.off
            return v

        def bf16(self, n):
            w = (n + 1) // 2
            v = ARENA[:, self.off:self.off + w].bitcast(BF16)
            self.off += w
            assert self.off <= AW, self.off
            return v

    def xtn(ks, tbi):
        return ["XT%d_%d" % (k, tbi) for k in ks]

    def htn(ks, tbi):
        return ["HT%d_%d" % (k, tbi) for k in ks]

    ALLK = list(range(8))

    def gen(p):
        psi = [0]
        bset = [list(range(8))]

        def bank():
            b = bset[0][psi[0] % len(bset[0])]
            psi[0] += 1
            return b

        def wloader(p, desc, slot):
            kind = desc[0]
            dst = RING[:, slot, :]
            if kind in ("mod", "fi", "wi"):
                _, l, c0, n = desc
                wsrc = {"mod": w_mod, "fi": w_fi, "wi": w_in}[kind]
                src = wsrc[l].rearrange("(k p) f -> p k f", p=128)[:, :, c0:c0 + n]
                d3 = dst[:, 0:8 * n].rearrange("p (k f) -> p k f", k=8)
            elif kind in ("fo", "wo"):
                _, l, r0, nch = desc
                wsrc = {"fo": w_fo, "wo": w_out}[kind]
                src = wsrc[l][r0:r0 + nch * 128, :].rearrange("(c p) f -> p c f", p=128)
                d3 = dst[:, 0:nch * 1024].rearrange("p (c f) -> p c f", c=nch)
            elif kind == "dft":
                _, part, half, tb = desc
                src = dftl_d[part][half * 1024:(half + 1) * 1024, tb * 512:(tb + 1) * 512].rearrange(
                    "(st p) t -> p st t", p=128)
                d3 = dst.rearrange("p (k f) -> p k f", k=8)
            elif kind == "dftc":
                src = dftc_d[:, :]
                d3 = dst[:, 0:1024]
            else:
                raise ValueError(kind)
            p.op("gpsimd", lambda e, d3=d3, src=src: e.dma_start(out=d3, in_=src),
                 writes=["WS%d" % slot], dma_sem="ws%d" % slot)

        p.wloader = wloader

        consts = []

        def cload(dst, src, name):
            p.op("sync", lambda e: e.dma_start(out=dst, in_=src), writes=[name], dma_sem="cst")
            consts.append(name)

        cload(CV[:], cvec[:, :], "CV")
        cload(BMOD[:].rearrange("p l j -> p (l j)"), b_modT[:, :], "BMOD")
        cload(N1[:].rearrange("p l k -> p (l k)"), n1T[:, :], "N1")
        cload(N2[:].rearrange("p l k -> p (l k)"), n2T[:, :], "N2")
        cload(FN[:], fnT[:, :], "FN")
        cload(BD[:].rearrange("p a b -> p (a b)"), bd_d[:, :], "BD")
        cload(PERM[:].rearrange("p a b -> p (a b)"), perm_d[:, :], "PERM")
        cload(QKG[:].rearrange("p a b -> p (a b)"), qkg_d[:, :], "QKG")
        cload(DNG[:], dng_d[:, :], "DNG")
        cload(IDENT[:], ident_d[:, :], "IDENT")
        cload(BMASK[:].rearrange("p a b -> p (a b)"), bmask_d[:, :], "BMASK")
        cload(HNG[:], hng_d[:, :], "HNG")
        cload(OML[:].rearrange("p a b -> p (a b)"), lbl_d[:, :], "OML")
        cv0 = Carve()
        DLAM = cv0.f32(DEPTH * 128)
        PRD = cv0.f32(DEPTH * 64)
        SMX = cv0.f32(DEPTH * 2)
        cload(DLAM, dlam_d[:, :], "DLAM")
        p.seal(consts, "cst")
        p.op("vector", lambda e: e.memset(ONES[:], 1.0), writes=["ONES"])
        p.op("vector", lambda e: e.memset(BONES[:], 0.0), writes=["BONES"])
        p.op("vector", lambda e: e.memset(BONES[0:64, 0:64], 1.0), reads=["BONES"], writes=["BONES"])
        p.op("vector", lambda e: e.memset(BONES[64:128, 64:128], 1.0), reads=["BONES"], writes=["BONES"])
        dl4 = DLAM.rearrange("p (l a d) -> p l a d", l=DEPTH, a=4)
        pr4 = PRD.rearrange("p (l a d) -> p l a d", l=DEPTH, a=2)
        for l in range(DEPTH):
            p.op("vector", lambda e, l=l: e.tensor_tensor(out=pr4[:, l], in0=dl4[:, l, 0:4:2, :], in1=dl4[:, l, 1:4:2, :],
                                                          op=ALU.mult), reads=["DLAM"], writes=["PRD"])
        p.op("vector", lambda e: e.tensor_reduce(out=SMX, in_=PRD.rearrange("p (a d) -> p a d", d=32),
                                                 axis=mybir.AxisListType.X, op=ALU.add), reads=["PRD"], writes=["SMX"])
        p.op("scalar", lambda e: e.activation(out=SMX, in_=SMX, func=AF.Exp), reads=["SMX"], writes=["SMX"])
        for l in range(DEPTH):
            li = 0.8 - 0.6 * float(np.exp(-0.3 * l))
            p.op("vector", lambda e, l=l, li=li: e.scalar_tensor_tensor(
                out=NLAM[:, l:l + 1], in0=SMX[:, 2 * l + 1:2 * l + 2], scalar=-li, in1=SMX[:, 2 * l:2 * l + 1],
                op0=ALU.add, op1=ALU.subtract), reads=["SMX"], writes=["NLAM"])

        LSM = cv0.f32(4)
        p.op("scalar", lambda e: e.activation(out=OML[:], in_=OML[:], func=AF.Exp), reads=["OML"], writes=["OML"])
        p.op("vector", lambda e: e.tensor_reduce(out=LSM, in_=OML[:], axis=mybir.AxisListType.X, op=ALU.add),
             reads=["OML"], writes=["LSM"])
        p.op("vector", lambda e: e.reciprocal(out=LSM, in_=LSM), reads=["LSM"], writes=["LSM"])
        p.op("vector", lambda e: e.tensor_tensor(out=OML[:], in0=OML[:], in1=LSM.unsqueeze(2).broadcast_to([128, 4, DEPTH]),
                                                 op=ALU.mult), reads=["OML", "LSM"], writes=["OML"])
        p.op("vector", lambda e: e.memset(LB[:, :, 0:1], 0.0), writes=["LB"])
        for l in range(1, DEPTH):
            p.op("vector", lambda e, l=l: e.tensor_tensor(out=LB[:, :, l:l + 1], in0=LB[:, :, l - 1:l], in1=OML[:, :, l:l + 1],
                                                          op=ALU.add), reads=["LB", "OML"], writes=["LB"])
        p.op("vector", lambda e: e.tensor_scalar(out=OML[:], in0=LB[:], scalar1=-1.0, scalar2=1.0, op0=ALU.mult, op1=ALU.add),
             reads=["LB", "OML"], writes=["OML"])
        p.op("vector", lambda e: e.tensor_scalar(out=NOML[:], in0=LB[:], scalar1=1.0, scalar2=-1.0, op0=ALU.mult, op1=ALU.add),
             reads=["LB"], writes=["OML"])

        p.op("scalar", lambda e: e.activation(out=SCT[:].rearrange("p k j -> p (k j)"), in_=CV[:], func=AF.Silu),
             reads=["CV"], writes=["SCT"])
        for l in range(NL):
            b = bank()
            for jb in range(12):
                slot = p.wnext(("mod", l, jb * 512, 512))
                W = RING[:, slot, :].rearrange("p (k f) -> p k f", k=8)

                def f(e, W=W, b=b, jb=jb):
                    ins = None
                    for j in range(4):
                        jj = jb * 4 + j
                        for k in range(8):
                            ins = e.matmul(PS[b][:, jj * 3:jj * 3 + 3], lhsT=W[:, k, j * 128:(j + 1) * 128],
                                           rhs=SCT[:, k, :], start=(k == 0), stop=(k == 7))
                    return ins
                p.op("tensor", f, reads=["WS%d" % slot, "SCT"], writes=["PS%d" % b])
                p.wrelease()
            psv = PS[b][:, 0:144].rearrange("p (j c) -> p j c", c=3)
            for c in range(3):
                p.op("vector", lambda e, c=c, l=l, psv=psv: e.tensor_tensor(
                    out=MOD[:, l, :, c], in0=psv[:, :, c], in1=BMOD[:, l, :], op=ALU.add),
                    reads=["PS%d" % b, "BMOD"], writes=["MOD%d_%d" % (l, c)])
            for c in range(3):
                p.op("vector", lambda e, c=c, l=l: e.scalar_tensor_tensor(
                    out=A1[:, l, :, c], in0=MOD[:, l, 8:16, c], scalar=1.0, in1=N1[:, l, :],
                    op0=ALU.add, op1=ALU.mult), reads=["MOD%d_%d" % (l, c), "N1"], writes=["A1_%d_%d" % (l, c)])
                p.op("vector", lambda e, c=c, l=l: e.scalar_tensor_tensor(
                    out=A2[:, l, :, c], in0=MOD[:, l, 32:40, c], scalar=1.0, in1=N2[:, l, :],
                    op0=ALU.add, op1=ALU.mult), reads=["MOD%d_%d" % (l, c), "N2"], writes=["A2_%d_%d" % (l, c)])

        def rstd_block(cv_bufs, tbi):
            SQB, LNV, RSTD = cv_bufs
            t0, n = TB[tbi]
            sq3 = SQB[:, 0:8 * n].rearrange("p (k t) -> p k t", k=8)
            p.op("scalar", lambda e: e.activation(out=sq3, in_=XT[:, :, t0:t0 + n], func=AF.Square),
                 reads=xtn(ALLK, tbi), writes=["SQ"])
            b = bank()

            def f(e):
                ins = None
                for k in range(8):
                    ins = e.matmul(PS[b][:, 0:n], lhsT=ONES[:], rhs=sq3[:, k, :], start=(k == 0), stop=(k == 7))
                return ins
            p.op("tensor", f, reads=["SQ", "ONES"], writes=["PS%d" % b])
            p.op("scalar", lambda e: e.activation(out=LNV[:, 0:n], in_=PS[b][:, 0:n], func=AF.Ln,
                                                  scale=1.0 / D, bias=EPS),
                 reads=["PS%d" % b], writes=["LNV"])
            p.op("scalar", lambda e: e.activation(out=RSTD[:, 0:n], in_=LNV[:, 0:n], func=AF.Exp, scale=-0.5),
                 reads=["LNV"], writes=["RSTD"])

        def norm_phase(l, s, which):
            p.sync_all()
            cv = Carve()
            bufs = (cv.bf16(4096), cv.f32(512), cv.f32(512))
            RSTD = bufs[2]
            TMP = [cv.f32(512) for _ in range(4)]
            Aw = A1 if which == 1 else A2
            awn = "A1_%d_" % l if which == 1 else "A2_%d_" % l
            sh0 = 0 if which == 1 else 24
            ti = 0
            for tbi, (t0, n) in enumerate(TB):
                col = 0 if tbi == 0 else 1 + s
                rstd_block(bufs, tbi)
                for k in range(8):
                    tm = TMP[ti % 4]
                    tmn = "TMP%d" % (ti % 4)
                    ti += 1
                    p.op("vector", lambda e, k=k, tm=tm, t0=t0, n=n, col=col: e.scalar_tensor_tensor(
                        out=tm[:, 0:n], in0=XT[:, k, t0:t0 + n], scalar=Aw[:, l, k, col:col + 1], in1=RSTD[:, 0:n],
                        op0=ALU.mult, op1=ALU.mult),
                        reads=xtn([k], tbi) + ["RSTD", awn + str(col)], writes=[tmn])
                    if k % 3 == 0:
                        p.op("vector", lambda e, k=k, tm=tm, t0=t0, n=n, col=col: e.tensor_scalar(
                            out=HT[:, k, t0:t0 + n], in0=tm[:, 0:n], scalar1=MOD[:, l, sh0 + k, col:col + 1], scalar2=None,
                            op0=ALU.add), reads=[tmn, "MOD%d_%d" % (l, col)], writes=htn([k], tbi))
                    else:
                        p.op("scalar", lambda e, k=k, tm=tm, t0=t0, n=n, col=col: e.activation(
                            out=HT[:, k, t0:t0 + n], in_=tm[:, 0:n], func=AF.Identity,
                            bias=MOD[:, l, sh0 + k, col:col + 1], scale=1.0),
                            reads=[tmn, "MOD%d_%d" % (l, col)], writes=htn([k], tbi))

        def ffn_phase(l, s):
            p.sync_all()
            cv = Carve()
            ACT3 = cv.bf16(4 * T).rearrange("p (c t) -> p c t", c=4)
            SG = [cv.f32(512) for _ in range(2)]
            sgi = 0
            for hb in range(NHB):
                nch = 4 if hb < 5 else 2
                sg_ = p.wnext(("fi", l, hb * 512, nch * 128))
                su_ = p.wnext(("fi", l, FH + hb * 512, nch * 128))
                Wg = RING[:, sg_, 0:8 * nch * 128].rearrange("p (k f) -> p k f", k=8)
                Wu = RING[:, su_, 0:8 * nch * 128].rearrange("p (k f) -> p k f", k=8)
                for tbi, (t0, n) in enumerate(TB):
                    for c in range(nch):
                        bg = bank()
                        bu = bank()

                        def fg(e, b=bg, W=Wg, c=c, t0=t0, n=n):
                            ins = None
                            for k in range(8):
                                ins = e.matmul(PS[b][:, 0:n], lhsT=W[:, k, c * 128:(c + 1) * 128],
                                               rhs=HT[:, k, t0:t0 + n], start=(k == 0), stop=(k == 7))
                            return ins
                        p.op("tensor", fg, reads=["WS%d" % sg_] + htn(ALLK, tbi), writes=["PS%d" % bg])
                        p.op("tensor", lambda e, b=bu, W=Wu, c=c, t0=t0, n=n, fg=fg: fg(e, b, W, c, t0, n),
                             reads=["WS%d" % su_] + htn(ALLK, tbi), writes=["PS%d" % bu])
                        sg = SG[sgi % 2]
                        sgn = "SG%d" % (sgi % 2)
                        sgi += 1
                        p.op("scalar", lambda e, b=bg, sg=sg, n=n: e.activation(
                            out=sg[:, 0:n], in_=PS[b][:, 0:n], func=AF.Silu),
                            reads=["PS%d" % bg], writes=[sgn])
                        p.op("vector", lambda e, b=bu, sg=sg, c=c, t0=t0, n=n: e.tensor_tensor(
                            out=ACT3[:, c, t0:t0 + n], in0=PS[b][:, 0:n], in1=sg[:, 0:n], op=ALU.mult),
                            reads=["PS%d" % bu, sgn], writes=["ACT%d_%d" % (c, tbi)])
                p.wrelease(2)
                so_ = p.wnext(("fo", l, hb * 512, nch))
                Wo = RING[:, so_, 0:nch * 1024].rearrange("p (c f) -> p c f", c=nch)
                for tbi, (t0, n) in enumerate(TB):
                    col = 0 if tbi == 0 else 1 + s
                    for m in range(8):
                        b = bank()

                        def fo(e, b=b, Wo=Wo, m=m, t0=t0, n=n, nch=nch):
                            ins = None
                            for c in range(nch):
                                ins = e.matmul(PS[b][:, 0:n], lhsT=Wo[:, c, m * 128:(m + 1) * 128],
                                               rhs=ACT3[:, c, t0:t0 + n], start=(c == 0), stop=(c == nch - 1))
                            return ins
                        p.op("tensor", fo, reads=["WS%d" % so_] + ["ACT%d_%d" % (c, tbi) for c in range(nch)],
                             writes=["PS%d" % b])
                        p.op("vector", lambda e, b=b, m=m, t0=t0, n=n, col=col: e.scalar_tensor_tensor(
                            out=XT[:, m, t0:t0 + n], in0=PS[b][:, 0:n], scalar=MOD[:, l, 40 + m, col:col + 1],
                            in1=XT[:, m, t0:t0 + n], op0=ALU.mult, op1=ALU.add),
                            reads=["PS%d" % b, "MOD%d_%d" % (l, col)] + xtn([m], tbi), writes=xtn([m], tbi))
                p.wrelease()

        def final_phase(s):
            p.sync_all()
            cv = Carve()
            bufs = (cv.bf16(4096), cv.f32(512), cv.f32(512))
            RSTD = bufs[2]
            OB = [cv.f32(512) for _ in range(2)]
            oi = 0
            for tbi in range(1, NTB):
                t0, n = TB[tbi]
                rstd_block(bufs, tbi)
                for k in range(8):
                    ob = OB[oi % 2]
                    obn = "OB%d" % (oi % 2)
                    oi += 1
                    p.op("vector", lambda e, k=k, ob=ob, t0=t0, n=n: e.scalar_tensor_tensor(
                        out=ob[:, 0:n], in0=XT[:, k, t0:t0 + n], scalar=FN[:, k:k + 1], in1=RSTD[:, 0:n],
                        op0=ALU.mult, op1=ALU.mult), reads=xtn([k], tbi) + ["RSTD", "FN"], writes=[obn])
                    p.op("sync", lambda e, k=k, ob=ob, t0=t0, n=n: e.dma_start(
                        out=outT[s, k * 128:(k + 1) * 128, t0 - TCX:t0 - TCX + n], in_=ob[:, 0:n]),
                        reads=[obn], dma_sem="o" + obn)


        def tbi_of_tile(tt):
            return 0 if tt < 2 else 1 + (tt - 2) // 4

        def proj_fm(slot, W, c0, tbi, b):
            t0, n = TB[tbi]

            def f(e):
                ins = None
                for k in range(8):
                    ins = e.matmul(PS[b][:, 0:n], lhsT=W[:, k, c0:c0 + 128], rhs=HT[:, k, t0:t0 + n],
                                   start=(k == 0), stop=(k == 7))
                return ins
            p.op("tensor", f, reads=["WS%d" % slot] + htn(ALLK, tbi), writes=["PS%d" % b])

        def proj_tm(slot, W, c0, ncols, tt, b):
            def f(e):
                ins = None
                for k in range(8):
                    ins = e.matmul(PS[b][:, 0:ncols], lhsT=HT[:, k, tt * 128:(tt + 1) * 128], rhs=W[:, k, c0:c0 + ncols],
                                   start=(k == 0), stop=(k == 7))
                return ins
            p.op("tensor", f, reads=["WS%d" % slot] + htn(ALLK, tbi_of_tile(tt)), writes=["PS%d" % b])

        evi = [0]

        def evac(out, in_, reads, writes, eng=None):
            if eng is None:
                eng = ("scalar", "vector")[evi[0] % 2]
                evi[0] += 1
            if eng == "scalar":
                p.op("scalar", lambda e: e.activation(out=out, in_=in_, func=AF.Copy), reads=reads, writes=writes)
            else:
                p.op("vector", lambda e: e.tensor_copy(out=out, in_=in_), reads=reads, writes=writes)

        def outproj_block(l, s, so, mx_fn, mxnames, tbi):
            Wo = RING[:, so, 0:2048].rearrange("p (c f) -> p c f", c=2)
            t0, n = TB[tbi]
            col = 0 if tbi == 0 else 1 + s
            for m in range(8):
                b = bank()

                def fo(e, b=b, m=m):
                    ins = None
                    for c in range(2):
                        ins = e.matmul(PS[b][:, 0:n], lhsT=Wo[:, c, m * 128:(m + 1) * 128], rhs=mx_fn(c),
                                       start=(c == 0), stop=(c == 1))
                    return ins
                p.op("tensor", fo, reads=["WS%d" % so] + mxnames, writes=["PS%d" % b])
                p.op("vector", lambda e, b=b, m=m: e.scalar_tensor_tensor(
                    out=XT[:, m, t0:t0 + n], in0=PS[b][:, 0:n], scalar=MOD[:, l, 16 + m, col:col + 1],
                    in1=XT[:, m, t0:t0 + n], op0=ALU.mult, op1=ALU.add),
                    reads=["PS%d" % b, "MOD%d_%d" % (l, col)] + xtn([m], tbi), writes=xtn([m], tbi))

        def mixer_D(l, s):
            p.sync_all()
            cv = Carve()
            UT = cv.bf16(2 * T).rearrange("p (c t) -> p c t", c=2)
            UCS = cv.bf16(18 * 512).rearrange("p (a f) -> p a f", f=512)
            MX = cv.bf16(2 * T).rearrange("p (c t) -> p c t", c=2)
            sw = p.wnext(("wi", l, COL_D, 256))
            W = RING[:, sw, 0:2048].rearrange("p (k f) -> p k f", k=8)
            for fc in range(2):
                for tbi, (t0, n) in enumerate(TB):
                    b = bank()
                    proj_fm(sw, W, fc * 128, tbi, b)
                    evac(UT[:, fc, t0:t0 + n], PS[b][:, 0:n], ["PS%d" % b], ["UT%d_%d" % (fc, tbi)])
            p.wrelease()
            for tt in range(18):
                b = bank()

                def f(e, tt=tt, b=b):
                    ins = None
                    for part in range(2):
                        for fc in range(2):
                            i = part * 2 + fc
                            ins = e.matmul(PS[b][:, i * 128:(i + 1) * 128], lhsT=UT[:, fc, tt * 128:(tt + 1) * 128],
                                           rhs=BD[:, part, :], start=True, stop=True)
                    return ins
                p.op("tensor", f, reads=["BD"] + ["UT%d_%d" % (fc, tbi_of_tile(tt)) for fc in range(2)],
                     writes=["PS%d" % b])
                evac(UCS[:, tt, :], PS[b][:, :], ["PS%d" % b], ["UCS%d" % tt])
            sc = p.wnext(("dftc",))
            DC = RING[:, sc, 0:1024].rearrange("p (st pt t) -> p st pt t", st=2, pt=2)
            for fc in range(2):
                b = bank()

                def f(e, fc=fc, b=b):
                    ins = None
                    i = 0
                    for st in range(2):
                        for part in range(2):
                            ins = e.matmul(PS[b][:, 0:256], lhsT=UCS[:, st, (part * 2 + fc) * 128:(part * 2 + fc + 1) * 128],
                                           rhs=DC[:, st, part, :], start=(i == 0), stop=(i == 3))
                            i += 1
                    return ins
                p.op("tensor", f, reads=["WS%d" % sc, "UCS0", "UCS1"], writes=["PS%d" % b])
                evac(MX[:, fc, 0:256], PS[b][:, 0:256], ["PS%d" % b], ["MX%d_0" % fc])
            p.wrelease()
            for tb in range(4):
                t0, n = TB[1 + tb]
                bb = [bank(), bank()]
                for part in range(2):
                    for half in range(2):
                        sd = p.wnext(("dft", part, half, tb))
                        DT = RING[:, sd, :].rearrange("p (st t) -> p st t", st=8)
                        for fc in range(2):
                            def f(e, fc=fc, part=part, half=half, DT=DT, b=bb[fc]):
                                ins = None
                                for st in range(8):
                                    ins = e.matmul(PS[b][:, :], lhsT=UCS[:, 2 + half * 8 + st, (part * 2 + fc) * 128:(part * 2 + fc + 1) * 128],
                                                   rhs=DT[:, st, :], start=(part == 0 and half == 0 and st == 0),
                                                   stop=(part == 1 and half == 1 and st == 7))
                                return ins
                            p.op("tensor", f, reads=["WS%d" % sd] + ["UCS%d" % (2 + half * 8 + st) for st in range(8)],
                                 writes=["PS%d" % bb[fc]])
                        p.wrelease()
                for fc in range(2):
                    evac(MX[:, fc, t0:t0 + n], PS[bb[fc]][:, :], ["PS%d" % bb[fc]], ["MX%d_%d" % (fc, 1 + tb)])
            so = p.wnext(("wo", l, 768, 2))
            for tbi, (t0, n) in enumerate(TB):
                outproj_block(l, s, so, lambda c, t0=t0, n=n: MX[:, c, t0:t0 + n], ["MX0_%d" % tbi, "MX1_%d" % tbi], tbi)
            p.wrelease()

        def load_rope(cv, which):
            TABc = cv.bf16(SEQ)
            TABs = cv.bf16(SEQ)
            if not p.dry:
                evs = [(e_, p.cnt.get(e_, 0), e_) for e_ in ("tensor", "vector", "scalar") if p.cnt.get(e_, 0) > 0]
                p.barrier["sync"] = list(evs)
            p.op("sync", lambda e: e.dma_start(out=TABc, in_=rope_d[2 * which]), writes=["TABC"], dma_sem="tab")
            p.op("sync", lambda e: e.dma_start(out=TABs, in_=rope_d[2 * which + 1]), writes=["TABS"], dma_sem="tab")
            p.seal(["TABC", "TABS"], "tab")
            return TABc, TABs

        def rope_apply(pidx, QN, TABc, TABs, S1, S2, dest, destnames, tbi):
            t0, n = TB[tbi]
            lt0 = t0 - TCX
            b3 = bank()
            p.op("tensor", lambda e: e.matmul(PS[b3][:, 0:n], lhsT=PERM[:, pidx, :], rhs=QN[:, 0:n], start=True, stop=True),
                 reads=["QN", "PERM"], writes=["PS%d" % b3])
            p.op("vector", lambda e: e.tensor_tensor(out=S1[:, 0:n], in0=QN[:, 0:n], in1=TABc[:, lt0:lt0 + n], op=ALU.mult),
                 reads=["QN", "TABC"], writes=["S1"])
            p.op("vector", lambda e: e.tensor_tensor(out=S2[:, 0:n], in0=PS[b3][:, 0:n], in1=TABs[:, lt0:lt0 + n], op=ALU.mult),
                 reads=["PS%d" % b3, "TABS"], writes=["S2"])
            p.op("vector", lambda e: e.tensor_tensor(out=dest, in0=S1[:, 0:n], in1=S2[:, 0:n], op=ALU.add),
                 reads=["S1", "S2"], writes=destnames)

        NWARM = int(os.environ.get("NWARM", "0"))
        NBURST = int(os.environ.get("NBURST", "0"))
        NGB = int(os.environ.get("NGB", "0"))

        def pe_warm(n):
            if n <= 0:
                return

            def f(e):
                ins = None
                for _ in range(n):
                    ins = e.matmul(PS[7][:, :], lhsT=ONES[:], rhs=HT[:, 0, 0:512], start=True, stop=True)
                return ins
            p.op("tensor", f, reads=["ONES"], writes=["PS7"])

        def attention_core(groups, PT, sbanks, G=2):
            batches = []
            sid = 0
            for g in groups:
                nk = len(g["kts"])
                st = []
                for ki, kt in enumerate(g["kts"]):
                    for mp in g["maps"]:
                        st.append((kt, mp, ki == 0, ki == nk - 1, sid))
                        sid += 1
                for i in range(0, len(st), G):
                    batches.append((g, st[i:i + G], i + G >= len(st)))
            assert len(sbanks) >= 2 * G and len(PT) >= 2 * G
            nb = len(batches)
            pe_warm(NBURST)
            for j in range(nb + 1):
                if j < nb:
                    g, st, _ = batches[j]
                    nq = g["nq"]

                    def fs(e, st=st, nq=nq, g=g):
                        ins = None
                        if g.get("filler") is not None:
                            g["filler"](e, PS[7], st[0][0])
                        for (kt, mp, first, last, sid_) in st:
                            ins = mp[0](e, PS[sbanks[sid_ % len(sbanks)]], kt)
                        return ins
                    p.op("tensor", fs, reads=g["reads"], writes=["PS%d" % sbanks[x[4] % len(sbanks)] for x in st] +
                         (["PS7"] if g.get("filler") is not None else []))
                    for (kt, mp, first, last, sid_) in st:
                        sbk = sbanks[sid_ % len(sbanks)]
                        pt = PT[sid_ % len(PT)]
                        p.op("scalar", lambda e, pt=pt, sbk=sbk, nq=nq, scale=mp[1]: e.activation(
                            out=pt[:, 0:nq], in_=PS[sbk][:, 0:nq], func=AF.Exp, scale=scale),
                            reads=["PS%d" % sbk], writes=["PT%d" % (sid_ % len(PT))])
                if j >= 1:
                    g, st, glast = batches[j - 1]
                    nq = g["nq"]

                    def fpv(e, st=st, nq=nq):
                        ins = None
                        for (kt, mp, first, last, sid_) in st:
                            ins = mp[2](e, PS[mp[3]], kt, PT[sid_ % len(PT)][:, 0:nq], first, last)
                        return ins
                    p.op("tensor", fpv, reads=["PT%d" % (x[4] % len(PT)) for x in st] + g["vreads"],
                         writes=sorted(set("PS%d" % x[1][3] for x in st)))
                    for _ in range(NWARM):
                        p.op("tensor", lambda e: e.matmul(PS[7][:, :], lhsT=ONES[:], rhs=HT[:, 0, 0:512], start=True, stop=True),
                             reads=["ONES"], writes=["PS7"])
                    if glast:
                        g["fin"]()
                        pe_warm(NGB)


        def mixer_A(l, s):
            p.sync_all()
            cv = Carve()
            TABc, TABs = load_rope(cv, 0)
            QT = cv.bf16(2 * T).rearrange("p (c t) -> p c t", c=2)
            KT = cv.bf16(2 * T).rearrange("p (c t) -> p c t", c=2)
            VA = cv.bf16(18 * 4 * 128).rearrange("p (a c h f) -> p a c h f", c=2, h=2, f=128)
            MXB = [cv.bf16(1024).rearrange("p (c t) -> p c t", c=2) for _ in range(2)]
            PT = [cv.bf16(512) for _ in range(4)]
            S0 = cv.f32(512)
            S1 = cv.f32(512)
            S2 = cv.f32(512)
            SQ = cv.bf16(512)
            QN = cv.bf16(512)
            li = 0.8 - 0.6 * float(np.exp(-0.3 * l))
            lnf = float(np.log(1.0 - li))
            VA2 = VA.rearrange("p a c h f -> p (a c) h f")
            p.op("vector", lambda e: e.memset(VA2[:, :, 0, 64:128], 1.0), writes=["VA1a"])
            p.op("vector", lambda e: e.memset(VA2[:, :, 1, 0:64], 1.0), writes=["VA1b"])
            sw = p.wnext(("wi", l, 0, 512))
            W = RING[:, sw, :].rearrange("p (k f) -> p k f", k=8)
            for X in range(4):
                isq = X < 2
                c = X % 2
                dst = QT if isq else KT
                dn = "QT" if isq else "KT"
                for tbi, (t0, n) in enumerate(TB):
                    b = bank()
                    proj_fm(sw, W, X * 128, tbi, b)
                    if tbi == 0:
                        evac(dst[:, c, t0:t0 + n], PS[b][:, 0:n], ["PS%d" % b], ["%s%d_%d" % (dn, c, tbi)])
                    else:
                        evac(QN[:, 0:n], PS[b][:, 0:n], ["PS%d" % b], ["QN"], eng="scalar")
                        rope_apply(0, QN, TABc, TABs, S1, S2, dst[:, c, t0:t0 + n], ["%s%d_%d" % (dn, c, tbi)], tbi)
            p.wrelease()
            sv = p.wnext(("wi", l, 512, 256))
            Wv = RING[:, sv, 0:2048].rearrange("p (k f) -> p k f", k=8)
            for tt in range(18):
                b = bank()
                proj_tm(sv, Wv, 0, 256, tt, b)
                psv = PS[b][:, 0:256].rearrange("p (c h f) -> p c h f", c=2, h=2)
                evac(VA[:, tt, :, 0, 0:64], psv[:, :, 0, :], ["PS%d" % b], ["VAa%d" % tt])
                evac(VA[:, tt, :, 1, 64:128], psv[:, :, 1, :], ["PS%d" % b], ["VAb%d" % tt])
            p.wrelease()
            so = p.wnext(("wo", l, 0, 2))
            pend = []

            def flush():
                while pend:
                    pend.pop(0)()
            groups = []
            gi = 0
            for qb, (q0, nq) in enumerate(TB):
                kts = [0, 1] if qb == 0 else list(range(18))
                mxb = MXB[qb % 2]
                mxn = "MXB%d" % (qb % 2)
                for c in range(2):
                    for hh in range(2):
                        obs = [4, 5]
                        gi += 1
                        lo = 64 * hh
                        maps = []
                        for comp in range(2):
                            u = 2 * hh + comp

                            def s_fn(e, ps, kt, c=c, u=u, q0=q0, nq=nq):
                                kw = dict(tile_position=(96, 0)) if u == 3 else {}
                                return e.matmul(ps[:, 0:nq], lhsT=KT[32 * u:32 * u + 32, c, kt * 128:(kt + 1) * 128],
                                                rhs=QT[32 * u:32 * u + 32, c, q0:q0 + nq], start=True, stop=True, **kw)

                            def pv_fn(e, ps, kt, pt, first, last, c=c, hh=hh, nq=nq):
                                return e.matmul(ps[:, 0:nq], lhsT=VA[:, kt, c, hh, :], rhs=pt, start=first, stop=last)
                            maps.append((s_fn, 32.0 ** -0.5, pv_fn, obs[comp]))

                        def fin(obs=obs, lo=lo, c=c, nq=nq, mxb=mxb, mxn=mxn, hh=hh, qb=qb):
                            flush()
                            dp = 64 - lo
                            R = slice(lo, lo + 64)
                            DP = slice(dp, dp + 64)
                            o1, o2 = obs
                            p.op("vector", lambda e: e.tensor_copy(out=S0[:, 0:nq], in_=PS[o1][:, 0:nq]),
                                 reads=["PS%d" % o1], writes=["S0"])
                            p.op("vector", lambda e: e.tensor_copy(out=S2[:, 0:nq], in_=PS[o2][:, 0:nq]),
                                 reads=["PS%d" % o2], writes=["S2"])
                            p.op("vector", lambda e: e.tensor_copy(out=S1[R, 0:nq], in_=S0[DP, 0:nq]), reads=["S0"], writes=["S1"])
                            p.op("vector", lambda e: e.reciprocal(out=S1[R, 0:nq], in_=S1[R, 0:nq]), reads=["S1"], writes=["S1"])
                            p.op("vector", lambda e: e.tensor_tensor(out=S0[R, 0:nq], in0=S0[R, 0:nq], in1=S1[R, 0:nq], op=ALU.mult),
                                 reads=["S0", "S1"], writes=["S0"])
                            p.op("vector", lambda e: e.tensor_copy(out=S1[R, 0:nq], in_=S2[DP, 0:nq]), reads=["S2", "S0"], writes=["S1"])
                            p.op("vector", lambda e: e.reciprocal(out=S1[R, 0:nq], in_=S1[R, 0:nq]), reads=["S1"], writes=["S1"])
                            p.op("vector", lambda e: e.tensor_tensor(out=S2[R, 0:nq], in0=S2[R, 0:nq], in1=S1[R, 0:nq], op=ALU.mult),
                                 reads=["S2", "S1"], writes=["S2"])
                            p.op("vector", lambda e: e.scalar_tensor_tensor(out=S0[R, 0:nq], in0=S2[R, 0:nq], scalar=NLAM[R, l:l + 1],
                                                                            in1=S0[R, 0:nq], op0=ALU.mult, op1=ALU.add),
                                 reads=["S2", "S0", "NLAM"], writes=["S0"])
                            p.op("scalar", lambda e: e.activation(out=SQ[R, 0:nq], in_=S0[R, 0:nq], func=AF.Square),
                                 reads=["S0"], writes=["SQ"])
                            mb = bank()
                            p.op("tensor", lambda e: e.matmul(PS[mb][:, 0:nq], lhsT=ONES[R, :], rhs=SQ[R, 0:nq], start=True, stop=True),
                                 reads=["SQ", "ONES"], writes=["PS%d" % mb])
                            p.op("scalar", lambda e: e.activation(out=S1[R, 0:nq], in_=PS[mb][R, 0:nq], func=AF.Ln,
                                                                  scale=1.0 / 64, bias=EPS), reads=["PS%d" % mb], writes=["S1"])
                            p.op("scalar", lambda e: e.activation(out=S1[R, 0:nq], in_=S1[R, 0:nq], func=AF.Exp, scale=-0.5, bias=lnf),
                                 reads=["S1"], writes=["S1"])
                            p.op("vector", lambda e: e.scalar_tensor_tensor(out=mxb[R, c, 0:nq], in0=S0[R, 0:nq], scalar=DNG[R, l:l + 1],
                                                                            in1=S1[R, 0:nq], op0=ALU.mult, op1=ALU.mult),
                                 reads=["S0", "S1", "DNG"], writes=[mxn + "_%d%d" % (c, hh)])
                            if c == 1 and hh == 1:
                                pend.append(lambda: outproj_block(l, s, so, lambda cc: mxb[:, cc, 0:nq],
                                            [mxn + "_%d%d" % (a_, b_) for a_ in range(2) for b_ in range(2)], qb))
                        olo = 64 - lo

                        def filler(e, ps, kt, c=c, olo=olo, q0=q0, nq=nq):
                            return e.matmul(ps[:, 0:nq], lhsT=KT[olo:olo + 64, c, kt * 128:(kt + 1) * 128],
                                            rhs=QT[olo:olo + 64, c, q0:q0 + nq], start=True, stop=True)
                        kn = ["KT%d_%d" % (c, t_) for t_ in range(NTB)]
                        groups.append(dict(maps=maps, nq=nq, kts=kts, reads=kn + ["QT%d_%d" % (c, qb)],
                                           filler=(filler if "afill" in KD else None),
                                           vreads=["VA1a", "VA1b"] + ["VAa%d" % t_ for t_ in kts] + ["VAb%d" % t_ for t_ in kts],
                                           fin=fin))
            bset[0] = [6, 7]
            attention_core(groups, PT, sbanks=[0, 1, 2, 3], G=2)
            flush()
            bset[0] = list(range(8))
            p.wrelease()


        def mixer_B(l, s):
            for hc in range(2):
                mixer_B_hc(l, s, hc)

        def mixer_B_hc(l, s, hc):
            p.sync_all()
            cv = Carve()
            VZ = cv.bf16(18 * 2 * 128).rearrange("p (a h f) -> p a h f", h=2, f=128)
            OF = cv.f32(T)
            SZd = [[cv.f32(128) for _ in range(2)] for _ in range(2)]
            SHd = [[cv.bf16(128) for _ in range(2)] for _ in range(2)]
            F_ = cv.f32(512)
            L_ = cv.f32(512)
            K_ = cv.f32(512)
            P_ = cv.f32(512)
            EE = cv.f32(512)
            BBb = cv.f32(512)
            T1 = cv.f32(512)
            T2 = cv.f32(512)
            T3 = cv.f32(512)
            T4 = cv.f32(512)
            EK = cv.f32(512)
            RST = cv.bf16(512)
            QTLd = [cv.bf16(512) for _ in range(2)]
            KTLd = [cv.bf16(512) for _ in range(2)]
            QBd = [cv.bf16(512) for _ in range(2)]
            KETd = [cv.bf16(512) for _ in range(2)]
            EBd = [cv.f32(16) for _ in range(2)]
            AT = [cv.bf16(256).rearrange("p (h t) -> p h t", h=2) for _ in range(2)]
            KEZ = [cv.bf16(384) for _ in range(2)]
            MXB = [cv.bf16(512) for _ in range(2)]
            SQ = cv.bf16(512)
            p.op("vector", lambda e: e.memset(VZ.rearrange("p a h f -> p (a h f)"), 0.0), writes=["VZ0"])
            for i in range(2):
                p.op("vector", lambda e, i=i: e.memset(KEZ[i], 0.0), writes=["KEZ%d" % i])
            p.op("vector", lambda e: e.memset(RST, 1.0), writes=["RST"])
            p.op("vector", lambda e: e.memset(RST.rearrange("p (c t) -> p c t", t=32)[:, :, 0:1], 0.0),
                 reads=["RST"], writes=["RST"])
            sW = p.wnext(("wi", l, 768 + 512 * hc, 512))
            sG = p.wnext(("wi", l, 1792 + 128 * hc, 128))
            so = p.wnext(("wo", l, 256 + 128 * hc, 1))
            W = RING[:, sW, :].rearrange("p (k f) -> p k f", k=8)
            WG = RING[:, sG, 0:1024].rearrange("p (k f) -> p k f", k=8)
            Wo = RING[:, so, 0:1024]
            bset[0] = [0, 1]
            for tt in range(18):
                b = bank()
                proj_tm(sW, W, 384, 128, tt, b)
                evac(VZ[:, tt].rearrange("p h f -> p (h f)")[:, 0:256].rearrange("p (h x) -> p h x", h=2)[:, :, 0:64]
                     if False else VZ[:, tt, 0, 0:64], PS[b][:, 0:64], ["PS%d" % b, "VZ0"], ["VZa%d" % tt])
                evac(VZ[:, tt, 1, 64:128], PS[b][:, 64:128], ["PS%d" % b, "VZ0"], ["VZb%d" % tt])
            vzn = lambda tt: ["VZa%d" % tt, "VZb%d" % tt]

            def ew(dr, blk):
                t0, n = TB[blk]
                ntl = n // 128
                order = list(range(ntl)) if dr == 0 else list(range(ntl - 1, -1, -1))
                zc = 128 if dr == 0 else 256
                lbp = LB[:, dr * 2 + hc, l:l + 1]
                omp = OML[:, dr * 2 + hc, l:l + 1]
                nomp = NOML[:, dr * 2 + hc, l:l + 1]
                zb, qb_ = 0, 1
                QTL, KTL, QB, KET, EB = QTLd[dr], KTLd[dr], QBd[dr], KETd[dr], EBd[dr]
                D_ = "d%d_" % dr
                proj_fm(sW, W, zc, blk, zb)
                proj_fm(sW, W, 0, blk, qb_)

                def v3(buf, S):
                    return buf[:, S].rearrange("p (c t) -> p c t", t=32)

                def st_sig(tl, S, x):
                    p.op("scalar", lambda e: e.activation(out=F_[:, S], in_=PS[zb][:, S], func=AF.Sigmoid),
                         reads=["PS%d" % zb], writes=["FF" + x])

                def st_f(tl, S, x):
                    pass

                def st_ln(tl, S, x):
                    p.op("scalar", lambda e: e.activation(out=L_[:, S], in_=F_[:, S], func=AF.Ln, scale=omp, bias=lbp),
                         reads=["FF" + x, "LB", "OML"], writes=["LF" + x])

                def st_k(tl, S, x):
                    p.op("scalar", lambda e: e.activation(out=K_[:, S], in_=F_[:, S], func=AF.Identity, scale=nomp, bias=omp),
                         reads=["FF" + x, "OML"], writes=["KK" + x])

                def st_scan(tl, S, x):
                    p.op("vector", lambda e: e.tensor_tensor_scan(out=P_[:, S], data0=RST[:, S], data1=L_[:, S], initial=0.0,
                                                                  op0=ALU.mult, op1=ALU.add), reads=["RST", "LF" + x], writes=["PP" + x])

                def st_b(tl, S, x):
                    P3 = v3(P_, S)
                    TOTb = P3[:, :, 31:32].broadcast_to([128, 4, 32])
                    if dr == 1:
                        p.op("vector", lambda e: e.tensor_tensor(out=v3(EE, S), in0=TOTb, in1=P3, op=ALU.subtract),
                             reads=["PP" + x], writes=["EE" + x])
                        p.op("vector", lambda e: e.tensor_tensor(out=BBb[:, S], in0=EE[:, S], in1=L_[:, S], op=ALU.add),
                             reads=["EE" + x, "LF" + x], writes=["BB" + x])

                def st_e(tl, S, x):
                    P3 = v3(P_, S)
                    TOTb = P3[:, :, 31:32].broadcast_to([128, 4, 32])
                    B3 = P3 if dr == 0 else v3(BBb, S)
                    bn = ("PP" if dr == 0 else "BB") + x
                    p.op("vector", lambda e: e.tensor_tensor(out=v3(EE, S), in0=B3, in1=B3[:, :, 16:17].broadcast_to([128, 4, 32]),
                                                             op=ALU.subtract), reads=[bn], writes=["EE" + x])
                    if dr == 0:
                        p.op("vector", lambda e: e.tensor_tensor(out=v3(EK, S), in0=TOTb, in1=P3, op=ALU.subtract),
                             reads=["PP" + x], writes=["EK" + x])
                    else:
                        p.op("vector", lambda e: e.tensor_tensor(out=EK[:, S], in0=P_[:, S], in1=L_[:, S], op=ALU.subtract),
                             reads=["PP" + x, "LF" + x], writes=["EK" + x])

                def st_exp(tl, S, x):
                    Bv = P_ if dr == 0 else BBb
                    bn = ("PP" if dr == 0 else "BB") + x
                    p.op("scalar", lambda e: e.activation(out=T1[:, S], in_=EE[:, S], func=AF.Exp), reads=["EE" + x], writes=["T1" + x])
                    p.op("scalar", lambda e: e.activation(out=T2[:, S], in_=EE[:, S], func=AF.Exp, scale=-1.0), reads=["EE" + x], writes=["T2" + x])
                    p.op("scalar", lambda e: e.activation(out=T3[:, S], in_=Bv[:, S], func=AF.Exp), reads=[bn], writes=["T3" + x])
                    p.op("scalar", lambda e: e.activation(out=T4[:, S], in_=EK[:, S], func=AF.Exp), reads=["EK" + x], writes=["T4" + x])
                    p.op("scalar", lambda e: e.activation(out=EB[:, tl * 4:tl * 4 + 4], in_=v3(P_, S)[:, :, 31], func=AF.Exp),
                         reads=["PP" + x], writes=[D_ + "EB" + x])

                def st_mul(tl, S, x):
                    p.op("vector", lambda e: e.tensor_tensor(out=QTL[:, S], in0=PS[qb_][:, S], in1=T1[:, S], op=ALU.mult),
                         reads=["PS%d" % qb_, "T1" + x], writes=[D_ + "QTL" + x])
                    p.op("vector", lambda e: e.tensor_tensor(out=KTL[:, S], in0=K_[:, S], in1=T2[:, S], op=ALU.mult),
                         reads=["KK" + x, "T2" + x], writes=[D_ + "KTL" + x])
                    p.op("vector", lambda e: e.tensor_tensor(out=QB[:, S], in0=PS[qb_][:, S], in1=T3[:, S], op=ALU.mult),
                         reads=["PS%d" % qb_, "T3" + x], writes=[D_ + "QB" + x])
                    p.op("vector", lambda e: e.tensor_tensor(out=KET[:, S], in0=K_[:, S], in1=T4[:, S], op=ALU.mult),
                         reads=["KK" + x, "T4" + x], writes=[D_ + "KET" + x])

                for stg in (st_sig, st_f, st_ln, st_k, st_scan, st_b, st_e, st_exp, st_mul):
                    for tl in order:
                        stg(tl, slice(tl * 128, tl * 128 + 128), str(tl))

            cnt = {"sc": 0, "tr": 0, "o": 0, "st": 0, "sz0": 0, "sz1": 0}
            of_done = set()

            def tile_ops(dr, blk, tl):
                t0, n = TB[blk]
                lo = tl * 128
                tt = (t0 + lo) // 128
                QTL, KTL, QB, KET, EB = QTLd[dr], KTLd[dr], QBd[dr], KETd[dr], EBd[dr]
                SZ = SZd[dr]
                D_ = "d%d_" % dr
                sbs = [0, 1]
                ai = cnt["sc"] % 2
                cnt["sc"] += 1
                at = AT[ai]
                for h in range(2):
                    R = slice(64 * h, 64 * h + 64)
                    p.op("tensor", lambda e, h=h, R=R: e.matmul(PS[sbs[h]][:, 0:128], lhsT=KTL[R, lo:lo + 128],
                                                                rhs=QTL[R, lo:lo + 128], start=True, stop=True),
                         reads=[D_ + "KTL%d" % tl, D_ + "QTL%d" % tl], writes=["PS%d" % sbs[h]])
                for h in range(2):
                    p.op("vector", lambda e, h=h: e.tensor_tensor(out=at[:, h, :], in0=PS[sbs[h]][:, 0:128],
                                                                  in1=BMASK[:, dr, :], op=ALU.mult),
                         reads=["PS%d" % sbs[h], "BMASK"], writes=["AT%d_%d" % (ai, h)])
                tb_ = 4
                ki = cnt["tr"] % 2
                cnt["tr"] += 1
                trv = PS[tb_][:, 0:64].bitcast(BF16)
                kez = KEZ[ki]
                if "b_notr" not in KD:
                    p.op("tensor", lambda e: e.transpose(out=trv, in_=KET[:, lo:lo + 128], identity=IDENT[:]),
                         reads=[D_ + "KET%d" % tl, "IDENT"], writes=["PS%d" % tb_])
                    evac(kez.rearrange("p (h x) -> p h x", x=192)[:, :, 0:64], trv.rearrange("p (h x) -> p h x", x=64),
                         ["PS%d" % tb_], ["KEZ%d" % ki])
                ob = 5 + dr
                cnt["o"] += 1

                def fpv(e):
                    ins = None
                    for h in range(2):
                        ins = e.matmul(PS[ob][:, 0:128], lhsT=VZ[:, tt, h, :], rhs=at[:, h, :], start=(h == 0), stop=False)
                    return ins
                p.op("tensor", fpv, reads=["AT%d_0" % ai, "AT%d_1" % ai] + vzn(tt), writes=["PS%d" % ob])
                yield
                chs = range(4) if dr == 0 else range(3, -1, -1)
                if "b_nostate" in KD:
                    chs = []
                for ci, c in enumerate(chs):
                    zi = cnt["sz%d" % dr] % 2
                    szr = SHd[dr][zi]
                    co = lo + c * 32

                    def fin_(e, c=c, ci=ci, szr=szr, co=co):
                        return e.matmul(PS[ob][:, c * 32:(c + 1) * 32], lhsT=szr[:, :], rhs=QB[:, co:co + 32],
                                        start=False, stop=(ci == 3))
                    p.op("tensor", fin_, reads=[D_ + "SH%d" % zi, D_ + "QB%d" % tl], writes=["PS%d" % ob])
                    stb = (2, 3, 7)[cnt["st"] % 3]
                    cnt["st"] += 1

                    def fst(e, c=c, stb=stb):
                        ins = None
                        for h in range(2):
                            kw = dict(tile_position=(96, 0)) if c == 3 else {}
                            ins = e.matmul(PS[stb][:, 0:128], lhsT=kez[32 * c:32 * c + 32, 128 * h:128 * h + 128],
                                           rhs=VZ[32 * c:32 * c + 32, tt, h, :], start=(h == 0), stop=(h == 1), **kw)
                        return ins
                    p.op("tensor", fst, reads=["KEZ%d" % ki] + vzn(tt), writes=["PS%d" % stb])
                    cg = (lo // 32) + c
                    cnt["sz%d" % dr] += 1
                    zn = cnt["sz%d" % dr] % 2
                    p.op("vector", lambda e, stb=stb, cg=cg, zi=zi, zn=zn: e.scalar_tensor_tensor(
                        out=SZ[zn], in0=SZ[zi], scalar=EB[:, cg:cg + 1], in1=PS[stb][:, 0:128],
                        op0=ALU.mult, op1=ALU.add), reads=[D_ + "SZ%d" % zi, D_ + "EB%d" % tl, "PS%d" % stb], writes=[D_ + "SZ%d" % zn])
                    p.op("vector", lambda e, zn=zn: e.tensor_copy(out=SHd[dr][zn], in_=SZ[zn]),
                         reads=[D_ + "SZ%d" % zn], writes=[D_ + "SH%d" % zn])
                    yield
                tok = slice(t0 + lo, t0 + lo + 128)
                if tt not in of_done:
                    of_done.add(tt)
                    p.op("scalar", lambda e: e.activation(out=OF[:, tok], in_=PS[ob][:, 0:128], func=AF.Copy),
                         reads=["PS%d" % ob], writes=["OF%d" % tt])
                else:
                    p.op("vector", lambda e: e.tensor_tensor(out=OF[:, tok], in0=PS[ob][:, 0:128], in1=OF[:, tok], op=ALU.add),
                         reads=["PS%d" % ob, "OF%d" % tt], writes=["OF%d" % tt])

            def sweep(dr):
                SZ = SZd[dr]
                D_ = "d%d_" % dr
                p.op("vector", lambda e: e.memset(SZ[cnt["sz%d" % dr] % 2], 0.0), writes=[D_ + "SZ%d" % (cnt["sz%d" % dr] % 2)])
                p.op("vector", lambda e: e.memset(SHd[dr][cnt["sz%d" % dr] % 2], 0.0), writes=[D_ + "SH%d" % (cnt["sz%d" % dr] % 2)])
                blocks = [0, 1, 2, 3, 4] if dr == 0 else [0, 4, 3, 2, 1]
                for blk in blocks:
                    ntl = TB[blk][1] // 128
                    ew(dr, blk)
                    yield
                    tls = range(ntl) if dr == 0 else range(ntl - 1, -1, -1)
                    for tl in tls:
                        yield from tile_ops(dr, blk, tl)
                        yield

            gens = [sweep(0), sweep(1)]
            for _ in range(int(os.environ.get("BSTAG", "0"))):
                next(gens[0])
            while gens:
                for g_ in list(gens):
                    try:
                        next(g_)
                    except StopIteration:
                        gens.remove(g_)
            p.wrelease()
            bset[0] = [0, 1, 2, 3]
            S0, S1 = F_, L_
            for blk, (t0, n) in enumerate(TB):
                N = slice(0, n)
                col = 0 if blk == 0 else 1 + s
                mxb = MXB[blk % 2]
                mxn = "MXB%d" % (blk % 2)
                tok = slice(t0, t0 + n)
                ofn = ["OF%d" % t_ for t_ in range(t0 // 128, (t0 + n) // 128)]
                p.op("scalar", lambda e, tok=tok, N=N: e.activation(out=SQ[:, N], in_=OF[:, tok], func=AF.Square),
                     reads=ofn, writes=["SQ"])
                b2 = bank()
                p.op("tensor", lambda e, b2=b2, N=N: e.matmul(PS[b2][:, N], lhsT=BONES[:], rhs=SQ[:, N], start=True, stop=True),
                     reads=["SQ", "BONES"], writes=["PS%d" % b2])
                p.op("scalar", lambda e, b2=b2, N=N: e.activation(out=S0[:, N], in_=PS[b2][:, N], func=AF.Ln, scale=1.0 / 64, bias=EPS),
                     reads=["PS%d" % b2], writes=["FF"])
                p.op("scalar", lambda e, N=N: e.activation(out=S0[:, N], in_=S0[:, N], func=AF.Exp, scale=-0.5), reads=["FF"], writes=["FF"])
                p.op("vector", lambda e, tok=tok, N=N: e.scalar_tensor_tensor(
                    out=S0[:, N], in0=OF[:, tok], scalar=HNG[:, l:l + 1], in1=S0[:, N], op0=ALU.mult, op1=ALU.mult),
                    reads=ofn + ["FF", "HNG"], writes=["FF"])
                bg = bank()
                proj_fm(sG, WG, 0, blk, bg)
                p.op("scalar", lambda e, bg=bg, N=N: e.activation(out=S1[:, N], in_=PS[bg][:, N], func=AF.Silu),
                     reads=["PS%d" % bg], writes=["LF"])
                p.op("vector", lambda e, mxb=mxb, N=N: e.tensor_tensor(out=mxb[:, N], in0=S0[:, N], in1=S1[:, N], op=ALU.mult),
                     reads=["FF", "LF"], writes=[mxn])
                bset[0] = [4, 5, 6, 7]
                for m in range(8):
                    b = bank()
                    p.op("tensor", lambda e, b=b, m=m, mxb=mxb, N=N: e.matmul(PS[b][:, N], lhsT=Wo[:, m * 128:(m + 1) * 128],
                                                                              rhs=mxb[:, N], start=True, stop=True),
                         reads=["WS%d" % so, mxn], writes=["PS%d" % b])
                    p.op("vector", lambda e, b=b, m=m, N=N, tok=tok, col=col: e.scalar_tensor_tensor(
                        out=XT[:, m, tok], in0=PS[b][:, N], scalar=MOD[:, l, 16 + m, col:col + 1],
                        in1=XT[:, m, tok], op0=ALU.mult, op1=ALU.add),
                        reads=["PS%d" % b, "MOD%d_%d" % (l, col)] + xtn([m], blk), writes=xtn([m], blk))
                bset[0] = [0, 1, 2, 3]
            bset[0] = list(range(8))
            p.wrelease(2)

        def mixer_C(l, s):
            p.sync_all()
            cv = Carve()
            TABc, TABs = load_rope(cv, 1)
            QT = cv.bf16(2 * T).rearrange("p (c t) -> p c t", c=2)
            KT = cv.bf16(2 * T).rearrange("p (c t) -> p c t", c=2)
            VA = cv.bf16(18 * 2 * 192).rearrange("p (a h f) -> p a h f", h=2, f=192)
            MXB = [cv.bf16(1024).rearrange("p (c t) -> p c t", c=2) for _ in range(2)]
            PT = [cv.bf16(512) for _ in range(4)]
            S0 = cv.f32(512)
            S1 = cv.f32(512)
            S2 = cv.f32(512)
            SQ = cv.bf16(512)
            QN = cv.bf16(512)
            if "nomemset" not in KD:
                p.op("vector", lambda e: e.memset(VA.rearrange("p a h f -> p (a h) f")[:, :, 64:128], 1.0), writes=["VA1"])
            sw = p.wnext(("wi", l, COL_CQK, 512))
            W = RING[:, sw, :].rearrange("p (k f) -> p k f", k=8)
            for X in range(4):
                isq = X < 2
                c = X % 2
                dst = QT if isq else KT
                dn = "QT" if isq else "KT"
                for tbi, (t0, n) in enumerate(TB):
                    b = bank()
                    proj_fm(sw, W, X * 128, tbi, b)
                    p.op("scalar", lambda e, b=b, n=n: e.activation(out=SQ[:, 0:n], in_=PS[b][:, 0:n], func=AF.Square),
                         reads=["PS%d" % b], writes=["SQ"])
                    b2 = bank()
                    p.op("tensor", lambda e, b2=b2, n=n: e.matmul(PS[b2][:, 0:n], lhsT=BONES[:], rhs=SQ[:, 0:n], start=True, stop=True),
                         reads=["SQ", "BONES"], writes=["PS%d" % b2])
                    p.op("scalar", lambda e, b2=b2, n=n: e.activation(out=S0[:, 0:n], in_=PS[b2][:, 0:n], func=AF.Ln,
                                                                     scale=1.0 / 64, bias=EPS), reads=["PS%d" % b2], writes=["S0"])
                    p.op("scalar", lambda e, n=n: e.activation(out=S0[:, 0:n], in_=S0[:, 0:n], func=AF.Exp, scale=-0.5),
                         reads=["S0"], writes=["S0"])
                    gap = QKG[:, l, (0 if isq else 1):(1 if isq else 2)]
                    if tbi == 0:
                        p.op("vector", lambda e, b=b, n=n, t0=t0, dst=dst, c=c, gap=gap: e.scalar_tensor_tensor(
                            out=dst[:, c, t0:t0 + n], in0=PS[b][:, 0:n], scalar=gap, in1=S0[:, 0:n],
                            op0=ALU.mult, op1=ALU.mult), reads=["PS%d" % b, "S0", "QKG"], writes=["%s%d_%d" % (dn, c, tbi)])
                    else:
                        p.op("vector", lambda e, b=b, n=n, gap=gap: e.scalar_tensor_tensor(
                            out=QN[:, 0:n], in0=PS[b][:, 0:n], scalar=gap, in1=S0[:, 0:n],
                            op0=ALU.mult, op1=ALU.mult), reads=["PS%d" % b, "S0", "QKG"], writes=["QN"])
                        if "norope" in KD:
                            evac(dst[:, c, t0:t0 + n], QN[:, 0:n], ["QN"], ["%s%d_%d" % (dn, c, tbi)], eng="vector")
                        else:
                            rope_apply(1, QN, TABc, TABs, S1, S2, dst[:, c, t0:t0 + n], ["%s%d_%d" % (dn, c, tbi)], tbi)
            p.wrelease()
            vwn = 256 if "vw256" in KD else 128
            sv = p.wnext(("wi", l, COL_CV, vwn))
            Wv = RING[:, sv, 0:8 * vwn].rearrange("p (k f) -> p k f", k=8)
            for tt in range(0 if "nov" in KD else 18):
                b = bank()
                proj_tm(sv, Wv, 0, 128, tt, b)
                psv = PS[b][:, 0:128].rearrange("p (h f) -> p h f", h=2)
                if "vnoevac" in KD:
                    continue
                if "v2d" in KD:
                    ve = "scalar" if "vact" in KD else ("vector" if "vdve" in KD else None)
                    for h in range(2):
                        evac(VA[:, tt, h, 0:64], psv[:, h, :], ["PS%d" % b], ["VAa%d_%d" % (tt, h)], eng=ve)
                        if "vone" not in KD:
                            evac(VA[:, tt, h, 128:192], psv[:, h, :], ["PS%d" % b], ["VAb%d_%d" % (tt, h)], eng=ve)
                    continue
                evac(VA[:, tt, :, 0:64], psv, ["PS%d" % b], ["VAa%d" % tt])
                evac(VA[:, tt, :, 128:192], psv, ["PS%d" % b], ["VAb%d" % tt])
            p.wrelease()
            so = p.wnext(("wo", l, 512, 2))
            pend = []

            def flush():
                while pend:
                    pend.pop(0)()
            groups = []
            for qb, (q0, nq) in enumerate(TB):
                kts = [0, 1] if qb == 0 else list(range(18))
                mxb = MXB[qb % 2]
                mxn = "MXB%d" % (qb % 2)
                for c in range(2):
                    maps = []
                    for hh in range(2):
                        lo = 64 * hh

                        def s_fn(e, ps, kt, c=c, lo=lo, q0=q0, nq=nq):
                            return e.matmul(ps[:, 0:nq], lhsT=KT[lo:lo + 64, c, kt * 128:(kt + 1) * 128],
                                            rhs=QT[lo:lo + 64, c, q0:q0 + nq], start=True, stop=True)

                        def pv_fn(e, ps, kt, pt, first, last, c=c, lo=lo, nq=nq):
                            return e.matmul(ps[:, 0:nq], lhsT=VA[:, kt, c, lo:lo + 128], rhs=pt, start=first, stop=last)
                        maps.append((s_fn, 0.125, pv_fn, 4 + hh))

                    def fin(c=c, nq=nq, mxb=mxb, mxn=mxn, qb=qb):
                        flush()
                        N = slice(0, nq)
                        p.op("vector", lambda e: e.tensor_copy(out=S0[:, N], in_=PS[4][:, N]), reads=["PS4"], writes=["S0"])
                        p.op("vector", lambda e: e.tensor_copy(out=S2[:, N], in_=PS[5][:, N]), reads=["PS5"], writes=["S2"])
                        p.op("vector", lambda e: e.tensor_copy(out=S1[0:64, N], in_=S0[64:128, N]), reads=["S0"], writes=["S1"])
                        p.op("vector", lambda e: e.tensor_copy(out=S1[64:128, N], in_=S2[0:64, N]), reads=["S2", "S1"], writes=["S1"])
                        p.op("vector", lambda e: e.reciprocal(out=S1[:, N], in_=S1[:, N]), reads=["S1"], writes=["S1"])
                        p.op("vector", lambda e: e.tensor_tensor(out=mxb[0:64, c, N], in0=S0[0:64, N], in1=S1[0:64, N], op=ALU.mult),
                             reads=["S0", "S1"], writes=[mxn + "_%d0" % c])
                        p.op("vector", lambda e: e.tensor_tensor(out=mxb[64:128, c, N], in0=S2[64:128, N], in1=S1[64:128, N], op=ALU.mult),
                             reads=["S2", "S1"], writes=[mxn + "_%d1" % c])
                        if c == 1:
                            pend.append(lambda: outproj_block(l, s, so, lambda cc: mxb[:, cc, 0:nq],
                                        [mxn + "_%d%d" % (a_, b_) for a_ in range(2) for b_ in range(2)], qb))
                    kn = ["KT%d_%d" % (c, t_) for t_ in range(NTB)]
                    groups.append(dict(maps=maps, nq=nq, kts=kts, reads=kn + ["QT%d_%d" % (c, qb)],
                                       vreads=["VA1"] + ["VAa%d" % t_ for t_ in kts] + ["VAb%d" % t_ for t_ in kts], fin=fin))
            bset[0] = [6, 7]
            attention_core(groups, PT, sbanks=[0, 1, 2, 3], G=2)
            flush()
            bset[0] = list(range(8))
            p.wrelease()

        for s in range(NS):
            for k in range(8):
                p.op("sync", lambda e, k=k, s=s: e.dma_start(out=XT[:, k, :], in_=xT[s, k * 128:(k + 1) * 128, :]),
                     writes=xtn([k], 0) + xtn([k], 1) + xtn([k], 2) + xtn([k], 3) + xtn([k], 4), dma_sem="xt")
            p.seal([n_ for tbi in range(NTB) for n_ in xtn(ALLK, tbi)], "xt")
            for l in range(NL):
                norm_phase(l, s, 1)
                if "A" in mixers:
                    mixer_A(l, s)
                if "B" in mixers:
                    mixer_B(l, s)
                if "C" in mixers:
                    mixer_C(l, s)
                if "D" in mixers:
                    mixer_D(l, s)
                norm_phase(l, s, 2)
                ffn_phase(l, s)
            final_phase(s)
        if not p.dry:
            for sem in ("oOB0", "oOB1"):
                p.final_waits[sem] = p.cnt[sem]

    p0 = Prog(dry=True)
    gen(p0)
    p = Prog(dry=False, wlist=p0.wrec)
    gen(p)
    assert p.wpos == len(p.wlist)

    sems = {name: es.enter_context(nc.semaphore(name)) for name in sorted(p.semnames)}
    with nc.Block() as block:
        for eng in ENGS:
            def section(e, eng=eng):
                for waits, fn, sem, inc in p.streams[eng]:
                    for wsem, wval in waits:
                        e.wait_ge(sems[wsem], wval)
                    ins = fn(e)
                    ins.then_inc(sems[sem], inc)
                if eng == "sync":
                    for wsem, wval in p.final_waits.items():
                        e.wait_ge(sems[wsem], wval)
            getattr(block, eng)(section)
    es.close()
    ninstr = {e: len(p.streams[e]) for e in ENGS}
    return nc, ninstr


_CONST = {}
import os
KD = set(os.environ.get('KDBG', '').split(','))


def _consts():
    if _CONST:
        return _CONST
    bf = ml_dtypes.bfloat16
    i = np.arange(64)
    ang = 2 * np.pi * np.outer(i, i) / 64
    C64, S64 = np.cos(ang), np.sin(ang)
    bd = np.zeros((128, 256))
    for h in range(2):
        bd[64 * h:64 * h + 64, 64 * h:64 * h + 64] = C64
        bd[64 * h:64 * h + 64, 128 + 64 * h:128 + 64 * h + 64] = -S64
    n = np.arange(SEQ)
    ang = 2 * np.pi * ((np.outer(n, n) % SEQ).astype(np.float64)) / SEQ
    sc = 1.0 / np.sqrt(64.0 * SEQ)
    dftl = np.stack([np.cos(ang) * sc, np.sin(ang) * sc]).astype(bf)
    m = np.arange(TCX)
    angc = 2 * np.pi * ((np.outer(m, m) % TCX).astype(np.float64)) / TCX
    scc = 1.0 / np.sqrt(64.0 * TCX)
    dftc = np.stack([np.cos(angc) * scc, np.sin(angc) * scc], axis=1)
    dftc = dftc.reshape(2, 128, 2, TCX).transpose(1, 0, 2, 3).reshape(128, 1024)
    row = np.repeat(np.arange(SEQ // 64), 64).astype(np.float64)
    col = np.tile(np.arange(64), SEQ // 64).astype(np.float64)
    rope = np.zeros((4, 128, SEQ))
    perm = np.zeros((128, 256))
    for w, unit in enumerate((32, 64)):
        half, nf = unit // 2, unit // 4
        inv = (np.float32(10000.0) ** (-np.arange(nf, dtype=np.float32) / np.float32(nf))).astype(np.float64)
        for pp in range(128):
            d = pp % unit
            dh = d % half
            pos = row if d < half else col
            a = pos * inv[dh % nf]
            rope[2 * w, pp] = np.cos(a)
            rope[2 * w + 1, pp] = np.sin(a)
            if dh < nf:
                perm[pp + nf, 128 * w + pp] = -1.0
            else:
                perm[pp - nf, 128 * w + pp] = 1.0
    ss_, tt_ = np.meshgrid(np.arange(128), np.arange(128), indexing="ij")
    same = (ss_ // 32) == (tt_ // 32)
    bmask = np.concatenate([(same & (ss_ <= tt_)), (same & (ss_ >= tt_))], axis=1).astype(np.float64)
    _CONST.update(ident=np.eye(128).astype(bf), bmask=bmask.astype(bf))
    _CONST.update(bd=bd.astype(bf), dftl=dftl, dftc=dftc.astype(bf), rope=rope.astype(bf), perm=perm.astype(bf))
    return _CONST


def host_prepare(inputs, ncores, NS):
    f = lambda a: np.ascontiguousarray(np.asarray(a, dtype=np.float32))
    x, c, ctx, c_ctx = (np.asarray(inputs[k]) for k in ("x", "c", "ctx", "c_ctx"))

    def fm(v):
        v = np.asarray(v, dtype=np.float32)
        lead = v.shape[:-1]
        return np.moveaxis(v.reshape(*lead, 8, 128), -1, 0)

    shared = {
        "w_mod": f(inputs["w_mod"]),
        "b_modT": f(np.moveaxis(np.asarray(inputs["b_mod"]).reshape(DEPTH, 48, 128), -1, 0).reshape(128, DEPTH * 48)),
        "n1T": f(fm(inputs["norm1"]).reshape(128, DEPTH * KC)),
        "n2T": f(fm(inputs["norm2"]).reshape(128, DEPTH * KC)),
        "fnT": f(fm(inputs["final_norm"]).reshape(128, KC)),
        "w_in": f(np.asarray(inputs["w_in"])[:, :, WIN_IDX]),
        "qkg": f(np.stack([np.tile(np.asarray(inputs["q_norm"]), (1, 2)), np.tile(np.asarray(inputs["k_norm"]), (1, 2))],
                          axis=-1).transpose(1, 0, 2).reshape(128, DEPTH * 2)),
        "dng": f(np.tile(np.asarray(inputs["diff_norm"]), (1, 2)).T),
        "hng": f(np.tile(np.asarray(inputs["hgrn_norm"]), (1, 2)).T),
        "lbl": f(np.asarray(inputs["hgrn_lb_logits"]).reshape(2, DEPTH, 2, 128).transpose(3, 0, 2, 1).reshape(128, 4 * DEPTH)),
        "dlam": f(np.broadcast_to(np.asarray(inputs["diff_lambda"]).reshape(1, DEPTH * 128), (128, DEPTH * 128))),
        "w_out": f(inputs["w_out"]),
        "w_fi": f(inputs["w_ffn_in"]),
        "w_fo": f(inputs["w_ffn_out"]),
    }
    shared.update(_consts())
    in_maps = []
    for ci in range(ncores):
        bs = list(range(ci * NS, (ci + 1) * NS))
        xt = np.empty((NS, D, T), np.float32)
        for j, b in enumerate(bs):
            xt[j, :, :TCX] = ctx[b].T
            xt[j, :, TCX:] = x[b].T
        cols = [c_ctx] + [c[b] for b in bs] + [c_ctx] * (2 - NS)
        cv = np.stack([np.asarray(v, np.float32) for v in cols], axis=-1)
        cv = cv.reshape(8, 128, 3).transpose(1, 0, 2).reshape(128, 24)
        m = dict(shared)
        m["xT"] = xt
        m["cvec"] = f(cv)
        in_maps.append(m)
    return in_maps


_CACHE = {}


def run(inputs, ncores=8, NS=2, NL=DEPTH, mixers="ABCD"):
    key = (NS, NL, mixers)
    if key not in _CACHE:
        _CACHE[key] = build_program(NS, NL, mixers)
    nc, ninstr = _CACHE[key]
    in_maps = host_prepare(inputs, ncores, NS)
    res = run_bass_kernel_spmd(nc, in_maps, core_ids=list(range(ncores)))
    out = np.empty((ncores * NS, SEQ, D), np.float32)
    for ci in range(ncores):
        o = res.results[ci]["outT"]
        for j in range(NS):
            out[ci * NS + j] = o[j].T
    return out


def kernel(**inputs):
    return run(inputs, ncores=8, NS=2, NL=DEPTH, mixers="ABCD")
```

```python
import numpy as np
import ml_dtypes
from contextlib import ExitStack
import concourse.bass as bass
import concourse.mybir as mybir
from concourse.bass_utils import run_bass_kernel_spmd

F32 = mybir.dt.float32
BF16 = mybir.dt.bfloat16
AF = mybir.ActivationFunctionType
ALU = mybir.AluOpType

D = 1024
KC = 8
TCX = 256
SEQ = 2048
T = TCX + SEQ
DEPTH = 4
FH = 2816
NHB = 6
TB = [(0, 256)] + [(256 + 512 * i, 512) for i in range(4)]
NTB = len(TB)
EPS = 1e-6
NSLOT = 3
AW = 16896
COL_CQK = 2048
COL_CV = 2560
COL_D = 2688
WIN_COLS = 2944
ENGS = ("tensor", "vector", "scalar", "gpsimd", "sync")
_BIDX = np.concatenate([np.arange(768 + 256 * j + 128 * hc, 768 + 256 * j + 128 * hc + 128)
                        for hc in range(2) for j in range(4)])
WIN_IDX = np.concatenate([np.arange(0, 768), _BIDX, np.arange(1792, 2304), np.arange(2304, 2368), np.arange(2304, 2368),
                          np.arange(2368, 2432), np.arange(2368, 2432), np.arange(2432, 2816)])


class Prog:
    def __init__(self, dry=False, wlist=None):
        self.dry = dry
        self.streams = {e: [] for e in ENGS}
        self.cnt = {}
        self.waited = {e: {} for e in ENGS}
        self.res = {}
        self.barrier = {e: [] for e in ENGS}
        self.semnames = set(ENGS)
        self.final_waits = {}
        self.wlist = wlist if wlist is not None else []
        self.wrec = []
        self.wpos = 0
        self.wissued = 0
        self.wreleased = 0
        self.wloader = None

    def _need(self, eng, sem, val, weng, waits):
        if weng == eng and eng == "tensor":
            return
        if self.waited[eng].get(sem, 0) >= val:
            return
        if waits.get(sem, 0) < val:
            waits[sem] = val

    def op(self, eng, fn, reads=(), writes=(), dma_sem=None):
        if self.dry:
            return
        waits = {}
        for (sem, val, weng) in self.barrier[eng]:
            self._need(eng, sem, val, weng, waits)
        self.barrier[eng] = []
        for r in reads:
            st = self.res.get(r)
            if st is not None and st["w"] is not None:
                self._need(eng, *st["w"], waits)
            if st is not None and r.startswith("PS"):
                for sem, (val, reng) in st["r"].items():
                    if reng != eng:
                        self._need(eng, sem, val, reng, waits)
        for w in writes:
            st = self.res.get(w)
            if st is not None:
                if st["w"] is not None:
                    self._need(eng, *st["w"], waits)
                for sem, (val, reng) in st["r"].items():
                    self._need(eng, sem, val, reng, waits)
        for sem, val in waits.items():
            self.waited[eng][sem] = val
        if dma_sem is None:
            sem, inc, weng = eng, 1, eng
        else:
            sem, inc, weng = dma_sem, 16, "dma"
            self.semnames.add(sem)
        val = self.cnt.get(sem, 0) + inc
        self.cnt[sem] = val
        for r in reads:
            st = self.res.setdefault(r, {"w": None, "r": {}})
            st["r"][sem] = (val, weng)
        for w in writes:
            self.res[w] = {"w": (sem, val, weng), "r": {}}
        self.streams[eng].append((list(waits.items()), fn, sem, inc))

    def seal(self, resources, sem):
        if self.dry:
            return
        ev = (sem, self.cnt[sem], "dma")
        for r in resources:
            self.res[r] = {"w": ev, "r": {}}

    def sync_all(self, engines=("tensor", "vector", "scalar")):
        if self.dry:
            return
        evs = [(e, self.cnt.get(e, 0), e) for e in engines if self.cnt.get(e, 0) > 0]
        for e in engines:
            self.barrier[e] = list(evs)

    def _wpump(self):
        while self.wissued < len(self.wlist) and self.wissued < self.wreleased + NSLOT:
            j = self.wissued
            self.wissued += 1
            self.wloader(self, self.wlist[j], j % NSLOT)

    def wnext(self, desc):
        if self.dry:
            self.wrec.append(desc)
            return 0
        i = self.wpos
        assert self.wlist[i] == desc, (i, self.wlist[i], desc)
        self._wpump()
        assert i < self.wissued, "weight ring too small for this access pattern"
        self.wpos += 1
        return i % NSLOT

    def wrelease(self, n=1):
        if self.dry:
            return
        self.wreleased += n
        self._wpump()


def build_program(NS, NL, mixers=""):
    nc = bass.Bass("TRN2", target_bir_lowering=False)

    def dt(name, shape, dtype=F32, kind="ExternalInput"):
        return nc.dram_tensor(name, shape, dtype, kind=kind).ap()

    xT = dt("xT", [NS, D, T])
    cvec = dt("cvec", [128, KC * 3])
    w_mod = dt("w_mod", [DEPTH, D, 6 * D])
    b_modT = dt("b_modT", [128, DEPTH * 48])
    n1T = dt("n1T", [128, DEPTH * KC])
    n2T = dt("n2T", [128, DEPTH * KC])
    fnT = dt("fnT", [128, KC])
    w_in = dt("w_in", [DEPTH, D, WIN_COLS])
    w_out = dt("w_out", [DEPTH, D, D])
    w_fi = dt("w_fi", [DEPTH, D, 2 * FH])
    w_fo = dt("w_fo", [DEPTH, FH, D])
    outT = dt("outT", [NS, D, SEQ], kind="ExternalOutput")
    bd_d = dt("bd", [128, 256], BF16)
    perm_d = dt("perm", [128, 256], BF16)
    dftl_d = dt("dftl", [2, SEQ, SEQ], BF16)
    dftc_d = dt("dftc", [128, 1024], BF16)
    rope_d = dt("rope", [4, 128, SEQ], BF16)
    qkg_d = dt("qkg", [128, DEPTH * 2])
    dng_d = dt("dng", [128, DEPTH])
    dlam_d = dt("dlam", [128, DEPTH * 128])
    ident_d = dt("ident", [128, 128], BF16)
    bmask_d = dt("bmask", [128, 256], BF16)
    lbl_d = dt("lbl", [128, 4 * DEPTH])
    hng_d = dt("hng", [128, DEPTH])

    es = ExitStack()

    def sb(name, shape, dtype):
        return es.enter_context(nc.sbuf_tensor(name, shape, dtype))

    XT = sb("XT", [128, KC, T], F32)
    HT = sb("HT", [128, KC, T], BF16)
    RING = sb("RING", [128, NSLOT, 4096], BF16)
    ARENA = sb("ARENA", [128, AW], F32)
    MOD = sb("MOD", [128, DEPTH, 48, 3], F32)
    A1 = sb("A1", [128, DEPTH, KC, 3], F32)
    A2 = sb("A2", [128, DEPTH, KC, 3], F32)
    BMOD = sb("BMOD", [128, DEPTH, 48], F32)
    N1 = sb("N1", [128, DEPTH, KC], F32)
    N2 = sb("N2", [128, DEPTH, KC], F32)
    FN = sb("FN", [128, KC], F32)
    CV = sb("CV", [128, KC * 3], F32)
    SCT = sb("SCT", [128, KC, 3], BF16)
    ONES = sb("ONES", [128, 128], BF16)
    BONES = sb("BONES", [128, 128], BF16)
    BD = sb("BD", [128, 2, 128], BF16)
    PERM = sb("PERM", [128, 2, 128], BF16)
    QKG = sb("QKG", [128, DEPTH, 2], F32)
    DNG = sb("DNG", [128, DEPTH], F32)
    NLAM = sb("NLAM", [128, DEPTH], F32)
    IDENT = sb("IDENT", [128, 128], BF16)
    BMASK = sb("BMASK", [128, 2, 128], BF16)
    LB = sb("LB", [128, 4, DEPTH], F32)
    OML = sb("OML", [128, 4, DEPTH], F32)
    HNG = sb("HNG", [128, DEPTH], F32)
    NOML = sb("NOML", [128, 4, DEPTH], F32)
    PS = [es.enter_context(nc.psum_tensor("PS%d" % i, [128, 512], F32)) for i in range(8)]

    class Carve:
        def __init__(self):
            self.off = 0

        def f32(self, n):
            v = ARENA[:, self.off:self.off + n]
            self.off += n
            assert self.off <= AW, self.off
            return v

        def bf16(self, n):
            w = (n + 1) // 2
            v = ARENA[:, self.off:self.off + w].bitcast(BF16)
            self.off += w
            assert self.off <= AW, self.off
            return v

    def xtn(ks, tbi):
        return ["XT%d_%d" % (k, tbi) for k in ks]

    def htn(ks, tbi):
        return ["HT%d_%d" % (k, tbi) for k in ks]

    ALLK = list(range(8))

    def gen(p):
        psi = [0]
        bset = [list(range(8))]

        def bank():
            b = bset[0][psi[0] % len(bset[0])]
            psi[0] += 1
            return b

        def wloader(p, desc, slot):
            kind = desc[0]
            dst = RING[:, slot, :]
            if kind in ("mod", "fi", "wi"):
                _, l, c0, n = desc
                wsrc = {"mod": w_mod, "fi": w_fi, "wi": w_in}[kind]
                src = wsrc[l].rearrange("(k p) f -> p k f", p=128)[:, :, c0:c0 + n]
                d3 = dst[:, 0:8 * n].rearrange("p (k f) -> p k f", k=8)
            elif kind in ("fo", "wo"):
                _, l, r0, nch = desc
                wsrc = {"fo": w_fo, "wo": w_out}[kind]
                src = wsrc[l][r0:r0 + nch * 128, :].rearrange("(c p) f -> p c f", p=128)
                d3 = dst[:, 0:nch * 1024].rearrange("p (c f) -> p c f", c=nch)
            elif kind == "dft":
                _, part, half, tb = desc
                src = dftl_d[part][half * 1024:(half + 1) * 1024, tb * 512:(tb + 1) * 512].rearrange(
                    "(st p) t -> p st t", p=128)
                d3 = dst.rearrange("p (k f) -> p k f", k=8)
            elif kind == "dftc":
                src = dftc_d[:, :]
                d3 = dst[:, 0:1024]
            else:
                raise ValueError(kind)
            p.op("gpsimd", lambda e, d3=d3, src=src: e.dma_start(out=d3, in_=src),
                 writes=["WS%d" % slot], dma_sem="ws%d" % slot)

        p.wloader = wloader

        consts = []

        def cload(dst, src, name):
            p.op("sync", lambda e: e.dma_start(out=dst, in_=src), writes=[name], dma_sem="cst")
            consts.append(name)

        cload(CV[:], cvec[:, :], "CV")
        cload(BMOD[:].rearrange("p l j -> p (l j)"), b_modT[:, :], "BMOD")
        cload(N1[:].rearrange("p l k -> p (l k)"), n1T[:, :], "N1")
        cload(N2[:].rearrange("p l k -> p (l k)"), n2T[:, :], "N2")
        cload(FN[:], fnT[:, :], "FN")
        cload(BD[:].rearrange("p a b -> p (a b)"), bd_d[:, :], "BD")
        cload(PERM[:].rearrange("p a b -> p (a b)"), perm_d[:, :], "PERM")
        cload(QKG[:].rearrange("p a b -> p (a b)"), qkg_d[:, :], "QKG")
        cload(DNG[:], dng_d[:, :], "DNG")
        cload(IDENT[:], ident_d[:, :], "IDENT")
        cload(BMASK[:].rearrange("p a b -> p (a b)"), bmask_d[:, :], "BMASK")
        cload(HNG[:], hng_d[:, :], "HNG")
        cload(OML[:].rearrange("p a b -> p (a b)"), lbl_d[:, :], "OML")
        cv0 = Carve()
        DLAM = cv0.f32(DEPTH * 128)
        PRD = cv0.f32(DEPTH * 64)
        SMX = cv0.f32(DEPTH * 2)
        cload(DLAM, dlam_d[:, :], "DLAM")
        p.seal(consts, "cst")
        p.op("vector", lambda e: e.memset(ONES[:], 1.0), writes=["ONES"])
        p.op("vector", lambda e: e.memset(BONES[:], 0.0), writes=["BONES"])
        p.op("vector", lambda e: e.memset(BONES[0:64, 0:64], 1.0), reads=["BONES"], writes=["BONES"])
        p.op("vector", lambda e: e.memset(BONES[64:128, 64:128], 1.0), reads=["BONES"], writes=["BONES"])
        dl4 = DLAM.rearrange("p (l a d) -> p l a d", l=DEPTH, a=4)
        pr4 = PRD.rearrange("p (l a d) -> p l a d", l=DEPTH, a=2)
        for l in range(DEPTH):
            p.op("vector", lambda e, l=l: e.tensor_tensor(out=pr4[:, l], in0=dl4[:, l, 0:4:2, :], in1=dl4[:, l, 1:4:2, :],
                                                          op=ALU.mult), reads=["DLAM"], writes=["PRD"])
        p.op("vector", lambda e: e.tensor_reduce(out=SMX, in_=PRD.rearrange("p (a d) -> p a d", d=32),
                                                 axis=mybir.AxisListType.X, op=ALU.add), reads=["PRD"], writes=["SMX"])
        p.op("scalar", lambda e: e.activation(out=SMX, in_=SMX, func=AF.Exp), reads=["SMX"], writes=["SMX"])
        for l in range(DEPTH):
            li = 0.8 - 0.6 * float(np.exp(-0.3 * l))
            p.op("vector", lambda e, l=l, li=li: e.scalar_tensor_tensor(
                out=NLAM[:, l:l + 1], in0=SMX[:, 2 * l + 1:2 * l + 2], scalar=-li, in1=SMX[:, 2 * l:2 * l + 1],
                op0=ALU.add, op1=ALU.subtract), reads=["SMX"], writes=["NLAM"])

        LSM = cv0.f32(4)
        p.op("scalar", lambda e: e.activation(out=OML[:], in_=OML[:], func=AF.Exp), reads=["OML"], writes=["OML"])
        p.op("vector", lambda e: e.tensor_reduce(out=LSM, in_=OML[:], axis=mybir.AxisListType.X, op=ALU.add),
             reads=["OML"], writes=["LSM"])
        p.op("vector", lambda e: e.reciprocal(out=LSM, in_=LSM), reads=["LSM"], writes=["LSM"])
        p.op("vector", lambda e: e.tensor_tensor(out=OML[:], in0=OML[:], in1=LSM.unsqueeze(2).broadcast_to([128, 4, DEPTH]),
                                                 op=ALU.mult), reads=["OML", "LSM"], writes=["OML"])
        p.op("vector", lambda e: e.memset(LB[:, :, 0:1], 0.0), writes=["LB"])
        for l in range(1, DEPTH):
            p.op("vector", lambda e, l=l: e.tensor_tensor(out=LB[:, :, l:l + 1], in0=LB[:, :, l - 1:l], in1=OML[:, :, l:l + 1],
                                                          op=ALU.add), reads=["LB", "OML"], writes=["LB"])
        p.op("vector", lambda e: e.tensor_scalar(out=OML[:], in0=LB[:], scalar1=-1.0, scalar2=1.0, op0=ALU.mult, op1=ALU.add),
             reads=["LB", "OML"], writes=["OML"])
        p.op("vector", lambda e: e.tensor_scalar(out=NOML[:], in0=LB[:], scalar1=1.0, scalar2=-1.0, op0=ALU.mult, op1=ALU.add),
             reads=["LB"], writes=["OML"])

        p.op("scalar", lambda e: e.activation(out=SCT[:].rearrange("p k j -> p (k j)"), in_=CV[:], func=AF.Silu),
             reads=["CV"], writes=["SCT"])
        for l in range(NL):
            b = bank()
            for jb in range(12):
                slot = p.wnext(("mod", l, jb * 512, 512))
                W = RING[:, slot, :].rearrange("p (k f) -> p k f", k=8)

                def f(e, W=W, b=b, jb=jb):
                    ins = None
                    for j in range(4):
                        jj = jb * 4 + j
                        for k in range(8):
                            ins = e.matmul(PS[b][:, jj * 3:jj * 3 + 3], lhsT=W[:, k, j * 128:(j + 1) * 128],
                                           rhs=SCT[:, k, :], start=(k == 0), stop=(k == 7))
                    return ins
                p.op("tensor", f, reads=["WS%d" % slot, "SCT"], writes=["PS%d" % b])
                p.wrelease()
            psv = PS[b][:, 0:144].rearrange("p (j c) -> p j c", c=3)
            for c in range(3):
                p.op("vector", lambda e, c=c, l=l, psv=psv: e.tensor_tensor(
                    out=MOD[:, l, :, c], in0=psv[:, :, c], in1=BMOD[:, l, :], op=ALU.add),
                    reads=["PS%d" % b, "BMOD"], writes=["MOD%d_%d" % (l, c)])
            for c in range(3):
                p.op("vector", lambda e, c=c, l=l: e.scalar_tensor_tensor(
                    out=A1[:, l, :, c], in0=MOD[:, l, 8:16, c], scalar=1.0, in1=N1[:, l, :],
                    op0=ALU.add, op1=ALU.mult), reads=["MOD%d_%d" % (l, c), "N1"], writes=["A1_%d_%d" % (l, c)])
                p.op("vector", lambda e, c=c, l=l: e.scalar_tensor_tensor(
                    out=A2[:, l, :, c], in0=MOD[:, l, 32:40, c], scalar=1.0, in1=N2[:, l, :],
                    op0=ALU.add, op1=ALU.mult), reads=["MOD%d_%d" % (l, c), "N2"], writes=["A2_%d_%d" % (l, c)])

        def rstd_block(cv_bufs, tbi):
            SQB, LNV, RSTD = cv_bufs
            t0, n = TB[tbi]
            sq3 = SQB[:, 0:8 * n].rearrange("p (k t) -> p k t", k=8)
            p.op("scalar", lambda e: e.activation(out=sq3, in_=XT[:, :, t0:t0 + n], func=AF.Square),
                 reads=xtn(ALLK, tbi), writes=["SQ"])
            b = bank()

            def f(e):
                ins = None
                for k in range(8):
                    ins = e.matmul(PS[b][:, 0:n], lhsT=ONES[:], rhs=sq3[:, k, :], start=(k == 0), stop=(k == 7))
                return ins
            p.op("tensor", f, reads=["SQ", "ONES"], writes=["PS%d" % b])
            p.op("scalar", lambda e: e.activation(out=LNV[:, 0:n], in_=PS[b][:, 0:n], func=AF.Ln,
                                                  scale=1.0 / D, bias=EPS),
                 reads=["PS%d" % b], writes=["LNV"])
            p.op("scalar", lambda e: e.activation(out=RSTD[:, 0:n], in_=LNV[:, 0:n], func=AF.Exp, scale=-0.5),
                 reads=["LNV"], writes=["RSTD"])

        def norm_phase(l, s, which):
            p.sync_all()
            cv = Carve()
            bufs = (cv.bf16(4096), cv.f32(512), cv.f32(512))
            RSTD = bufs[2]
            TMP = [cv.f32(512) for _ in range(4)]
            Aw = A1 if which == 1 else A2
            awn = "A1_%d_" % l if which == 1 else "A2_%d_" % l
            sh0 = 0 if which == 1 else 24
            ti = 0
            for tbi, (t0, n) in enumerate(TB):
                col = 0 if tbi == 0 else 1 + s
                rstd_block(bufs, tbi)
                for k in range(8):
                    tm = TMP[ti % 4]
                    tmn = "TMP%d" % (ti % 4)
                    ti += 1
                    p.op("vector", lambda e, k=k, tm=tm, t0=t0, n=n, col=col: e.scalar_tensor_tensor(
                        out=tm[:, 0:n], in0=XT[:, k, t0:t0 + n], scalar=Aw[:, l, k, col:col + 1], in1=RSTD[:, 0:n],
                        op0=ALU.mult, op1=ALU.mult),
                        reads=xtn([k], tbi) + ["RSTD", awn + str(col)], writes=[tmn])
                    if k % 3 == 0:
                        p.op("vector", lambda e, k=k, tm=tm, t0=t0, n=n, col=col: e.tensor_scalar(
                            out=HT[:, k, t0:t0 + n], in0=tm[:, 0:n], scalar1=MOD[:, l, sh0 + k, col:col + 1], scalar2=None,
                            op0=ALU.add), reads=[tmn, "MOD%d_%d" % (l, col)], writes=htn([k], tbi))
                    else:
                        p.op("scalar", lambda e, k=k, tm=tm, t0=t0, n=n, col=col: e.activation(
                            out=HT[:, k, t0:t0 + n], in_=tm[:, 0:n], func=AF.Identity,
                            bias=MOD[:, l, sh0 + k, col:col + 1], scale=1.0),
                            reads=[tmn, "MOD%d_%d" % (l, col)], writes=htn([k], tbi))

        def ffn_phase(l, s):
            p.sync_all()
            cv = Carve()
            ACT3 = cv.bf16(4 * T).rearrange("p (c t) -> p c t", c=4)
            SG = [cv.f32(512) for _ in range(2)]
            sgi = 0
            for hb in range(NHB):
                nch = 4 if hb < 5 else 2
                sg_ = p.wnext(("fi", l, hb * 512, nch * 128))
                su_ = p.wnext(("fi", l, FH + hb * 512, nch * 128))
                Wg = RING[:, sg_, 0:8 * nch * 128].rearrange("p (k f) -> p k f", k=8)
                Wu = RING[:, su_, 0:8 * nch * 128].rearrange("p (k f) -> p k f", k=8)
                for tbi, (t0, n) in enumerate(TB):
                    for c in range(nch):
                        bg = bank()
                        bu = bank()

                        def fg(e, b=bg, W=Wg, c=c, t0=t0, n=n):
                            ins = None
                            for k in range(8):
                                ins = e.matmul(PS[b][:, 0:n], lhsT=W[:, k, c * 128:(c + 1) * 128],
                                               rhs=HT[:, k, t0:t0 + n], start=(k == 0), stop=(k == 7))
                            return ins
                        p.op("tensor", fg, reads=["WS%d" % sg_] + htn(ALLK, tbi), writes=["PS%d" % bg])
                        p.op("tensor", lambda e, b=bu, W=Wu, c=c, t0=t0, n=n, fg=fg: fg(e, b, W, c, t0, n),
                             reads=["WS%d" % su_] + htn(ALLK, tbi), writes=["PS%d" % bu])
                        sg = SG[sgi % 2]
                        sgn = "SG%d" % (sgi % 2)
                        sgi += 1
                        p.op("scalar", lambda e, b=bg, sg=sg, n=n: e.activation(
                            out=sg[:, 0:n], in_=PS[b][:, 0:n], func=AF.Silu),
                            reads=["PS%d" % bg], writes=[sgn])
                        p.op("vector", lambda e, b=bu, sg=sg, c=c, t0=t0, n=n: e.tensor_tensor(
                            out=ACT3[:, c, t0:t0 + n], in0=PS[b][:, 0:n], in1=sg[:, 0:n], op=ALU.mult),
                            reads=["PS%d" % bu, sgn], writes=["ACT%d_%d" % (c, tbi)])
                p.wrelease(2)
                so_ = p.wnext(("fo", l, hb * 512, nch))
                Wo = RING[:, so_, 0:nch * 1024].rearrange("p (c f) -> p c f", c=nch)
                for tbi, (t0, n) in enumerate(TB):
                    col = 0 if tbi == 0 else 1 + s
                    for m in range(8):
                        b = bank()

                        def fo(e, b=b, Wo=Wo, m=m, t0=t0, n=n, nch=nch):
                            ins = None
                            for c in range(nch):
                                ins = e.matmul(PS[b][:, 0:n], lhsT=Wo[:, c, m * 128:(m + 1) * 128],
                                               rhs=ACT3[:, c, t0:t0 + n], start=(c == 0), stop=(c == nch - 1))
                            return ins
                        p.op("tensor", fo, reads=["WS%d" % so_] + ["ACT%d_%d" % (c, tbi) for c in range(nch)],
                             writes=["PS%d" % b])
                        p.op("vector", lambda e, b=b, m=m, t0=t0, n=n, col=col: e.scalar_tensor_tensor(
                            out=XT[:, m, t0:t0 + n], in0=PS[b][:, 0:n], scalar=MOD[:, l, 40 + m, col:col + 1],
                            in1=XT[:, m, t0:t0 + n], op0=ALU.mult, op1=ALU.add),
                            reads=["PS%d" % b, "MOD%d_%d" % (l, col)] + xtn([m], tbi), writes=xtn([m], tbi))
                p.wrelease()

        def final_phase(s):
            p.sync_all()
            cv = Carve()
            bufs = (cv.bf16(4096), cv.f32(512), cv.f32(512))
            RSTD = bufs[2]
            OB = [cv.f32(512) for _ in range(2)]
            oi = 0
            for tbi in range(1, NTB):
                t0, n = TB[tbi]
                rstd_block(bufs, tbi)
                for k in range(8):
                    ob = OB[oi % 2]
                    obn = "OB%d" % (oi % 2)
                    oi += 1
                    p.op("vector", lambda e, k=k, ob=ob, t0=t0, n=n: e.scalar_tensor_tensor(
                        out=ob[:, 0:n], in0=XT[:, k, t0:t0 + n], scalar=FN[:, k:k + 1], in1=RSTD[:, 0:n],
                        op0=ALU.mult, op1=ALU.mult), reads=xtn([k], tbi) + ["RSTD", "FN"], writes=[obn])
                    p.op("sync", lambda e, k=k, ob=ob, t0=t0, n=n: e.dma_start(
                        out=outT[s, k * 128:(k + 1) * 128, t0 - TCX:t0 - TCX + n], in_=ob[:, 0:n]),
                        reads=[obn], dma_sem="o" + obn)


        def tbi_of_tile(tt):
            return 0 if tt < 2 else 1 + (tt - 2) // 4

        def proj_fm(slot, W, c0, tbi, b):
            t0, n = TB[tbi]

            def f(e):
                ins = None
                for k in range(8):
                    ins = e.matmul(PS[b][:, 0:n], lhsT=W[:, k, c0:c0 + 128], rhs=HT[:, k, t0:t0 + n],
                                   start=(k == 0), stop=(k == 7))
                return ins
            p.op("tensor", f, reads=["WS%d" % slot] + htn(ALLK, tbi), writes=["PS%d" % b])

        def proj_tm(slot, W, c0, ncols, tt, b):
            def f(e):
                ins = None
                for k in range(8):
                    ins = e.matmul(PS[b][:, 0:ncols], lhsT=HT[:, k, tt * 128:(tt + 1) * 128], rhs=W[:, k, c0:c0 + ncols],
                                   start=(k == 0), stop=(k == 7))
                return ins
            p.op("tensor", f, reads=["WS%d" % slot] + htn(ALLK, tbi_of_tile(tt)), writes=["PS%d" % b])

        evi = [0]

        def evac(out, in_, reads, writes, eng=None):
            if eng is None:
                eng = ("scalar", "vector")[evi[0] % 2]
                evi[0] += 1
            if eng == "scalar":
                p.op("scalar", lambda e: e.activation(out=out, in_=in_, func=AF.Copy), reads=reads, writes=writes)
            else:
                p.op("vector", lambda e: e.tensor_copy(out=out, in_=in_), reads=reads, writes=writes)

        def outproj_block(l, s, so, mx_fn, mxnames, tbi):
            Wo = RING[:, so, 0:2048].rearrange("p (c f) -> p c f", c=2)
            t0, n = TB[tbi]
            col = 0 if tbi == 0 else 1 + s
            for m in range(8):
                b = bank()

                def fo(e, b=b, m=m):
                    ins = None
                    for c in range(2):
                        ins = e.matmul(PS[b][:, 0:n], lhsT=Wo[:, c, m * 128:(m + 1) * 128], rhs=mx_fn(c),
                                       start=(c == 0), stop=(c == 1))
                    return ins
                p.op("tensor", fo, reads=["WS%d" % so] + mxnames, writes=["PS%d" % b])
                p.op("vector", lambda e, b=b, m=m: e.scalar_tensor_tensor(
                    out=XT[:, m, t0:t0 + n], in0=PS[b][:, 0:n], scalar=MOD[:, l, 16 + m, col:col + 1],
                    in1=XT[:, m, t0:t0 + n], op0=ALU.mult, op1=ALU.add),
                    reads=["PS%d" % b, "MOD%d_%d" % (l, col)] + xtn([m], tbi), writes=xtn([m], tbi))

        def mixer_D(l, s):
            p.sync_all()
            cv = Carve()
            UT = cv.bf16(2 * T).rearrange("p (c t) -> p c t", c=2)
            UCS = cv.bf16(18 * 512).rearrange("p (a f) -> p a f", f=512)
            MX = cv.bf16(2 * T).rearrange("p (c t) -> p c t", c=2)
            sw = p.wnext(("wi", l, COL_D, 256))
            W = RING[:, sw, 0:2048].rearrange("p (k f) -> p k f", k=8)
            for fc in range(2):
                for tbi, (t0, n) in enumerate(TB):
                    b = bank()
                    proj_fm(sw, W, fc * 128, tbi, b)
                    evac(UT[:, fc, t0:t0 + n], PS[b][:, 0:n], ["PS%d" % b], ["UT%d_%d" % (fc, tbi)])
            p.wrelease()
            for tt in range(18):
                b = bank()

                def f(e, tt=tt, b=b):
                    ins = None
                    for part in range(2):
                        for fc in range(2):
                            i = part * 2 + fc
                            ins = e.matmul(PS[b][:, i * 128:(i + 1) * 128], lhsT=UT[:, fc, tt * 128:(tt + 1) * 128],
                                           rhs=BD[:, part, :], start=True, stop=True)
                    return ins
                p.op("tensor", f, reads=["BD"] + ["UT%d_%d" % (fc, tbi_of_tile(tt)) for fc in range(2)],
                     writes=["PS%d" % b])
                evac(UCS[:, tt, :], PS[b][:, :], ["PS%d" % b], ["UCS%d" % tt])
            sc = p.wnext(("dftc",))
            DC = RING[:, sc, 0:1024].rearrange("p (st pt t) -> p st pt t", st=2, pt=2)
            for fc in range(2):
                b = bank()

                def f(e, fc=fc, b=b):
                    ins = None
                    i = 0
                    for st in range(2):
                        for part in range(2):
                            ins = e.matmul(PS[b][:, 0:256], lhsT=UCS[:, st, (part * 2 + fc) * 128:(part * 2 + fc + 1) * 128],
                                           rhs=DC[:, st, part, :], start=(i == 0), stop=(i == 3))
                            i += 1
                    return ins
                p.op("tensor", f, reads=["WS%d" % sc, "UCS0", "UCS1"], writes=["PS%d" % b])
                evac(MX[:, fc, 0:256], PS[b][:, 0:256], ["PS%d" % b], ["MX%d_0" % fc])
            p.wrelease()
            for tb in range(4):
                t0, n = TB[1 + tb]
                bb = [bank(), bank()]
                for part in range(2):
                    for half in range(2):
                        sd = p.wnext(("dft", part, half, tb))
                        DT = RING[:, sd, :].rearrange("p (st t) -> p st t", st=8)
                        for fc in range(2):
                            def f(e, fc=fc, part=part, half=half, DT=DT, b=bb[fc]):
                                ins = None
                                for st in range(8):
                                    ins = e.matmul(PS[b][:, :], lhsT=UCS[:, 2 + half * 8 + st, (part * 2 + fc) * 128:(part * 2 + fc + 1) * 128],
                                                   rhs=DT[:, st, :], start=(part == 0 and half == 0 and st == 0),
                                                   stop=(part == 1 and half == 1 and st == 7))
                                return ins
                            p.op("tensor", f, reads=["WS%d" % sd] + ["UCS%d" % (2 + half * 8 + st) for st in range(8)],
                                 writes=["PS%d" % bb[fc]])
                        p.wrelease()
                for fc in range(2):
                    evac(MX[:, fc, t0:t0 + n], PS[bb[fc]][:, :], ["PS%d" % bb[fc]], ["MX%d_%d" % (fc, 1 + tb)])
            so = p.wnext(("wo", l, 768, 2))
            for tbi, (t0, n) in enumerate(TB):
                outproj_block(l, s, so, lambda c, t0=t0, n=n: MX[:, c, t0:t0 + n], ["MX0_%d" % tbi, "MX1_%d" % tbi], tbi)
            p.wrelease()

        def load_rope(cv, which):
            TABc = cv.bf16(SEQ)
            TABs = cv.bf16(SEQ)
            if not p.dry:
                evs = [(e_, p.cnt.get(e_, 0), e_) for e_ in ("tensor", "vector", "scalar") if p.cnt.get(e_, 0) > 0]
                p.barrier["sync"] = list(evs)
            p.op("sync", lambda e: e.dma_start(out=TABc, in_=rope_d[2 * which]), writes=["TABC"], dma_sem="tab")
            p.op("sync", lambda e: e.dma_start(out=TABs, in_=rope_d[2 * which + 1]), writes=["TABS"], dma_sem="tab")
            p.seal(["TABC", "TABS"], "tab")
            return TABc, TABs

        def rope_apply(pidx, QN, TABc, TABs, S1, S2, dest, destnames, tbi, sx=""):
            t0, n = TB[tbi]
            lt0 = t0 - TCX
            b3 = bank()
            p.op("tensor", lambda e: e.matmul(PS[b3][:, 0:n], lhsT=PERM[:, pidx, :], rhs=QN[:, 0:n], start=True, stop=True),
                 reads=["QN" + sx, "PERM"], writes=["PS%d" % b3])
            p.op("vector", lambda e: e.tensor_tensor(out=S1[:, 0:n], in0=QN[:, 0:n], in1=TABc[:, lt0:lt0 + n], op=ALU.mult),
                 reads=["QN" + sx, "TABC"], writes=["S1" + sx])
            p.op("vector", lambda e: e.tensor_tensor(out=S2[:, 0:n], in0=PS[b3][:, 0:n], in1=TABs[:, lt0:lt0 + n], op=ALU.mult),
                 reads=["PS%d" % b3, "TABS"], writes=["S2" + sx])
            p.op("vector", lambda e: e.tensor_tensor(out=dest, in0=S1[:, 0:n], in1=S2[:, 0:n], op=ALU.add),
                 reads=["S1" + sx, "S2" + sx], writes=destnames)

        NWARM = int(os.environ.get("NWARM", "0"))
        NBURST = int(os.environ.get("NBURST", "0"))
        NGB = int(os.environ.get("NGB", "0"))

        def pe_warm(n):
            if n <= 0:
                return

            def f(e):
                ins = None
                for _ in range(n):
                    ins = e.matmul(PS[7][:, :], lhsT=ONES[:], rhs=HT[:, 0, 0:512], start=True, stop=True)
                return ins
            p.op("tensor", f, reads=["ONES"], writes=["PS7"])

        def attention_core(groups, PT, sbanks, G=2):
            batches = []
            sid = 0
            for g in groups:
                nk = len(g["kts"])
                st = []
                for ki, kt in enumerate(g["kts"]):
                    for mp in g["maps"]:
                        st.append((kt, mp, ki == 0, ki == nk - 1, sid))
                        sid += 1
                for i in range(0, len(st), G):
                    batches.append((g, st[i:i + G], i + G >= len(st)))
            assert len(sbanks) >= 2 * G and len(PT) >= 2 * G
            nb = len(batches)
            pe_warm(NBURST)
            for j in range(nb + 1):
                if j < nb:
                    g, st, _ = batches[j]
                    nq = g["nq"]

                    def fs(e, st=st, nq=nq, g=g):
                        ins = None
                        if g.get("filler") is not None:
                            g["filler"](e, PS[7], st[0][0])
                        for (kt, mp, first, last, sid_) in st:
                            ins = mp[0](e, PS[sbanks[sid_ % len(sbanks)]], kt)
                        return ins
                    p.op("tensor", fs, reads=g["reads"], writes=["PS%d" % sbanks[x[4] % len(sbanks)] for x in st] +
                         (["PS7"] if g.get("filler") is not None else []))
                    for (kt, mp, first, last, sid_) in st:
                        sbk = sbanks[sid_ % len(sbanks)]
                        pt = PT[sid_ % len(PT)]
                        p.op("scalar", lambda e, pt=pt, sbk=sbk, nq=nq, scale=mp[1]: e.activation(
                            out=pt[:, 0:nq], in_=PS[sbk][:, 0:nq], func=AF.Exp, scale=scale),
                            reads=["PS%d" % sbk], writes=["PT%d" % (sid_ % len(PT))])
                if j >= 1:
                    g, st, glast = batches[j - 1]
                    nq = g["nq"]

                    def fpv(e, st=st, nq=nq):
                        ins = None
                        for (kt, mp, first, last, sid_) in st:
                            ins = mp[2](e, PS[mp[3]], kt, PT[sid_ % len(PT)][:, 0:nq], first, last)
                        return ins
                    p.op("tensor", fpv, reads=["PT%d" % (x[4] % len(PT)) for x in st] + g["vreads"],
                         writes=sorted(set("PS%d" % x[1][3] for x in st)))
                    for _ in range(NWARM):
                        p.op("tensor", lambda e: e.matmul(PS[7][:, :], lhsT=ONES[:], rhs=HT[:, 0, 0:512], start=True, stop=True),
                             reads=["ONES"], writes=["PS7"])
                    if glast:
                        g["fin"]()
                        pe_warm(NGB)


        def mixer_A(l, s):
            p.sync_all()
            cv = Carve()
            TABc, TABs = load_rope(cv, 0)
            QT = cv.bf16(2 * T).rearrange("p (c t) -> p c t", c=2)
            KT = cv.bf16(2 * T).rearrange("p (c t) -> p c t", c=2)
            VA = cv.bf16(18 * 4 * 128).rearrange("p (a c h f) -> p a c h f", c=2, h=2, f=128)
            MXB = [cv.bf16(1024).rearrange("p (c t) -> p c t", c=2) for _ in range(2)]
            PT = [cv.bf16(512) for _ in range(4)]
            S0 = cv.f32(512)
            S1 = cv.f32(512)
            S2 = cv.f32(512)
            SQ = cv.bf16(512)
            QN = cv.bf16(512)
            li = 0.8 - 0.6 * float(np.exp(-0.3 * l))
            lnf = float(np.log(1.0 - li))
            VA2 = VA.rearrange("p a c h f -> p (a c) h f")
            p.op("vector", lambda e: e.memset(VA2[:, :, 0, 64:128], 1.0), writes=["VA1a"])
            p.op("vector", lambda e: e.memset(VA2[:, :, 1, 0:64], 1.0), writes=["VA1b"])
            sw = p.wnext(("wi", l, 0, 512))
            W = RING[:, sw, :].rearrange("p (k f) -> p k f", k=8)
            QNb, S1b, S2b = cv.bf16(512), cv.f32(512), cv.f32(512)
            asets = [(QN, S1, S2, ""), (QNb, S1b, S2b, "b")]
            ait = [0]
            for X in range(4):
                isq = X < 2
                c = X % 2
                dst = QT if isq else KT
                dn = "QT" if isq else "KT"
                for tbi, (t0, n) in enumerate(TB):
                    b = bank()
                    proj_fm(sw, W, X * 128, tbi, b)
                    if tbi == 0:
                        evac(dst[:, c, t0:t0 + n], PS[b][:, 0:n], ["PS%d" % b], ["%s%d_%d" % (dn, c, tbi)])
                    else:
                        QN_, S1_, S2_, sx = asets[ait[0] % 2]
                        ait[0] += 1
                        evac(QN_[:, 0:n], PS[b][:, 0:n], ["PS%d" % b], ["QN" + sx], eng="scalar")
                        rope_apply(0, QN_, TABc, TABs, S1_, S2_, dst[:, c, t0:t0 + n], ["%s%d_%d" % (dn, c, tbi)], tbi, sx=sx)
            p.wrelease()
            sv = p.wnext(("wi", l, 512, 256))
            Wv = RING[:, sv, 0:2048].rearrange("p (k f) -> p k f", k=8)
            for tt in range(18):
                b = bank()
                proj_tm(sv, Wv, 0, 256, tt, b)
                psv = PS[b][:, 0:256].rearrange("p (c h f) -> p c h f", c=2, h=2)
                evac(VA[:, tt, :, 0, 0:64], psv[:, :, 0, :], ["PS%d" % b], ["VAa%d" % tt])
                evac(VA[:, tt, :, 1, 64:128], psv[:, :, 1, :], ["PS%d" % b], ["VAb%d" % tt])
            p.wrelease()
            so = p.wnext(("wo", l, 0, 2))
            pend = []

            def flush():
                while pend:
                    pend.pop(0)()
            groups = []
            gi = 0
            for qb, (q0, nq) in enumerate(TB):
                kts = [0, 1] if qb == 0 else list(range(18))
                mxb = MXB[qb % 2]
                mxn = "MXB%d" % (qb % 2)
                for c in range(2):
                    for hh in range(2):
                        obs = [4, 5]
                        gi += 1
                        lo = 64 * hh
                        maps = []
                        for comp in range(2):
                            u = 2 * hh + comp

                            def s_fn(e, ps, kt, c=c, u=u, q0=q0, nq=nq):
                                kw = dict(tile_position=(96, 0)) if u == 3 else {}
                                return e.matmul(ps[:, 0:nq], lhsT=KT[32 * u:32 * u + 32, c, kt * 128:(kt + 1) * 128],
                                                rhs=QT[32 * u:32 * u + 32, c, q0:q0 + nq], start=True, stop=True, **kw)

                            def pv_fn(e, ps, kt, pt, first, last, c=c, hh=hh, nq=nq):
                                return e.matmul(ps[:, 0:nq], lhsT=VA[:, kt, c, hh, :], rhs=pt, start=first, stop=last)
                            maps.append((s_fn, 32.0 ** -0.5, pv_fn, obs[comp]))

                        def fin(obs=obs, lo=lo, c=c, nq=nq, mxb=mxb, mxn=mxn, hh=hh, qb=qb):
                            flush()
                            dp = 64 - lo
                            R = slice(lo, lo + 64)
                            DP = slice(dp, dp + 64)
                            o1, o2 = obs
                            p.op("vector", lambda e: e.tensor_copy(out=S0[:, 0:nq], in_=PS[o1][:, 0:nq]),
                                 reads=["PS%d" % o1], writes=["S0"])
                            p.op("vector", lambda e: e.tensor_copy(out=S2[:, 0:nq], in_=PS[o2][:, 0:nq]),
                                 reads=["PS%d" % o2], writes=["S2"])
                            p.op("vector", lambda e: e.tensor_copy(out=S1[R, 0:nq], in_=S0[DP, 0:nq]), reads=["S0"], writes=["S1"])
                            p.op("vector", lambda e: e.reciprocal(out=S1[R, 0:nq], in_=S1[R, 0:nq]), reads=["S1"], writes=["S1"])
                            p.op("vector", lambda e: e.tensor_tensor(out=S0[R, 0:nq], in0=S0[R, 0:nq], in1=S1[R, 0:nq], op=ALU.mult),
                                 reads=["S0", "S1"], writes=["S0"])
                            p.op("vector", lambda e: e.tensor_copy(out=S1[R, 0:nq], in_=S2[DP, 0:nq]), reads=["S2", "S0"], writes=["S1"])
                            p.op("vector", lambda e: e.reciprocal(out=S1[R, 0:nq], in_=S1[R, 0:nq]), reads=["S1"], writes=["S1"])
                            p.op("vector", lambda e: e.tensor_tensor(out=S2[R, 0:nq], in0=S2[R, 0:nq], in1=S1[R, 0:nq], op=ALU.mult),
                                 reads=["S2", "S1"], writes=["S2"])
                            p.op("vector", lambda e: e.scalar_tensor_tensor(out=S0[R, 0:nq], in0=S2[R, 0:nq], scalar=NLAM[R, l:l + 1],
                                                                            in1=S0[R, 0:nq], op0=ALU.mult, op1=ALU.add),
                                 reads=["S2", "S0", "NLAM"], writes=["S0"])
                            p.op("scalar", lambda e: e.activation(out=SQ[R, 0:nq], in_=S0[R, 0:nq], func=AF.Square),
                                 reads=["S0"], writes=["SQ"])
                            mb = bank()
                            p.op("tensor", lambda e: e.matmul(PS[mb][:, 0:nq], lhsT=ONES[R, :], rhs=SQ[R, 0:nq], start=True, stop=True),
                                 reads=["SQ", "ONES"], writes=["PS%d" % mb])
                            p.op("scalar", lambda e: e.activation(out=S1[R, 0:nq], in_=PS[mb][R, 0:nq], func=AF.Ln,
                                                                  scale=1.0 / 64, bias=EPS), reads=["PS%d" % mb], writes=["S1"])
                            p.op("scalar", lambda e: e.activation(out=S1[R, 0:nq], in_=S1[R, 0:nq], func=AF.Exp, scale=-0.5, bias=lnf),
                                 reads=["S1"], writes=["S1"])
                            p.op("vector", lambda e: e.scalar_tensor_tensor(out=mxb[R, c, 0:nq], in0=S0[R, 0:nq], scalar=DNG[R, l:l + 1],
                                                                            in1=S1[R, 0:nq], op0=ALU.mult, op1=ALU.mult),
                                 reads=["S0", "S1", "DNG"], writes=[mxn + "_%d%d" % (c, hh)])
                            if c == 1 and hh == 1:
                                pend.append(lambda: outproj_block(l, s, so, lambda cc: mxb[:, cc, 0:nq],
                                            [mxn + "_%d%d" % (a_, b_) for a_ in range(2) for b_ in range(2)], qb))
                        olo = 64 - lo

                        def filler(e, ps, kt, c=c, olo=olo, q0=q0, nq=nq):
                            return e.matmul(ps[:, 0:nq], lhsT=KT[olo:olo + 64, c, kt * 128:(kt + 1) * 128],
                                            rhs=QT[olo:olo + 64, c, q0:q0 + nq], start=True, stop=True)
                        kn = ["KT%d_%d" % (c, t_) for t_ in range(NTB)]
                        groups.append(dict(maps=maps, nq=nq, kts=kts, reads=kn + ["QT%d_%d" % (c, qb)],
                                           filler=(filler if "afill" in KD else None),
                                           vreads=["VA1a", "VA1b"] + ["VAa%d" % t_ for t_ in kts] + ["VAb%d" % t_ for t_ in kts],
                                           fin=fin))
            bset[0] = [6, 7]
            attention_core(groups, PT, sbanks=[0, 1, 2, 3], G=2)
            flush()
            bset[0] = list(range(8))
            p.wrelease()


        def mixer_B(l, s):
            for hc in range(2):
                mixer_B_hc(l, s, hc)

        def mixer_B_hc(l, s, hc):
            p.sync_all()
            cv = Carve()
            VZ = cv.bf16(18 * 2 * 128).rearrange("p (a h f) -> p a h f", h=2, f=128)
            OF = cv.f32(T)
            SZd = [[cv.f32(128) for _ in range(2)] for _ in range(2)]
            SHd = [[cv.bf16(128) for _ in range(2)] for _ in range(2)]
            F_ = cv.f32(512)
            L_ = cv.f32(512)
            K_ = cv.f32(512)
            P_ = cv.f32(512)
            EE = cv.f32(512)
            BBb = cv.f32(512)
            T1 = cv.f32(512)
            T2 = cv.f32(512)
            T3 = cv.f32(512)
            T4 = cv.f32(512)
            EK = cv.f32(512)
            RST = cv.bf16(512)
            QTLd = [cv.bf16(512) for _ in range(2)]
            KTLd = [cv.bf16(512) for _ in range(2)]
            QBd = [cv.bf16(512) for _ in range(2)]
            KETd = [cv.bf16(512) for _ in range(2)]
            EBd = [cv.f32(16) for _ in range(2)]
            AT = [cv.bf16(256).rearrange("p (h t) -> p h t", h=2) for _ in range(2)]
            KEZ = [cv.bf16(384) for _ in range(2)]
            MXB = [cv.bf16(512) for _ in range(2)]
            SQ = cv.bf16(512)
            p.op("vector", lambda e: e.memset(VZ.rearrange("p a h f -> p (a h f)"), 0.0), writes=["VZ0"])
            for i in range(2):
                p.op("vector", lambda e, i=i: e.memset(KEZ[i], 0.0), writes=["KEZ%d" % i])
            p.op("vector", lambda e: e.memset(RST, 1.0), writes=["RST"])
            p.op("vector", lambda e: e.memset(RST.rearrange("p (c t) -> p c t", t=32)[:, :, 0:1], 0.0),
                 reads=["RST"], writes=["RST"])
            sW = p.wnext(("wi", l, 768 + 512 * hc, 512))
            sG = p.wnext(("wi", l, 1792 + 128 * hc, 128))
            so = p.wnext(("wo", l, 256 + 128 * hc, 1))
            W = RING[:, sW, :].rearrange("p (k f) -> p k f", k=8)
            WG = RING[:, sG, 0:1024].rearrange("p (k f) -> p k f", k=8)
            Wo = RING[:, so, 0:1024]
            bset[0] = [0, 1]
            for tt in range(18):
                b = bank()
                proj_tm(sW, W, 384, 128, tt, b)
                evac(VZ[:, tt].rearrange("p h f -> p (h f)")[:, 0:256].rearrange("p (h x) -> p h x", h=2)[:, :, 0:64]
                     if False else VZ[:, tt, 0, 0:64], PS[b][:, 0:64], ["PS%d" % b, "VZ0"], ["VZa%d" % tt])
                evac(VZ[:, tt, 1, 64:128], PS[b][:, 64:128], ["PS%d" % b, "VZ0"], ["VZb%d" % tt])
            vzn = lambda tt: ["VZa%d" % tt, "VZb%d" % tt]

            def ew(dr, blk):
                t0, n = TB[blk]
                ntl = n // 128
                order = list(range(ntl)) if dr == 0 else list(range(ntl - 1, -1, -1))
                zc = 128 if dr == 0 else 256
                lbp = LB[:, dr * 2 + hc, l:l + 1]
                omp = OML[:, dr * 2 + hc, l:l + 1]
                nomp = NOML[:, dr * 2 + hc, l:l + 1]
                zb, qb_ = 0, 1
                QTL, KTL, QB, KET, EB = QTLd[dr], KTLd[dr], QBd[dr], KETd[dr], EBd[dr]
                D_ = "d%d_" % dr
                proj_fm(sW, W, zc, blk, zb)
                proj_fm(sW, W, 0, blk, qb_)

                def v3(buf, S):
                    return buf[:, S].rearrange("p (c t) -> p c t", t=32)

                def st_sig(tl, S, x):
                    p.op("scalar", lambda e: e.activation(out=F_[:, S], in_=PS[zb][:, S], func=AF.Sigmoid),
                         reads=["PS%d" % zb], writes=["FF" + x])

                def st_f(tl, S, x):
                    pass

                def st_ln(tl, S, x):
                    p.op("scalar", lambda e: e.activation(out=L_[:, S], in_=F_[:, S], func=AF.Ln, scale=omp, bias=lbp),
                         reads=["FF" + x, "LB", "OML"], writes=["LF" + x])

                def st_k(tl, S, x):
                    p.op("scalar", lambda e: e.activation(out=K_[:, S], in_=F_[:, S], func=AF.Identity, scale=nomp, bias=omp),
                         reads=["FF" + x, "OML"], writes=["KK" + x])

                def st_scan(tl, S, x):
                    p.op("vector", lambda e: e.tensor_tensor_scan(out=P_[:, S], data0=RST[:, S], data1=L_[:, S], initial=0.0,
                                                                  op0=ALU.mult, op1=ALU.add), reads=["RST", "LF" + x], writes=["PP" + x])

                def st_b(tl, S, x):
                    P3 = v3(P_, S)
                    TOTb = P3[:, :, 31:32].broadcast_to([128, 4, 32])
                    if dr == 1:
                        p.op("vector", lambda e: e.tensor_tensor(out=v3(EE, S), in0=TOTb, in1=P3, op=ALU.subtract),
                             reads=["PP" + x], writes=["EE" + x])
                        p.op("vector", lambda e: e.tensor_tensor(out=BBb[:, S], in0=EE[:, S], in1=L_[:, S], op=ALU.add),
                             reads=["EE" + x, "LF" + x], writes=["BB" + x])

                def st_e(tl, S, x):
                    P3 = v3(P_, S)
                    TOTb = P3[:, :, 31:32].broadcast_to([128, 4, 32])
                    B3 = P3 if dr == 0 else v3(BBb, S)
                    bn = ("PP" if dr == 0 else "BB") + x
                    p.op("vector", lambda e: e.tensor_tensor(out=v3(EE, S), in0=B3, in1=B3[:, :, 16:17].broadcast_to([128, 4, 32]),
                                                             op=ALU.subtract), reads=[bn], writes=["EE" + x])
                    if dr == 0:
                        p.op("vector", lambda e: e.tensor_tensor(out=v3(EK, S), in0=TOTb, in1=P3, op=ALU.subtract),
                             reads=["PP" + x], writes=["EK" + x])
                    else:
                        p.op("vector", lambda e: e.tensor_tensor(out=EK[:, S], in0=P_[:, S], in1=L_[:, S], op=ALU.subtract),
                             reads=["PP" + x, "LF" + x], writes=["EK" + x])

                def st_exp(tl, S, x):
                    Bv = P_ if dr == 0 else BBb
                    bn = ("PP" if dr == 0 else "BB") + x
                    p.op("scalar", lambda e: e.activation(out=T1[:, S], in_=EE[:, S], func=AF.Exp), reads=["EE" + x], writes=["T1" + x])
                    p.op("scalar", lambda e: e.activation(out=T2[:, S], in_=EE[:, S], func=AF.Exp, scale=-1.0), reads=["EE" + x], writes=["T2" + x])
                    p.op("scalar", lambda e: e.activation(out=T3[:, S], in_=Bv[:, S], func=AF.Exp), reads=[bn], writes=["T3" + x])
                    p.op("scalar", lambda e: e.activation(out=T4[:, S], in_=EK[:, S], func=AF.Exp), reads=["EK" + x], writes=["T4" + x])
                    p.op("scalar", lambda e: e.activation(out=EB[:, tl * 4:tl * 4 + 4], in_=v3(P_, S)[:, :, 31], func=AF.Exp),
                         reads=["PP" + x], writes=[D_ + "EB" + x])

                def st_mul(tl, S, x):
                    p.op("vector", lambda e: e.tensor_tensor(out=QTL[:, S], in0=PS[qb_][:, S], in1=T1[:, S], op=ALU.mult),
                         reads=["PS%d" % qb_, "T1" + x], writes=[D_ + "QTL" + x])
                    p.op("vector", lambda e: e.tensor_tensor(out=KTL[:, S], in0=K_[:, S], in1=T2[:, S], op=ALU.mult),
                         reads=["KK" + x, "T2" + x], writes=[D_ + "KTL" + x])
                    p.op("vector", lambda e: e.tensor_tensor(out=QB[:, S], in0=PS[qb_][:, S], in1=T3[:, S], op=ALU.mult),
                         reads=["PS%d" % qb_, "T3" + x], writes=[D_ + "QB" + x])
                    p.op("vector", lambda e: e.tensor_tensor(out=KET[:, S], in0=K_[:, S], in1=T4[:, S], op=ALU.mult),
                         reads=["KK" + x, "T4" + x], writes=[D_ + "KET" + x])

                for stg in (st_sig, st_f, st_ln, st_k, st_scan, st_b, st_e, st_exp, st_mul):
                    for tl in order:
                        stg(tl, slice(tl * 128, tl * 128 + 128), str(tl))

            cnt = {"sc": 0, "tr": 0, "o": 0, "st": 0, "sz0": 0, "sz1": 0}
            of_done = set()

            def tile_ops(dr, blk, tl):
                t0, n = TB[blk]
                lo = tl * 128
                tt = (t0 + lo) // 128
                QTL, KTL, QB, KET, EB = QTLd[dr], KTLd[dr], QBd[dr], KETd[dr], EBd[dr]
                SZ = SZd[dr]
                D_ = "d%d_" % dr
                sbs = [0, 1]
                ai = cnt["sc"] % 2
                cnt["sc"] += 1
                at = AT[ai]
                for h in range(2):
                    R = slice(64 * h, 64 * h + 64)
                    p.op("tensor", lambda e, h=h, R=R: e.matmul(PS[sbs[h]][:, 0:128], lhsT=KTL[R, lo:lo + 128],
                                                                rhs=QTL[R, lo:lo + 128], start=True, stop=True),
                         reads=[D_ + "KTL%d" % tl, D_ + "QTL%d" % tl], writes=["PS%d" % sbs[h]])
                for h in range(2):
                    p.op("vector", lambda e, h=h: e.tensor_tensor(out=at[:, h, :], in0=PS[sbs[h]][:, 0:128],
                                                                  in1=BMASK[:, dr, :], op=ALU.mult),
                         reads=["PS%d" % sbs[h], "BMASK"], writes=["AT%d_%d" % (ai, h)])
                tb_ = 4
                ki = cnt["tr"] % 2
                cnt["tr"] += 1
                trv = PS[tb_][:, 0:64].bitcast(BF16)
                kez = KEZ[ki]
                if "b_notr" not in KD:
                    p.op("tensor", lambda e: e.transpose(out=trv, in_=KET[:, lo:lo + 128], identity=IDENT[:]),
                         reads=[D_ + "KET%d" % tl, "IDENT"], writes=["PS%d" % tb_])
                    evac(kez.rearrange("p (h x) -> p h x", x=192)[:, :, 0:64], trv.rearrange("p (h x) -> p h x", x=64),
                         ["PS%d" % tb_], ["KEZ%d" % ki])
                ob = 5 + dr
                cnt["o"] += 1

                def fpv(e):
                    ins = None
                    for h in range(2):
                        ins = e.matmul(PS[ob][:, 0:128], lhsT=VZ[:, tt, h, :], rhs=at[:, h, :], start=(h == 0), stop=False)
                    return ins
                p.op("tensor", fpv, reads=["AT%d_0" % ai, "AT%d_1" % ai] + vzn(tt), writes=["PS%d" % ob])
                yield
                chs = range(4) if dr == 0 else range(3, -1, -1)
                if "b_nostate" in KD:
                    chs = []
                for ci, c in enumerate(chs):
                    zi = cnt["sz%d" % dr] % 2
                    szr = SHd[dr][zi]
                    co = lo + c * 32

                    def fin_(e, c=c, ci=ci, szr=szr, co=co):
                        return e.matmul(PS[ob][:, c * 32:(c + 1) * 32], lhsT=szr[:, :], rhs=QB[:, co:co + 32],
                                        start=False, stop=(ci == 3))
                    p.op("tensor", fin_, reads=[D_ + "SH%d" % zi, D_ + "QB%d" % tl], writes=["PS%d" % ob])
                    stb = (2, 3, 7)[cnt["st"] % 3]
                    cnt["st"] += 1

                    def fst(e, c=c, stb=stb):
                        ins = None
                        for h in range(2):
                            kw = dict(tile_position=(96, 0)) if c == 3 else {}
                            ins = e.matmul(PS[stb][:, 0:128], lhsT=kez[32 * c:32 * c + 32, 128 * h:128 * h + 128],
                                           rhs=VZ[32 * c:32 * c + 32, tt, h, :], start=(h == 0), stop=(h == 1), **kw)
                        return ins
                    p.op("tensor", fst, reads=["KEZ%d" % ki] + vzn(tt), writes=["PS%d" % stb])
                    cg = (lo // 32) + c
                    cnt["sz%d" % dr] += 1
                    zn = cnt["sz%d" % dr] % 2
                    p.op("vector", lambda e, stb=stb, cg=cg, zi=zi, zn=zn: e.scalar_tensor_tensor(
                        out=SZ[zn], in0=SZ[zi], scalar=EB[:, cg:cg + 1], in1=PS[stb][:, 0:128],
                        op0=ALU.mult, op1=ALU.add), reads=[D_ + "SZ%d" % zi, D_ + "EB%d" % tl, "PS%d" % stb], writes=[D_ + "SZ%d" % zn])
                    p.op("vector", lambda e, zn=zn: e.tensor_copy(out=SHd[dr][zn], in_=SZ[zn]),
                         reads=[D_ + "SZ%d" % zn], writes=[D_ + "SH%d" % zn])
                    yield
                tok = slice(t0 + lo, t0 + lo + 128)
                if tt not in of_done:
                    of_done.add(tt)
                    p.op("scalar", lambda e: e.activation(out=OF[:, tok], in_=PS[ob][:, 0:128], func=AF.Copy),
                         reads=["PS%d" % ob], writes=["OF%d" % tt])
                else:
                    p.op("vector", lambda e: e.tensor_tensor(out=OF[:, tok], in0=PS[ob][:, 0:128], in1=OF[:, tok], op=ALU.add),
                         reads=["PS%d" % ob, "OF%d" % tt], writes=["OF%d" % tt])

            def sweep(dr):
                SZ = SZd[dr]
                D_ = "d%d_" % dr
                p.op("vector", lambda e: e.memset(SZ[cnt["sz%d" % dr] % 2], 0.0), writes=[D_ + "SZ%d" % (cnt["sz%d" % dr] % 2)])
                p.op("vector", lambda e: e.memset(SHd[dr][cnt["sz%d" % dr] % 2], 0.0), writes=[D_ + "SH%d" % (cnt["sz%d" % dr] % 2)])
                blocks = [0, 1, 2, 3, 4] if dr == 0 else [0, 4, 3, 2, 1]
                for blk in blocks:
                    ntl = TB[blk][1] // 128
                    ew(dr, blk)
                    yield
                    tls = range(ntl) if dr == 0 else range(ntl - 1, -1, -1)
                    for tl in tls:
                        yield from tile_ops(dr, blk, tl)
                        yield

            gens = [sweep(0), sweep(1)]
            for _ in range(int(os.environ.get("BSTAG", "0"))):
                next(gens[0])
            while gens:
                for g_ in list(gens):
                    try:
                        next(g_)
                    except StopIteration:
                        gens.remove(g_)
            p.wrelease()
            bset[0] = [0, 1, 2, 3]
            S0, S1 = F_, L_
            for blk, (t0, n) in enumerate(TB):
                N = slice(0, n)
                col = 0 if blk == 0 else 1 + s
                mxb = MXB[blk % 2]
                mxn = "MXB%d" % (blk % 2)
                tok = slice(t0, t0 + n)
                ofn = ["OF%d" % t_ for t_ in range(t0 // 128, (t0 + n) // 128)]
                p.op("scalar", lambda e, tok=tok, N=N: e.activation(out=SQ[:, N], in_=OF[:, tok], func=AF.Square),
                     reads=ofn, writes=["SQ"])
                b2 = bank()
                p.op("tensor", lambda e, b2=b2, N=N: e.matmul(PS[b2][:, N], lhsT=BONES[:], rhs=SQ[:, N], start=True, stop=True),
                     reads=["SQ", "BONES"], writes=["PS%d" % b2])
                p.op("scalar", lambda e, b2=b2, N=N: e.activation(out=S0[:, N], in_=PS[b2][:, N], func=AF.Ln, scale=1.0 / 64, bias=EPS),
                     reads=["PS%d" % b2], writes=["FF"])
                p.op("scalar", lambda e, N=N: e.activation(out=S0[:, N], in_=S0[:, N], func=AF.Exp, scale=-0.5), reads=["FF"], writes=["FF"])
                p.op("vector", lambda e, tok=tok, N=N: e.scalar_tensor_tensor(
                    out=S0[:, N], in0=OF[:, tok], scalar=HNG[:, l:l + 1], in1=S0[:, N], op0=ALU.mult, op1=ALU.mult),
                    reads=ofn + ["FF", "HNG"], writes=["FF"])
                bg = bank()
                proj_fm(sG, WG, 0, blk, bg)
                p.op("scalar", lambda e, bg=bg, N=N: e.activation(out=S1[:, N], in_=PS[bg][:, N], func=AF.Silu),
                     reads=["PS%d" % bg], writes=["LF"])
                p.op("vector", lambda e, mxb=mxb, N=N: e.tensor_tensor(out=mxb[:, N], in0=S0[:, N], in1=S1[:, N], op=ALU.mult),
                     reads=["FF", "LF"], writes=[mxn])
                bset[0] = [4, 5, 6, 7]
                for m in range(8):
                    b = bank()
                    p.op("tensor", lambda e, b=b, m=m, mxb=mxb, N=N: e.matmul(PS[b][:, N], lhsT=Wo[:, m * 128:(m + 1) * 128],
                                                                              rhs=mxb[:, N], start=True, stop=True),
                         reads=["WS%d" % so, mxn], writes=["PS%d" % b])
                    p.op("vector", lambda e, b=b, m=m, N=N, tok=tok, col=col: e.scalar_tensor_tensor(
                        out=XT[:, m, tok], in0=PS[b][:, N], scalar=MOD[:, l, 16 + m, col:col + 1],
                        in1=XT[:, m, tok], op0=ALU.mult, op1=ALU.add),
                        reads=["PS%d" % b, "MOD%d_%d" % (l, col)] + xtn([m], blk), writes=xtn([m], blk))
                bset[0] = [0, 1, 2, 3]
            bset[0] = list(range(8))
            p.wrelease(2)

        def mixer_C(l, s):
            p.sync_all()
            cv = Carve()
            TABc, TABs = load_rope(cv, 1)
            QT = cv.bf16(2 * T).rearrange("p (c t) -> p c t", c=2)
            KT = cv.bf16(2 * T).rearrange("p (c t) -> p c t", c=2)
            VA = cv.bf16(18 * 2 * 192).rearrange("p (a h f) -> p a h f", h=2, f=192)
            MXB = [cv.bf16(1024).rearrange("p (c t) -> p c t", c=2) for _ in range(2)]
            PT = [cv.bf16(512) for _ in range(4)]
            S0 = cv.f32(512)
            S1 = cv.f32(512)
            S2 = cv.f32(512)
            SQ = cv.bf16(512)
            QN = cv.bf16(512)
            if "nomemset" not in KD:
                p.op("vector", lambda e: e.memset(VA.rearrange("p a h f -> p (a h) f")[:, :, 64:128], 1.0), writes=["VA1"])
            sw = p.wnext(("wi", l, COL_CQK, 512))
            W = RING[:, sw, :].rearrange("p (k f) -> p k f", k=8)
            SQb, QNb = cv.bf16(512), cv.bf16(512)
            S0b, S1b, S2b = cv.f32(512), cv.f32(512), cv.f32(512)
            csets = [(SQ, S0, QN, S1, S2, ""), (SQb, S0b, QNb, S1b, S2b, "b")]

            def prep_c(X, tbi, SQ_, S0_, QN_, S1_, S2_, sx):
                isq = X < 2
                c = X % 2
                dst = QT if isq else KT
                dn = "QT" if isq else "KT"
                t0, n = TB[tbi]
                b = bank()
                proj_fm(sw, W, X * 128, tbi, b)
                p.op("scalar", lambda e: e.activation(out=SQ_[:, 0:n], in_=PS[b][:, 0:n], func=AF.Square),
                     reads=["PS%d" % b], writes=["SQ" + sx])
                b2 = bank()
                p.op("tensor", lambda e: e.matmul(PS[b2][:, 0:n], lhsT=BONES[:], rhs=SQ_[:, 0:n], start=True, stop=True),
                     reads=["SQ" + sx, "BONES"], writes=["PS%d" % b2])
                p.op("scalar", lambda e: e.activation(out=S0_[:, 0:n], in_=PS[b2][:, 0:n], func=AF.Ln,
                                                      scale=1.0 / 64, bias=EPS), reads=["PS%d" % b2], writes=["S0" + sx])
                p.op("scalar", lambda e: e.activation(out=S0_[:, 0:n], in_=S0_[:, 0:n], func=AF.Exp, scale=-0.5),
                     reads=["S0" + sx], writes=["S0" + sx])
                gap = QKG[:, l, (0 if isq else 1):(1 if isq else 2)]
                if tbi == 0:
                    p.op("vector", lambda e: e.scalar_tensor_tensor(
                        out=dst[:, c, t0:t0 + n], in0=PS[b][:, 0:n], scalar=gap, in1=S0_[:, 0:n],
                        op0=ALU.mult, op1=ALU.mult), reads=["PS%d" % b, "S0" + sx, "QKG"], writes=["%s%d_%d" % (dn, c, tbi)])
                else:
                    p.op("vector", lambda e: e.scalar_tensor_tensor(
                        out=QN_[:, 0:n], in0=PS[b][:, 0:n], scalar=gap, in1=S0_[:, 0:n],
                        op0=ALU.mult, op1=ALU.mult), reads=["PS%d" % b, "S0" + sx, "QKG"], writes=["QN" + sx])
                    rope_apply(1, QN_, TABc, TABs, S1_, S2_, dst[:, c, t0:t0 + n], ["%s%d_%d" % (dn, c, tbi)], tbi, sx=sx)

            it_ = 0
            for X in range(4):
                for tbi in range(NTB):
                    prep_c(X, tbi, *csets[it_ % 2])
                    it_ += 1
            p.wrelease()
            vwn = 256 if "vw256" in KD else 128
            sv = p.wnext(("wi", l, COL_CV, vwn))
            Wv = RING[:, sv, 0:8 * vwn].rearrange("p (k f) -> p k f", k=8)
            for tt in range(0 if "nov" in KD else 18):
                b = bank()
                proj_tm(sv, Wv, 0, 128, tt, b)
                psv = PS[b][:, 0:128].rearrange("p (h f) -> p h f", h=2)
                if "vnoevac" in KD:
                    continue
                if "v2d" in KD:
                    ve = "scalar" if "vact" in KD else ("vector" if "vdve" in KD else None)
                    for h in range(2):
                        evac(VA[:, tt, h, 0:64], psv[:, h, :], ["PS%d" % b], ["VAa%d_%d" % (tt, h)], eng=ve)
                        if "vone" not in KD:
                            evac(VA[:, tt, h, 128:192], psv[:, h, :], ["PS%d" % b], ["VAb%d_%d" % (tt, h)], eng=ve)
                    continue
                evac(VA[:, tt, :, 0:64], psv, ["PS%d" % b], ["VAa%d" % tt])
                evac(VA[:, tt, :, 128:192], psv, ["PS%d" % b], ["VAb%d" % tt])
            p.wrelease()
            so = p.wnext(("wo", l, 512, 2))
            pend = []

            def flush():
                while pend:
                    pend.pop(0)()
            groups = []
            for qb, (q0, nq) in enumerate(TB):
                kts = [0, 1] if qb == 0 else list(range(18))
                mxb = MXB[qb % 2]
                mxn = "MXB%d" % (qb % 2)
                for c in range(2):
                    maps = []
                    for hh in range(2):
                        lo = 64 * hh

                        def s_fn(e, ps, kt, c=c, lo=lo, q0=q0, nq=nq):
                            return e.matmul(ps[:, 0:nq], lhsT=KT[lo:lo + 64, c, kt * 128:(kt + 1) * 128],
                                            rhs=QT[lo:lo + 64, c, q0:q0 + nq], start=True, stop=True)

                        def pv_fn(e, ps, kt, pt, first, last, c=c, lo=lo, nq=nq):
                            return e.matmul(ps[:, 0:nq], lhsT=VA[:, kt, c, lo:lo + 128], rhs=pt, start=first, stop=last)
                        maps.append((s_fn, 0.125, pv_fn, 4 + hh))

                    def fin(c=c, nq=nq, mxb=mxb, mxn=mxn, qb=qb):
                        flush()
                        N = slice(0, nq)
                        p.op("vector", lambda e: e.tensor_copy(out=S0[:, N], in_=PS[4][:, N]), reads=["PS4"], writes=["S0"])
                        p.op("vector", lambda e: e.tensor_copy(out=S2[:, N], in_=PS[5][:, N]), reads=["PS5"], writes=["S2"])
                        p.op("vector", lambda e: e.tensor_copy(out=S1[0:64, N], in_=S0[64:128, N]), reads=["S0"], writes=["S1"])
                        p.op("vector", lambda e: e.tensor_copy(out=S1[64:128, N], in_=S2[0:64, N]), reads=["S2", "S1"], writes=["S1"])
                        p.op("vector", lambda e: e.reciprocal(out=S1[:, N], in_=S1[:, N]), reads=["S1"], writes=["S1"])
                        p.op("vector", lambda e: e.tensor_tensor(out=mxb[0:64, c, N], in0=S0[0:64, N], in1=S1[0:64, N], op=ALU.mult),
                             reads=["S0", "S1"], writes=[mxn + "_%d0" % c])
                        p.op("vector", lambda e: e.tensor_tensor(out=mxb[64:128, c, N], in0=S2[64:128, N], in1=S1[64:128, N], op=ALU.mult),
                             reads=["S2", "S1"], writes=[mxn + "_%d1" % c])
                        if c == 1:
                            pend.append(lambda: outproj_block(l, s, so, lambda cc: mxb[:, cc, 0:nq],
                                        [mxn + "_%d%d" % (a_, b_) for a_ in range(2) for b_ in range(2)], qb))
                    kn = ["KT%d_%d" % (c, t_) for t_ in range(NTB)]
                    groups.append(dict(maps=maps, nq=nq, kts=kts, reads=kn + ["QT%d_%d" % (c, qb)],
                                       vreads=["VA1"] + ["VAa%d" % t_ for t_ in kts] + ["VAb%d" % t_ for t_ in kts], fin=fin))
            bset[0] = [6, 7]
            attention_core(groups, PT, sbanks=[0, 1, 2, 3], G=2)
            flush()
            bset[0] = list(range(8))
            p.wrelease()

        for s in range(NS):
            for k in range(8):
                p.op("sync", lambda e, k=k, s=s: e.dma_start(out=XT[:, k, :], in_=xT[s, k * 128:(k + 1) * 128, :]),
                     writes=xtn([k], 0) + xtn([k], 1) + xtn([k], 2) + xtn([k], 3) + xtn([k], 4), dma_sem="xt")
            p.seal([n_ for tbi in range(NTB) for n_ in xtn(ALLK, tbi)], "xt")
            for l in range(NL):
                norm_phase(l, s, 1)
                if "A" in mixers:
                    mixer_A(l, s)
                if "B" in mixers:
                    mixer_B(l, s)
                if "C" in mixers:
                    mixer_C(l, s)
                if "D" in mixers:
                    mixer_D(l, s)
                norm_phase(l, s, 2)
                ffn_phase(l, s)
            final_phase(s)
        if not p.dry:
            for sem in ("oOB0", "oOB1"):
                p.final_waits[sem] = p.cnt[sem]

    p0 = Prog(dry=True)
    gen(p0)
    p = Prog(dry=False, wlist=p0.wrec)
    gen(p)
    assert p.wpos == len(p.wlist)

    sems = {name: es.enter_context(nc.semaphore(name)) for name in sorted(p.semnames)}
    with nc.Block() as block:
        for eng in ENGS:
            def section(e, eng=eng):
                for waits, fn, sem, inc in p.streams[eng]:
                    for wsem, wval in waits:
                        e.wait_ge(sems[wsem], wval)
                    ins = fn(e)
                    ins.then_inc(sems[sem], inc)
                if eng == "sync":
                    for wsem, wval in p.final_waits.items():
                        e.wait_ge(sems[wsem], wval)
            getattr(block, eng)(section)
    es.close()
    ninstr = {e: len(p.streams[e]) for e in ENGS}
    return nc, ninstr


_CONST = {}
import os
KD = set(os.environ.get('KDBG', '').split(','))


def _consts():
    if _CONST:
        return _CONST
    bf = ml_dtypes.bfloat16
    i = np.arange(64)
    ang = 2 * np.pi * np.outer(i, i) / 64
    C64, S64 = np.cos(ang), np.sin(ang)
    bd = np.zeros((128, 256))
    for h in range(2):
        bd[64 * h:64 * h + 64, 64 * h:64 * h + 64] = C64
        bd[64 * h:64 * h + 64, 128 + 64 * h:128 + 64 * h + 64] = -S64
    n = np.arange(SEQ)
    ang = 2 * np.pi * ((np.outer(n, n) % SEQ).astype(np.float64)) / SEQ
    sc = 1.0 / np.sqrt(64.0 * SEQ)
    dftl = np.stack([np.cos(ang) * sc, np.sin(ang) * sc]).astype(bf)
    m = np.arange(TCX)
    angc = 2 * np.pi * ((np.outer(m, m) % TCX).astype(np.float64)) / TCX
    scc = 1.0 / np.sqrt(64.0 * TCX)
    dftc = np.stack([np.cos(angc) * scc, np.sin(angc) * scc], axis=1)
    dftc = dftc.reshape(2, 128, 2, TCX).transpose(1, 0, 2, 3).reshape(128, 1024)
    row = np.repeat(np.arange(SEQ // 64), 64).astype(np.float64)
    col = np.tile(np.arange(64), SEQ // 64).astype(np.float64)
    rope = np.zeros((4, 128, SEQ))
    perm = np.zeros((128, 256))
    for w, unit in enumerate((32, 64)):
        half, nf = unit // 2, unit // 4
        inv = (np.float32(10000.0) ** (-np.arange(nf, dtype=np.float32) / np.float32(nf))).astype(np.float64)
        for pp in range(128):
            d = pp % unit
            dh = d % half
            pos = row if d < half else col
            a = pos * inv[dh % nf]
            rope[2 * w, pp] = np.cos(a)
            rope[2 * w + 1, pp] = np.sin(a)
            if dh < nf:
                perm[pp + nf, 128 * w + pp] = -1.0
            else:
                perm[pp - nf, 128 * w + pp] = 1.0
    ss_, tt_ = np.meshgrid(np.arange(128), np.arange(128), indexing="ij")
    same = (ss_ // 32) == (tt_ // 32)
    bmask = np.concatenate([(same & (ss_ <= tt_)), (same & (ss_ >= tt_))], axis=1).astype(np.float64)
    _CONST.update(ident=np.eye(128).astype(bf), bmask=bmask.astype(bf))
    _CONST.update(bd=bd.astype(bf), dftl=dftl, dftc=dftc.astype(bf), rope=rope.astype(bf), perm=perm.astype(bf))
    return _CONST


def host_prepare(inputs, ncores, NS):
    f = lambda a: np.ascontiguousarray(np.asarray(a, dtype=np.float32))
    x, c, ctx, c_ctx = (np.asarray(inputs[k]) for k in ("x", "c", "ctx", "c_ctx"))

    def fm(v):
        v = np.asarray(v, dtype=np.float32)
        lead = v.shape[:-1]
        return np.moveaxis(v.reshape(*lead, 8, 128), -1, 0)

    shared = {
        "w_mod": f(inputs["w_mod"]),
        "b_modT": f(np.moveaxis(np.asarray(inputs["b_mod"]).reshape(DEPTH, 48, 128), -1, 0).reshape(128, DEPTH * 48)),
        "n1T": f(fm(inputs["norm1"]).reshape(128, DEPTH * KC)),
        "n2T": f(fm(inputs["norm2"]).reshape(128, DEPTH * KC)),
        "fnT": f(fm(inputs["final_norm"]).reshape(128, KC)),
        "w_in": f(np.asarray(inputs["w_in"])[:, :, WIN_IDX]),
        "qkg": f(np.stack([np.tile(np.asarray(inputs["q_norm"]), (1, 2)), np.tile(np.asarray(inputs["k_norm"]), (1, 2))],
                          axis=-1).transpose(1, 0, 2).reshape(128, DEPTH * 2)),
        "dng": f(np.tile(np.asarray(inputs["diff_norm"]), (1, 2)).T),
        "hng": f(np.tile(np.asarray(inputs["hgrn_norm"]), (1, 2)).T),
        "lbl": f(np.asarray(inputs["hgrn_lb_logits"]).reshape(2, DEPTH, 2, 128).transpose(3, 0, 2, 1).reshape(128, 4 * DEPTH)),
        "dlam": f(np.broadcast_to(np.asarray(inputs["diff_lambda"]).reshape(1, DEPTH * 128), (128, DEPTH * 128))),
        "w_out": f(inputs["w_out"]),
        "w_fi": f(inputs["w_ffn_in"]),
        "w_fo": f(inputs["w_ffn_out"]),
    }
    shared.update(_consts())
    in_maps = []
    for ci in range(ncores):
        bs = list(range(ci * NS, (ci + 1) * NS))
        xt = np.empty((NS, D, T), np.float32)
        for j, b in enumerate(bs):
            xt[j, :, :TCX] = ctx[b].T
            xt[j, :, TCX:] = x[b].T
        cols = [c_ctx] + [c[b] for b in bs] + [c_ctx] * (2 - NS)
        cv = np.stack([np.asarray(v, np.float32) for v in cols], axis=-1)
        cv = cv.reshape(8, 128, 3).transpose(1, 0, 2).reshape(128, 24)
        m = dict(shared)
        m["xT"] = xt
        m["cvec"] = f(cv)
        in_maps.append(m)
    return in_maps


_CACHE = {}


def run(inputs, ncores=8, NS=2, NL=DEPTH, mixers="ABCD"):
    key = (NS, NL, mixers)
    if key not in _CACHE:
        _CACHE[key] = build_program(NS, NL, mixers)
    nc, ninstr = _CACHE[key]
    in_maps = host_prepare(inputs, ncores, NS)
    res = run_bass_kernel_spmd(nc, in_maps, core_ids=list(range(ncores)))
    out = np.empty((ncores * NS, SEQ, D), np.float32)
    for ci in range(ncores):
        o = res.results[ci]["outT"]
        for j in range(NS):
            out[ci * NS + j] = o[j].T
    return out


def kernel(**inputs):
    return run(inputs, ncores=8, NS=2, NL=DEPTH, mixers="ABCD")
```

```python
import numpy as np
import ml_dtypes
from contextlib import ExitStack
import concourse.bass as bass
import concourse.mybir as mybir
from concourse.bass_utils import run_bass_kernel_spmd

F32 = mybir.dt.float32
BF16 = mybir.dt.bfloat16
AF = mybir.ActivationFunctionType
ALU = mybir.AluOpType

D = 1024
KC = 8
TCX = 256
SEQ = 2048
T = TCX + SEQ
DEPTH = 4
FH = 2816
NHB = 6
TB = [(0, 256)] + [(256 + 512 * i, 512) for i in range(4)]
NTB = len(TB)
EPS = 1e-6
NSLOT = 3
AW = 15360
COL_CQK = 2048
COL_CV = 2560
COL_D = 2688
WIN_COLS = 2944
ENGS = ("tensor", "vector", "scalar", "gpsimd", "sync")
_BIDX = np.concatenate([np.arange(768 + 256 * j + 128 * hc, 768 + 256 * j + 128 * hc + 128)
                        for hc in range(2) for j in range(4)])
WIN_IDX = np.concatenate([np.arange(0, 768), _BIDX, np.arange(1792, 2304), np.arange(2304, 2368), np.arange(2304, 2368),
                          np.arange(2368, 2432), np.arange(2368, 2432), np.arange(2432, 2816)])


class Prog:
    def __init__(self, dry=False, wlist=None):
        self.dry = dry
        self.streams = {e: [] for e in ENGS}
        self.cnt = {}
        self.waited = {e: {} for e in ENGS}
        self.res = {}
        self.barrier = {e: [] for e in ENGS}
        self.semnames = set(ENGS)
        self.final_waits = {}
        self.wlist = wlist if wlist is not None else []
        self.wrec = []
        self.wpos = 0
        self.wissued = 0
        self.wreleased = 0
        self.wloader = None

    def _need(self, eng, sem, val, weng, waits):
        if weng == eng and eng == "tensor":
            return
        if self.waited[eng].get(sem, 0) >= val:
            return
        if waits.get(sem, 0) < val:
            waits[sem] = val

    def op(self, eng, fn, reads=(), writes=(), dma_sem=None):
        if self.dry:
            return
        waits = {}
        for (sem, val, weng) in self.barrier[eng]:
            self._need(eng, sem, val, weng, waits)
        self.barrier[eng] = []
        for r in reads:
            st = self.res.get(r)
            if st is not None and st["w"] is not None:
                self._need(eng, *st["w"], waits)
            if st is not None and r.startswith("PS"):
                for sem, (val, reng) in st["r"].items():
                    if reng != eng:
                        self._need(eng, sem, val, reng, waits)
        for w in writes:
            st = self.res.get(w)
            if st is not None:
                if st["w"] is not None:
                    self._need(eng, *st["w"], waits)
                for sem, (val, reng) in st["r"].items():
                    self._need(eng, sem, val, reng, waits)
        for sem, val in waits.items():
            self.waited[eng][sem] = val
        if dma_sem is None:
            sem, inc, weng = eng, 1, eng
        else:
            sem, inc, weng = dma_sem, 16, "dma"
            self.semnames.add(sem)
        val = self.cnt.get(sem, 0) + inc
        self.cnt[sem] = val
        for r in reads:
            st = self.res.setdefault(r, {"w": None, "r": {}})
            st["r"][sem] = (val, weng)
        for w in writes:
            self.res[w] = {"w": (sem, val, weng), "r": {}}
        self.streams[eng].append((list(waits.items()), fn, sem, inc))

    def seal(self, resources, sem):
        if self.dry:
            return
        ev = (sem, self.cnt[sem], "dma")
        for r in resources:
            self.res[r] = {"w": ev, "r": {}}

    def sync_all(self, engines=("tensor", "vector", "scalar")):
        if self.dry:
            return
        evs = [(e, self.cnt.get(e, 0), e) for e in engines if self.cnt.get(e, 0) > 0]
        for e in engines:
            self.barrier[e] = list(evs)

    def _wpump(self):
        while self.wissued < len(self.wlist) and self.wissued < self.wreleased + NSLOT:
            j = self.wissued
            self.wissued += 1
            self.wloader(self, self.wlist[j], j % NSLOT)

    def wnext(self, desc):
        if self.dry:
            self.wrec.append(desc)
            return 0
        i = self.wpos
        assert self.wlist[i] == desc, (i, self.wlist[i], desc)
        self._wpump()
        assert i < self.wissued, "weight ring too small for this access pattern"
        self.wpos += 1
        return i % NSLOT

    def wrelease(self, n=1):
        if self.dry:
            return
        self.wreleased += n
        self._wpump()


def build_program(NS, NL, mixers=""):
    nc = bass.Bass("TRN2", target_bir_lowering=False)

    def dt(name, shape, dtype=F32, kind="ExternalInput"):
        return nc.dram_tensor(name, shape, dtype, kind=kind).ap()

    xT = dt("xT", [NS, D, T])
    cvec = dt("cvec", [128, KC * 3])
    w_mod = dt("w_mod", [DEPTH, D, 6 * D])
    b_modT = dt("b_modT", [128, DEPTH * 48])
    n1T = dt("n1T", [128, DEPTH * KC])
    n2T = dt("n2T", [128, DEPTH * KC])
    fnT = dt("fnT", [128, KC])
    w_in = dt("w_in", [DEPTH, D, WIN_COLS])
    w_out = dt("w_out", [DEPTH, D, D])
    w_fi = dt("w_fi", [DEPTH, D, 2 * FH])
    w_fo = dt("w_fo", [DEPTH, FH, D])
    outT = dt("outT", [NS, D, SEQ], kind="ExternalOutput")
    bd_d = dt("bd", [128, 256], BF16)
    perm_d = dt("perm", [128, 256], BF16)
    dftl_d = dt("dftl", [2, SEQ, SEQ], BF16)
    dftc_d = dt("dftc", [128, 1024], BF16)
    rope_d = dt("rope", [4, 128, SEQ], BF16)
    qkg_d = dt("qkg", [128, DEPTH * 2])
    dng_d = dt("dng", [128, DEPTH])
    dlam_d = dt("dlam", [128, DEPTH * 128])
    ident_d = dt("ident", [128, 128], BF16)
    bmask_d = dt("bmask", [128, 256], BF16)
    lbl_d = dt("lbl", [128, 4 * DEPTH])
    hng_d = dt("hng", [128, DEPTH])

    es = ExitStack()

    def sb(name, shape, dtype):
        return es.enter_context(nc.sbuf_tensor(name, shape, dtype))

    XT = sb("XT", [128, KC, T], F32)
    HT = sb("HT", [128, KC, T], BF16)
    RING = sb("RING", [128, NSLOT, 4096], BF16)
    ARENA = sb("ARENA", [128, AW], F32)
    MOD = sb("MOD", [128, DEPTH, 48, 3], F32)
    A1 = sb("A1", [128, DEPTH, KC, 3], F32)
    A2 = sb("A2", [128, DEPTH, KC, 3], F32)
    BMOD = sb("BMOD", [128, DEPTH, 48], F32)
    N1 = sb("N1", [128, DEPTH, KC], F32)
    N2 = sb("N2", [128, DEPTH, KC], F32)
    FN = sb("FN", [128, KC], F32)
    CV = sb("CV", [128, KC * 3], F32)
    SCT = sb("SCT", [128, KC, 3], BF16)
    ONES = sb("ONES", [128, 128], BF16)
    BONES = sb("BONES", [128, 128], BF16)
    BD = sb("BD", [128, 2, 128], BF16)
    PERM = sb("PERM", [128, 2, 128], BF16)
    QKG = sb("QKG", [128, DEPTH, 2], F32)
    DNG = sb("DNG", [128, DEPTH], F32)
    NLAM = sb("NLAM", [128, DEPTH], F32)
    IDENT = sb("IDENT", [128, 128], BF16)
    BMASK = sb("BMASK", [128, 2, 128], BF16)
    LB = sb("LB", [128, 4, DEPTH], F32)
    OML = sb("OML", [128, 4, DEPTH], F32)
    HNG = sb("HNG", [128, DEPTH], F32)
    NOML = sb("NOML", [128, 4, DEPTH], F32)
    PS = [es.enter_context(nc.psum_tensor("PS%d" % i, [128, 512], F32)) for i in range(8)]

    class Carve:
        def __init__(self):
            self.off = 0

        def f32(self, n):
            v = ARENA[:, self.off:self.off + n]
            self.off += n
            assert self.off <= AW, self.off
            return v

        def bf16(self, n):
            w = (n + 1) // 2
            v = ARENA[:, self.off:self.off + w].bitcast(BF16)
            self.off += w
            assert self.off <= AW, self.off
            return v

    def xtn(ks, tbi):
        return ["XT%d_%d" % (k, tbi) for k in ks]

    def htn(ks, tbi):
        return ["HT%d_%d" % (k, tbi) for k in ks]

    ALLK = list(range(8))

    def gen(p):
        psi = [0]
        bset = [list(range(8))]

        def bank():
            b = bset[0][psi[0] % len(bset[0])]
            psi[0] += 1
            return b

        def wloader(p, desc, slot):
            kind = desc[0]
            dst = RING[:, slot, :]
            if kind in ("mod", "fi", "wi"):
                _, l, c0, n = desc
                wsrc = {"mod": w_mod, "fi": w_fi, "wi": w_in}[kind]
                src = wsrc[l].rearrange("(k p) f -> p k f", p=128)[:, :, c0:c0 + n]
                d3 = dst[:, 0:8 * n].rearrange("p (k f) -> p k f", k=8)
            elif kind in ("fo", "wo"):
                _, l, r0, nch = desc
                wsrc = {"fo": w_fo, "wo": w_out}[kind]
                src = wsrc[l][r0:r0 + nch * 128, :].rearrange("(c p) f -> p c f", p=128)
                d3 = dst[:, 0:nch * 1024].rearrange("p (c f) -> p c f", c=nch)
            elif kind == "dft":
                _, part, half, tb = desc
                src = dftl_d[part][half * 1024:(half + 1) * 1024, tb * 512:(tb + 1) * 512].rearrange(
                    "(st p) t -> p st t", p=128)
                d3 = dst.rearrange("p (k f) -> p k f", k=8)
            elif kind == "dftc":
                src = dftc_d[:, :]
                d3 = dst[:, 0:1024]
            else:
                raise ValueError(kind)
            p.op("gpsimd", lambda e, d3=d3, src=src: e.dma_start(out=d3, in_=src),
                 writes=["WS%d" % slot], dma_sem="ws%d" % slot)

        p.wloader = wloader

        consts = []

        def cload(dst, src, name):
            p.op("sync", lambda e: e.dma_start(out=dst, in_=src), writes=[name], dma_sem="cst")
            consts.append(name)

        cload(CV[:], cvec[:, :], "CV")
        cload(BMOD[:].rearrange("p l j -> p (l j)"), b_modT[:, :], "BMOD")
        cload(N1[:].rearrange("p l k -> p (l k)"), n1T[:, :], "N1")
        cload(N2[:].rearrange("p l k -> p (l k)"), n2T[:, :], "N2")
        cload(FN[:], fnT[:, :], "FN")
        cload(BD[:].rearrange("p a b -> p (a b)"), bd_d[:, :], "BD")
        cload(PERM[:].rearrange("p a b -> p (a b)"), perm_d[:, :], "PERM")
        cload(QKG[:].rearrange("p a b -> p (a b)"), qkg_d[:, :], "QKG")
        cload(DNG[:], dng_d[:, :], "DNG")
        cload(IDENT[:], ident_d[:, :], "IDENT")
        cload(BMASK[:].rearrange("p a b -> p (a b)"), bmask_d[:, :], "BMASK")
        cload(HNG[:], hng_d[:, :], "HNG")
        cload(OML[:].rearrange("p a b -> p (a b)"), lbl_d[:, :], "OML")
        cv0 = Carve()
        DLAM = cv0.f32(DEPTH * 128)
        PRD = cv0.f32(DEPTH * 64)
        SMX = cv0.f32(DEPTH * 2)
        cload(DLAM, dlam_d[:, :], "DLAM")
        p.seal(consts, "cst")
        p.op("vector", lambda e: e.memset(ONES[:], 1.0), writes=["ONES"])
        p.op("vector", lambda e: e.memset(BONES[:], 0.0), writes=["BONES"])
        p.op("vector", lambda e: e.memset(BONES[0:64, 0:64], 1.0), reads=["BONES"], writes=["BONES"])
        p.op("vector", lambda e: e.memset(BONES[64:128, 64:128], 1.0), reads=["BONES"], writes=["BONES"])
        dl4 = DLAM.rearrange("p (l a d) -> p l a d", l=DEPTH, a=4)
        pr4 = PRD.rearrange("p (l a d) -> p l a d", l=DEPTH, a=2)
        for l in range(DEPTH):
            p.op("vector", lambda e, l=l: e.tensor_tensor(out=pr4[:, l], in0=dl4[:, l, 0:4:2, :], in1=dl4[:, l, 1:4:2, :],
                                                          op=ALU.mult), reads=["DLAM"], writes=["PRD"])
        p.op("vector", lambda e: e.tensor_reduce(out=SMX, in_=PRD.rearrange("p (a d) -> p a d", d=32),
                                                 axis=mybir.AxisListType.X, op=ALU.add), reads=["PRD"], writes=["SMX"])
        p.op("scalar", lambda e: e.activation(out=SMX, in_=SMX, func=AF.Exp), reads=["SMX"], writes=["SMX"])
        for l in range(DEPTH):
            li = 0.8 - 0.6 * float(np.exp(-0.3 * l))
            p.op("vector", lambda e, l=l, li=li: e.scalar_tensor_tensor(
                out=NLAM[:, l:l + 1], in0=SMX[:, 2 * l + 1:2 * l + 2], scalar=-li, in1=SMX[:, 2 * l:2 * l + 1],
                op0=ALU.add, op1=ALU.subtract), reads=["SMX"], writes=["NLAM"])

        LSM = cv0.f32(4)
        p.op("scalar", lambda e: e.activation(out=OML[:], in_=OML[:], func=AF.Exp), reads=["OML"], writes=["OML"])
        p.op("vector", lambda e: e.tensor_reduce(out=LSM, in_=OML[:], axis=mybir.AxisListType.X, op=ALU.add),
             reads=["OML"], writes=["LSM"])
        p.op("vector", lambda e: e.reciprocal(out=LSM, in_=LSM), reads=["LSM"], writes=["LSM"])
        p.op("vector", lambda e: e.tensor_tensor(out=OML[:], in0=OML[:], in1=LSM.unsqueeze(2).broadcast_to([128, 4, DEPTH]),
                                                 op=ALU.mult), reads=["OML", "LSM"], writes=["OML"])
        p.op("vector", lambda e: e.memset(LB[:, :, 0:1], 0.0), writes=["LB"])
        for l in range(1, DEPTH):
            p.op("vector", lambda e, l=l: e.tensor_tensor(out=LB[:, :, l:l + 1], in0=LB[:, :, l - 1:l], in1=OML[:, :, l:l + 1],
                                                          op=ALU.add), reads=["LB", "OML"], writes=["LB"])
        p.op("vector", lambda e: e.tensor_scalar(out=OML[:], in0=LB[:], scalar1=-1.0, scalar2=1.0, op0=ALU.mult, op1=ALU.add),
             reads=["LB", "OML"], writes=["OML"])
        p.op("vector", lambda e: e.tensor_scalar(out=NOML[:], in0=LB[:], scalar1=1.0, scalar2=-1.0, op0=ALU.mult, op1=ALU.add),
             reads=["LB"], writes=["OML"])

        p.op("scalar", lambda e: e.activation(out=SCT[:].rearrange("p k j -> p (k j)"), in_=CV[:], func=AF.Silu),
             reads=["CV"], writes=["SCT"])
        for l in range(NL):
            b = bank()
            for jb in range(12):
                slot = p.wnext(("mod", l, jb * 512, 512))
                W = RING[:, slot, :].rearrange("p (k f) -> p k f", k=8)

                def f(e, W=W, b=b, jb=jb):
                    ins = None
                    for j in range(4):
                        jj = jb * 4 + j
                        for k in range(8):
                            ins = e.matmul(PS[b][:, jj * 3:jj * 3 + 3], lhsT=W[:, k, j * 128:(j + 1) * 128],
                                           rhs=SCT[:, k, :], start=(k == 0), stop=(k == 7))
                    return ins
                p.op("tensor", f, reads=["WS%d" % slot, "SCT"], writes=["PS%d" % b])
                p.wrelease()
            psv = PS[b][:, 0:144].rearrange("p (j c) -> p j c", c=3)
            for c in range(3):
                p.op("vector", lambda e, c=c, l=l, psv=psv: e.tensor_tensor(
                    out=MOD[:, l, :, c], in0=psv[:, :, c], in1=BMOD[:, l, :], op=ALU.add),
                    reads=["PS%d" % b, "BMOD"], writes=["MOD%d_%d" % (l, c)])
            for c in range(3):
                p.op("vector", lambda e, c=c, l=l: e.scalar_tensor_tensor(
                    out=A1[:, l, :, c], in0=MOD[:, l, 8:16, c], scalar=1.0, in1=N1[:, l, :],
                    op0=ALU.add, op1=ALU.mult), reads=["MOD%d_%d" % (l, c), "N1"], writes=["A1_%d_%d" % (l, c)])
                p.op("vector", lambda e, c=c, l=l: e.scalar_tensor_tensor(
                    out=A2[:, l, :, c], in0=MOD[:, l, 32:40, c], scalar=1.0, in1=N2[:, l, :],
                    op0=ALU.add, op1=ALU.mult), reads=["MOD%d_%d" % (l, c), "N2"], writes=["A2_%d_%d" % (l, c)])

        def rstd_block(cv_bufs, tbi):
            SQB, LNV, RSTD = cv_bufs
            t0, n = TB[tbi]
            sq3 = SQB[:, 0:8 * n].rearrange("p (k t) -> p k t", k=8)
            p.op("scalar", lambda e: e.activation(out=sq3, in_=XT[:, :, t0:t0 + n], func=AF.Square),
                 reads=xtn(ALLK, tbi), writes=["SQ"])
            b = bank()

            def f(e):
                ins = None
                for k in range(8):
                    ins = e.matmul(PS[b][:, 0:n], lhsT=ONES[:], rhs=sq3[:, k, :], start=(k == 0), stop=(k == 7))
                return ins
            p.op("tensor", f, reads=["SQ", "ONES"], writes=["PS%d" % b])
            p.op("scalar", lambda e: e.activation(out=LNV[:, 0:n], in_=PS[b][:, 0:n], func=AF.Ln,
                                                  scale=1.0 / D, bias=EPS),
                 reads=["PS%d" % b], writes=["LNV"])
            p.op("scalar", lambda e: e.activation(out=RSTD[:, 0:n], in_=LNV[:, 0:n], func=AF.Exp, scale=-0.5),
                 reads=["LNV"], writes=["RSTD"])

        def norm_phase(l, s, which):
            p.sync_all()
            cv = Carve()
            bufs = (cv.bf16(4096), cv.f32(512), cv.f32(512))
            RSTD = bufs[2]
            TMP = [cv.f32(512) for _ in range(4)]
            Aw = A1 if which == 1 else A2
            awn = "A1_%d_" % l if which == 1 else "A2_%d_" % l
            sh0 = 0 if which == 1 else 24
            ti = 0
            for tbi, (t0, n) in enumerate(TB):
                col = 0 if tbi == 0 else 1 + s
                rstd_block(bufs, tbi)
                for k in range(8):
                    tm = TMP[ti % 4]
                    tmn = "TMP%d" % (ti % 4)
                    ti += 1
                    p.op("vector", lambda e, k=k, tm=tm, t0=t0, n=n, col=col: e.scalar_tensor_tensor(
                        out=tm[:, 0:n], in0=XT[:, k, t0:t0 + n], scalar=Aw[:, l, k, col:col + 1], in1=RSTD[:, 0:n],
                        op0=ALU.mult, op1=ALU.mult),
                        reads=xtn([k], tbi) + ["RSTD", awn + str(col)], writes=[tmn])
                    if k % 3 == 0:
                        p.op("vector", lambda e, k=k, tm=tm, t0=t0, n=n, col=col: e.tensor_scalar(
                            out=HT[:, k, t0:t0 + n], in0=tm[:, 0:n], scalar1=MOD[:, l, sh0 + k, col:col + 1], scalar2=None,
                            op0=ALU.add), reads=[tmn, "MOD%d_%d" % (l, col)], writes=htn([k], tbi))
                    else:
                        p.op("scalar", lambda e, k=k, tm=tm, t0=t0, n=n, col=col: e.activation(
                            out=HT[:, k, t0:t0 + n], in_=tm[:, 0:n], func=AF.Identity,
                            bias=MOD[:, l, sh0 + k, col:col + 1], scale=1.0),
                            reads=[tmn, "MOD%d_%d" % (l, col)], writes=htn([k], tbi))

        def ffn_phase(l, s):
            p.sync_all()
            cv = Carve()
            ACT3 = cv.bf16(4 * T).rearrange("p (c t) -> p c t", c=4)
            SG = [cv.f32(512) for _ in range(2)]
            sgi = 0
            for hb in range(NHB):
                nch = 4 if hb < 5 else 2
                sg_ = p.wnext(("fi", l, hb * 512, nch * 128))
                su_ = p.wnext(("fi", l, FH + hb * 512, nch * 128))
                Wg = RING[:, sg_, 0:8 * nch * 128].rearrange("p (k f) -> p k f", k=8)
                Wu = RING[:, su_, 0:8 * nch * 128].rearrange("p (k f) -> p k f", k=8)
                for tbi, (t0, n) in enumerate(TB):
                    for c in range(nch):
                        bg = bank()
                        bu = bank()

                        def fg(e, b=bg, W=Wg, c=c, t0=t0, n=n):
                            ins = None
                            for k in range(8):
                                ins = e.matmul(PS[b][:, 0:n], lhsT=W[:, k, c * 128:(c + 1) * 128],
                                               rhs=HT[:, k, t0:t0 + n], start=(k == 0), stop=(k == 7))
                            return ins
                        p.op("tensor", fg, reads=["WS%d" % sg_] + htn(ALLK, tbi), writes=["PS%d" % bg])
                        p.op("tensor", lambda e, b=bu, W=Wu, c=c, t0=t0, n=n, fg=fg: fg(e, b, W, c, t0, n),
                             reads=["WS%d" % su_] + htn(ALLK, tbi), writes=["PS%d" % bu])
                        sg = SG[sgi % 2]
                        sgn = "SG%d" % (sgi % 2)
                        sgi += 1
                        p.op("scalar", lambda e, b=bg, sg=sg, n=n: e.activation(
                            out=sg[:, 0:n], in_=PS[b][:, 0:n], func=AF.Silu),
                            reads=["PS%d" % bg], writes=[sgn])
                        p.op("vector", lambda e, b=bu, sg=sg, c=c, t0=t0, n=n: e.tensor_tensor(
                            out=ACT3[:, c, t0:t0 + n], in0=PS[b][:, 0:n], in1=sg[:, 0:n], op=ALU.mult),
                            reads=["PS%d" % bu, sgn], writes=["ACT%d_%d" % (c, tbi)])
                p.wrelease(2)
                so_ = p.wnext(("fo", l, hb * 512, nch))
                Wo = RING[:, so_, 0:nch * 1024].rearrange("p (c f) -> p c f", c=nch)
                for tbi, (t0, n) in enumerate(TB):
                    col = 0 if tbi == 0 else 1 + s
                    for m in range(8):
                        b = bank()

                        def fo(e, b=b, Wo=Wo, m=m, t0=t0, n=n, nch=nch):
                            ins = None
                            for c in range(nch):
                                ins = e.matmul(PS[b][:, 0:n], lhsT=Wo[:, c, m * 128:(m + 1) * 128],
                                               rhs=ACT3[:, c, t0:t0 + n], start=(c == 0), stop=(c == nch - 1))
                            return ins
                        p.op("tensor", fo, reads=["WS%d" % so_] + ["ACT%d_%d" % (c, tbi) for c in range(nch)],
                             writes=["PS%d" % b])
                        p.op("vector", lambda e, b=b, m=m, t0=t0, n=n, col=col: e.scalar_tensor_tensor(
                            out=XT[:, m, t0:t0 + n], in0=PS[b][:, 0:n], scalar=MOD[:, l, 40 + m, col:col + 1],
                            in1=XT[:, m, t0:t0 + n], op0=ALU.mult, op1=ALU.add),
                            reads=["PS%d" % b, "MOD%d_%d" % (l, col)] + xtn([m], tbi), writes=xtn([m], tbi))
                p.wrelease()

        def final_phase(s):
            p.sync_all()
            cv = Carve()
            bufs = (cv.bf16(4096), cv.f32(512), cv.f32(512))
            RSTD = bufs[2]
            OB = [cv.f32(512) for _ in range(2)]
            oi = 0
            for tbi in range(1, NTB):
                t0, n = TB[tbi]
                rstd_block(bufs, tbi)
                for k in range(8):
                    ob = OB[oi % 2]
                    obn = "OB%d" % (oi % 2)
                    oi += 1
                    p.op("vector", lambda e, k=k, ob=ob, t0=t0, n=n: e.scalar_tensor_tensor(
                        out=ob[:, 0:n], in0=XT[:, k, t0:t0 + n], scalar=FN[:, k:k + 1], in1=RSTD[:, 0:n],
                        op0=ALU.mult, op1=ALU.mult), reads=xtn([k], tbi) + ["RSTD", "FN"], writes=[obn])
                    p.op("sync", lambda e, k=k, ob=ob, t0=t0, n=n: e.dma_start(
                        out=outT[s, k * 128:(k + 1) * 128, t0 - TCX:t0 - TCX + n], in_=ob[:, 0:n]),
                        reads=[obn], dma_sem="o" + obn)


        def tbi_of_tile(tt):
            return 0 if tt < 2 else 1 + (tt - 2) // 4

        def proj_fm(slot, W, c0, tbi, b):
            t0, n = TB[tbi]

            def f(e):
                ins = None
                for k in range(8):
                    ins = e.matmul(PS[b][:, 0:n], lhsT=W[:, k, c0:c0 + 128], rhs=HT[:, k, t0:t0 + n],
                                   start=(k == 0), stop=(k == 7))
                return ins
            p.op("tensor", f, reads=["WS%d" % slot] + htn(ALLK, tbi), writes=["PS%d" % b])

        def proj_tm(slot, W, c0, ncols, tt, b):
            def f(e):
                ins = None
                for k in range(8):
                    ins = e.matmul(PS[b][:, 0:ncols], lhsT=HT[:, k, tt * 128:(tt + 1) * 128], rhs=W[:, k, c0:c0 + ncols],
                                   start=(k == 0), stop=(k == 7))
                return ins
            p.op("tensor", f, reads=["WS%d" % slot] + htn(ALLK, tbi_of_tile(tt)), writes=["PS%d" % b])

        evi = [0]

        def evac(out, in_, reads, writes, eng=None):
            if eng is None:
                eng = ("scalar", "vector")[evi[0] % 2]
                evi[0] += 1
            if eng == "scalar":
                p.op("scalar", lambda e: e.activation(out=out, in_=in_, func=AF.Copy), reads=reads, writes=writes)
            else:
                p.op("vector", lambda e: e.tensor_copy(out=out, in_=in_), reads=reads, writes=writes)

        def outproj_block(l, s, so, mx_fn, mxnames, tbi):
            Wo = RING[:, so, 0:2048].rearrange("p (c f) -> p c f", c=2)
            t0, n = TB[tbi]
            col = 0 if tbi == 0 else 1 + s
            for m in range(8):
                b = bank()

                def fo(e, b=b, m=m):
                    ins = None
                    for c in range(2):
                        ins = e.matmul(PS[b][:, 0:n], lhsT=Wo[:, c, m * 128:(m + 1) * 128], rhs=mx_fn(c),
                                       start=(c == 0), stop=(c == 1))
                    return ins
                p.op("tensor", fo, reads=["WS%d" % so] + mxnames, writes=["PS%d" % b])
                p.op("vector", lambda e, b=b, m=m: e.scalar_tensor_tensor(
                    out=XT[:, m, t0:t0 + n], in0=PS[b][:, 0:n], scalar=MOD[:, l, 16 + m, col:col + 1],
                    in1=XT[:, m, t0:t0 + n], op0=ALU.mult, op1=ALU.add),
                    reads=["PS%d" % b, "MOD%d_%d" % (l, col)] + xtn([m], tbi), writes=xtn([m], tbi))

        def mixer_D(l, s):
            p.sync_all()
            cv = Carve()
            UT = cv.bf16(2 * T).rearrange("p (c t) -> p c t", c=2)
            UCS = cv.bf16(18 * 512).rearrange("p (a f) -> p a f", f=512)
            MX = cv.bf16(2 * T).rearrange("p (c t) -> p c t", c=2)
            sw = p.wnext(("wi", l, COL_D, 256))
            W = RING[:, sw, 0:2048].rearrange("p (k f) -> p k f", k=8)
            for fc in range(2):
                for tbi, (t0, n) in enumerate(TB):
                    b = bank()
                    proj_fm(sw, W, fc * 128, tbi, b)
                    evac(UT[:, fc, t0:t0 + n], PS[b][:, 0:n], ["PS%d" % b], ["UT%d_%d" % (fc, tbi)])
            p.wrelease()
            for tt in range(18):
                b = bank()

                def f(e, tt=tt, b=b):
                    ins = None
                    for part in range(2):
                        for fc in range(2):
                            i = part * 2 + fc
                            ins = e.matmul(PS[b][:, i * 128:(i + 1) * 128], lhsT=UT[:, fc, tt * 128:(tt + 1) * 128],
                                           rhs=BD[:, part, :], start=True, stop=True)
                    return ins
                p.op("tensor", f, reads=["BD"] + ["UT%d_%d" % (fc, tbi_of_tile(tt)) for fc in range(2)],
                     writes=["PS%d" % b])
                evac(UCS[:, tt, :], PS[b][:, :], ["PS%d" % b], ["UCS%d" % tt])
            sc = p.wnext(("dftc",))
            DC = RING[:, sc, 0:1024].rearrange("p (st pt t) -> p st pt t", st=2, pt=2)
            for fc in range(2):
                b = bank()

                def f(e, fc=fc, b=b):
                    ins = None
                    i = 0
                    for st in range(2):
                        for part in range(2):
                            ins = e.matmul(PS[b][:, 0:256], lhsT=UCS[:, st, (part * 2 + fc) * 128:(part * 2 + fc + 1) * 128],
                                           rhs=DC[:, st, part, :], start=(i == 0), stop=(i == 3))
                            i += 1
                    return ins
                p.op("tensor", f, reads=["WS%d" % sc, "UCS0", "UCS1"], writes=["PS%d" % b])
                evac(MX[:, fc, 0:256], PS[b][:, 0:256], ["PS%d" % b], ["MX%d_0" % fc])
            p.wrelease()
            for tb in range(4):
                t0, n = TB[1 + tb]
                bb = [bank(), bank()]
                for part in range(2):
                    for half in range(2):
                        sd = p.wnext(("dft", part, half, tb))
                        DT = RING[:, sd, :].rearrange("p (st t) -> p st t", st=8)
                        for fc in range(2):
                            def f(e, fc=fc, part=part, half=half, DT=DT, b=bb[fc]):
                                ins = None
                                for st in range(8):
                                    ins = e.matmul(PS[b][:, :], lhsT=UCS[:, 2 + half * 8 + st, (part * 2 + fc) * 128:(part * 2 + fc + 1) * 128],
                                                   rhs=DT[:, st, :], start=(part == 0 and half == 0 and st == 0),
                                                   stop=(part == 1 and half == 1 and st == 7))
                                return ins
                            p.op("tensor", f, reads=["WS%d" % sd] + ["UCS%d" % (2 + half * 8 + st) for st in range(8)],
                                 writes=["PS%d" % bb[fc]])
                        p.wrelease()
                for fc in range(2):
                    evac(MX[:, fc, t0:t0 + n], PS[bb[fc]][:, :], ["PS%d" % bb[fc]], ["MX%d_%d" % (fc, 1 + tb)])
            so = p.wnext(("wo", l, 768, 2))
            for tbi, (t0, n) in enumerate(TB):
                outproj_block(l, s, so, lambda c, t0=t0, n=n: MX[:, c, t0:t0 + n], ["MX0_%d" % tbi, "MX1_%d" % tbi], tbi)
            p.wrelease()

        def load_rope(cv, which):
            TABc = cv.bf16(SEQ)
            TABs = cv.bf16(SEQ)
            if not p.dry:
                evs = [(e_, p.cnt.get(e_, 0), e_) for e_ in ("tensor", "vector", "scalar") if p.cnt.get(e_, 0) > 0]
                p.barrier["sync"] = list(evs)
            p.op("sync", lambda e: e.dma_start(out=TABc, in_=rope_d[2 * which]), writes=["TABC"], dma_sem="tab")
            p.op("sync", lambda e: e.dma_start(out=TABs, in_=rope_d[2 * which + 1]), writes=["TABS"], dma_sem="tab")
            p.seal(["TABC", "TABS"], "tab")
            return TABc, TABs

        def rope_apply(pidx, QN, TABc, TABs, S1, S2, dest, destnames, tbi):
            t0, n = TB[tbi]
            lt0 = t0 - TCX
            b3 = bank()
            p.op("tensor", lambda e: e.matmul(PS[b3][:, 0:n], lhsT=PERM[:, pidx, :], rhs=QN[:, 0:n], start=True, stop=True),
                 reads=["QN", "PERM"], writes=["PS%d" % b3])
            p.op("vector", lambda e: e.tensor_tensor(out=S1[:, 0:n], in0=QN[:, 0:n], in1=TABc[:, lt0:lt0 + n], op=ALU.mult),
                 reads=["QN", "TABC"], writes=["S1"])
            p.op("vector", lambda e: e.tensor_tensor(out=S2[:, 0:n], in0=PS[b3][:, 0:n], in1=TABs[:, lt0:lt0 + n], op=ALU.mult),
                 reads=["PS%d" % b3, "TABS"], writes=["S2"])
            p.op("vector", lambda e: e.tensor_tensor(out=dest, in0=S1[:, 0:n], in1=S2[:, 0:n], op=ALU.add),
                 reads=["S1", "S2"], writes=destnames)

        NWARM = int(os.environ.get("NWARM", "0"))
        NBURST = int(os.environ.get("NBURST", "0"))
        NGB = int(os.environ.get("NGB", "0"))

        def pe_warm(n):
            if n <= 0:
                return

            def f(e):
                ins = None
                for _ in range(n):
                    ins = e.matmul(PS[7][:, :], lhsT=ONES[:], rhs=HT[:, 0, 0:512], start=True, stop=True)
                return ins
            p.op("tensor", f, reads=["ONES"], writes=["PS7"])

        def attention_core(groups, PT, sbanks, G=2):
            batches = []
            sid = 0
            for g in groups:
                nk = len(g["kts"])
                st = []
                for ki, kt in enumerate(g["kts"]):
                    for mp in g["maps"]:
                        st.append((kt, mp, ki == 0, ki == nk - 1, sid))
                        sid += 1
                for i in range(0, len(st), G):
                    batches.append((g, st[i:i + G], i + G >= len(st)))
            assert len(sbanks) >= 2 * G and len(PT) >= 2 * G
            nb = len(batches)
            pe_warm(NBURST)
            for j in range(nb + 1):
                if j < nb:
                    g, st, _ = batches[j]
                    nq = g["nq"]

                    def fs(e, st=st, nq=nq, g=g):
                        ins = None
                        if g.get("filler") is not None:
                            g["filler"](e, PS[7], st[0][0])
                        for (kt, mp, first, last, sid_) in st:
                            ins = mp[0](e, PS[sbanks[sid_ % len(sbanks)]], kt)
                        return ins
                    p.op("tensor", fs, reads=g["reads"], writes=["PS%d" % sbanks[x[4] % len(sbanks)] for x in st] +
                         (["PS7"] if g.get("filler") is not None else []))
                    for (kt, mp, first, last, sid_) in st:
                        sbk = sbanks[sid_ % len(sbanks)]
                        pt = PT[sid_ % len(PT)]
                        p.op("scalar", lambda e, pt=pt, sbk=sbk, nq=nq, scale=mp[1]: e.activation(
                            out=pt[:, 0:nq], in_=PS[sbk][:, 0:nq], func=AF.Exp, scale=scale),
                            reads=["PS%d" % sbk], writes=["PT%d" % (sid_ % len(PT))])
                if j >= 1:
                    g, st, glast = batches[j - 1]
                    nq = g["nq"]

                    def fpv(e, st=st, nq=nq):
                        ins = None
                        for (kt, mp, first, last, sid_) in st:
                            ins = mp[2](e, PS[mp[3]], kt, PT[sid_ % len(PT)][:, 0:nq], first, last)
                        return ins
                    p.op("tensor", fpv, reads=["PT%d" % (x[4] % len(PT)) for x in st] + g["vreads"],
                         writes=sorted(set("PS%d" % x[1][3] for x in st)))
                    for _ in range(NWARM):
                        p.op("tensor", lambda e: e.matmul(PS[7][:, :], lhsT=ONES[:], rhs=HT[:, 0, 0:512], start=True, stop=True),
                             reads=["ONES"], writes=["PS7"])
                    if glast:
                        g["fin"]()
                        pe_warm(NGB)


        def mixer_A(l, s):
            p.sync_all()
            cv = Carve()
            TABc, TABs = load_rope(cv, 0)
            QT = cv.bf16(2 * T).rearrange("p (c t) -> p c t", c=2)
            KT = cv.bf16(2 * T).rearrange("p (c t) -> p c t", c=2)
            VA = cv.bf16(18 * 4 * 128).rearrange("p (a c h f) -> p a c h f", c=2, h=2, f=128)
            MXB = [cv.bf16(1024).rearrange("p (c t) -> p c t", c=2) for _ in range(2)]
            PT = [cv.bf16(512) for _ in range(4)]
            S0 = cv.f32(512)
            S1 = cv.f32(512)
            S2 = cv.f32(512)
            SQ = cv.bf16(512)
            QN = cv.bf16(512)
            li = 0.8 - 0.6 * float(np.exp(-0.3 * l))
            lnf = float(np.log(1.0 - li))
            VA2 = VA.rearrange("p a c h f -> p (a c) h f")
            p.op("vector", lambda e: e.memset(VA2[:, :, 0, 64:128], 1.0), writes=["VA1a"])
            p.op("vector", lambda e: e.memset(VA2[:, :, 1, 0:64], 1.0), writes=["VA1b"])
            sw = p.wnext(("wi", l, 0, 512))
            W = RING[:, sw, :].rearrange("p (k f) -> p k f", k=8)
            for X in range(4):
                isq = X < 2
                c = X % 2
                dst = QT if isq else KT
                dn = "QT" if isq else "KT"
                for tbi, (t0, n) in enumerate(TB):
                    b = bank()
                    proj_fm(sw, W, X * 128, tbi, b)
                    if tbi == 0:
                        evac(dst[:, c, t0:t0 + n], PS[b][:, 0:n], ["PS%d" % b], ["%s%d_%d" % (dn, c, tbi)])
                    else:
                        evac(QN[:, 0:n], PS[b][:, 0:n], ["PS%d" % b], ["QN"], eng="scalar")
                        rope_apply(0, QN, TABc, TABs, S1, S2, dst[:, c, t0:t0 + n], ["%s%d_%d" % (dn, c, tbi)], tbi)
            p.wrelease()
            sv = p.wnext(("wi", l, 512, 256))
            Wv = RING[:, sv, 0:2048].rearrange("p (k f) -> p k f", k=8)
            for tt in range(18):
                b = bank()
                proj_tm(sv, Wv, 0, 256, tt, b)
                psv = PS[b][:, 0:256].rearrange("p (c h f) -> p c h f", c=2, h=2)
                evac(VA[:, tt, :, 0, 0:64], psv[:, :, 0, :], ["PS%d" % b], ["VAa%d" % tt])
                evac(VA[:, tt, :, 1, 64:128], psv[:, :, 1, :], ["PS%d" % b], ["VAb%d" % tt])
            p.wrelease()
            so = p.wnext(("wo", l, 0, 2))
            pend = []

            def flush():
                while pend:
                    pend.pop(0)()
            groups = []
            gi = 0
            for qb, (q0, nq) in enumerate(TB):
                kts = [0, 1] if qb == 0 else list(range(18))
                mxb = MXB[qb % 2]
                mxn = "MXB%d" % (qb % 2)
                for c in range(2):
                    for hh in range(2):
                        obs = [4, 5]
                        gi += 1
                        lo = 64 * hh
                        maps = []
                        for comp in range(2):
                            u = 2 * hh + comp

                            def s_fn(e, ps, kt, c=c, u=u, q0=q0, nq=nq):
                                kw = dict(tile_position=(96, 0)) if u == 3 else {}
                                return e.matmul(ps[:, 0:nq], lhsT=KT[32 * u:32 * u + 32, c, kt * 128:(kt + 1) * 128],
                                                rhs=QT[32 * u:32 * u + 32, c, q0:q0 + nq], start=True, stop=True, **kw)

                            def pv_fn(e, ps, kt, pt, first, last, c=c, hh=hh, nq=nq):
                                return e.matmul(ps[:, 0:nq], lhsT=VA[:, kt, c, hh, :], rhs=pt, start=first, stop=last)
                            maps.append((s_fn, 32.0 ** -0.5, pv_fn, obs[comp]))

                        def fin(obs=obs, lo=lo, c=c, nq=nq, mxb=mxb, mxn=mxn, hh=hh, qb=qb):
                            flush()
                            dp = 64 - lo
                            R = slice(lo, lo + 64)
                            DP = slice(dp, dp + 64)
                            o1, o2 = obs
                            p.op("vector", lambda e: e.tensor_copy(out=S0[:, 0:nq], in_=PS[o1][:, 0:nq]),
                                 reads=["PS%d" % o1], writes=["S0"])
                            p.op("vector", lambda e: e.tensor_copy(out=S2[:, 0:nq], in_=PS[o2][:, 0:nq]),
                                 reads=["PS%d" % o2], writes=["S2"])
                            p.op("vector", lambda e: e.tensor_copy(out=S1[R, 0:nq], in_=S0[DP, 0:nq]), reads=["S0"], writes=["S1"])
                            p.op("vector", lambda e: e.reciprocal(out=S1[R, 0:nq], in_=S1[R, 0:nq]), reads=["S1"], writes=["S1"])
                            p.op("vector", lambda e: e.tensor_tensor(out=S0[R, 0:nq], in0=S0[R, 0:nq], in1=S1[R, 0:nq], op=ALU.mult),
                                 reads=["S0", "S1"], writes=["S0"])
                            p.op("vector", lambda e: e.tensor_copy(out=S1[R, 0:nq], in_=S2[DP, 0:nq]), reads=["S2", "S0"], writes=["S1"])
                            p.op("vector", lambda e: e.reciprocal(out=S1[R, 0:nq], in_=S1[R, 0:nq]), reads=["S1"], writes=["S1"])
                            p.op("vector", lambda e: e.tensor_tensor(out=S2[R, 0:nq], in0=S2[R, 0:nq], in1=S1[R, 0:nq], op=ALU.mult),
                                 reads=["S2", "S1"], writes=["S2"])
                            p.op("vector", lambda e: e.scalar_tensor_tensor(out=S0[R, 0:nq], in0=S2[R, 0:nq], scalar=NLAM[R, l:l + 1],
                                                                            in1=S0[R, 0:nq], op0=ALU.mult, op1=ALU.add),
                                 reads=["S2", "S0", "NLAM"], writes=["S0"])
                            p.op("scalar", lambda e: e.activation(out=SQ[R, 0:nq], in_=S0[R, 0:nq], func=AF.Square),
                                 reads=["S0"], writes=["SQ"])
                            mb = bank()
                            p.op("tensor", lambda e: e.matmul(PS[mb][:, 0:nq], lhsT=ONES[R, :], rhs=SQ[R, 0:nq], start=True, stop=True),
                                 reads=["SQ", "ONES"], writes=["PS%d" % mb])
                            p.op("scalar", lambda e: e.activation(out=S1[R, 0:nq], in_=PS[mb][R, 0:nq], func=AF.Ln,
                                                                  scale=1.0 / 64, bias=EPS), reads=["PS%d" % mb], writes=["S1"])
                            p.op("scalar", lambda e: e.activation(out=S1[R, 0:nq], in_=S1[R, 0:nq], func=AF.Exp, scale=-0.5, bias=lnf),
                                 reads=["S1"], writes=["S1"])
                            p.op("vector", lambda e: e.scalar_tensor_tensor(out=mxb[R, c, 0:nq], in0=S0[R, 0:nq], scalar=DNG[R, l:l + 1],
                                                                            in1=S1[R, 0:nq], op0=ALU.mult, op1=ALU.mult),
                                 reads=["S0", "S1", "DNG"], writes=[mxn + "_%d%d" % (c, hh)])
                            if c == 1 and hh == 1:
                                pend.append(lambda: outproj_block(l, s, so, lambda cc: mxb[:, cc, 0:nq],
                                            [mxn + "_%d%d" % (a_, b_) for a_ in range(2) for b_ in range(2)], qb))
                        olo = 64 - lo

                        def filler(e, ps, kt, c=c, olo=olo, q0=q0, nq=nq):
                            return e.matmul(ps[:, 0:nq], lhsT=KT[olo:olo + 64, c, kt * 128:(kt + 1) * 128],
                                            rhs=QT[olo:olo + 64, c, q0:q0 + nq], start=True, stop=True)
                        kn = ["KT%d_%d" % (c, t_) for t_ in range(NTB)]
                        groups.append(dict(maps=maps, nq=nq, kts=kts, reads=kn + ["QT%d_%d" % (c, qb)],
                                           filler=(filler if "afill" in KD else None),
                                           vreads=["VA1a", "VA1b"] + ["VAa%d" % t_ for t_ in kts] + ["VAb%d" % t_ for t_ in kts],
                                           fin=fin))
            bset[0] = [6, 7]
            attention_core(groups, PT, sbanks=[0, 1, 2, 3], G=2)
            flush()
            bset[0] = list(range(8))
            p.wrelease()


        def mixer_B(l, s):
            for hc in range(2):
                mixer_B_hc(l, s, hc)

        def mixer_B_hc(l, s, hc):
            if hc == 0:
                p.sync_all()
            cv = Carve()
            VZ = cv.bf16(18 * 2 * 128).rearrange("p (a h f) -> p a h f", h=2, f=128)
            OF = cv.f32(T)
            SZd = [[cv.f32(128) for _ in range(2)] for _ in range(2)]
            SHd = [[cv.bf16(128) for _ in range(2)] for _ in range(2)]
            F_ = cv.f32(512)
            L_ = cv.f32(512)
            K_ = cv.f32(512)
            P_ = cv.f32(512)
            EE = cv.f32(512)
            BBb = cv.f32(512)
            T1 = cv.f32(512)
            T2 = cv.f32(512)
            T3 = cv.f32(512)
            T4 = cv.f32(512)
            EK = cv.f32(512)
            RST = cv.bf16(512)
            QTLd = [cv.bf16(512) for _ in range(2)]
            KTLd = [cv.bf16(512) for _ in range(2)]
            QBd = [cv.bf16(512) for _ in range(2)]
            KETd = [cv.bf16(512) for _ in range(2)]
            EBd = [cv.f32(16) for _ in range(2)]
            AT = [cv.bf16(256).rearrange("p (h t) -> p h t", h=2) for _ in range(2)]
            KEZ = [cv.bf16(384) for _ in range(2)]
            MXB = [cv.bf16(512) for _ in range(2)]
            SQ = cv.bf16(512)
            p.op("vector", lambda e: e.memset(VZ.rearrange("p a h f -> p (a h f)"), 0.0), writes=["VZ0"])
            for i in range(2):
                p.op("vector", lambda e, i=i: e.memset(KEZ[i], 0.0), writes=["KEZ%d" % i])
            p.op("vector", lambda e: e.memset(RST, 1.0), writes=["RST"])
            p.op("vector", lambda e: e.memset(RST.rearrange("p (c t) -> p c t", t=32)[:, :, 0:1], 0.0),
                 reads=["RST"], writes=["RST"])
            sW = p.wnext(("wi", l, 768 + 512 * hc, 512))
            sG = p.wnext(("wi", l, 1792 + 128 * hc, 128))
            so = p.wnext(("wo", l, 256 + 128 * hc, 1))
            W = RING[:, sW, :].rearrange("p (k f) -> p k f", k=8)
            WG = RING[:, sG, 0:1024].rearrange("p (k f) -> p k f", k=8)
            Wo = RING[:, so, 0:1024]
            bset[0] = [0, 1]
            for tt in range(18):
                b = bank()
                proj_tm(sW, W, 384, 128, tt, b)
                evac(VZ[:, tt].rearrange("p h f -> p (h f)")[:, 0:256].rearrange("p (h x) -> p h x", h=2)[:, :, 0:64]
                     if False else VZ[:, tt, 0, 0:64], PS[b][:, 0:64], ["PS%d" % b, "VZ0"], ["VZa%d" % tt])
                evac(VZ[:, tt, 1, 64:128], PS[b][:, 64:128], ["PS%d" % b, "VZ0"], ["VZb%d" % tt])
            vzn = lambda tt: ["VZa%d" % tt, "VZb%d" % tt]

            def ew(dr, blk):
                t0, n = TB[blk]
                ntl = n // 128
                order = list(range(ntl)) if dr == 0 else list(range(ntl - 1, -1, -1))
                zc = 128 if dr == 0 else 256
                lbp = LB[:, dr * 2 + hc, l:l + 1]
                omp = OML[:, dr * 2 + hc, l:l + 1]
                nomp = NOML[:, dr * 2 + hc, l:l + 1]
                zb, qb_ = 0, 1
                QTL, KTL, QB, KET, EB = QTLd[dr], KTLd[dr], QBd[dr], KETd[dr], EBd[dr]
                D_ = "d%d_" % dr
                proj_fm(sW, W, zc, blk, zb)
                proj_fm(sW, W, 0, blk, qb_)

                def v3(buf, S):
                    return buf[:, S].rearrange("p (c t) -> p c t", t=32)

                def st_sig(tl, S, x):
                    p.op("scalar", lambda e: e.activation(out=F_[:, S], in_=PS[zb][:, S], func=AF.Sigmoid),
                         reads=["PS%d" % zb], writes=["FF" + x])

                def st_f(tl, S, x):
                    pass

                def st_ln(tl, S, x):
                    p.op("scalar", lambda e: e.activation(out=L_[:, S], in_=F_[:, S], func=AF.Ln, scale=omp, bias=lbp),
                         reads=["FF" + x, "LB", "OML"], writes=["LF" + x])

                def st_k(tl, S, x):
                    p.op("scalar", lambda e: e.activation(out=K_[:, S], in_=F_[:, S], func=AF.Identity, scale=nomp, bias=omp),
                         reads=["FF" + x, "OML"], writes=["KK" + x])

                def st_scan(tl, S, x):
                    p.op("vector", lambda e: e.tensor_tensor_scan(out=P_[:, S], data0=RST[:, S], data1=L_[:, S], initial=0.0,
                                                                  op0=ALU.mult, op1=ALU.add), reads=["RST", "LF" + x], writes=["PP" + x])

                def st_b(tl, S, x):
                    P3 = v3(P_, S)
                    TOTb = P3[:, :, 31:32].broadcast_to([128, 4, 32])
                    if dr == 1:
                        p.op("vector", lambda e: e.tensor_tensor(out=v3(EE, S), in0=TOTb, in1=P3, op=ALU.subtract),
                             reads=["PP" + x], writes=["EE" + x])
                        p.op("vector", lambda e: e.tensor_tensor(out=BBb[:, S], in0=EE[:, S], in1=L_[:, S], op=ALU.add),
                             reads=["EE" + x, "LF" + x], writes=["BB" + x])

                def st_e(tl, S, x):
                    P3 = v3(P_, S)
                    TOTb = P3[:, :, 31:32].broadcast_to([128, 4, 32])
                    B3 = P3 if dr == 0 else v3(BBb, S)
                    bn = ("PP" if dr == 0 else "BB") + x
                    p.op("vector", lambda e: e.tensor_tensor(out=v3(EE, S), in0=B3, in1=B3[:, :, 16:17].broadcast_to([128, 4, 32]),
                                                             op=ALU.subtract), reads=[bn], writes=["EE" + x])
                    if dr == 0:
                        p.op("vector", lambda e: e.tensor_tensor(out=v3(EK, S), in0=TOTb, in1=P3, op=ALU.subtract),
                             reads=["PP" + x], writes=["EK" + x])
                    else:
                        p.op("vector", lambda e: e.tensor_tensor(out=EK[:, S], in0=P_[:, S], in1=L_[:, S], op=ALU.subtract),
                             reads=["PP" + x, "LF" + x], writes=["EK" + x])

                def st_exp(tl, S, x):
                    Bv = P_ if dr == 0 else BBb
                    bn = ("PP" if dr == 0 else "BB") + x
                    p.op("scalar", lambda e: e.activation(out=T1[:, S], in_=EE[:, S], func=AF.Exp), reads=["EE" + x], writes=["T1" + x])
                    p.op("scalar", lambda e: e.activation(out=T2[:, S], in_=EE[:, S], func=AF.Exp, scale=-1.0), reads=["EE" + x], writes=["T2" + x])
                    p.op("scalar", lambda e: e.activation(out=T3[:, S], in_=Bv[:, S], func=AF.Exp), reads=[bn], writes=["T3" + x])
                    p.op("scalar", lambda e: e.activation(out=T4[:, S], in_=EK[:, S], func=AF.Exp), reads=["EK" + x], writes=["T4" + x])
                    p.op("scalar", lambda e: e.activation(out=EB[:, tl * 4:tl * 4 + 4], in_=v3(P_, S)[:, :, 31], func=AF.Exp),
                         reads=["PP" + x], writes=[D_ + "EB" + x])

                def st_mul(tl, S, x):
                    p.op("vector", lambda e: e.tensor_tensor(out=QTL[:, S], in0=PS[qb_][:, S], in1=T1[:, S], op=ALU.mult),
                         reads=["PS%d" % qb_, "T1" + x], writes=[D_ + "QTL" + x])
                    p.op("vector", lambda e: e.tensor_tensor(out=KTL[:, S], in0=K_[:, S], in1=T2[:, S], op=ALU.mult),
                         reads=["KK" + x, "T2" + x], writes=[D_ + "KTL" + x])
                    p.op("vector", lambda e: e.tensor_tensor(out=QB[:, S], in0=PS[qb_][:, S], in1=T3[:, S], op=ALU.mult),
                         reads=["PS%d" % qb_, "T3" + x], writes=[D_ + "QB" + x])
                    p.op("vector", lambda e: e.tensor_tensor(out=KET[:, S], in0=K_[:, S], in1=T4[:, S], op=ALU.mult),
                         reads=["KK" + x, "T4" + x], writes=[D_ + "KET" + x])

                for stg in (st_sig, st_f, st_ln, st_k, st_scan, st_b, st_e, st_exp, st_mul):
                    for tl in order:
                        stg(tl, slice(tl * 128, tl * 128 + 128), str(tl))

            cnt = {"sc": 0, "tr": 0, "o": 0, "st": 0, "sz0": 0, "sz1": 0}
            of_done = set()

            def tile_ops(dr, blk, tl):
                t0, n = TB[blk]
                lo = tl * 128
                tt = (t0 + lo) // 128
                QTL, KTL, QB, KET, EB = QTLd[dr], KTLd[dr], QBd[dr], KETd[dr], EBd[dr]
                SZ = SZd[dr]
                D_ = "d%d_" % dr
                sbs = [0, 1]
                ai = cnt["sc"] % 2
                cnt["sc"] += 1
                at = AT[ai]
                for h in range(2):
                    R = slice(64 * h, 64 * h + 64)
                    p.op("tensor", lambda e, h=h, R=R: e.matmul(PS[sbs[h]][:, 0:128], lhsT=KTL[R, lo:lo + 128],
                                                                rhs=QTL[R, lo:lo + 128], start=True, stop=True),
                         reads=[D_ + "KTL%d" % tl, D_ + "QTL%d" % tl], writes=["PS%d" % sbs[h]])
                for h in range(2):
                    p.op("vector", lambda e, h=h: e.tensor_tensor(out=at[:, h, :], in0=PS[sbs[h]][:, 0:128],
                                                                  in1=BMASK[:, dr, :], op=ALU.mult),
                         reads=["PS%d" % sbs[h], "BMASK"], writes=["AT%d_%d" % (ai, h)])
                tb_ = 4
                ki = cnt["tr"] % 2
                cnt["tr"] += 1
                trv = PS[tb_][:, 0:64].bitcast(BF16)
                kez = KEZ[ki]
                if "b_notr" not in KD:
                    p.op("tensor", lambda e: e.transpose(out=trv, in_=KET[:, lo:lo + 128], identity=IDENT[:]),
                         reads=[D_ + "KET%d" % tl, "IDENT"], writes=["PS%d" % tb_])
                    evac(kez.rearrange("p (h x) -> p h x", x=192)[:, :, 0:64], trv.rearrange("p (h x) -> p h x", x=64),
                         ["PS%d" % tb_], ["KEZ%d" % ki])
                ob = 5 + dr
                cnt["o"] += 1

                def fpv(e):
                    ins = None
                    for h in range(2):
                        ins = e.matmul(PS[ob][:, 0:128], lhsT=VZ[:, tt, h, :], rhs=at[:, h, :], start=(h == 0), stop=False)
                    return ins
                p.op("tensor", fpv, reads=["AT%d_0" % ai, "AT%d_1" % ai] + vzn(tt), writes=["PS%d" % ob])
                yield
                chs = range(4) if dr == 0 else range(3, -1, -1)
                if "b_nostate" in KD:
                    chs = []
                for ci, c in enumerate(chs):
                    zi = cnt["sz%d" % dr] % 2
                    szr = SHd[dr][zi]
                    co = lo + c * 32

                    def fin_(e, c=c, ci=ci, szr=szr, co=co):
                        return e.matmul(PS[ob][:, c * 32:(c + 1) * 32], lhsT=szr[:, :], rhs=QB[:, co:co + 32],
                                        start=False, stop=(ci == 3))
                    p.op("tensor", fin_, reads=[D_ + "SH%d" % zi, D_ + "QB%d" % tl], writes=["PS%d" % ob])
                    stb = (2, 3, 7)[cnt["st"] % 3]
                    cnt["st"] += 1

                    def fst(e, c=c, stb=stb):
                        ins = None
                        for h in range(2):
                            kw = dict(tile_position=(96, 0)) if c == 3 else {}
                            ins = e.matmul(PS[stb][:, 0:128], lhsT=kez[32 * c:32 * c + 32, 128 * h:128 * h + 128],
                                           rhs=VZ[32 * c:32 * c + 32, tt, h, :], start=(h == 0), stop=(h == 1), **kw)
                        return ins
                    p.op("tensor", fst, reads=["KEZ%d" % ki] + vzn(tt), writes=["PS%d" % stb])
                    cg = (lo // 32) + c
                    cnt["sz%d" % dr] += 1
                    zn = cnt["sz%d" % dr] % 2
                    p.op("vector", lambda e, stb=stb, cg=cg, zi=zi, zn=zn: e.scalar_tensor_tensor(
                        out=SZ[zn], in0=SZ[zi], scalar=EB[:, cg:cg + 1], in1=PS[stb][:, 0:128],
                        op0=ALU.mult, op1=ALU.add), reads=[D_ + "SZ%d" % zi, D_ + "EB%d" % tl, "PS%d" % stb], writes=[D_ + "SZ%d" % zn])
                    p.op("vector", lambda e, zn=zn: e.tensor_copy(out=SHd[dr][zn], in_=SZ[zn]),
                         reads=[D_ + "SZ%d" % zn], writes=[D_ + "SH%d" % zn])
                    yield
                tok = slice(t0 + lo, t0 + lo + 128)
                if tt not in of_done:
                    of_done.add(tt)
                    p.op("scalar", lambda e: e.activation(out=OF[:, tok], in_=PS[ob][:, 0:128], func=AF.Copy),
                         reads=["PS%d" % ob], writes=["OF%d" % tt])
                else:
                    p.op("vector", lambda e: e.tensor_tensor(out=OF[:, tok], in0=PS[ob][:, 0:128], in1=OF[:, tok], op=ALU.add),
                         reads=["PS%d" % ob, "OF%d" % tt], writes=["OF%d" % tt])

            def sweep(dr):
                SZ = SZd[dr]
                D_ = "d%d_" % dr
                p.op("vector", lambda e: e.memset(SZ[cnt["sz%d" % dr] % 2], 0.0), writes=[D_ + "SZ%d" % (cnt["sz%d" % dr] % 2)])
                p.op("vector", lambda e: e.memset(SHd[dr][cnt["sz%d" % dr] % 2], 0.0), writes=[D_ + "SH%d" % (cnt["sz%d" % dr] % 2)])
                blocks = [0, 1, 2, 3, 4] if dr == 0 else [0, 4, 3, 2, 1]
                for blk in blocks:
                    ntl = TB[blk][1] // 128
                    ew(dr, blk)
                    yield
                    tls = range(ntl) if dr == 0 else range(ntl - 1, -1, -1)
                    for tl in tls:
                        yield from tile_ops(dr, blk, tl)
                        yield

            gens = [sweep(0), sweep(1)]
            for _ in range(int(os.environ.get("BSTAG", "0"))):
                next(gens[0])
            while gens:
                for g_ in list(gens):
                    try:
                        next(g_)
                    except StopIteration:
                        gens.remove(g_)
            p.wrelease()
            bset[0] = [0, 1, 2, 3]
            S0, S1 = F_, L_
            for blk, (t0, n) in enumerate(TB):
                N = slice(0, n)
                col = 0 if blk == 0 else 1 + s
                mxb = MXB[blk % 2]
                mxn = "MXB%d" % (blk % 2)
                tok = slice(t0, t0 + n)
                ofn = ["OF%d" % t_ for t_ in range(t0 // 128, (t0 + n) // 128)]
                p.op("scalar", lambda e, tok=tok, N=N: e.activation(out=SQ[:, N], in_=OF[:, tok], func=AF.Square),
                     reads=ofn, writes=["SQ"])
                b2 = bank()
                p.op("tensor", lambda e, b2=b2, N=N: e.matmul(PS[b2][:, N], lhsT=BONES[:], rhs=SQ[:, N], start=True, stop=True),
                     reads=["SQ", "BONES"], writes=["PS%d" % b2])
                p.op("scalar", lambda e, b2=b2, N=N: e.activation(out=S0[:, N], in_=PS[b2][:, N], func=AF.Ln, scale=1.0 / 64, bias=EPS),
                     reads=["PS%d" % b2], writes=["FF"])
                p.op("scalar", lambda e, N=N: e.activation(out=S0[:, N], in_=S0[:, N], func=AF.Exp, scale=-0.5), reads=["FF"], writes=["FF"])
                p.op("vector", lambda e, tok=tok, N=N: e.scalar_tensor_tensor(
                    out=S0[:, N], in0=OF[:, tok], scalar=HNG[:, l:l + 1], in1=S0[:, N], op0=ALU.mult, op1=ALU.mult),
                    reads=ofn + ["FF", "HNG"], writes=["FF"])
                bg = bank()
                proj_fm(sG, WG, 0, blk, bg)
                p.op("scalar", lambda e, bg=bg, N=N: e.activation(out=S1[:, N], in_=PS[bg][:, N], func=AF.Silu),
                     reads=["PS%d" % bg], writes=["LF"])
                p.op("vector", lambda e, mxb=mxb, N=N: e.tensor_tensor(out=mxb[:, N], in0=S0[:, N], in1=S1[:, N], op=ALU.mult),
                     reads=["FF", "LF"], writes=[mxn])
                bset[0] = [4, 5, 6, 7]
                for m in range(8):
                    b = bank()
                    p.op("tensor", lambda e, b=b, m=m, mxb=mxb, N=N: e.matmul(PS[b][:, N], lhsT=Wo[:, m * 128:(m + 1) * 128],
                                                                              rhs=mxb[:, N], start=True, stop=True),
                         reads=["WS%d" % so, mxn], writes=["PS%d" % b])
                    p.op("vector", lambda e, b=b, m=m, N=N, tok=tok, col=col: e.scalar_tensor_tensor(
                        out=XT[:, m, tok], in0=PS[b][:, N], scalar=MOD[:, l, 16 + m, col:col + 1],
                        in1=XT[:, m, tok], op0=ALU.mult, op1=ALU.add),
                        reads=["PS%d" % b, "MOD%d_%d" % (l, col)] + xtn([m], blk), writes=xtn([m], blk))
                bset[0] = [0, 1, 2, 3]
            bset[0] = list(range(8))
            p.wrelease(2)

        def mixer_C(l, s):
            p.sync_all()
            cv = Carve()
            TABc, TABs = load_rope(cv, 1)
            QT = cv.bf16(2 * T).rearrange("p (c t) -> p c t", c=2)
            KT = cv.bf16(2 * T).rearrange("p (c t) -> p c t", c=2)
            VA = cv.bf16(18 * 2 * 192).rearrange("p (a h f) -> p a h f", h=2, f=192)
            MXB = [cv.bf16(1024).rearrange("p (c t) -> p c t", c=2) for _ in range(2)]
            PT = [cv.bf16(512) for _ in range(4)]
            S0 = cv.f32(512)
            S1 = cv.f32(512)
            S2 = cv.f32(512)
            SQ = cv.bf16(512)
            QN = cv.bf16(512)
            if "nomemset" not in KD:
                p.op("vector", lambda e: e.memset(VA.rearrange("p a h f -> p (a h) f")[:, :, 64:128], 1.0), writes=["VA1"])
            sw = p.wnext(("wi", l, COL_CQK, 512))
            W = RING[:, sw, :].rearrange("p (k f) -> p k f", k=8)
            for X in range(4):
                isq = X < 2
                c = X % 2
                dst = QT if isq else KT
                dn = "QT" if isq else "KT"
                for tbi, (t0, n) in enumerate(TB):
                    b = bank()
                    proj_fm(sw, W, X * 128, tbi, b)
                    p.op("scalar", lambda e, b=b, n=n: e.activation(out=SQ[:, 0:n], in_=PS[b][:, 0:n], func=AF.Square),
                         reads=["PS%d" % b], writes=["SQ"])
                    b2 = bank()
                    p.op("tensor", lambda e, b2=b2, n=n: e.matmul(PS[b2][:, 0:n], lhsT=BONES[:], rhs=SQ[:, 0:n], start=True, stop=True),
                         reads=["SQ", "BONES"], writes=["PS%d" % b2])
                    p.op("scalar", lambda e, b2=b2, n=n: e.activation(out=S0[:, 0:n], in_=PS[b2][:, 0:n], func=AF.Ln,
                                                                     scale=1.0 / 64, bias=EPS), reads=["PS%d" % b2], writes=["S0"])
                    p.op("scalar", lambda e, n=n: e.activation(out=S0[:, 0:n], in_=S0[:, 0:n], func=AF.Exp, scale=-0.5),
                         reads=["S0"], writes=["S0"])
                    gap = QKG[:, l, (0 if isq else 1):(1 if isq else 2)]
                    if tbi == 0:
                        p.op("vector", lambda e, b=b, n=n, t0=t0, dst=dst, c=c, gap=gap: e.scalar_tensor_tensor(
                            out=dst[:, c, t0:t0 + n], in0=PS[b][:, 0:n], scalar=gap, in1=S0[:, 0:n],
                            op0=ALU.mult, op1=ALU.mult), reads=["PS%d" % b, "S0", "QKG"], writes=["%s%d_%d" % (dn, c, tbi)])
                    else:
                        p.op("vector", lambda e, b=b, n=n, gap=gap: e.scalar_tensor_tensor(
                            out=QN[:, 0:n], in0=PS[b][:, 0:n], scalar=gap, in1=S0[:, 0:n],
                            op0=ALU.mult, op1=ALU.mult), reads=["PS%d" % b, "S0", "QKG"], writes=["QN"])
                        if "norope" in KD:
                            evac(dst[:, c, t0:t0 + n], QN[:, 0:n], ["QN"], ["%s%d_%d" % (dn, c, tbi)], eng="vector")
                        else:
                            rope_apply(1, QN, TABc, TABs, S1, S2, dst[:, c, t0:t0 + n], ["%s%d_%d" % (dn, c, tbi)], tbi)
            p.wrelease()
            vwn = 256 if "vw256" in KD else 128
            sv = p.wnext(("wi", l, COL_CV, vwn))
            Wv = RING[:, sv, 0:8 * vwn].rearrange("p (k f) -> p k f", k=8)
            for tt in range(0 if "nov" in KD else 18):
                b = bank()
                proj_tm(sv, Wv, 0, 128, tt, b)
                psv = PS[b][:, 0:128].rearrange("p (h f) -> p h f", h=2)
                if "vnoevac" in KD:
                    continue
                if "v2d" in KD:
                    ve = "scalar" if "vact" in KD else ("vector" if "vdve" in KD else None)
                    for h in range(2):
                        evac(VA[:, tt, h, 0:64], psv[:, h, :], ["PS%d" % b], ["VAa%d_%d" % (tt, h)], eng=ve)
                        if "vone" not in KD:
                            evac(VA[:, tt, h, 128:192], psv[:, h, :], ["PS%d" % b], ["VAb%d_%d" % (tt, h)], eng=ve)
                    continue
                evac(VA[:, tt, :, 0:64], psv, ["PS%d" % b], ["VAa%d" % tt])
                evac(VA[:, tt, :, 128:192], psv, ["PS%d" % b], ["VAb%d" % tt])
            p.wrelease()
            so = p.wnext(("wo", l, 512, 2))
            pend = []

            def flush():
                while pend:
                    pend.pop(0)()
            groups = []
            for qb, (q0, nq) in enumerate(TB):
                kts = [0, 1] if qb == 0 else list(range(18))
                mxb = MXB[qb % 2]
                mxn = "MXB%d" % (qb % 2)
                for c in range(2):
                    maps = []
                    for hh in range(2):
                        lo = 64 * hh

                        def s_fn(e, ps, kt, c=c, lo=lo, q0=q0, nq=nq):
                            return e.matmul(ps[:, 0:nq], lhsT=KT[lo:lo + 64, c, kt * 128:(kt + 1) * 128],
                                            rhs=QT[lo:lo + 64, c, q0:q0 + nq], start=True, stop=True)

                        def pv_fn(e, ps, kt, pt, first, last, c=c, lo=lo, nq=nq):
                            return e.matmul(ps[:, 0:nq], lhsT=VA[:, kt, c, lo:lo + 128], rhs=pt, start=first, stop=last)
                        maps.append((s_fn, 0.125, pv_fn, 4 + hh))

                    def fin(c=c, nq=nq, mxb=mxb, mxn=mxn, qb=qb):
                        flush()
                        N = slice(0, nq)
                        p.op("vector", lambda e: e.tensor_copy(out=S0[:, N], in_=PS[4][:, N]), reads=["PS4"], writes=["S0"])
                        p.op("vector", lambda e: e.tensor_copy(out=S2[:, N], in_=PS[5][:, N]), reads=["PS5"], writes=["S2"])
                        p.op("vector", lambda e: e.tensor_copy(out=S1[0:64, N], in_=S0[64:128, N]), reads=["S0"], writes=["S1"])
                        p.op("vector", lambda e: e.tensor_copy(out=S1[64:128, N], in_=S2[0:64, N]), reads=["S2", "S1"], writes=["S1"])
                        p.op("vector", lambda e: e.reciprocal(out=S1[:, N], in_=S1[:, N]), reads=["S1"], writes=["S1"])
                        p.op("vector", lambda e: e.tensor_tensor(out=mxb[0:64, c, N], in0=S0[0:64, N], in1=S1[0:64, N], op=ALU.mult),
                             reads=["S0", "S1"], writes=[mxn + "_%d0" % c])
                        p.op("vector", lambda e: e.tensor_tensor(out=mxb[64:128, c, N], in0=S2[64:128, N], in1=S1[64:128, N], op=ALU.mult),
                             reads=["S2", "S1"], writes=[mxn + "_%d1" % c])
                        if c == 1:
                            pend.append(lambda: outproj_block(l, s, so, lambda cc: mxb[:, cc, 0:nq],
                                        [mxn + "_%d%d" % (a_, b_) for a_ in range(2) for b_ in range(2)], qb))
                    kn = ["KT%d_%d" % (c, t_) for t_ in range(NTB)]
                    groups.append(dict(maps=maps, nq=nq, kts=kts, reads=kn + ["QT%d_%d" % (c, qb)],
                                       vreads=["VA1"] + ["VAa%d" % t_ for t_ in kts] + ["VAb%d" % t_ for t_ in kts], fin=fin))
            bset[0] = [6, 7]
            attention_core(groups, PT, sbanks=[0, 1, 2, 3], G=2)
            flush()
            bset[0] = list(range(8))
            p.wrelease()

        for s in range(NS):
            for k in range(8):
                p.op("sync", lambda e, k=k, s=s: e.dma_start(out=XT[:, k, :], in_=xT[s, k * 128:(k + 1) * 128, :]),
                     writes=xtn([k], 0) + xtn([k], 1) + xtn([k], 2) + xtn([k], 3) + xtn([k], 4), dma_sem="xt")
            p.seal([n_ for tbi in range(NTB) for n_ in xtn(ALLK, tbi)], "xt")
            for l in range(NL):
                norm_phase(l, s, 1)
                if "A" in mixers:
                    mixer_A(l, s)
                if "B" in mixers:
                    mixer_B(l, s)
                if "C" in mixers:
                    mixer_C(l, s)
                if "D" in mixers:
                    mixer_D(l, s)
                norm_phase(l, s, 2)
                ffn_phase(l, s)
            final_phase(s)
        if not p.dry:
            for sem in ("oOB0", "oOB1"):
                p.final_waits[sem] = p.cnt[sem]

    p0 = Prog(dry=True)
    gen(p0)
    p = Prog(dry=False, wlist=p0.wrec)
    gen(p)
    assert p.wpos == len(p.wlist)

    sems = {name: es.enter_context(nc.semaphore(name)) for name in sorted(p.semnames)}
    with nc.Block() as block:
        for eng in ENGS:
            def section(e, eng=eng):
                for waits, fn, sem, inc in p.streams[eng]:
                    for wsem, wval in waits:
                        e.wait_ge(sems[wsem], wval)
                    ins = fn(e)
                    ins.then_inc(sems[sem], inc)
                if eng == "sync":
                    for wsem, wval in p.final_waits.items():
                        e.wait_ge(sems[wsem], wval)
            getattr(block, eng)(section)
    es.close()
    ninstr = {e: len(p.streams[e]) for e in ENGS}
    return nc, ninstr


_CONST = {}
import os
KD = set(os.environ.get('KDBG', '').split(','))


def _consts():
    if _CONST:
        return _CONST
    bf = ml_dtypes.bfloat16
    i = np.arange(64)
    ang = 2 * np.pi * np.outer(i, i) / 64
    C64, S64 = np.cos(ang), np.sin(ang)
    bd = np.zeros((128, 256))
    for h in range(2):
        bd[64 * h:64 * h + 64, 64 * h:64 * h + 64] = C64
        bd[64 * h:64 * h + 64, 128 + 64 * h:128 + 64 * h + 64] = -S64
    n = np.arange(SEQ)
    ang = 2 * np.pi * ((np.outer(n, n) % SEQ).astype(np.float64)) / SEQ
    sc = 1.0 / np.sqrt(64.0 * SEQ)
    dftl = np.stack([np.cos(ang) * sc, np.sin(ang) * sc]).astype(bf)
    m = np.arange(TCX)
    angc = 2 * np.pi * ((np.outer(m, m) % TCX).astype(np.float64)) / TCX
    scc = 1.0 / np.sqrt(64.0 * TCX)
    dftc = np.stack([np.cos(angc) * scc, np.sin(angc) * scc], axis=1)
    dftc = dftc.reshape(2, 128, 2, TCX).transpose(1, 0, 2, 3).reshape(128, 1024)
    row = np.repeat(np.arange(SEQ // 64), 64).astype(np.float64)
    col = np.tile(np.arange(64), SEQ // 64).astype(np.float64)
    rope = np.zeros((4, 128, SEQ))
    perm = np.zeros((128, 256))
    for w, unit in enumerate((32, 64)):
        half, nf = unit // 2, unit // 4
        inv = (np.float32(10000.0) ** (-np.arange(nf, dtype=np.float32) / np.float32(nf))).astype(np.float64)
        for pp in range(128):
            d = pp % unit
            dh = d % half
            pos = row if d < half else col
            a = pos * inv[dh % nf]
            rope[2 * w, pp] = np.cos(a)
            rope[2 * w + 1, pp] = np.sin(a)
            if dh < nf:
                perm[pp + nf, 128 * w + pp] = -1.0
            else:
                perm[pp - nf, 128 * w + pp] = 1.0
    ss_, tt_ = np.meshgrid(np.arange(128), np.arange(128), indexing="ij")
    same = (ss_ // 32) == (tt_ // 32)
    bmask = np.concatenate([(same & (ss_ <= tt_)), (same & (ss_ >= tt_))], axis=1).astype(np.float64)
    _CONST.update(ident=np.eye(128).astype(bf), bmask=bmask.astype(bf))
    _CONST.update(bd=bd.astype(bf), dftl=dftl, dftc=dftc.astype(bf), rope=rope.astype(bf), perm=perm.astype(bf))
    return _CONST


def host_prepare(inputs, ncores, NS):
    f = lambda a: np.ascontiguousarray(np.asarray(a, dtype=np.float32))
    x, c, ctx, c_ctx = (np.asarray(inputs[k]) for k in ("x", "c", "ctx", "c_ctx"))

    def fm(v):
        v = np.asarray(v, dtype=np.float32)
        lead = v.shape[:-1]
        return np.moveaxis(v.reshape(*lead, 8, 128), -1, 0)

    shared = {
        "w_mod": f(inputs["w_mod"]),
        "b_modT": f(np.moveaxis(np.asarray(inputs["b_mod"]).reshape(DEPTH, 48, 128), -1, 0).reshape(128, DEPTH * 48)),
        "n1T": f(fm(inputs["norm1"]).reshape(128, DEPTH * KC)),
        "n2T": f(fm(inputs["norm2"]).reshape(128, DEPTH * KC)),
        "fnT": f(fm(inputs["final_norm"]).reshape(128, KC)),
        "w_in": f(np.asarray(inputs["w_in"])[:, :, WIN_IDX]),
        "qkg": f(np.stack([np.tile(np.asarray(inputs["q_norm"]), (1, 2)), np.tile(np.asarray(inputs["k_norm"]), (1, 2))],
                          axis=-1).transpose(1, 0, 2).reshape(128, DEPTH * 2)),
        "dng": f(np.tile(np.asarray(inputs["diff_norm"]), (1, 2)).T),
        "hng": f(np.tile(np.asarray(inputs["hgrn_norm"]), (1, 2)).T),
        "lbl": f(np.asarray(inputs["hgrn_lb_logits"]).reshape(2, DEPTH, 2, 128).transpose(3, 0, 2, 1).reshape(128, 4 * DEPTH)),
        "dlam": f(np.broadcast_to(np.asarray(inputs["diff_lambda"]).reshape(1, DEPTH * 128), (128, DEPTH * 128))),
        "w_out": f(inputs["w_out"]),
        "w_fi": f(inputs["w_ffn_in"]),
        "w_fo": f(inputs["w_ffn_out"]),
    }
    shared.update(_consts())
    in_maps = []
    for ci in range(ncores):
        bs = list(range(ci * NS, (ci + 1) * NS))
        xt = np.empty((NS, D, T), np.float32)
        for j, b in enumerate(bs):
            xt[j, :, :TCX] = ctx[b].T
            xt[j, :, TCX:] = x[b].T
        cols = [c_ctx] + [c[b] for b in bs] + [c_ctx] * (2 - NS)
        cv = np.stack([np.asarray(v, np.float32) for v in cols], axis=-1)
        cv = cv.reshape(8, 128, 3).transpose(1, 0, 2).reshape(128, 24)
        m = dict(shared)
        m["xT"] = xt
        m["cvec"] = f(cv)
        in_maps.append(m)
    return in_maps


_CACHE = {}


def run(inputs, ncores=8, NS=2, NL=DEPTH, mixers="ABCD"):
    key = (NS, NL, mixers)
    if key not in _CACHE:
        _CACHE[key] = build_program(NS, NL, mixers)
    nc, ninstr = _CACHE[key]
    in_maps = host_prepare(inputs, ncores, NS)
    res = run_bass_kernel_spmd(nc, in_maps, core_ids=list(range(ncores)))
    out = np.empty((ncores * NS, SEQ, D), np.float32)
    for ci in range(ncores):
        o = res.results[ci]["outT"]
        for j in range(NS):
            out[ci * NS + j] = o[j].T
    return out


def kernel(**inputs):
    return run(inputs, ncores=8, NS=2, NL=DEPTH, mixers="ABCD")
```
